# Optimizing a Trainium2 kernel written in Bass

```python
import math
import jax, jax.numpy as jnp
from jax import lax
import numpy as np

D_MODEL = 1024
BATCH = 4
SEQ = 8192
DEPTH = 2

CHUNK = 64
SB_HEADS = 4
SB_HEAD_DIM = 64
SB_BLOCK = 128
GDN_HEADS = 4
GDN_HEAD_DIM = 128
GDN_CONV = 4
SC_WIDTH = 256
SC_CONV = 3
D_FF = 2816
PLE_DIM = 256
LN_EPS = 1e-5
NORM_EPS = 1e-6
ALPHA = (2 * DEPTH) ** 0.25
BETA = (8 * DEPTH) ** -0.25

SB_WIDTH = SB_HEADS * SB_HEAD_DIM
GDN_WIDTH = GDN_HEADS * GDN_HEAD_DIM
MIX_WIDTH = SB_WIDTH + GDN_WIDTH + SC_WIDTH
OFF_SB = 3 * SB_WIDTH
OFF_GDN_QKV = OFF_SB + 3 * GDN_WIDTH
OFF_GDN_Z = OFF_GDN_QKV + GDN_WIDTH
OFF_GDN_A = OFF_GDN_Z + GDN_HEADS
OFF_GDN_B = OFF_GDN_A + GDN_HEADS
IN_COLS = OFF_GDN_B + 3 * SC_WIDTH

kernel_name = 'hybrid_streaming_encoder_block'


def layer_norm(x, g, b):
    xf = x.astype(jnp.float32)
    mu = jnp.mean(xf, axis=-1, keepdims=True)
    var = jnp.mean(jnp.square(xf - mu), axis=-1, keepdims=True)
    return ((xf - mu) * lax.rsqrt(var + LN_EPS) * g + b).astype(x.dtype)


def swiglu(x, w_in, w_out):
    gate, up = jnp.split(x @ w_in, 2, axis=-1)
    return (jax.nn.silu(gate) * up) @ w_out


def causal_dwconv(x, w):
    K = w.shape[0]
    S = x.shape[1]
    xp = jnp.pad(x, ((0, 0), (K - 1, 0), (0, 0)))
    return sum(xp[:, i:i + S] * w[i] for i in range(K))


def l2_normalize(t):
    tf = t.astype(jnp.float32)
    return tf * lax.rsqrt(jnp.sum(tf * tf, axis=-1, keepdims=True) + NORM_EPS)


def gated_rms_norm(o, z, w):
    of = o.astype(jnp.float32)
    y = of * lax.rsqrt(jnp.mean(of * of, axis=-1, keepdims=True) + NORM_EPS) * w
    return (y * jax.nn.silu(z.astype(jnp.float32))).astype(z.dtype)


def stick_breaking_attention(q, k, v):
    S = q.shape[2]
    scale = q.shape[-1] ** -0.5
    outs = []
    for t0 in range(0, S, SB_BLOCK):
        t1 = t0 + SB_BLOCK
        qb = q[:, :, t0:t1].astype(jnp.float32)
        kp = k[:, :, :t1].astype(jnp.float32)
        vp = v[:, :, :t1].astype(jnp.float32)
        z = jnp.einsum('bhqd,bhkd->bhqk', qb, kp) * scale
        mask = jnp.arange(t1)[None, :] < jnp.arange(t0, t1)[:, None]
        log_skip = jnp.where(mask, jax.nn.log_sigmoid(-z), 0.0)
        later = lax.cumsum(log_skip, axis=3, reverse=True) - log_skip
        att = jnp.where(mask, jnp.exp(jax.nn.log_sigmoid(z) + later), 0.0)
        outs.append(jnp.einsum('bhqk,bhkd->bhqd', att, vp))
    return jnp.concatenate(outs, axis=2).astype(v.dtype)


def gated_delta_rule_chunked(q, k, v, g, beta):
    Bsz, S, H, dk = q.shape
    dv = v.shape[-1]
    n = S // CHUNK

    def blocks(t):
        t = t.reshape(Bsz, n, CHUNK, H, *t.shape[3:])
        return jnp.swapaxes(jnp.moveaxis(t, 1, 0), 2, 3)

    q, k, v, g, beta = blocks(q), blocks(k), blocks(v), blocks(g), blocks(beta)
    gcum = jnp.cumsum(g, axis=-1)
    incl = jnp.tril(jnp.ones((CHUNK, CHUNK), dtype=bool))
    strict = jnp.tril(jnp.ones((CHUNK, CHUNK), dtype=bool), k=-1)
    decay = jnp.exp(jnp.where(incl, gcum[..., :, None] - gcum[..., None, :], -jnp.inf))
    k_beta = k * beta[..., None]
    m = jnp.where(strict, jnp.einsum('nbhik,nbhjk->nbhij', k_beta, k) * decay, 0.0)
    eye = jnp.eye(CHUNK, dtype=m.dtype)
    t_inv = lax.linalg.triangular_solve(eye + m, jnp.broadcast_to(eye, m.shape),
                                        left_side=True, lower=True)
    u = jnp.einsum('nbhij,nbhjv->nbhiv', t_inv, v * beta[..., None])
    w = jnp.einsum('nbhij,nbhjk->nbhik', t_inv, k_beta * jnp.exp(gcum)[..., None])
    qk = jnp.einsum('nbhik,nbhjk->nbhij', q, k) * decay
    q_dec = q * jnp.exp(gcum)[..., None]
    k_dec = k * jnp.exp(gcum[..., -1:] - gcum)[..., None]
    g_last = jnp.exp(gcum[..., -1])

    def step(state, xs):
        u_c, w_c, qk_c, q_c, k_c, gl = xs
        v_new = u_c - jnp.einsum('bhik,bhkv->bhiv', w_c, state)
        o = jnp.einsum('bhik,bhkv->bhiv', q_c, state) + jnp.einsum('bhij,bhjv->bhiv', qk_c, v_new)
        state = state * gl[..., None, None] + jnp.einsum('bhik,bhiv->bhkv', k_c, v_new)
        return state, o

    s0 = jnp.zeros((Bsz, H, dk, dv), jnp.float32)
    _, o = lax.scan(step, s0, (u, w, qk, q_dec, k_dec, g_last))
    return jnp.moveaxis(jnp.swapaxes(o, 2, 3), 0, 1).reshape(Bsz, S, H, dv)


def token_mix(h, w_in, gdn_conv_w, gdn_a_log, gdn_dt_bias, gdn_norm_w, sc_conv_w, w_out):
    Bsz, S, _ = h.shape
    proj = h @ w_in
    sb_qkv, gdn_qkv, gdn_z, gdn_a, gdn_b, sc_bch = jnp.split(
        proj, [OFF_SB, OFF_GDN_QKV, OFF_GDN_Z, OFF_GDN_A, OFF_GDN_B], axis=-1)

    sb_q, sb_k, sb_v = [t.reshape(Bsz, S, SB_HEADS, SB_HEAD_DIM).transpose(0, 2, 1, 3)
                        for t in jnp.split(sb_qkv, 3, axis=-1)]
    o_sb = stick_breaking_attention(sb_q, sb_k, sb_v).transpose(0, 2, 1, 3).reshape(Bsz, S, SB_WIDTH)

    gdn_qkv = jax.nn.silu(causal_dwconv(gdn_qkv, gdn_conv_w))
    g_q, g_k, g_v = [t.reshape(Bsz, S, GDN_HEADS, GDN_HEAD_DIM) for t in jnp.split(gdn_qkv, 3, axis=-1)]
    g_q = l2_normalize(g_q) * (GDN_HEAD_DIM ** -0.5)
    g_k = l2_normalize(g_k)
    beta = jax.nn.sigmoid(gdn_b.astype(jnp.float32))
    log_decay = -jnp.exp(gdn_a_log.astype(jnp.float32)) * jax.nn.softplus(
        gdn_a.astype(jnp.float32) + gdn_dt_bias.astype(jnp.float32))
    o_gdn = gated_delta_rule_chunked(g_q, g_k, g_v.astype(jnp.float32), log_decay, beta)
    o_gdn = gated_rms_norm(o_gdn, gdn_z.reshape(Bsz, S, GDN_HEADS, GDN_HEAD_DIM), gdn_norm_w)
    o_gdn = o_gdn.reshape(Bsz, S, GDN_WIDTH)

    sc_b, sc_c, sc_h = jnp.split(sc_bch, 3, axis=-1)
    o_sc = sc_b * causal_dwconv(sc_c * sc_h, sc_conv_w)

    mixed = jnp.concatenate([o_sb.astype(h.dtype), o_gdn.astype(h.dtype), o_sc], axis=-1)
    return mixed @ w_out


def setup_inputs(seed: int = 0) -> dict:
    key = jax.random.key(seed)
    ks = jax.random.split(key, 16)

    def nrm(k, shape, scale):
        return scale * jax.random.normal(k, shape, jnp.float32)

    dt = jnp.exp(jax.random.uniform(ks[9], (DEPTH, GDN_HEADS), jnp.float32,
                                    math.log(1e-3), math.log(1e-1)))
    return {
        'x': nrm(ks[0], (BATCH, SEQ, D_MODEL), 1.0),
        'p': nrm(ks[1], (DEPTH, BATCH, SEQ, PLE_DIM), 1.0),
        'ln_g': 1.0 + nrm(ks[2], (DEPTH, 4, D_MODEL), 0.02),
        'ln_b': nrm(ks[3], (DEPTH, 4, D_MODEL), 0.02),
        'ffn_w_in': nrm(ks[4], (DEPTH, 2, D_MODEL, 2 * D_FF), D_MODEL ** -0.5),
        'ffn_w_out': nrm(ks[5], (DEPTH, 2, D_FF, D_MODEL), BETA * D_FF ** -0.5),
        'mix_w_in': nrm(ks[6], (DEPTH, D_MODEL, IN_COLS), D_MODEL ** -0.5),
        'gdn_conv_w': nrm(ks[7], (DEPTH, GDN_CONV, 3 * GDN_WIDTH), GDN_CONV ** -0.5),
        'gdn_a_log': jnp.log(jax.random.uniform(ks[8], (DEPTH, GDN_HEADS), jnp.float32, 1.0, 16.0)),
        'gdn_dt_bias': jnp.log(jnp.expm1(dt)),
        'gdn_norm_w': 1.0 + nrm(ks[10], (DEPTH, GDN_HEAD_DIM), 0.02),
        'sc_conv_w': nrm(ks[11], (DEPTH, SC_CONV, SC_WIDTH), SC_CONV ** -0.5),
        'mix_w_out': nrm(ks[12], (DEPTH, MIX_WIDTH, D_MODEL), BETA * MIX_WIDTH ** -0.5),
        'ple_w_proj': nrm(ks[13], (DEPTH, PLE_DIM, D_MODEL), BETA * PLE_DIM ** -0.5),
        'ple_w_gate': nrm(ks[14], (DEPTH, D_MODEL, D_MODEL), D_MODEL ** -0.5),
        'ple_b_gate': nrm(ks[15], (DEPTH, D_MODEL), 0.02),
    }


def reference(x, p, ln_g, ln_b, ffn_w_in, ffn_w_out, mix_w_in, gdn_conv_w, gdn_a_log,
              gdn_dt_bias, gdn_norm_w, sc_conv_w, mix_w_out, ple_w_proj, ple_w_gate, ple_b_gate):
    for i in range(DEPTH):
        x = layer_norm(ALPHA * x + 0.5 * swiglu(x, ffn_w_in[i, 0], ffn_w_out[i, 0]), ln_g[i, 0], ln_b[i, 0])
        mix = token_mix(x, mix_w_in[i], gdn_conv_w[i], gdn_a_log[i], gdn_dt_bias[i],
                        gdn_norm_w[i], sc_conv_w[i], mix_w_out[i])
        x = layer_norm(ALPHA * x + mix, ln_g[i, 1], ln_b[i, 1])
        x = layer_norm(ALPHA * x + 0.5 * swiglu(x, ffn_w_in[i, 1], ffn_w_out[i, 1]), ln_g[i, 2], ln_b[i, 2])
        ple = jax.nn.sigmoid(x @ ple_w_gate[i] + ple_b_gate[i]) * (p[i] @ ple_w_proj[i])
        x = layer_norm(ALPHA * x + ple, ln_g[i, 3], ln_b[i, 3])
    return x
```

```python
import numpy as np
import concourse.bass as bass
import concourse.mybir as mybir

F32 = mybir.dt.float32
BF16 = mybir.dt.bfloat16
AF = mybir.ActivationFunctionType
ALU = mybir.AluOpType
AX = mybir.AxisListType


def _prod(xs):
    r = 1
    for x in xs:
        r *= int(x)
    return r


class Op:
    __slots__ = ("eng", "fn", "reads", "writes", "dma", "deps", "sig", "dsem", "dval", "dprev", "pe_mm")

    def __init__(self, eng, fn, reads, writes, dma, pe_mm=False):
        self.eng = eng
        self.fn = fn
        self.reads = reads
        self.writes = writes
        self.dma = dma
        self.deps = ()
        self.sig = None
        self.dsem = None
        self.dval = None
        self.dprev = None
        self.pe_mm = pe_mm


class Prog:
    NDSEM = 24

    def __init__(self, nc, same_engine_sync=True):
        self.nc = nc
        self.ops = []
        self.tinfo = {}
        self.hist = {}
        self.same_engine_sync = same_engine_sync
        self.engs = {"pe": nc.tensor, "act": nc.scalar, "dve": nc.vector, "pool": nc.gpsimd, "sp": nc.sync}
        self._n = 0

    def sbuf(self, name, shape, dt):
        t = self.nc.alloc_sbuf_tensor(name, [int(s) for s in shape], dt)
        self.tinfo[name] = ("sb", _prod(shape[1:]))
        return t.ap()

    def psum(self, name, shape, dt=F32):
        t = self.nc.alloc_psum_tensor(name, [int(s) for s in shape], dt)
        self.tinfo[name] = ("ps", _prod(shape[1:]))
        return t.ap()

    def dram(self, name, shape, dt, kind):
        t = self.nc.dram_tensor(name, [int(s) for s in shape], dt, kind=kind)
        self.tinfo[name] = ("const" if kind == "ExternalInput" else "dram", None)
        return t.ap()

    def rect(self, ap):
        name = ap.tensor.name
        kind, ps = self.tinfo[name]
        off = int(ap.offset)
        dims = ap.ap
        if kind in ("dram", "const"):
            hi = off + sum((c - 1) * abs(s) for s, c in dims) + 1
            return (name, 0, 1, off, hi)
        p0 = off // ps
        f0 = off % ps
        pc = dims[0][1]
        hi = f0 + sum((c - 1) * abs(s) for s, c in dims[1:]) + 1
        return (name, p0, p0 + pc, f0, hi)

    def add(self, eng, fn, reads=(), writes=(), dma=False, pe_mm=False):
        rr = []
        for a in reads:
            if a is None or isinstance(a, (int, float)):
                continue
            r = self.rect(a)
            if self.tinfo[r[0]][0] == "const":
                continue
            rr.append(r)
        ww = [self.rect(a) for a in writes]
        op = Op(eng, fn, rr, ww, dma, pe_mm)
        self.ops.append(op)
        return op

    def mm(self, out, lhsT, rhs, start=True, stop=True):
        self.add("pe", lambda e: e.matmul(out, lhsT, rhs, start=start, stop=stop),
                 reads=[lhsT, rhs], writes=[out], pe_mm=True)

    def transpose(self, out, in_, ident):
        self.add("pe", lambda e: e.transpose(out, in_, ident), reads=[in_, ident], writes=[out], pe_mm=True)

    def act(self, out, in_, func, bias=None, scale=None, accum_out=None):
        kw = {}
        if bias is not None:
            kw["bias"] = bias
        if scale is not None:
            kw["scale"] = scale
        if accum_out is not None:
            kw["accum_out"] = accum_out
        rd = [in_]
        if bias is not None and not isinstance(bias, (int, float)):
            rd.append(bias)
        if scale is not None and not isinstance(scale, (int, float)):
            rd.append(scale)
        wr = [out] + ([accum_out] if accum_out is not None else [])
        self.add("act", lambda e: e.activation(out, in_, func, **kw), reads=rd, writes=wr)

    def tt(self, eng, out, in0, in1, op):
        self.add(eng, lambda e: e.tensor_tensor(out, in0, in1, op), reads=[in0, in1], writes=[out])

    def ts(self, eng, out, in0, s1, s2, op0, op1=None):
        rd = [in0] + [s for s in (s1, s2) if s is not None and not isinstance(s, (int, float))]
        if op1 is None:
            self.add(eng, lambda e: e.tensor_scalar(out, in0, s1, None, op0), reads=rd, writes=[out])
        else:
            self.add(eng, lambda e: e.tensor_scalar(out, in0, s1, s2, op0, op1), reads=rd, writes=[out])

    def stt(self, eng, out, in0, scalar, in1, op0, op1):
        rd = [in0, in1] + ([scalar] if not isinstance(scalar, (int, float)) else [])
        self.add(eng, lambda e: e.scalar_tensor_tensor(out, in0, scalar, in1, op0, op1), reads=rd, writes=[out])

    def copy(self, eng, out, in_):
        if eng == "act":
            self.add(eng, lambda e: e.copy(out, in_), reads=[in_], writes=[out])
        else:
            self.add(eng, lambda e: e.tensor_copy(out, in_), reads=[in_], writes=[out])

    def recip(self, out, in_):
        self.add("dve", lambda e: e.reciprocal(out, in_), reads=[in_], writes=[out])

    def memset(self, eng, out, val):
        self.add(eng, lambda e: e.memset(out, val), reads=[], writes=[out])

    def dma(self, q, out, in_):
        self.add(q, lambda e: e.dma_start(out=out, in_=in_), reads=[in_], writes=[out], dma=True)

    @staticmethod
    def _ov(a, b):
        return a[1] < b[2] and b[1] < a[2] and a[3] < b[4] and b[3] < a[4]

    @staticmethod
    def _contains(a, b):
        return a[1] <= b[1] and b[2] <= a[2] and a[3] <= b[3] and b[4] <= a[4]

    def finalize(self):
        ops = self.ops
        hist = {}
        for i, op in enumerate(ops):
            deps = set()
            for r in op.reads:
                for seg in hist.get(r[0], ()):
                    if seg[1] is not None and self._ov(seg[0], r):
                        deps.add(seg[1])
            for w in op.writes:
                for seg in hist.get(w[0], ()):
                    if self._ov(seg[0], w):
                        if seg[1] is not None:
                            deps.add(seg[1])
                        deps.update(seg[2].values())
                        deps.update(seg[3])
            for r in op.reads:
                lst = hist.setdefault(r[0], [])
                found = None
                for seg in lst:
                    if seg[0] == r:
                        found = seg
                        break
                if found is None:
                    found = [r, None, {}, []]
                    lst.append(found)
                if op.dma:
                    found[3].append(i)
                else:
                    found[2][op.eng] = i
            for w in op.writes:
                lst = hist.setdefault(w[0], [])
                lst[:] = [seg for seg in lst if not self._contains(w, seg[0])]
                lst.append([w, i, {}, []])
            deps.discard(i)
            op.deps = sorted(deps)
        need = [False] * len(ops)
        for i, op in enumerate(ops):
            for j in op.deps:
                pj = ops[j]
                if pj.dma:
                    continue
                if pj.eng == op.eng and not op.dma:
                    if pj.eng == "pe" or not self.same_engine_sync:
                        continue
                need[j] = True
        nc = self.nc
        esem = {k: nc.alloc_semaphore(name=f"e_{k}") for k in self.engs}
        dsems = [nc.alloc_semaphore(name=f"d_{i}") for i in range(self.NDSEM)]
        ecount = {k: 0 for k in self.engs}
        dcount = [0] * self.NDSEM
        nd = 0
        for i, op in enumerate(ops):
            if op.dma:
                s = nd % self.NDSEM
                nd += 1
                op.dprev = dcount[s]
                dcount[s] += 16
                op.dsem = s
                op.dval = dcount[s]
            elif need[i]:
                ecount[op.eng] += 1
                op.sig = ecount[op.eng]
        known = {k: {} for k in self.engs}
        nwaits = 0
        for i, op in enumerate(ops):
            e = self.engs[op.eng]
            kn = known[op.eng]
            waits = {}
            for j in op.deps:
                pj = ops[j]
                if pj.dma:
                    key = ("d", pj.dsem)
                    val = pj.dval
                else:
                    if pj.eng == op.eng and not op.dma:
                        if pj.eng == "pe" or not self.same_engine_sync:
                            continue
                    key = ("e", pj.eng)
                    val = pj.sig
                if kn.get(key, 0) >= val:
                    continue
                if waits.get(key, 0) < val:
                    waits[key] = val
            if op.dma and op.dprev > 0:
                key = ("d", op.dsem)
                if kn.get(key, 0) < op.dprev and waits.get(key, 0) < op.dprev:
                    waits[key] = op.dprev
            for key, val in waits.items():
                sem = dsems[key[1]] if key[0] == "d" else esem[key[1]]
                e.wait_ge(sem, val)
                kn[key] = val
                nwaits += 1
            ins = op.fn(e)
            if op.dma:
                ins.then_inc(dsems[op.dsem], 16)
            elif op.sig is not None:
                ins.then_inc(esem[op.eng], 1)
        sp = self.engs["sp"]
        for s in range(self.NDSEM):
            if dcount[s] > 0:
                sp.wait_ge(dsems[s], dcount[s])
        for k in ("pe", "act", "dve", "pool"):
            if ecount[k] > 0:
                sp.wait_ge(esem[k], ecount[k])
        self.stats = dict(n_ops=len(ops), n_waits=nwaits, sigs=dict(ecount), n_dma=nd)
        return self.stats


D = 1024
DFF = 2816
KT = 8
TN = 512
ALPHA = 4.0 ** 0.25
LN_EPS = 1e-5


class DenseCtx:
    pass


def load_w(P, name, w_dram, rows, cols, nsplit=1, q="pool"):
    kt = rows // 128
    w = P.sbuf(name, [128, kt, cols], BF16)
    step = cols // nsplit
    for s in range(nsplit):
        for k in range(kt):
            P.dma(q, w[:, k, s * step:(s + 1) * step], w_dram[k * 128:(k + 1) * 128, s * step:(s + 1) * step])
    return w


def emit_ln(P, C, g, b):
    x32, xb = C.x32, C.xb
    pm = C.pstat[0]
    for m in range(KT):
        P.mm(pm, C.onesF, x32[:, m, :], start=(m == 0), stop=(m == KT - 1))
    for m in range(KT):
        P.tt("dve", x32[:, m, :], x32[:, m, :], pm, ALU.subtract)
    pv = C.pstat[1]
    for m in range(KT):
        sq = C.sq[m % 2]
        P.act(sq, x32[:, m, :], AF.Square)
        P.mm(pv, C.onesF, sq, start=(m == 0), stop=(m == KT - 1))
    P.act(C.rstd, pv, AF.Sqrt, bias=C.epsc[:, 0:1])
    P.recip(C.rstd, C.rstd)
    for m in range(KT):
        P.tt("dve", x32[:, m, :], x32[:, m, :], C.rstd, ALU.mult)
        P.act(x32[:, m, :], x32[:, m, :], AF.Identity, bias=b[:, m:m + 1], scale=g[:, m:m + 1])
        P.act(xb[:, m, :], x32[:, m, :], AF.Copy)


def emit_ffn(P, C, w1, w2, g, b):
    x32, xb, hT = C.x32, C.xb, C.hT
    NC = DFF // 128
    for c in range(NC):
        pg = C.pg[c % 2]
        pu = C.pu[c % 2]
        for k in range(KT):
            P.mm(pg, w1[:, k, c * 128:(c + 1) * 128], xb[:, k, :], start=(k == 0), stop=(k == KT - 1))
        for k in range(KT):
            P.mm(pu, w1[:, k, DFF + c * 128:DFF + (c + 1) * 128], xb[:, k, :], start=(k == 0), stop=(k == KT - 1))
        sg = C.sg[c % 2]
        P.act(sg, pg, AF.Silu)
        P.stt("dve", hT[:, c, :], sg, 0.5, pu, ALU.mult, ALU.mult)
    for m in range(KT):
        py = C.py[m % 2]
        for c in range(NC):
            P.mm(py, w2[:, c, m * 128:(m + 1) * 128], hT[:, c, :], start=(c == 0), stop=(c == NC - 1))
        P.stt("dve", x32[:, m, :], x32[:, m, :], ALPHA, py, ALU.mult, ALU.add)
    emit_ln(P, C, g, b)


def emit_mixout(P, C, mixb, wo, g, b):
    x32 = C.x32
    for m in range(KT):
        py = C.py[m % 2]
        for k in range(KT):
            P.mm(py, wo[:, k, m * 128:(m + 1) * 128], mixb[:, k, :], start=(k == 0), stop=(k == KT - 1))
        P.stt("dve", x32[:, m, :], x32[:, m, :], ALPHA, py, ALU.mult, ALU.add)
    emit_ln(P, C, g, b)


def emit_ple(P, C, pb, wg, bg, wp, g, b):
    x32, xb = C.x32, C.xb
    for m in range(KT):
        pgt = C.pg[m % 2]
        ppj = C.pu[m % 2]
        for k in range(KT):
            P.mm(pgt, wg[:, k, m * 128:(m + 1) * 128], xb[:, k, :], start=(k == 0), stop=(k == KT - 1))
        for k in range(2):
            P.mm(ppj, wp[:, k, m * 128:(m + 1) * 128], pb[:, k, :], start=(k == 0), stop=(k == 1))
        sg = C.sg[m % 2]
        P.act(sg, pgt, AF.Sigmoid, bias=bg[:, m:m + 1])
        P.tt("dve", sg, sg, ppj, ALU.mult)
        P.stt("dve", x32[:, m, :], x32[:, m, :], ALPHA, sg, ALU.mult, ALU.add)
    emit_ln(P, C, g, b)


def build_dense(steps, ntok, nc=None):
    if nc is None:
        nc = bass.Bass("TRN2", target_bir_lowering=False)
    P = Prog(nc)
    nt = ntok // TN
    xT = P.dram("xT", [D, ntok], F32, "ExternalInput")
    outT = P.dram("outT", [D, ntok], F32, "ExternalOutput")
    nln = len(steps)
    lng_d = P.dram("lng", [128, nln * KT], F32, "ExternalInput")
    lnb_d = P.dram("lnb", [128, nln * KT], F32, "ExternalInput")
    wd = []
    for i, s in enumerate(steps):
        if s == "ffn":
            wd.append((P.dram(f"w1_{i}", [D, 2 * DFF], F32, "ExternalInput"),
                       P.dram(f"w2_{i}", [DFF, D], F32, "ExternalInput")))
        elif s == "mixout":
            wd.append((P.dram(f"wo_{i}", [D, D], F32, "ExternalInput"),
                       P.dram(f"mixT_{i}", [D, ntok], BF16, "ExternalInput")))
        elif s == "ple":
            wd.append((P.dram(f"wg_{i}", [D, D], F32, "ExternalInput"),
                       P.dram(f"wp_{i}", [256, D], F32, "ExternalInput"),
                       P.dram(f"bg_{i}", [128, KT], F32, "ExternalInput"),
                       P.dram(f"pT_{i}", [256, ntok], F32, "ExternalInput")))
    C = DenseCtx()
    C.x32 = P.sbuf("x32", [128, KT, TN], F32)
    C.xb = P.sbuf("xb", [128, KT, TN], BF16)
    C.onesF = P.sbuf("onesF", [128, 128], F32)
    C.sq = [P.sbuf(f"sq{i}", [128, TN], F32) for i in range(2)]
    C.sg = [P.sbuf(f"sg{i}", [128, TN], F32) for i in range(2)]
    C.rstd = P.sbuf("rstd", [128, TN], F32)
    C.pg = [P.psum(f"pg{i}", [128, TN]) for i in range(2)]
    C.pu = [P.psum(f"pu{i}", [128, TN]) for i in range(2)]
    C.py = [P.psum(f"py{i}", [128, TN]) for i in range(2)]
    C.pstat = [P.psum(f"pst{i}", [128, TN]) for i in range(2)]
    lng = P.sbuf("lng_s", [128, nln * KT], F32)
    lnb = P.sbuf("lnb_s", [128, nln * KT], F32)
    P.dma("sp", lng, lng_d)
    P.dma("sp", lnb, lnb_d)
    P.memset("dve", C.onesF, 1.0 / D)
    C.epsc = P.sbuf("epsc", [128, 1], F32)
    P.memset("dve", C.epsc, LN_EPS)
    ws = []
    need_hT = False
    for i, s in enumerate(steps):
        if s == "ffn":
            need_hT = True
            w1 = load_w(P, f"w1s_{i}", wd[i][0], D, 2 * DFF, nsplit=4)
            w2 = load_w(P, f"w2s_{i}", wd[i][1], DFF, D)
            ws.append((w1, w2))
        elif s == "mixout":
            wo = load_w(P, f"wos_{i}", wd[i][0], D, D)
            mixb = P.sbuf(f"mixb_{i}", [128, KT, TN], BF16)
            ws.append((wo, mixb))
        elif s == "ple":
            wg = load_w(P, f"wgs_{i}", wd[i][0], D, D)
            wp = load_w(P, f"wps_{i}", wd[i][1], 256, D)
            bg = P.sbuf(f"bgs_{i}", [128, KT], F32)
            P.dma("sp", bg, wd[i][2])
            pb = P.sbuf(f"pb_{i}", [128, 2, TN], BF16)
            ws.append((wg, wp, bg, pb))
    if need_hT:
        C.hT = P.sbuf("hT", [128, DFF // 128, TN], BF16)
    xTr = xT.rearrange("(k p) n -> p k n", p=128)
    oTr = outT.rearrange("(k p) n -> p k n", p=128)
    for t in range(nt):
        tsl = slice(t * TN, (t + 1) * TN)
        P.dma("sp", C.x32, xTr[:, :, tsl])
        for i, s in enumerate(steps):
            if s == "mixout":
                P.dma("sp", ws[i][1], wd[i][1].rearrange("(k p) n -> p k n", p=128)[:, :, tsl])
            elif s == "ple":
                P.dma("pool", ws[i][3], wd[i][3].rearrange("(k p) n -> p k n", p=128)[:, :, tsl])
        if steps[0] != "mixout":
            for m in range(KT):
                P.act(C.xb[:, m, :], C.x32[:, m, :], AF.Copy)
        for i, s in enumerate(steps):
            g = lng[:, i * KT:(i + 1) * KT]
            b = lnb[:, i * KT:(i + 1) * KT]
            if s == "ffn":
                emit_ffn(P, C, ws[i][0], ws[i][1], g, b)
            elif s == "mixout":
                emit_mixout(P, C, ws[i][1], ws[i][0], g, b)
            elif s == "ple":
                emit_ple(P, C, ws[i][3], ws[i][0], ws[i][2], ws[i][1], g, b)
        P.dma("sp", oTr[:, :, tsl], C.x32)
    st = P.finalize()
    return nc, st

import os
POOL = os.environ.get("MIX_POOL", "pool")
LVL = int(os.environ.get("MIX_LVL", "9"))
NOCARRY = os.environ.get("MIX_NOCARRY", "0") == "1"

D = 1024
KT = 8
TN = 512
NW = 1796
C_SQ, C_SK, C_SV = 0, 128, 256
C_GQ, C_GK, C_GV, C_GZ = 384, 640, 896, 1152
C_AB = 1408
C_SCB, C_SCC, C_SCH = 1412, 1540, 1668
NORM_EPS = 1e-6


def mix_consts():
    p = np.arange(128)[:, None]
    f = np.arange(128)[None, :]
    c = {}
    c["ident"] = (p == f).astype(np.float32)
    c["triU"] = (p <= f).astype(np.float32)
    c["triNeg"] = -(p >= f).astype(np.float32)
    c["SL"] = (p > f).astype(np.float32)
    c["SU"] = (p < f).astype(np.float32)
    c["UI"] = (p <= f).astype(np.float32)
    m = np.zeros((128, 4, 512), np.float32)
    tq = np.arange(512)[None, :]
    for i in range(4):
        m[:, i, :] = ((128 * i + p) < tq).astype(np.float32)
    c["mask"] = m.reshape(128, 2048)
    order = ["ident", "triU", "triNeg", "SL", "SU", "UI", "mask"]
    arr = np.concatenate([c[k] for k in order], axis=1)
    offs = {}
    o = 0
    for k in order:
        offs[k] = (o, o + c[k].shape[1])
        o += c[k].shape[1]
    return arr, offs


class Ring:
    def __init__(self, items):
        self.items = items
        self.i = 0

    def __call__(self):
        x = self.items[self.i % len(self.items)]
        self.i += 1
        return x


def build_mix(S, nc=None, do_attn=True, do_gdn=True, do_sc=True):
    if nc is None:
        nc = bass.Bass("TRN2", target_bir_lowering=False)
    P = Prog(nc)
    NT = S // TN
    NKB = S // 128
    carr, coff = mix_consts()
    hT = P.dram("hT", [D, S], F32, "ExternalInput")
    wm_d = P.dram("wm", [D, NW], F32, "ExternalInput")
    cst_d = P.dram("cst", list(carr.shape), F32, "ExternalInput")
    gcw_d = P.dram("gcw", [128, 24], F32, "ExternalInput")
    scw_d = P.dram("scw", [128, 3], F32, "ExternalInput")
    hp_d = P.dram("hp", [128, 4], F32, "ExternalInput")
    nw_d = P.dram("nw", [128, 128], F32, "ExternalInput")
    o_sb = P.dram("o_sb", [128, S], BF16, "ExternalOutput")
    o_gdn = P.dram("o_gdn", [S, 256], BF16, "ExternalOutput")
    o_sc = P.dram("o_sc", [128, S], BF16, "ExternalOutput")

    cst = P.sbuf("cst_s", list(carr.shape), F32)
    P.dma("sp", cst, cst_d)

    def cs(k):
        a, b = coff[k]
        return cst[:, a:b]
    identF, triU, SL, SU, UI = cs("ident"), cs("triU"), cs("SL"), cs("SU"), cs("UI")
    maskF = cs("mask")
    identB = P.sbuf("identB", [128, 128], BF16)
    triNegB = P.sbuf("triNegB", [128, 128], BF16)
    P.copy("dve", identB, identF)
    P.copy("dve", triNegB, cs("triNeg"))
    ones1 = P.sbuf("ones1", [128, 128], F32)
    P.memset("dve", ones1, 1.0)
    onesRowB = P.sbuf("onesRowB", [1, 128], BF16)
    P.memset("dve", onesRowB, 1.0)
    onec = P.sbuf("onec", [128, 1], F32)
    P.memset("dve", onec, 1.0)
    epsc = P.sbuf("epsc", [128, 1], F32)
    P.memset("dve", epsc, NORM_EPS)
    gcw = P.sbuf("gcw_s", [128, 24], F32)
    scw = P.sbuf("scw_s", [128, 3], F32)
    hp = P.sbuf("hp_s", [128, 4], F32)
    nwb = P.sbuf("nw_s", [128, 128], F32)
    P.dma("sp", gcw, gcw_d)
    P.dma("sp", scw, scw_d)
    P.dma("sp", hp, hp_d)
    P.dma("sp", nwb, nw_d)
    nA = P.sbuf("nA", [128, 2], F32)
    P.act(nA, hp[:, 0:2], AF.Exp)
    P.ts("dve", nA, nA, -1.0, None, ALU.mult)
    dtb = hp[:, 2:4]
    wm = P.sbuf("wm_s", [128, KT, NW], BF16)
    for k in range(KT):
        P.dma("pool", wm[:, k, :], wm_d[k * 128:(k + 1) * 128, :])

    hb = P.sbuf("hb", [128, KT, TN], BF16)
    qT = P.sbuf("qT", [128, S], BF16)
    kT = P.sbuf("kT", [128, S], BF16)
    vA = P.sbuf("vA", [128, NKB, 128], BF16)
    pf = P.psum("pf", [128, 6, 512], F32)
    pbf = P.psum("pbf", [128, 2, 1024], BF16)
    bank = Ring([pf[:, i, :] for i in range(0, 5)])
    bank_o = Ring([pf[:, 5, :]])
    quart = lambda: bank()[:, 0:128]
    tbank = Ring([pbf[:, i, 0:128] for i in range(2)])
    eb = Ring([P.sbuf(f"e{i}", [128, TN], F32) for i in range(2)])
    Lb = Ring([P.sbuf(f"L{i}", [128, TN], BF16) for i in range(2)])
    eRb = Ring([P.sbuf(f"eR{i}", [128, TN], F32) for i in range(2)])
    prsb = Ring([P.sbuf(f"prs{i}", [128, TN], F32) for i in range(2)])
    attb = Ring([P.sbuf(f"att{i}", [128, TN], BF16) for i in range(2)])
    chi = Ring([P.sbuf(f"chi{i}", [1, TN], BF16) for i in range(2)])
    clo = Ring([P.sbuf(f"clo{i}", [1, TN], BF16) for i in range(2)])
    osb = [P.sbuf(f"osb{i}", [64, TN], BF16) for i in range(2)]
    raw = [P.sbuf(f"raw{f}", [128, 3 + TN], F32) for f in range(6)]
    ycv = P.sbuf("ycv", [128, TN], F32)
    ysl = P.sbuf("ysl", [128, TN], F32)
    sqb = P.sbuf("sqb", [128, TN], F32)
    rnb = P.sbuf("rnb", [128, TN], F32)
    gqT = [P.sbuf(f"gqT{h}", [128, TN], BF16) for h in range(2)]
    gkT = [P.sbuf(f"gkT{h}", [128, TN], BF16) for h in range(2)]
    gvT = [P.sbuf(f"gvT{h}", [128, TN], BF16) for h in range(2)]
    szb = P.sbuf("szb", [128, 256], F32)
    sc5 = {n: P.sbuf(f"sc_{n}", [128, 2], F32) for n in
           ("beta", "nbeta", "g", "gc", "gtot", "egl", "eg", "bg", "kd", "tmp")}

    def t128(name, dt=F32):
        return P.sbuf(name, [128, 128], dt)
    gU, E1, Dall, DL, DUb, DUI, dB, egRow = [t128(n) for n in ("gU", "E1", "Dall", "DL", "DUb", "DUI", "dB", "egRow")]
    Nb = Ring([t128(f"Nb{i}", BF16) for i in range(3)])
    NTb = Ring([t128(f"NTb{i}", BF16) for i in range(3)])
    XTb = Ring([t128(f"XTb{i}", BF16) for i in range(3)])
    kbg, kdec, vb, wTs, qkTs, qdT, vnew = [t128(n, BF16) for n in ("kbg", "kdec", "vb", "wTs", "qkTs", "qdT", "vnew")]
    u_sb = t128("u_sb")
    junk = t128("junk")
    onb = t128("onb")
    Sst = [t128(f"S{h}") for h in range(2)]
    Sbf = [t128(f"Sb{h}", BF16) for h in range(2)]
    ab_sb = P.sbuf("ab_sb", [128, 4], F32)
    D1s = t128("D1s")
    o_s = t128("o_s")
    ss1 = P.sbuf("ss1", [128, 1], F32)
    rs1 = P.sbuf("rs1", [128, 1], F32)
    og = P.sbuf("og", [128, 256], BF16)
    for h in range(2):
        P.memset("dve", Sst[h], 0.0)
        P.memset("dve", Sbf[h], 0.0)
    for f in range(6):
        P.memset("dve", raw[f][:, 0:3], 0.0)
    rawc = P.sbuf("rawc", [128, 2 + TN], F32)
    P.memset("dve", rawc[:, 0:2], 0.0)
    scB = P.sbuf("scB", [128, TN], F32)
    scC = P.sbuf("scC", [128, TN], F32)
    scy = P.sbuf("scy", [128, TN], F32)
    sco = P.sbuf("sco", [128, TN], BF16)

    hTr = hT.rearrange("(k p) n -> p k n", p=128)

    def proj_fm(col0, ncol=128):
        pb = bank()
        for k in range(KT):
            P.mm(pb[0:ncol, :], wm[:, k, col0:col0 + ncol], hb[:, k, :], start=(k == 0), stop=(k == KT - 1))
        return pb

    for t in range(NT):
        tsl = slice(t * TN, (t + 1) * TN)
        for k in range(KT):
            P.dma("pool", hb[:, k, :], hTr[:, k, tsl])
        if do_attn:
            pq = proj_fm(C_SQ)
            P.ts("dve", qT[:, tsl], pq, 0.125, None, ALU.mult)
            if not os.environ.get("MIX_NOK"):
                pk = proj_fm(C_SK)
                P.act(kT[:, tsl], pk, AF.Copy)
            for s in range(0 if os.environ.get("MIX_NOV") else 4):
                pv = quart()
                for k in range(KT):
                    P.mm(pv, hb[:, k, s * 128:(s + 1) * 128], wm[:, k, C_SV:C_SV + 128], start=(k == 0), stop=(k == KT - 1))
                P.copy("dve", vA[:, 4 * t + s, :], pv)
            for hd in range(2 if LVL >= 1 else 0):
                ps = slice(64 * hd, 64 * hd + 64)
                qs = qT[ps, tsl]
                nkb = 4 * t + 4
                po = bank_o()
                first = True
                c_hi = c_lo = None
                for kb in range(nkb - 1, -1, -1):
                    pz = bank()
                    P.mm(pz, kT[ps, kb * 128:(kb + 1) * 128], qs)
                    e = eb()
                    P.act(e, pz, AF.Exp)
                    if LVL < 2:
                        continue
                    if kb >= 4 * t:
                        i = kb - 4 * t
                        P.tt(POOL, e, e, maskF[:, i * 512:(i + 1) * 512], ALU.mult)
                    L = Lb()
                    P.act(L, e, AF.Ln, bias=onec[:, 0:1])
                    if LVL < 3:
                        continue
                    pr = bank()
                    P.mm(pr, triNegB, L, start=True, stop=(first or NOCARRY))
                    if not first and not NOCARRY:
                        P.mm(pr, onesRowB[0:1, :], c_hi[0:1, :], start=False, stop=False)
                        P.mm(pr, onesRowB[0:1, :], c_lo[0:1, :], start=False, stop=True)
                    prs = prsb()
                    P.copy("dve", prs, pr)
                    if kb > 0:
                        c_hi = chi()
                        c_lo = clo()
                        P.copy("dve", c_hi[0:1, :], prs[0:1, :])
                        P.tt("dve", c_lo[0:1, :], prs[0:1, :], c_hi[0:1, :], ALU.subtract)
                    eR = eRb()
                    P.act(eR, prs, AF.Exp)
                    att = attb()
                    if os.environ.get("MIX_NOTT"):
                        first = False
                        continue
                    P.tt(POOL, att, e, eR, ALU.mult)
                    if not os.environ.get("MIX_NOAV"):
                        P.mm(po[0:64, :], vA[:, kb, 64 * hd:64 * hd + 64], att, start=first, stop=(kb == 0))
                    first = False
                if LVL >= 5 and not os.environ.get("MIX_NOAV"):
                    P.act(osb[hd], po[0:64, :], AF.Copy)
                    if not os.environ.get("MIX_NOOUT"):
                        P.dma("sp", o_sb[64 * hd:64 * hd + 64, tsl], osb[hd])
        if do_sc:
            pB = proj_fm(C_SCB)
            P.act(scB, pB, AF.Copy)
            pC = proj_fm(C_SCC)
            P.act(scC, pC, AF.Copy)
            pH = proj_fm(C_SCH)
            if t > 0:
                P.copy("dve", rawc[:, 0:2], rawc[:, TN:TN + 2])
            P.tt("dve", rawc[:, 2:2 + TN], scC, pH, ALU.mult)
            P.ts("dve", scy, rawc[:, 0:TN], scw[:, 0:1], None, ALU.mult)
            for i in (1, 2):
                P.stt("dve", scy, rawc[:, i:i + TN], scw[:, i:i + 1], scy, ALU.mult, ALU.add)
            P.tt("dve", sco, scB, scy, ALU.mult)
            P.dma("sp", o_sc[:, tsl], sco)
        if do_gdn:
            for f in range(6):
                col0 = C_GQ + f * 128
                pg = proj_fm(col0)
                if t > 0:
                    P.copy("dve", raw[f][:, 0:3], raw[f][:, TN:TN + 3])
                P.act(raw[f][:, 3:3 + TN], pg, AF.Copy)
                P.ts("dve", ycv, raw[f][:, 0:TN], gcw[:, f * 4:f * 4 + 1], None, ALU.mult)
                for i in (1, 2, 3):
                    P.stt("dve", ycv, raw[f][:, i:i + TN], gcw[:, f * 4 + i:f * 4 + i + 1], ycv, ALU.mult, ALU.add)
                h_ = f % 2
                if f >= 4:
                    P.act(gvT[h_], ycv, AF.Silu)
                    continue
                P.act(ysl, ycv, AF.Silu)
                P.act(sqb, ysl, AF.Square)
                pss = bank()
                P.mm(pss, ones1, sqb)
                P.act(rnb, pss, AF.Sqrt, bias=epsc[:, 0:1])
                P.recip(rnb, rnb)
                if f < 2:
                    P.stt("dve", gqT[h_], ysl, 128.0 ** -0.5, rnb, ALU.mult, ALU.mult)
                else:
                    P.tt("dve", gkT[h_], ysl, rnb, ALU.mult)
            for c in range(4):
                csl = slice(c * 128, (c + 1) * 128)
                pzz = bank()
                for k in range(KT):
                    P.mm(pzz[:, 0:256], hb[:, k, csl], wm[:, k, C_GZ:C_GZ + 256], start=(k == 0), stop=(k == KT - 1))
                P.act(szb, pzz[:, 0:256], AF.Silu)
                pab = quart()
                for k in range(KT):
                    P.mm(pab[:, 0:4], hb[:, k, csl], wm[:, k, C_AB:C_AB + 4], start=(k == 0), stop=(k == KT - 1))
                S5 = sc5
                P.copy("dve", ab_sb, pab[:, 0:4])
                pab = ab_sb
                P.act(S5["beta"], pab[:, 2:4], AF.Sigmoid)
                P.ts("dve", S5["nbeta"], S5["beta"], -1.0, None, ALU.mult)
                P.tt("dve", S5["tmp"], pab[:, 0:2], dtb, ALU.add)
                P.act(S5["tmp"], S5["tmp"], AF.Exp)
                P.act(S5["tmp"], S5["tmp"], AF.Ln, bias=onec[:, 0:1])
                P.tt("dve", S5["g"], S5["tmp"], nA, ALU.mult)
                pgc = quart()
                P.mm(pgc[:, 0:2], triU, S5["g"])
                P.copy("dve", S5["gc"], pgc[:, 0:2])
                pgt = quart()
                P.mm(pgt[:, 0:2], ones1, S5["g"])
                P.copy("dve", S5["gtot"], pgt[:, 0:2])
                P.act(S5["egl"], S5["gtot"], AF.Exp)
                P.act(S5["eg"], S5["gc"], AF.Exp)
                P.tt("dve", S5["bg"], S5["beta"], S5["eg"], ALU.mult)
                P.tt("dve", S5["kd"], S5["gtot"], S5["gc"], ALU.subtract)
                P.act(S5["kd"], S5["kd"], AF.Exp)
                for h_ in range(2):
                    hs = slice(h_, h_ + 1)
                    kTc = gkT[h_][:, csl]
                    qTc = gqT[h_][:, csl]
                    P.ts("dve", gU, triU, S5["g"][:, hs], None, ALU.mult)
                    pD1 = quart()
                    P.mm(pD1, ones1, gU)
                    P.copy("dve", D1s, pD1)
                    P.act(egRow, D1s, AF.Exp)
                    P.ts("dve", E1, D1s, S5["gc"][:, hs], None, ALU.subtract)
                    P.act(E1, E1, AF.Abs)
                    P.act(Dall, E1, AF.Exp, scale=-1.0)
                    P.tt("pool", DL, Dall, SL, ALU.mult)
                    P.tt("pool", DUI, Dall, UI, ALU.mult)
                    P.ts("dve", dB, identF, S5["beta"][:, hs], None, ALU.mult)
                    pBR = quart()
                    P.mm(pBR, ones1, dB)
                    P.tt("pool", DUb, Dall, SU, ALU.mult)
                    P.tt("dve", DUb, DUb, pBR, ALU.mult)
                    pKK = quart()
                    P.mm(pKK, kTc, kTc)
                    N = Nb()
                    NT_ = NTb()
                    XT = XTb()
                    P.stt("dve", N, pKK, S5["nbeta"][:, hs], DL, ALU.mult, ALU.mult)
                    P.stt("dve", NT_, pKK, -1.0, DUb, ALU.mult, ALU.mult)
                    P.tt("pool", XT, identB, NT_, ALU.add)
                    for kk in range(1, 7):
                        pN = quart()
                        P.mm(pN, NT_, N)
                        N2 = Nb()
                        P.copy("act", N2, pN)
                        if kk < 6:
                            pNT = quart()
                            P.mm(pNT, N, NT_)
                            NT2 = NTb()
                            P.copy("dve", NT2, pNT)
                        pX = quart()
                        P.mm(pX, N2, XT)
                        XT2 = XTb()
                        P.tt("dve", XT2, XT, pX, ALU.add)
                        N, XT = N2, XT2
                        if kk < 6:
                            NT_ = NT2
                    ptk = tbank()
                    P.transpose(ptk, kTc, identB)
                    P.ts("dve", kbg, ptk, S5["bg"][:, hs], None, ALU.mult)
                    P.ts("dve", kdec, ptk, S5["kd"][:, hs], None, ALU.mult)
                    ptv = tbank()
                    P.transpose(ptv, gvT[h_][:, csl], identB)
                    P.ts("dve", vb, ptv, S5["beta"][:, hs], None, ALU.mult)
                    pu = quart()
                    P.mm(pu, XT, vb)
                    P.copy("act", u_sb, pu)
                    pw = quart()
                    P.mm(pw, kbg, XT)
                    P.copy("act", wTs, pw)
                    pqk = quart()
                    P.mm(pqk, kTc, qTc)
                    P.tt("dve", qkTs, pqk, DUI, ALU.mult)
                    P.tt("pool", qdT, qTc, egRow, ALU.mult)
                    p1 = quart()
                    P.mm(p1, wTs, Sbf[h_])
                    P.tt("dve", vnew, u_sb, p1, ALU.subtract)
                    p2 = quart()
                    P.mm(p2, qdT, Sbf[h_], start=True, stop=False)
                    P.mm(p2, qkTs, vnew, start=False, stop=True)
                    p3 = quart()
                    P.mm(p3, kdec, vnew)
                    P.stt("dve", Sst[h_], Sst[h_], S5["egl"][:, hs], p3, ALU.mult, ALU.add)
                    P.copy("act", Sbf[h_], Sst[h_])
                    P.copy("dve", o_s, p2)
                    p2 = o_s
                    P.act(junk, p2, AF.Square, accum_out=ss1)
                    P.act(rs1, ss1, AF.Sqrt, scale=1.0 / 128.0, bias=epsc[:, 0:1])
                    P.recip(rs1, rs1)
                    P.stt("dve", onb, p2, rs1[:, 0:1], nwb, ALU.mult, ALU.mult)
                    P.tt("dve", og[:, h_ * 128:(h_ + 1) * 128], onb, szb[:, h_ * 128:(h_ + 1) * 128], ALU.mult)
                tok0 = t * TN + c * 128
                P.dma("sp", o_gdn[tok0:tok0 + 128, :], og)
    st = P.finalize()
    return nc, st, carr


import ml_dtypes
from concourse.bass_utils import run_bass_kernel_spmd

BATCH, SEQ, DEPTH = 4, 8192, 2
NCORES = 8
HALF = SEQ // 2
_PROGS = {}


def _prog(key, builder):
    return builder()


def _relay(v):
    return np.ascontiguousarray(np.asarray(v, np.float32).reshape(8, 128).T)


def _run_dense(steps, xTs, per_step):
    nc, _ = build_dense(steps, HALF)
    maps = []
    for c in range(NCORES):
        m = {"xT": xTs[c]}
        lg, lb = [], []
        for i, (s, d) in enumerate(zip(steps, per_step)):
            lg.append(_relay(d["g"]))
            lb.append(_relay(d["b"]))
            for k, v in d.items():
                if k in ("g", "b"):
                    continue
                m[f"{k}_{i}"] = v[c] if isinstance(v, list) else v
        m["lng"] = np.ascontiguousarray(np.concatenate(lg, 1))
        m["lnb"] = np.ascontiguousarray(np.concatenate(lb, 1))
        maps.append(m)
    res = run_bass_kernel_spmd(nc, maps, core_ids=list(range(NCORES)))
    return [res.results[c]["outT"] for c in range(NCORES)]


def _run_mix(hTs, w_in, gconv, a_log, dt_bias, norm_w, sconv):
    nc, _, carr = build_mix(SEQ)
    maps = []
    OFF_SB = 768
    OFF_QKV = OFF_SB + 1536
    OFF_Z = OFF_QKV + 512
    OFF_A = OFF_Z + 4
    OFF_B = OFF_A + 4
    for c in range(NCORES):
        b, j = c // 2, c % 2
        hT = np.ascontiguousarray(np.concatenate([hTs[2 * b], hTs[2 * b + 1]], axis=1))
        cols = []
        for base in (0, 256, 512):
            cols.append(np.arange(base + j * 128, base + (j + 1) * 128))
        for base in (OFF_SB, OFF_SB + 512, OFF_SB + 1024, OFF_QKV):
            cols.append(np.arange(base + j * 256, base + (j + 1) * 256))
        cols.append(np.arange(OFF_Z + j * 2, OFF_Z + j * 2 + 2))
        cols.append(np.arange(OFF_A + j * 2, OFF_A + j * 2 + 2))
        for base in (OFF_B, OFF_B + 256, OFF_B + 512):
            cols.append(np.arange(base + j * 128, base + (j + 1) * 128))
        cols = np.concatenate(cols)
        wm = np.ascontiguousarray(w_in[:, cols])
        gidx = np.concatenate([np.arange(base + j * 256, base + (j + 1) * 256) for base in (0, 512, 1024)])
        gcw = np.ascontiguousarray(gconv[:, gidx].reshape(4, 6, 128).transpose(2, 1, 0).reshape(128, 24))
        scw = np.ascontiguousarray(sconv[:, j * 128:(j + 1) * 128].T)
        hp = np.ascontiguousarray(np.tile(np.concatenate([a_log[2 * j:2 * j + 2], dt_bias[2 * j:2 * j + 2]])[None, :], (128, 1)).astype(np.float32))
        nw = np.ascontiguousarray(np.tile(norm_w[None, :], (128, 1)).astype(np.float32))
        maps.append({"hT": hT, "wm": wm, "cst": carr, "gcw": gcw, "scw": scw, "hp": hp, "nw": nw})
    res = run_bass_kernel_spmd(nc, maps, core_ids=list(range(NCORES)))
    mixTs = []
    for b in range(BATCH):
        mt = np.empty((D, SEQ), ml_dtypes.bfloat16)
        for j in range(2):
            r = res.results[2 * b + j]
            mt[j * 128:(j + 1) * 128] = r["o_sb"]
            mt[256 + j * 256:256 + (j + 1) * 256] = r["o_gdn"].T
            mt[768 + j * 128:768 + (j + 1) * 128] = r["o_sc"]
        for half in range(2):
            mixTs.append(np.ascontiguousarray(mt[:, half * HALF:(half + 1) * HALF]))
    return mixTs


def kernel(x, p, ln_g, ln_b, ffn_w_in, ffn_w_out, mix_w_in, gdn_conv_w, gdn_a_log,
           gdn_dt_bias, gdn_norm_w, sc_conv_w, mix_w_out, ple_w_proj, ple_w_gate, ple_b_gate):
    f = lambda a: np.asarray(a, np.float32)
    x, p, ln_g, ln_b = f(x), f(p), f(ln_g), f(ln_b)
    ffn_w_in, ffn_w_out, mix_w_in, mix_w_out = f(ffn_w_in), f(ffn_w_out), f(mix_w_in), f(mix_w_out)
    gdn_conv_w, gdn_a_log, gdn_dt_bias, gdn_norm_w = f(gdn_conv_w), f(gdn_a_log), f(gdn_dt_bias), f(gdn_norm_w)
    sc_conv_w, ple_w_proj, ple_w_gate, ple_b_gate = f(sc_conv_w), f(ple_w_proj), f(ple_w_gate), f(ple_b_gate)
    xTs = []
    for c in range(NCORES):
        b, half = c // 2, c % 2
        xTs.append(np.ascontiguousarray(x[b, half * HALF:(half + 1) * HALF, :].T))
    for i in range(DEPTH):
        xTs = _run_dense(["ffn"], xTs, [dict(g=ln_g[i, 0], b=ln_b[i, 0],
                                              w1=np.ascontiguousarray(ffn_w_in[i, 0]), w2=np.ascontiguousarray(ffn_w_out[i, 0]))])
        mixTs = _run_mix(xTs, mix_w_in[i], gdn_conv_w[i], gdn_a_log[i], gdn_dt_bias[i], gdn_norm_w[i], sc_conv_w[i])
        xTs = _run_dense(["mixout"], xTs, [dict(g=ln_g[i, 1], b=ln_b[i, 1],
                                                 wo=np.ascontiguousarray(mix_w_out[i]), mixT=mixTs)])
        xTs = _run_dense(["ffn"], xTs, [dict(g=ln_g[i, 2], b=ln_b[i, 2],
                                              w1=np.ascontiguousarray(ffn_w_in[i, 1]), w2=np.ascontiguousarray(ffn_w_out[i, 1]))])
        pTs = []
        for c in range(NCORES):
            b, half = c // 2, c % 2
            pTs.append(np.ascontiguousarray(p[i, b, half * HALF:(half + 1) * HALF, :].T))
        xTs = _run_dense(["ple"], xTs, [dict(g=ln_g[i, 3], b=ln_b[i, 3],
                                              wg=np.ascontiguousarray(ple_w_gate[i]), wp=np.ascontiguousarray(ple_w_proj[i]),
                                              bg=_relay(ple_b_gate[i]), pT=pTs)])
    out = np.empty((BATCH, SEQ, D), np.float32)
    for c in range(NCORES):
        b, half = c // 2, c % 2
        out[b, half * HALF:(half + 1) * HALF, :] = xTs[c].T
    return out
```

```python
import numpy as np
from contextlib import ExitStack, contextmanager
import concourse.bass as bass
import concourse.mybir as mybir

F32 = mybir.dt.float32
BF16 = mybir.dt.bfloat16
AF = mybir.ActivationFunctionType
ALU = mybir.AluOpType
AX = mybir.AxisListType


def _prod(xs):
    r = 1
    for x in xs:
        r *= int(x)
    return r


class Op:
    __slots__ = ("eng", "fn", "reads", "writes", "dma", "deps", "sig", "dsem", "dval", "dprev", "pe_mm")

    def __init__(self, eng, fn, reads, writes, dma, pe_mm=False):
        self.eng = eng
        self.fn = fn
        self.reads = reads
        self.writes = writes
        self.dma = dma
        self.deps = ()
        self.sig = None
        self.dsem = None
        self.dval = None
        self.dprev = None
        self.pe_mm = pe_mm


class Prog:
    NDSEM = 24

    def __init__(self, nc, same_engine_sync=True):
        self.nc = nc
        self.ops = []
        self.tinfo = {}
        self.hist = {}
        self.same_engine_sync = same_engine_sync
        self.engs = {"pe": nc.tensor, "act": nc.scalar, "dve": nc.vector, "pool": nc.gpsimd, "sp": nc.sync}
        self._n = 0
        self.stk = None
        self.sname = ""
        self.esem = None
        self.tot = dict(n_ops=0, n_waits=0, n_dma=0)

    @contextmanager
    def stage(self, name):
        self.stk = ExitStack()
        self.sname = name + "_"
        self.ops = []
        try:
            yield self
            self.finalize(barrier=True)
        finally:
            self.stk.close()
            self.stk = None
            self.sname = ""
            self.ops = []

    def sbuf(self, name, shape, dt):
        name = self.sname + name
        if self.stk is not None:
            t = self.stk.enter_context(self.nc.sbuf_tensor(name, [int(s) for s in shape], dt))
        else:
            t = self.nc.alloc_sbuf_tensor(name, [int(s) for s in shape], dt)
        self.tinfo[name] = ("sb", _prod(shape[1:]))
        return t.ap()

    def psum(self, name, shape, dt=F32):
        name = self.sname + name
        if self.stk is not None:
            t = self.stk.enter_context(self.nc.psum_tensor(name, [int(s) for s in shape], dt))
        else:
            t = self.nc.alloc_psum_tensor(name, [int(s) for s in shape], dt)
        self.tinfo[name] = ("ps", _prod(shape[1:]))
        return t.ap()

    def dram(self, name, shape, dt, kind):
        t = self.nc.dram_tensor(name, [int(s) for s in shape], dt, kind=kind)
        self.tinfo[name] = ("const" if kind == "ExternalInput" else "dram", None)
        return t.ap()

    def rect(self, ap):
        name = ap.tensor.name
        kind, ps = self.tinfo[name]
        off = int(ap.offset)
        dims = ap.ap
        if kind in ("dram", "const"):
            hi = off + sum((c - 1) * abs(s) for s, c in dims) + 1
            return (name, 0, 1, off, hi)
        p0 = off // ps
        f0 = off % ps
        pc = dims[0][1]
        hi = f0 + sum((c - 1) * abs(s) for s, c in dims[1:]) + 1
        return (name, p0, p0 + pc, f0, hi)

    def add(self, eng, fn, reads=(), writes=(), dma=False, pe_mm=False):
        rr = []
        for a in reads:
            if a is None or isinstance(a, (int, float)):
                continue
            r = self.rect(a)
            if self.tinfo[r[0]][0] == "const":
                continue
            rr.append(r)
        ww = [self.rect(a) for a in writes]
        op = Op(eng, fn, rr, ww, dma, pe_mm)
        self.ops.append(op)
        return op

    def mm(self, out, lhsT, rhs, start=True, stop=True):
        self.add("pe", lambda e: e.matmul(out, lhsT, rhs, start=start, stop=stop),
                 reads=[lhsT, rhs], writes=[out], pe_mm=True)

    def transpose(self, out, in_, ident):
        self.add("pe", lambda e: e.transpose(out, in_, ident), reads=[in_, ident], writes=[out], pe_mm=True)

    def act(self, out, in_, func, bias=None, scale=None, accum_out=None):
        kw = {}
        if bias is not None:
            kw["bias"] = bias
        if scale is not None:
            kw["scale"] = scale
        if accum_out is not None:
            kw["accum_out"] = accum_out
        rd = [in_]
        if bias is not None and not isinstance(bias, (int, float)):
            rd.append(bias)
        if scale is not None and not isinstance(scale, (int, float)):
            rd.append(scale)
        wr = [out] + ([accum_out] if accum_out is not None else [])
        self.add("act", lambda e: e.activation(out, in_, func, **kw), reads=rd, writes=wr)

    def tt(self, eng, out, in0, in1, op):
        self.add(eng, lambda e: e.tensor_tensor(out, in0, in1, op), reads=[in0, in1], writes=[out])

    def ts(self, eng, out, in0, s1, s2, op0, op1=None):
        rd = [in0] + [s for s in (s1, s2) if s is not None and not isinstance(s, (int, float))]
        if op1 is None:
            self.add(eng, lambda e: e.tensor_scalar(out, in0, s1, None, op0), reads=rd, writes=[out])
        else:
            self.add(eng, lambda e: e.tensor_scalar(out, in0, s1, s2, op0, op1), reads=rd, writes=[out])

    def stt(self, eng, out, in0, scalar, in1, op0, op1):
        rd = [in0, in1] + ([scalar] if not isinstance(scalar, (int, float)) else [])
        self.add(eng, lambda e: e.scalar_tensor_tensor(out, in0, scalar, in1, op0, op1), reads=rd, writes=[out])

    def copy(self, eng, out, in_):
        if eng == "act":
            self.add(eng, lambda e: e.copy(out, in_), reads=[in_], writes=[out])
        else:
            self.add(eng, lambda e: e.tensor_copy(out, in_), reads=[in_], writes=[out])

    def recip(self, out, in_):
        self.add("dve", lambda e: e.reciprocal(out, in_), reads=[in_], writes=[out])

    def memset(self, eng, out, val):
        self.add(eng, lambda e: e.memset(out, val), reads=[], writes=[out])

    def dma(self, q, out, in_):
        self.add(q, lambda e: e.dma_start(out=out, in_=in_), reads=[in_], writes=[out], dma=True)

    @staticmethod
    def _ov(a, b):
        return a[1] < b[2] and b[1] < a[2] and a[3] < b[4] and b[3] < a[4]

    @staticmethod
    def _contains(a, b):
        return a[1] <= b[1] and b[2] <= a[2] and a[3] <= b[3] and b[4] <= a[4]

    def finalize(self, barrier=False):
        ops = self.ops
        hist = {}
        for i, op in enumerate(ops):
            deps = set()
            for r in op.reads:
                for seg in hist.get(r[0], ()):
                    if seg[1] is not None and self._ov(seg[0], r):
                        deps.add(seg[1])
            for w in op.writes:
                for seg in hist.get(w[0], ()):
                    if self._ov(seg[0], w):
                        if seg[1] is not None:
                            deps.add(seg[1])
                        deps.update(seg[2].values())
                        deps.update(seg[3])
            for r in op.reads:
                lst = hist.setdefault(r[0], [])
                found = None
                for seg in lst:
                    if seg[0] == r:
                        found = seg
                        break
                if found is None:
                    found = [r, None, {}, []]
                    lst.append(found)
                if op.dma:
                    found[3].append(i)
                else:
                    found[2][op.eng] = i
            for w in op.writes:
                lst = hist.setdefault(w[0], [])
                lst[:] = [seg for seg in lst if not self._contains(w, seg[0])]
                lst.append([w, i, {}, []])
            deps.discard(i)
            op.deps = sorted(deps)
        need = [False] * len(ops)
        for i, op in enumerate(ops):
            for j in op.deps:
                pj = ops[j]
                if pj.dma:
                    continue
                if pj.eng == op.eng and not op.dma:
                    if pj.eng == "pe" or not self.same_engine_sync:
                        continue
                need[j] = True
        if barrier:
            last = {}
            for i, op in enumerate(ops):
                if not op.dma:
                    last[op.eng] = i
            for i in last.values():
                need[i] = True
        nc = self.nc
        if self.esem is None:
            self.esem = {k: nc.alloc_semaphore(name=f"e_{k}") for k in self.engs}
            self.dsems = [nc.alloc_semaphore(name=f"d_{i}") for i in range(self.NDSEM)]
            self.ecount = {k: 0 for k in self.engs}
            self.dcount = [0] * self.NDSEM
            self.nd = 0
            self.known = {k: {} for k in self.engs}
        esem, dsems, ecount, dcount, known = self.esem, self.dsems, self.ecount, self.dcount, self.known
        for i, op in enumerate(ops):
            if op.dma:
                sidx = self.nd % self.NDSEM
                self.nd += 1
                op.dprev = dcount[sidx]
                dcount[sidx] += 16
                op.dsem = sidx
                op.dval = dcount[sidx]
            elif need[i]:
                ecount[op.eng] += 1
                op.sig = ecount[op.eng]
        nwaits = 0
        for i, op in enumerate(ops):
            e = self.engs[op.eng]
            kn = known[op.eng]
            waits = {}
            for j in op.deps:
                pj = ops[j]
                if pj.dma:
                    key = ("d", pj.dsem)
                    val = pj.dval
                else:
                    if pj.eng == op.eng and not op.dma:
                        if pj.eng == "pe" or not self.same_engine_sync:
                            continue
                    key = ("e", pj.eng)
                    val = pj.sig
                if kn.get(key, 0) >= val:
                    continue
                if waits.get(key, 0) < val:
                    waits[key] = val
            if op.dma and op.dprev > 0:
                key = ("d", op.dsem)
                if kn.get(key, 0) < op.dprev and waits.get(key, 0) < op.dprev:
                    waits[key] = op.dprev
            for key, val in waits.items():
                sem = dsems[key[1]] if key[0] == "d" else esem[key[1]]
                e.wait_ge(sem, val)
                kn[key] = val
                nwaits += 1
            ins = op.fn(e)
            if op.dma:
                ins.then_inc(dsems[op.dsem], 16)
            elif op.sig is not None:
                ins.then_inc(esem[op.eng], 1)
        targets = list(self.engs) if barrier else ["sp"]
        for k in targets:
            e = self.engs[k]
            kn = known[k]
            for sidx in range(self.NDSEM):
                if dcount[sidx] > kn.get(("d", sidx), 0):
                    e.wait_ge(dsems[sidx], dcount[sidx])
                    kn[("d", sidx)] = dcount[sidx]
                    nwaits += 1
            for k2 in ("pe", "act", "dve", "pool"):
                if ecount[k2] > kn.get(("e", k2), 0):
                    e.wait_ge(esem[k2], ecount[k2])
                    kn[("e", k2)] = ecount[k2]
                    nwaits += 1
        self.stats = dict(n_ops=len(ops), n_waits=nwaits, sigs=dict(ecount), n_dma=self.nd)
        self.tot["n_ops"] += len(ops)
        self.tot["n_waits"] += nwaits
        return self.stats


D = 1024
DFF = 2816
KT = 8
TN = 512
ALPHA = 4.0 ** 0.25
LN_EPS = 1e-5


class DenseCtx:
    pass


def load_w(P, name, w_dram, rows, cols, nsplit=1, q="pool"):
    kt = rows // 128
    w = P.sbuf(name, [128, kt, cols], BF16)
    step = cols // nsplit
    for s in range(nsplit):
        for k in range(kt):
            P.dma(q, w[:, k, s * step:(s + 1) * step], w_dram[k * 128:(k + 1) * 128, s * step:(s + 1) * step])
    return w


def emit_ln(P, C, g, b):
    x32, xb = C.x32, C.xb
    pm = C.pstat[0]
    for m in range(KT):
        P.mm(pm, C.onesF, x32[:, m, :], start=(m == 0), stop=(m == KT - 1))
    for m in range(KT):
        P.tt("dve", x32[:, m, :], x32[:, m, :], pm, ALU.subtract)
    pv = C.pstat[1]
    for m in range(KT):
        sq = C.sq[m % 2]
        P.act(sq, x32[:, m, :], AF.Square)
        P.mm(pv, C.onesF, sq, start=(m == 0), stop=(m == KT - 1))
    P.act(C.rstd, pv, AF.Sqrt, bias=C.epsc[:, 0:1])
    P.recip(C.rstd, C.rstd)
    for m in range(KT):
        P.tt("dve", x32[:, m, :], x32[:, m, :], C.rstd, ALU.mult)
        P.act(x32[:, m, :], x32[:, m, :], AF.Identity, bias=b[:, m:m + 1], scale=g[:, m:m + 1])
        P.act(xb[:, m, :], x32[:, m, :], AF.Copy)


def emit_ffn(P, C, w1, w2, g, b):
    x32, xb, hT = C.x32, C.xb, C.hT
    NC = DFF // 128
    for c in range(NC):
        pg = C.pg[c % 2]
        pu = C.pu[c % 2]
        for k in range(KT):
            P.mm(pg, w1[:, k, c * 128:(c + 1) * 128], xb[:, k, :], start=(k == 0), stop=(k == KT - 1))
        for k in range(KT):
            P.mm(pu, w1[:, k, DFF + c * 128:DFF + (c + 1) * 128], xb[:, k, :], start=(k == 0), stop=(k == KT - 1))
        sg = C.sg[c % 2]
        P.act(sg, pg, AF.Silu)
        P.stt("dve", hT[:, c, :], sg, 0.5, pu, ALU.mult, ALU.mult)
    for m in range(KT):
        py = C.py[m % 2]
        for c in range(NC):
            P.mm(py, w2[:, c, m * 128:(m + 1) * 128], hT[:, c, :], start=(c == 0), stop=(c == NC - 1))
        P.stt("dve", x32[:, m, :], x32[:, m, :], ALPHA, py, ALU.mult, ALU.add)
    emit_ln(P, C, g, b)


def emit_mixout(P, C, mixb, wo, g, b):
    x32 = C.x32
    for m in range(KT):
        py = C.py[m % 2]
        for k in range(KT):
            P.mm(py, wo[:, k, m * 128:(m + 1) * 128], mixb[:, k, :], start=(k == 0), stop=(k == KT - 1))
        P.stt("dve", x32[:, m, :], x32[:, m, :], ALPHA, py, ALU.mult, ALU.add)
    emit_ln(P, C, g, b)


def emit_ple(P, C, pb, wg, bg, wp, g, b):
    x32, xb = C.x32, C.xb
    for m in range(KT):
        pgt = C.pg[m % 2]
        ppj = C.pu[m % 2]
        for k in range(KT):
            P.mm(pgt, wg[:, k, m * 128:(m + 1) * 128], xb[:, k, :], start=(k == 0), stop=(k == KT - 1))
        for k in range(2):
            P.mm(ppj, wp[:, k, m * 128:(m + 1) * 128], pb[:, k, :], start=(k == 0), stop=(k == 1))
        sg = C.sg[m % 2]
        P.act(sg, pgt, AF.Sigmoid, bias=bg[:, m:m + 1])
        P.tt("dve", sg, sg, ppj, ALU.mult)
        P.stt("dve", x32[:, m, :], x32[:, m, :], ALPHA, sg, ALU.mult, ALU.add)
    emit_ln(P, C, g, b)


def emit_dense(P, steps, ntok, xT, outT, lng_d, lnb_d, wd):
    nt = ntok // TN
    nln = len(steps)
    C = DenseCtx()
    C.x32 = P.sbuf("x32", [128, KT, TN], F32)
    C.xb = P.sbuf("xb", [128, KT, TN], BF16)
    C.onesF = P.sbuf("onesF", [128, 128], F32)
    C.sq = [P.sbuf(f"sq{i}", [128, TN], F32) for i in range(2)]
    C.sg = [P.sbuf(f"sg{i}", [128, TN], F32) for i in range(2)]
    C.rstd = P.sbuf("rstd", [128, TN], F32)
    C.pg = [P.psum(f"pg{i}", [128, TN]) for i in range(2)]
    C.pu = [P.psum(f"pu{i}", [128, TN]) for i in range(2)]
    C.py = [P.psum(f"py{i}", [128, TN]) for i in range(2)]
    C.pstat = [P.psum(f"pst{i}", [128, TN]) for i in range(2)]
    lng = P.sbuf("lng_s", [128, nln * KT], F32)
    lnb = P.sbuf("lnb_s", [128, nln * KT], F32)
    P.dma("sp", lng, lng_d)
    P.dma("sp", lnb, lnb_d)
    P.memset("dve", C.onesF, 1.0 / D)
    C.epsc = P.sbuf("epsc", [128, 1], F32)
    P.memset("dve", C.epsc, LN_EPS)
    ws = []
    need_hT = False
    for i, s in enumerate(steps):
        if s == "ffn":
            need_hT = True
            w1 = load_w(P, f"w1s_{i}", wd[i][0], D, 2 * DFF, nsplit=4)
            w2 = load_w(P, f"w2s_{i}", wd[i][1], DFF, D)
            ws.append((w1, w2))
        elif s == "mixout":
            wo = load_w(P, f"wos_{i}", wd[i][0], D, D)
            mixb = P.sbuf(f"mixb_{i}", [128, KT, TN], BF16)
            ws.append((wo, mixb))
        elif s == "ple":
            wg = load_w(P, f"wgs_{i}", wd[i][0], D, D)
            wp = load_w(P, f"wps_{i}", wd[i][1], 256, D)
            bg = P.sbuf(f"bgs_{i}", [128, KT], F32)
            P.dma("sp", bg, wd[i][2])
            pb = P.sbuf(f"pb_{i}", [128, 2, TN], BF16)
            ws.append((wg, wp, bg, pb))
    if need_hT:
        C.hT = P.sbuf("hT", [128, DFF // 128, TN], BF16)
    xTr = xT.rearrange("(k p) n -> p k n", p=128)
    oTr = outT.rearrange("(k p) n -> p k n", p=128)
    for t in range(nt):
        tsl = slice(t * TN, (t + 1) * TN)
        P.dma("sp", C.x32, xTr[:, :, tsl])
        for i, s in enumerate(steps):
            if s == "mixout":
                P.dma("sp", ws[i][1], wd[i][1].rearrange("(k p) n -> p k n", p=128)[:, :, tsl])
            elif s == "ple":
                pTr = wd[i][3].rearrange("(k p) n -> p k n", p=128)
                for k in range(2):
                    P.dma("pool", ws[i][3][:, k, :], pTr[:, k, tsl])
        if steps[0] != "mixout":
            for m in range(KT):
                P.act(C.xb[:, m, :], C.x32[:, m, :], AF.Copy)
        for i, s in enumerate(steps):
            g = lng[:, i * KT:(i + 1) * KT]
            b = lnb[:, i * KT:(i + 1) * KT]
            if s == "ffn":
                emit_ffn(P, C, ws[i][0], ws[i][1], g, b)
            elif s == "mixout":
                emit_mixout(P, C, ws[i][1], ws[i][0], g, b)
            elif s == "ple":
                emit_ple(P, C, ws[i][3], ws[i][0], ws[i][2], ws[i][1], g, b)
        P.dma("sp", oTr[:, :, tsl], C.x32)


def build_dense(steps, ntok, nc=None):
    if nc is None:
        nc = bass.Bass("TRN2", target_bir_lowering=False)
    P = Prog(nc)
    xT = P.dram("xT", [D, ntok], F32, "ExternalInput")
    outT = P.dram("outT", [D, ntok], F32, "ExternalOutput")
    nln = len(steps)
    lng_d = P.dram("lng", [128, nln * KT], F32, "ExternalInput")
    lnb_d = P.dram("lnb", [128, nln * KT], F32, "ExternalInput")
    wd = []
    for i, s in enumerate(steps):
        if s == "ffn":
            wd.append((P.dram(f"w1_{i}", [D, 2 * DFF], F32, "ExternalInput"),
                       P.dram(f"w2_{i}", [DFF, D], F32, "ExternalInput")))
        elif s == "mixout":
            wd.append((P.dram(f"wo_{i}", [D, D], F32, "ExternalInput"),
                       P.dram(f"mixT_{i}", [D, ntok], BF16, "ExternalInput")))
        elif s == "ple":
            wd.append((P.dram(f"wg_{i}", [D, D], F32, "ExternalInput"),
                       P.dram(f"wp_{i}", [256, D], F32, "ExternalInput"),
                       P.dram(f"bg_{i}", [128, KT], F32, "ExternalInput"),
                       P.dram(f"pT_{i}", [256, ntok], F32, "ExternalInput")))
    emit_dense(P, steps, ntok, xT, outT, lng_d, lnb_d, wd)
    st = P.finalize()
    return nc, st

import os
POOL = os.environ.get("MIX_POOL", "pool")
LVL = int(os.environ.get("MIX_LVL", "9"))
NOCARRY = os.environ.get("MIX_NOCARRY", "0") == "1"

D = 1024
KT = 8
TN = 512
NW = 1796
C_SQ, C_SK, C_SV = 0, 128, 256
C_GQ, C_GK, C_GV, C_GZ = 384, 640, 896, 1152
C_AB = 1408
C_SCB, C_SCC, C_SCH = 1412, 1540, 1668
NORM_EPS = 1e-6


def mix_consts():
    p = np.arange(128)[:, None]
    f = np.arange(128)[None, :]
    c = {}
    c["ident"] = (p == f).astype(np.float32)
    c["triU"] = (p <= f).astype(np.float32)
    c["triNeg"] = -(p >= f).astype(np.float32)
    c["SL"] = (p > f).astype(np.float32)
    c["SU"] = (p < f).astype(np.float32)
    c["UI"] = (p <= f).astype(np.float32)
    m = np.zeros((128, 4, 512), np.float32)
    tq = np.arange(512)[None, :]
    for i in range(4):
        m[:, i, :] = ((128 * i + p) < tq).astype(np.float32)
    c["mask"] = m.reshape(128, 2048)
    order = ["ident", "triU", "triNeg", "SL", "SU", "UI", "mask"]
    arr = np.concatenate([c[k] for k in order], axis=1)
    offs = {}
    o = 0
    for k in order:
        offs[k] = (o, o + c[k].shape[1])
        o += c[k].shape[1]
    return arr, offs


class Ring:
    def __init__(self, items):
        self.items = items
        self.i = 0

    def __call__(self):
        x = self.items[self.i % len(self.items)]
        self.i += 1
        return x


def emit_mix(P, S, hT, wm_d, cst_d, gcw_d, scw_d, hp_d, nw_d, o_sb, o_gdnT, o_sc,
             do_attn=True, do_gdn=True, do_sc=True):
    NT = S // TN
    NKB = S // 128
    carr, coff = mix_consts()

    cst = P.sbuf("cst_s", list(carr.shape), F32)
    P.dma("sp", cst, cst_d)

    def cs(k):
        a, b = coff[k]
        return cst[:, a:b]
    identF, triU, SL, SU, UI = cs("ident"), cs("triU"), cs("SL"), cs("SU"), cs("UI")
    maskF = cs("mask")
    identB = P.sbuf("identB", [128, 128], BF16)
    triNegB = P.sbuf("triNegB", [128, 128], BF16)
    P.copy("dve", identB, identF)
    P.copy("dve", triNegB, cs("triNeg"))
    ones1 = P.sbuf("ones1", [128, 128], F32)
    P.memset("dve", ones1, 1.0)
    onesRowB = P.sbuf("onesRowB", [1, 128], BF16)
    P.memset("dve", onesRowB, 1.0)
    onec = P.sbuf("onec", [128, 1], F32)
    P.memset("dve", onec, 1.0)
    epsc = P.sbuf("epsc", [128, 1], F32)
    P.memset("dve", epsc, NORM_EPS)
    gcw = P.sbuf("gcw_s", [128, 24], F32)
    scw = P.sbuf("scw_s", [128, 3], F32)
    hp = P.sbuf("hp_s", [128, 4], F32)
    nwb = P.sbuf("nw_s", [128, 128], F32)
    P.dma("sp", gcw, gcw_d)
    P.dma("sp", scw, scw_d)
    P.dma("sp", hp, hp_d)
    P.dma("sp", nwb, nw_d)
    nA = P.sbuf("nA", [128, 2], F32)
    P.act(nA, hp[:, 0:2], AF.Exp)
    P.ts("dve", nA, nA, -1.0, None, ALU.mult)
    dtb = hp[:, 2:4]
    wm = P.sbuf("wm_s", [128, KT, NW], BF16)
    for k in range(KT):
        P.dma("pool", wm[:, k, :], wm_d[k * 128:(k + 1) * 128, :])

    hb = P.sbuf("hb", [128, KT, TN], BF16)
    qT = P.sbuf("qT", [128, S], BF16)
    kT = P.sbuf("kT", [128, S], BF16)
    vA = P.sbuf("vA", [128, NKB, 128], BF16)
    pf = P.psum("pf", [128, 6, 512], F32)
    pbf = P.psum("pbf", [128, 2, 1024], BF16)
    bank = Ring([pf[:, i, :] for i in range(0, 5)])
    bank_o = Ring([pf[:, 5, :]])
    quart = lambda: bank()[:, 0:128]
    tbank = Ring([pbf[:, i, 0:128] for i in range(2)])
    eb = Ring([P.sbuf(f"e{i}", [128, TN], F32) for i in range(2)])
    Lb = Ring([P.sbuf(f"L{i}", [128, TN], BF16) for i in range(2)])
    eRb = Ring([P.sbuf(f"eR{i}", [128, TN], F32) for i in range(2)])
    prsb = Ring([P.sbuf(f"prs{i}", [128, TN], F32) for i in range(2)])
    attb = Ring([P.sbuf(f"att{i}", [128, TN], BF16) for i in range(2)])
    chi = Ring([P.sbuf(f"chi{i}", [1, TN], BF16) for i in range(2)])
    clo = Ring([P.sbuf(f"clo{i}", [1, TN], BF16) for i in range(2)])
    osb = [P.sbuf(f"osb{i}", [64, TN], BF16) for i in range(2)]
    raw = [P.sbuf(f"raw{f}", [128, 3 + TN], F32) for f in range(6)]
    ycv = P.sbuf("ycv", [128, TN], F32)
    ysl = P.sbuf("ysl", [128, TN], F32)
    sqb = P.sbuf("sqb", [128, TN], F32)
    rnb = P.sbuf("rnb", [128, TN], F32)
    gqT = [P.sbuf(f"gqT{h}", [128, TN], BF16) for h in range(2)]
    gkT = [P.sbuf(f"gkT{h}", [128, TN], BF16) for h in range(2)]
    gvT = [P.sbuf(f"gvT{h}", [128, TN], BF16) for h in range(2)]
    szb = P.sbuf("szb", [128, 256], F32)
    sc5 = {n: P.sbuf(f"sc_{n}", [128, 2], F32) for n in
           ("beta", "nbeta", "g", "gc", "gtot", "egl", "eg", "bg", "kd", "tmp")}

    def t128(name, dt=F32):
        return P.sbuf(name, [128, 128], dt)
    gU, E1, Dall, DL, DUb, DUI, dB, egRow = [t128(n) for n in ("gU", "E1", "Dall", "DL", "DUb", "DUI", "dB", "egRow")]
    Nb = Ring([t128(f"Nb{i}", F32) for i in range(3)])
    NTb = Ring([t128(f"NTb{i}", F32) for i in range(3)])
    XTb = Ring([t128(f"XTb{i}", F32) for i in range(3)])
    kdec, wTs, qkTs, qdT, vnew = [t128(n, BF16) for n in ("kdec", "wTs", "qkTs", "qdT", "vnew")]
    kbg, vb = [t128(n, F32) for n in ("kbg", "vb")]
    u_sb = t128("u_sb")
    junk = t128("junk")
    onb = t128("onb")
    Sst = [t128(f"S{h}") for h in range(2)]
    Sbf = [t128(f"Sb{h}", BF16) for h in range(2)]
    ab_sb = P.sbuf("ab_sb", [128, 4], F32)
    D1s = t128("D1s")
    o_s = t128("o_s")
    ss1 = P.sbuf("ss1", [128, 1], F32)
    rs1 = P.sbuf("rs1", [128, 1], F32)
    og = P.sbuf("og", [128, 256], BF16)
    ogT = P.sbuf("ogT", [128, 2, TN], BF16)
    for h in range(2):
        P.memset("dve", Sst[h], 0.0)
        P.memset("dve", Sbf[h], 0.0)
    for f in range(6):
        P.memset("dve", raw[f][:, 0:3], 0.0)
    rawc = P.sbuf("rawc", [128, 2 + TN], F32)
    P.memset("dve", rawc[:, 0:2], 0.0)
    scB = P.sbuf("scB", [128, TN], F32)
    scC = P.sbuf("scC", [128, TN], F32)
    scy = P.sbuf("scy", [128, TN], F32)
    sco = P.sbuf("sco", [128, TN], BF16)

    hTr = hT.rearrange("(k p) n -> p k n", p=128)

    def proj_fm(col0, ncol=128):
        pb = bank()
        for k in range(KT):
            P.mm(pb[0:ncol, :], wm[:, k, col0:col0 + ncol], hb[:, k, :], start=(k == 0), stop=(k == KT - 1))
        return pb

    for t in range(NT):
        tsl = slice(t * TN, (t + 1) * TN)
        for k in range(KT):
            P.dma("pool", hb[:, k, :], hTr[:, k, tsl])
        if do_attn:
            pq = proj_fm(C_SQ)
            P.ts("dve", qT[:, tsl], pq, 0.125, None, ALU.mult)
            if not os.environ.get("MIX_NOK"):
                pk = proj_fm(C_SK)
                P.act(kT[:, tsl], pk, AF.Copy)
            for s in range(0 if os.environ.get("MIX_NOV") else 4):
                pv = quart()
                for k in range(KT):
                    P.mm(pv, hb[:, k, s * 128:(s + 1) * 128], wm[:, k, C_SV:C_SV + 128], start=(k == 0), stop=(k == KT - 1))
                P.copy("dve", vA[:, 4 * t + s, :], pv)
            for hd in range(2 if LVL >= 1 else 0):
                ps = slice(64 * hd, 64 * hd + 64)
                qs = qT[ps, tsl]
                nkb = 4 * t + 4
                po = bank_o()
                first = True
                c_hi = c_lo = None
                for kb in range(nkb - 1, -1, -1):
                    pz = bank()
                    P.mm(pz, kT[ps, kb * 128:(kb + 1) * 128], qs)
                    e = eb()
                    P.act(e, pz, AF.Exp)
                    if LVL < 2:
                        continue
                    if kb >= 4 * t:
                        i = kb - 4 * t
                        P.tt(POOL, e, e, maskF[:, i * 512:(i + 1) * 512], ALU.mult)
                    L = Lb()
                    P.act(L, e, AF.Ln, bias=onec[:, 0:1])
                    if LVL < 3:
                        continue
                    pr = bank()
                    P.mm(pr, triNegB, L, start=True, stop=(first or NOCARRY))
                    if not first and not NOCARRY:
                        P.mm(pr, onesRowB[0:1, :], c_hi[0:1, :], start=False, stop=False)
                        P.mm(pr, onesRowB[0:1, :], c_lo[0:1, :], start=False, stop=True)
                    prs = prsb()
                    P.copy("dve", prs, pr)
                    if kb > 0:
                        c_hi = chi()
                        c_lo = clo()
                        P.copy("dve", c_hi[0:1, :], prs[0:1, :])
                        P.tt("dve", c_lo[0:1, :], prs[0:1, :], c_hi[0:1, :], ALU.subtract)
                    eR = eRb()
                    P.act(eR, prs, AF.Exp)
                    att = attb()
                    if os.environ.get("MIX_NOTT"):
                        first = False
                        continue
                    P.tt(POOL, att, e, eR, ALU.mult)
                    if not os.environ.get("MIX_NOAV"):
                        P.mm(po[0:64, :], vA[:, kb, 64 * hd:64 * hd + 64], att, start=first, stop=(kb == 0))
                    first = False
                if LVL >= 5 and not os.environ.get("MIX_NOAV"):
                    P.act(osb[hd], po[0:64, :], AF.Copy)
                    if not os.environ.get("MIX_NOOUT"):
                        P.dma("sp", o_sb[64 * hd:64 * hd + 64, tsl], osb[hd])
        if do_sc:
            pB = proj_fm(C_SCB)
            P.act(scB, pB, AF.Copy)
            pC = proj_fm(C_SCC)
            P.act(scC, pC, AF.Copy)
            pH = proj_fm(C_SCH)
            if t > 0:
                P.copy("dve", rawc[:, 0:2], rawc[:, TN:TN + 2])
            P.tt("dve", rawc[:, 2:2 + TN], scC, pH, ALU.mult)
            P.ts("dve", scy, rawc[:, 0:TN], scw[:, 0:1], None, ALU.mult)
            for i in (1, 2):
                P.stt("dve", scy, rawc[:, i:i + TN], scw[:, i:i + 1], scy, ALU.mult, ALU.add)
            P.tt("dve", sco, scB, scy, ALU.mult)
            P.dma("sp", o_sc[:, tsl], sco)
        if do_gdn:
            for f in range(6):
                col0 = C_GQ + f * 128
                pg = proj_fm(col0)
                if t > 0:
                    P.copy("dve", raw[f][:, 0:3], raw[f][:, TN:TN + 3])
                P.act(raw[f][:, 3:3 + TN], pg, AF.Copy)
                P.ts("dve", ycv, raw[f][:, 0:TN], gcw[:, f * 4:f * 4 + 1], None, ALU.mult)
                for i in (1, 2, 3):
                    P.stt("dve", ycv, raw[f][:, i:i + TN], gcw[:, f * 4 + i:f * 4 + i + 1], ycv, ALU.mult, ALU.add)
                h_ = f % 2
                if f >= 4:
                    P.act(gvT[h_], ycv, AF.Silu)
                    continue
                P.act(ysl, ycv, AF.Silu)
                P.act(sqb, ysl, AF.Square)
                pss = bank()
                P.mm(pss, ones1, sqb)
                P.act(rnb, pss, AF.Sqrt, bias=epsc[:, 0:1])
                P.recip(rnb, rnb)
                if f < 2:
                    P.stt("dve", gqT[h_], ysl, 128.0 ** -0.5, rnb, ALU.mult, ALU.mult)
                else:
                    P.tt("dve", gkT[h_], ysl, rnb, ALU.mult)
            for c in range(4):
                csl = slice(c * 128, (c + 1) * 128)
                pzz = bank()
                for k in range(KT):
                    P.mm(pzz[:, 0:256], hb[:, k, csl], wm[:, k, C_GZ:C_GZ + 256], start=(k == 0), stop=(k == KT - 1))
                P.act(szb, pzz[:, 0:256], AF.Silu)
                pab = quart()
                for k in range(KT):
                    P.mm(pab[:, 0:4], hb[:, k, csl], wm[:, k, C_AB:C_AB + 4], start=(k == 0), stop=(k == KT - 1))
                S5 = sc5
                P.copy("dve", ab_sb, pab[:, 0:4])
                pab = ab_sb
                P.act(S5["beta"], pab[:, 2:4], AF.Sigmoid)
                P.ts("dve", S5["nbeta"], S5["beta"], -1.0, None, ALU.mult)
                P.tt("dve", S5["tmp"], pab[:, 0:2], dtb, ALU.add)
                P.act(S5["tmp"], S5["tmp"], AF.Exp)
                P.act(S5["tmp"], S5["tmp"], AF.Ln, bias=onec[:, 0:1])
                P.tt("dve", S5["g"], S5["tmp"], nA, ALU.mult)
                pgc = quart()
                P.mm(pgc[:, 0:2], triU, S5["g"])
                P.copy("dve", S5["gc"], pgc[:, 0:2])
                pgt = quart()
                P.mm(pgt[:, 0:2], ones1, S5["g"])
                P.copy("dve", S5["gtot"], pgt[:, 0:2])
                P.act(S5["egl"], S5["gtot"], AF.Exp)
                P.act(S5["eg"], S5["gc"], AF.Exp)
                P.tt("dve", S5["bg"], S5["beta"], S5["eg"], ALU.mult)
                P.tt("dve", S5["kd"], S5["gtot"], S5["gc"], ALU.subtract)
                P.act(S5["kd"], S5["kd"], AF.Exp)
                for h_ in range(2):
                    hs = slice(h_, h_ + 1)
                    kTc = gkT[h_][:, csl]
                    qTc = gqT[h_][:, csl]
                    P.ts("dve", gU, triU, S5["g"][:, hs], None, ALU.mult)
                    pD1 = quart()
                    P.mm(pD1, ones1, gU)
                    P.copy("dve", D1s, pD1)
                    P.act(egRow, D1s, AF.Exp)
                    P.ts("dve", E1, D1s, S5["gc"][:, hs], None, ALU.subtract)
                    P.act(E1, E1, AF.Abs)
                    P.act(Dall, E1, AF.Exp, scale=-1.0)
                    P.tt("pool", DL, Dall, SL, ALU.mult)
                    P.tt("pool", DUI, Dall, UI, ALU.mult)
                    P.ts("dve", dB, identF, S5["beta"][:, hs], None, ALU.mult)
                    pBR = quart()
                    P.mm(pBR, ones1, dB)
                    P.tt("pool", DUb, Dall, SU, ALU.mult)
                    P.tt("dve", DUb, DUb, pBR, ALU.mult)
                    pKK = quart()
                    P.mm(pKK, kTc, kTc)
                    N = Nb()
                    NT_ = NTb()
                    XT = XTb()
                    P.stt("dve", N, pKK, S5["nbeta"][:, hs], DL, ALU.mult, ALU.mult)
                    P.stt("dve", NT_, pKK, -1.0, DUb, ALU.mult, ALU.mult)
                    P.tt("pool", XT, identF, NT_, ALU.add)
                    for kk in range(1, 7):
                        pN = quart()
                        P.mm(pN, NT_, N)
                        N2 = Nb()
                        P.copy("act", N2, pN)
                        if kk < 6:
                            pNT = quart()
                            P.mm(pNT, N, NT_)
                            NT2 = NTb()
                            P.copy("dve", NT2, pNT)
                        pX = quart()
                        P.mm(pX, N2, XT)
                        XT2 = XTb()
                        P.tt("dve", XT2, XT, pX, ALU.add)
                        N, XT = N2, XT2
                        if kk < 6:
                            NT_ = NT2
                    ptk = tbank()
                    P.transpose(ptk, kTc, identB)
                    P.ts("dve", kbg, ptk, S5["bg"][:, hs], None, ALU.mult)
                    P.ts("dve", kdec, ptk, S5["kd"][:, hs], None, ALU.mult)
                    ptv = tbank()
                    P.transpose(ptv, gvT[h_][:, csl], identB)
                    P.ts("dve", vb, ptv, S5["beta"][:, hs], None, ALU.mult)
                    pu = quart()
                    P.mm(pu, XT, vb)
                    P.copy("act", u_sb, pu)
                    pw = quart()
                    P.mm(pw, kbg, XT)
                    P.copy("act", wTs, pw)
                    pqk = quart()
                    P.mm(pqk, kTc, qTc)
                    P.tt("dve", qkTs, pqk, DUI, ALU.mult)
                    P.tt("pool", qdT, qTc, egRow, ALU.mult)
                    p1 = quart()
                    P.mm(p1, wTs, Sbf[h_])
                    P.tt("dve", vnew, u_sb, p1, ALU.subtract)
                    p2 = quart()
                    P.mm(p2, qdT, Sbf[h_], start=True, stop=False)
                    P.mm(p2, qkTs, vnew, start=False, stop=True)
                    p3 = quart()
                    P.mm(p3, kdec, vnew)
                    P.stt("dve", Sst[h_], Sst[h_], S5["egl"][:, hs], p3, ALU.mult, ALU.add)
                    P.copy("act", Sbf[h_], Sst[h_])
                    P.copy("dve", o_s, p2)
                    p2 = o_s
                    P.act(junk, p2, AF.Square, accum_out=ss1)
                    P.act(rs1, ss1, AF.Sqrt, scale=1.0 / 128.0, bias=epsc[:, 0:1])
                    P.recip(rs1, rs1)
                    P.stt("dve", onb, p2, rs1[:, 0:1], nwb, ALU.mult, ALU.mult)
                    P.tt("dve", og[:, h_ * 128:(h_ + 1) * 128], onb, szb[:, h_ * 128:(h_ + 1) * 128], ALU.mult)
                for h_ in range(2):
                    ptg = tbank()
                    P.transpose(ptg, og[:, h_ * 128:(h_ + 1) * 128], identB)
                    P.copy("dve", ogT[:, h_, csl], ptg)
            for h_ in range(2):
                P.dma("sp", o_gdnT[h_ * 128:(h_ + 1) * 128, tsl], ogT[:, h_, :])


def build_mix(S, nc=None, do_attn=True, do_gdn=True, do_sc=True):
    if nc is None:
        nc = bass.Bass("TRN2", target_bir_lowering=False)
    P = Prog(nc)
    carr, coff = mix_consts()
    hT = P.dram("hT", [D, S], F32, "ExternalInput")
    wm_d = P.dram("wm", [D, NW], F32, "ExternalInput")
    cst_d = P.dram("cst", list(carr.shape), F32, "ExternalInput")
    gcw_d = P.dram("gcw", [128, 24], F32, "ExternalInput")
    scw_d = P.dram("scw", [128, 3], F32, "ExternalInput")
    hp_d = P.dram("hp", [128, 4], F32, "ExternalInput")
    nw_d = P.dram("nw", [128, 128], F32, "ExternalInput")
    o_sb = P.dram("o_sb", [128, S], BF16, "ExternalOutput")
    o_gdnT = P.dram("o_gdnT", [256, S], BF16, "ExternalOutput")
    o_sc = P.dram("o_sc", [128, S], BF16, "ExternalOutput")
    emit_mix(P, S, hT, wm_d, cst_d, gcw_d, scw_d, hp_d, nw_d, o_sb, o_gdnT, o_sc, do_attn, do_gdn, do_sc)
    st = P.finalize()
    return nc, st, carr


NSTEP = 4


def build_fused(S, depth=2, nc=None):
    if nc is None:
        nc = bass.Bass("TRN2", target_bir_lowering=False)
    P = Prog(nc)
    carr, _ = mix_consts()
    xT = P.dram("xT", [D, S], F32, "ExternalInput")
    outT = P.dram("outT", [D, S], F32, "ExternalOutput")
    lng = P.dram("lng", [128, depth * NSTEP * KT], F32, "ExternalInput")
    lnb = P.dram("lnb", [128, depth * NSTEP * KT], F32, "ExternalInput")
    cst = P.dram("cst", list(carr.shape), F32, "ExternalInput")
    W = []
    for i in range(depth):
        w = {}
        for f in range(2):
            w[f"w1{f}"] = P.dram(f"w1_{i}_{f}", [D, 2 * DFF], F32, "ExternalInput")
            w[f"w2{f}"] = P.dram(f"w2_{i}_{f}", [DFF, D], F32, "ExternalInput")
        for j in range(2):
            w[f"wm{j}"] = P.dram(f"wm_{i}_{j}", [D, NW], F32, "ExternalInput")
            w[f"gcw{j}"] = P.dram(f"gcw_{i}_{j}", [128, 24], F32, "ExternalInput")
            w[f"scw{j}"] = P.dram(f"scw_{i}_{j}", [128, 3], F32, "ExternalInput")
            w[f"hp{j}"] = P.dram(f"hp_{i}_{j}", [128, 4], F32, "ExternalInput")
        w["nw"] = P.dram(f"nw_{i}", [128, 128], F32, "ExternalInput")
        w["wo"] = P.dram(f"wo_{i}", [D, D], F32, "ExternalInput")
        w["wg"] = P.dram(f"wg_{i}", [D, D], F32, "ExternalInput")
        w["wp"] = P.dram(f"wp_{i}", [256, D], F32, "ExternalInput")
        w["bg"] = P.dram(f"bg_{i}", [128, KT], F32, "ExternalInput")
        w["pT"] = P.dram(f"pT_{i}", [256, S], F32, "ExternalInput")
        W.append(w)
    H = P.dram("H_scr", [D, S], F32, "Internal")
    X1 = P.dram("X1_scr", [D, S], F32, "Internal")
    X2 = P.dram("X2_scr", [D, S], F32, "Internal")
    MIX = P.dram("MIX_scr", [D, S], BF16, "Internal")

    def ln(i, s):
        o = (i * NSTEP + s) * KT
        return lng[:, o:o + KT], lnb[:, o:o + KT]

    cur = xT
    for i in range(depth):
        w = W[i]
        with P.stage(f"A{i}"):
            g, b = ln(i, 0)
            emit_dense(P, ["ffn"], S, cur, H, g, b, [(w["w10"], w["w20"])])
        for j in range(2):
            with P.stage(f"B{i}{j}"):
                emit_mix(P, S, H, w[f"wm{j}"], cst, w[f"gcw{j}"], w[f"scw{j}"], w[f"hp{j}"], w["nw"],
                         MIX[j * 128:(j + 1) * 128, :], MIX[256 + j * 256:256 + (j + 1) * 256, :],
                         MIX[768 + j * 128:768 + (j + 1) * 128, :])
        with P.stage(f"M{i}"):
            g, b = ln(i, 1)
            emit_dense(P, ["mixout"], S, H, X1, g, b, [(w["wo"], MIX)])
        with P.stage(f"F{i}"):
            g, b = ln(i, 2)
            emit_dense(P, ["ffn"], S, X1, X2, g, b, [(w["w11"], w["w21"])])
        with P.stage(f"P{i}"):
            g, b = ln(i, 3)
            dst = outT if i == depth - 1 else X1
            emit_dense(P, ["ple"], S, X2, dst, g, b, [(w["wg"], w["wp"], w["bg"], w["pT"])])
        cur = X1
    return nc, P.tot, carr


def _relay(v):
    return np.ascontiguousarray(np.asarray(v, np.float32).reshape(8, 128).T)


def fused_inputs(b, S, depth, carr, x, p, ln_g, ln_b, ffn_w_in, ffn_w_out, mix_w_in, gdn_conv_w, gdn_a_log,
                 gdn_dt_bias, gdn_norm_w, sc_conv_w, mix_w_out, ple_w_proj, ple_w_gate, ple_b_gate, shared=None):
    m = {} if shared is None else dict(shared)
    m["xT"] = np.ascontiguousarray(x[b].T)
    for i in range(depth):
        m[f"pT_{i}"] = np.ascontiguousarray(p[i, b].T)
    if shared is not None:
        return m
    m["cst"] = carr
    m["lng"] = np.ascontiguousarray(np.concatenate([_relay(ln_g[i, s]) for i in range(depth) for s in range(4)], 1))
    m["lnb"] = np.ascontiguousarray(np.concatenate([_relay(ln_b[i, s]) for i in range(depth) for s in range(4)], 1))
    OFF_SB = 768
    OFF_QKV = OFF_SB + 1536
    OFF_A = OFF_QKV + 512
    OFF_Bt = OFF_A + 4
    OFF_SC = OFF_Bt + 4
    for i in range(depth):
        for f in range(2):
            m[f"w1_{i}_{f}"] = np.ascontiguousarray(ffn_w_in[i, f])
            m[f"w2_{i}_{f}"] = np.ascontiguousarray(ffn_w_out[i, f])
        for j in range(2):
            cols = []
            for base in (0, 256, 512):
                cols.append(np.arange(base + j * 128, base + (j + 1) * 128))
            for base in (OFF_SB, OFF_SB + 512, OFF_SB + 1024, OFF_QKV):
                cols.append(np.arange(base + j * 256, base + (j + 1) * 256))
            cols.append(np.arange(OFF_A + j * 2, OFF_A + j * 2 + 2))
            cols.append(np.arange(OFF_Bt + j * 2, OFF_Bt + j * 2 + 2))
            for base in (OFF_SC, OFF_SC + 256, OFF_SC + 512):
                cols.append(np.arange(base + j * 128, base + (j + 1) * 128))
            cols = np.concatenate(cols)
            m[f"wm_{i}_{j}"] = np.ascontiguousarray(mix_w_in[i][:, cols])
            gidx = np.concatenate([np.arange(base + j * 256, base + (j + 1) * 256) for base in (0, 512, 1024)])
            m[f"gcw_{i}_{j}"] = np.ascontiguousarray(gdn_conv_w[i][:, gidx].reshape(4, 6, 128).transpose(2, 1, 0).reshape(128, 24))
            m[f"scw_{i}_{j}"] = np.ascontiguousarray(sc_conv_w[i][:, j * 128:(j + 1) * 128].T)
            m[f"hp_{i}_{j}"] = np.ascontiguousarray(np.tile(np.concatenate(
                [gdn_a_log[i][2 * j:2 * j + 2], gdn_dt_bias[i][2 * j:2 * j + 2]])[None, :], (128, 1)).astype(np.float32))
        m[f"nw_{i}"] = np.ascontiguousarray(np.tile(gdn_norm_w[i][None, :], (128, 1)).astype(np.float32))
        m[f"wo_{i}"] = np.ascontiguousarray(mix_w_out[i])
        m[f"wg_{i}"] = np.ascontiguousarray(ple_w_gate[i])
        m[f"wp_{i}"] = np.ascontiguousarray(ple_w_proj[i])
        m[f"bg_{i}"] = _relay(ple_b_gate[i])
    return m


from concourse.bass_utils import run_bass_kernel_spmd

BATCH, SEQ, DEPTH = 4, 8192, 2


def kernel(x, p, ln_g, ln_b, ffn_w_in, ffn_w_out, mix_w_in, gdn_conv_w, gdn_a_log,
           gdn_dt_bias, gdn_norm_w, sc_conv_w, mix_w_out, ple_w_proj, ple_w_gate, ple_b_gate):
    f = lambda a: np.asarray(a, np.float32)
    args = dict(x=f(x), p=f(p), ln_g=f(ln_g), ln_b=f(ln_b), ffn_w_in=f(ffn_w_in), ffn_w_out=f(ffn_w_out),
                mix_w_in=f(mix_w_in), gdn_conv_w=f(gdn_conv_w), gdn_a_log=f(gdn_a_log), gdn_dt_bias=f(gdn_dt_bias),
                gdn_norm_w=f(gdn_norm_w), sc_conv_w=f(sc_conv_w), mix_w_out=f(mix_w_out), ple_w_proj=f(ple_w_proj),
                ple_w_gate=f(ple_w_gate), ple_b_gate=f(ple_b_gate))
    nc, _, carr = build_fused(SEQ, DEPTH)
    m0 = fused_inputs(0, SEQ, DEPTH, carr, **args)
    shared = {k: v for k, v in m0.items() if k != "xT" and not k.startswith("pT_")}
    maps = [m0] + [fused_inputs(b, SEQ, DEPTH, carr, shared=shared, **args) for b in range(1, BATCH)]
    res = run_bass_kernel_spmd(nc, maps, core_ids=list(range(BATCH)))
    out = np.empty((BATCH, SEQ, D), np.float32)
    for b in range(BATCH):
        out[b] = res.results[b]["outT"].T
    return out
```

```python
import numpy as np
from contextlib import ExitStack, contextmanager
import concourse.bass as bass
import concourse.mybir as mybir

F32 = mybir.dt.float32
BF16 = mybir.dt.bfloat16
AF = mybir.ActivationFunctionType
ALU = mybir.AluOpType
AX = mybir.AxisListType


def _prod(xs):
    r = 1
    for x in xs:
        r *= int(x)
    return r


class Op:
    __slots__ = ("eng", "fn", "reads", "writes", "dma", "deps", "sig", "dsem", "dval", "dprev", "pe_mm")

    def __init__(self, eng, fn, reads, writes, dma, pe_mm=False):
        self.eng = eng
        self.fn = fn
        self.reads = reads
        self.writes = writes
        self.dma = dma
        self.deps = ()
        self.sig = None
        self.dsem = None
        self.dval = None
        self.dprev = None
        self.pe_mm = pe_mm


class Prog:
    NDSEM = 24

    def __init__(self, nc, same_engine_sync=True):
        self.nc = nc
        self.ops = []
        self.tinfo = {}
        self.hist = {}
        self.same_engine_sync = same_engine_sync
        self.engs = {"pe": nc.tensor, "act": nc.scalar, "dve": nc.vector, "pool": nc.gpsimd, "sp": nc.sync}
        self._n = 0
        self.stk = None
        self.sname = ""
        self.esem = None
        self.tot = dict(n_ops=0, n_waits=0, n_dma=0)

    @contextmanager
    def stage(self, name):
        self.stk = ExitStack()
        self.sname = name + "_"
        self.ops = []
        try:
            yield self
            self.finalize(barrier=True)
        finally:
            self.stk.close()
            self.stk = None
            self.sname = ""
            self.ops = []

    def sbuf(self, name, shape, dt):
        name = self.sname + name
        if self.stk is not None:
            t = self.stk.enter_context(self.nc.sbuf_tensor(name, [int(s) for s in shape], dt))
        else:
            t = self.nc.alloc_sbuf_tensor(name, [int(s) for s in shape], dt)
        self.tinfo[name] = ("sb", _prod(shape[1:]))
        return t.ap()

    def psum(self, name, shape, dt=F32):
        name = self.sname + name
        if self.stk is not None:
            t = self.stk.enter_context(self.nc.psum_tensor(name, [int(s) for s in shape], dt))
        else:
            t = self.nc.alloc_psum_tensor(name, [int(s) for s in shape], dt)
        self.tinfo[name] = ("ps", _prod(shape[1:]))
        return t.ap()

    def dram(self, name, shape, dt, kind):
        t = self.nc.dram_tensor(name, [int(s) for s in shape], dt, kind=kind)
        self.tinfo[name] = ("const" if kind == "ExternalInput" else "dram", None)
        return t.ap()

    def rect(self, ap):
        name = ap.tensor.name
        kind, ps = self.tinfo[name]
        off = int(ap.offset)
        dims = ap.ap
        if kind in ("dram", "const"):
            hi = off + sum((c - 1) * abs(s) for s, c in dims) + 1
            return (name, 0, 1, off, hi)
        p0 = off // ps
        f0 = off % ps
        pc = dims[0][1]
        hi = f0 + sum((c - 1) * abs(s) for s, c in dims[1:]) + 1
        return (name, p0, p0 + pc, f0, hi)

    def add(self, eng, fn, reads=(), writes=(), dma=False, pe_mm=False):
        rr = []
        for a in reads:
            if a is None or isinstance(a, (int, float)):
                continue
            r = self.rect(a)
            if self.tinfo[r[0]][0] == "const":
                continue
            rr.append(r)
        ww = [self.rect(a) for a in writes]
        op = Op(eng, fn, rr, ww, dma, pe_mm)
        self.ops.append(op)
        return op

    def mm(self, out, lhsT, rhs, start=True, stop=True):
        self.add("pe", lambda e: e.matmul(out, lhsT, rhs, start=start, stop=stop),
                 reads=[lhsT, rhs], writes=[out], pe_mm=True)

    def transpose(self, out, in_, ident):
        self.add("pe", lambda e: e.transpose(out, in_, ident), reads=[in_, ident], writes=[out], pe_mm=True)

    def act(self, out, in_, func, bias=None, scale=None, accum_out=None):
        kw = {}
        if bias is not None:
            kw["bias"] = bias
        if scale is not None:
            kw["scale"] = scale
        if accum_out is not None:
            kw["accum_out"] = accum_out
        rd = [in_]
        if bias is not None and not isinstance(bias, (int, float)):
            rd.append(bias)
        if scale is not None and not isinstance(scale, (int, float)):
            rd.append(scale)
        wr = [out] + ([accum_out] if accum_out is not None else [])
        self.add("act", lambda e: e.activation(out, in_, func, **kw), reads=rd, writes=wr)

    def tt(self, eng, out, in0, in1, op):
        self.add(eng, lambda e: e.tensor_tensor(out, in0, in1, op), reads=[in0, in1], writes=[out])

    def ts(self, eng, out, in0, s1, s2, op0, op1=None):
        rd = [in0] + [s for s in (s1, s2) if s is not None and not isinstance(s, (int, float))]
        if op1 is None:
            self.add(eng, lambda e: e.tensor_scalar(out, in0, s1, None, op0), reads=rd, writes=[out])
        else:
            self.add(eng, lambda e: e.tensor_scalar(out, in0, s1, s2, op0, op1), reads=rd, writes=[out])

    def stt(self, eng, out, in0, scalar, in1, op0, op1):
        rd = [in0, in1] + ([scalar] if not isinstance(scalar, (int, float)) else [])
        self.add(eng, lambda e: e.scalar_tensor_tensor(out, in0, scalar, in1, op0, op1), reads=rd, writes=[out])

    def copy(self, eng, out, in_):
        if eng == "act":
            self.add(eng, lambda e: e.copy(out, in_), reads=[in_], writes=[out])
        else:
            self.add(eng, lambda e: e.tensor_copy(out, in_), reads=[in_], writes=[out])

    def recip(self, out, in_):
        self.add("dve", lambda e: e.reciprocal(out, in_), reads=[in_], writes=[out])

    def memset(self, eng, out, val):
        self.add(eng, lambda e: e.memset(out, val), reads=[], writes=[out])

    def dma(self, q, out, in_):
        self.add(q, lambda e: e.dma_start(out=out, in_=in_), reads=[in_], writes=[out], dma=True)

    @staticmethod
    def _ov(a, b):
        return a[1] < b[2] and b[1] < a[2] and a[3] < b[4] and b[3] < a[4]

    @staticmethod
    def _contains(a, b):
        return a[1] <= b[1] and b[2] <= a[2] and a[3] <= b[3] and b[4] <= a[4]

    def finalize(self, barrier=False):
        ops = self.ops
        hist = {}
        for i, op in enumerate(ops):
            deps = set()
            for r in op.reads:
                for seg in hist.get(r[0], ()):
                    if seg[1] is not None and self._ov(seg[0], r):
                        deps.add(seg[1])
            for w in op.writes:
                for seg in hist.get(w[0], ()):
                    if self._ov(seg[0], w):
                        if seg[1] is not None:
                            deps.add(seg[1])
                        deps.update(seg[2].values())
                        deps.update(seg[3])
            for r in op.reads:
                lst = hist.setdefault(r[0], [])
                found = None
                for seg in lst:
                    if seg[0] == r:
                        found = seg
                        break
                if found is None:
                    found = [r, None, {}, []]
                    lst.append(found)
                if op.dma:
                    found[3].append(i)
                else:
                    found[2][op.eng] = i
            for w in op.writes:
                lst = hist.setdefault(w[0], [])
                lst[:] = [seg for seg in lst if not self._contains(w, seg[0])]
                lst.append([w, i, {}, []])
            deps.discard(i)
            op.deps = sorted(deps)
        need = [False] * len(ops)
        for i, op in enumerate(ops):
            for j in op.deps:
                pj = ops[j]
                if pj.dma:
                    continue
                if pj.eng == op.eng and not op.dma:
                    if pj.eng == "pe" or not self.same_engine_sync:
                        continue
                need[j] = True
        if barrier:
            last = {}
            for i, op in enumerate(ops):
                if not op.dma:
                    last[op.eng] = i
            for i in last.values():
                need[i] = True
        nc = self.nc
        if self.esem is None:
            self.esem = {k: nc.alloc_semaphore(name=f"e_{k}") for k in self.engs}
            self.dsems = [nc.alloc_semaphore(name=f"d_{i}") for i in range(self.NDSEM)]
            self.ecount = {k: 0 for k in self.engs}
            self.dcount = [0] * self.NDSEM
            self.nd = 0
            self.known = {k: {} for k in self.engs}
        esem, dsems, ecount, dcount, known = self.esem, self.dsems, self.ecount, self.dcount, self.known
        for i, op in enumerate(ops):
            if op.dma:
                sidx = self.nd % self.NDSEM
                self.nd += 1
                op.dprev = dcount[sidx]
                dcount[sidx] += 16
                op.dsem = sidx
                op.dval = dcount[sidx]
            elif need[i]:
                ecount[op.eng] += 1
                op.sig = ecount[op.eng]
        nwaits = 0
        for i, op in enumerate(ops):
            e = self.engs[op.eng]
            kn = known[op.eng]
            waits = {}
            for j in op.deps:
                pj = ops[j]
                if pj.dma:
                    key = ("d", pj.dsem)
                    val = pj.dval
                else:
                    if pj.eng == op.eng and not op.dma:
                        if pj.eng == "pe" or not self.same_engine_sync:
                            continue
                    key = ("e", pj.eng)
                    val = pj.sig
                if kn.get(key, 0) >= val:
                    continue
                if waits.get(key, 0) < val:
                    waits[key] = val
            if op.dma and op.dprev > 0:
                key = ("d", op.dsem)
                if kn.get(key, 0) < op.dprev and waits.get(key, 0) < op.dprev:
                    waits[key] = op.dprev
            for key, val in waits.items():
                sem = dsems[key[1]] if key[0] == "d" else esem[key[1]]
                e.wait_ge(sem, val)
                kn[key] = val
                nwaits += 1
            ins = op.fn(e)
            if op.dma:
                ins.then_inc(dsems[op.dsem], 16)
            elif op.sig is not None:
                ins.then_inc(esem[op.eng], 1)
        targets = list(self.engs) if barrier else ["sp"]
        for k in targets:
            e = self.engs[k]
            kn = known[k]
            for sidx in range(self.NDSEM):
                if dcount[sidx] > kn.get(("d", sidx), 0):
                    e.wait_ge(dsems[sidx], dcount[sidx])
                    kn[("d", sidx)] = dcount[sidx]
                    nwaits += 1
            for k2 in ("pe", "act", "dve", "pool"):
                if ecount[k2] > kn.get(("e", k2), 0):
                    e.wait_ge(esem[k2], ecount[k2])
                    kn[("e", k2)] = ecount[k2]
                    nwaits += 1
        self.stats = dict(n_ops=len(ops), n_waits=nwaits, sigs=dict(ecount), n_dma=self.nd)
        self.tot["n_ops"] += len(ops)
        self.tot["n_waits"] += nwaits
        return self.stats


D = 1024
DFF = 2816
KT = 8
TN = 512
ALPHA = 4.0 ** 0.25
LN_EPS = 1e-5


class DenseCtx:
    pass


def load_w(P, name, w_dram, rows, cols, nsplit=1, q="pool"):
    kt = rows // 128
    w = P.sbuf(name, [128, kt, cols], BF16)
    step = cols // nsplit
    for s in range(nsplit):
        for k in range(kt):
            P.dma(q, w[:, k, s * step:(s + 1) * step], w_dram[k * 128:(k + 1) * 128, s * step:(s + 1) * step])
    return w


def emit_ln(P, C, g, b):
    x32, xb = C.x32, C.xb
    pm = C.pstat[0]
    for m in range(KT):
        P.mm(pm, C.onesF, x32[:, m, :], start=(m == 0), stop=(m == KT - 1))
    for m in range(KT):
        P.tt("dve", x32[:, m, :], x32[:, m, :], pm, ALU.subtract)
    pv = C.pstat[1]
    for m in range(KT):
        sq = C.sq[m % 2]
        P.act(sq, x32[:, m, :], AF.Square)
        P.mm(pv, C.onesF, sq, start=(m == 0), stop=(m == KT - 1))
    P.act(C.rstd, pv, AF.Sqrt, bias=C.epsc[:, 0:1])
    P.recip(C.rstd, C.rstd)
    for m in range(KT):
        P.tt("dve", x32[:, m, :], x32[:, m, :], C.rstd, ALU.mult)
        P.act(x32[:, m, :], x32[:, m, :], AF.Identity, bias=b[:, m:m + 1], scale=g[:, m:m + 1])
        P.act(xb[:, m, :], x32[:, m, :], AF.Copy)


def emit_ffn(P, C, w1, w2, g, b):
    x32, xb, hT = C.x32, C.xb, C.hT
    NC = DFF // 128
    for c in range(NC):
        pg = C.pg[c % 2]
        pu = C.pu[c % 2]
        for k in range(KT):
            P.mm(pg, w1[:, k, c * 128:(c + 1) * 128], xb[:, k, :], start=(k == 0), stop=(k == KT - 1))
        for k in range(KT):
            P.mm(pu, w1[:, k, DFF + c * 128:DFF + (c + 1) * 128], xb[:, k, :], start=(k == 0), stop=(k == KT - 1))
        sg = C.sg[c % 2]
        P.act(sg, pg, AF.Silu)
        P.stt("dve", hT[:, c, :], sg, 0.5, pu, ALU.mult, ALU.mult)
    for m in range(KT):
        py = C.py[m % 2]
        for c in range(NC):
            P.mm(py, w2[:, c, m * 128:(m + 1) * 128], hT[:, c, :], start=(c == 0), stop=(c == NC - 1))
        P.stt("dve", x32[:, m, :], x32[:, m, :], ALPHA, py, ALU.mult, ALU.add)
    emit_ln(P, C, g, b)


def emit_mixout(P, C, mixb, wo, g, b):
    x32 = C.x32
    for m in range(KT):
        py = C.py[m % 2]
        for k in range(KT):
            P.mm(py, wo[:, k, m * 128:(m + 1) * 128], mixb[:, k, :], start=(k == 0), stop=(k == KT - 1))
        P.stt("dve", x32[:, m, :], x32[:, m, :], ALPHA, py, ALU.mult, ALU.add)
    emit_ln(P, C, g, b)


def emit_ple(P, C, pb, wg, bg, wp, g, b):
    x32, xb = C.x32, C.xb
    for m in range(KT):
        pgt = C.pg[m % 2]
        ppj = C.pu[m % 2]
        for k in range(KT):
            P.mm(pgt, wg[:, k, m * 128:(m + 1) * 128], xb[:, k, :], start=(k == 0), stop=(k == KT - 1))
        for k in range(2):
            P.mm(ppj, wp[:, k, m * 128:(m + 1) * 128], pb[:, k, :], start=(k == 0), stop=(k == 1))
        sg = C.sg[m % 2]
        P.act(sg, pgt, AF.Sigmoid, bias=bg[:, m:m + 1])
        P.tt("dve", sg, sg, ppj, ALU.mult)
        P.stt("dve", x32[:, m, :], x32[:, m, :], ALPHA, sg, ALU.mult, ALU.add)
    emit_ln(P, C, g, b)


def emit_dense(P, steps, ntok, xT, outT, lng_d, lnb_d, wd):
    nt = ntok // TN
    nln = len(steps)
    C = DenseCtx()
    C.x32 = P.sbuf("x32", [128, KT, TN], F32)
    C.xb = P.sbuf("xb", [128, KT, TN], BF16)
    C.onesF = P.sbuf("onesF", [128, 128], F32)
    C.sq = [P.sbuf(f"sq{i}", [128, TN], F32) for i in range(2)]
    C.sg = [P.sbuf(f"sg{i}", [128, TN], F32) for i in range(2)]
    C.rstd = P.sbuf("rstd", [128, TN], F32)
    C.pg = [P.psum(f"pg{i}", [128, TN]) for i in range(2)]
    C.pu = [P.psum(f"pu{i}", [128, TN]) for i in range(2)]
    C.py = [P.psum(f"py{i}", [128, TN]) for i in range(2)]
    C.pstat = [P.psum(f"pst{i}", [128, TN]) for i in range(2)]
    lng = P.sbuf("lng_s", [128, nln * KT], F32)
    lnb = P.sbuf("lnb_s", [128, nln * KT], F32)
    P.dma("sp", lng, lng_d)
    P.dma("sp", lnb, lnb_d)
    P.memset("dve", C.onesF, 1.0 / D)
    C.epsc = P.sbuf("epsc", [128, 1], F32)
    P.memset("dve", C.epsc, LN_EPS)
    ws = []
    need_hT = False
    for i, s in enumerate(steps):
        if s == "ffn":
            need_hT = True
            w1 = load_w(P, f"w1s_{i}", wd[i][0], D, 2 * DFF, nsplit=4)
            w2 = load_w(P, f"w2s_{i}", wd[i][1], DFF, D)
            ws.append((w1, w2))
        elif s == "mixout":
            wo = load_w(P, f"wos_{i}", wd[i][0], D, D)
            mixb = P.sbuf(f"mixb_{i}", [128, KT, TN], BF16)
            ws.append((wo, mixb))
        elif s == "ple":
            wg = load_w(P, f"wgs_{i}", wd[i][0], D, D)
            wp = load_w(P, f"wps_{i}", wd[i][1], 256, D)
            bg = P.sbuf(f"bgs_{i}", [128, KT], F32)
            P.dma("sp", bg, wd[i][2])
            pb = P.sbuf(f"pb_{i}", [128, 2, TN], BF16)
            ws.append((wg, wp, bg, pb))
    if need_hT:
        C.hT = P.sbuf("hT", [128, DFF // 128, TN], BF16)
    xTr = xT.rearrange("(k p) n -> p k n", p=128)
    oTr = outT.rearrange("(k p) n -> p k n", p=128)
    for t in range(nt):
        tsl = slice(t * TN, (t + 1) * TN)
        P.dma("sp", C.x32, xTr[:, :, tsl])
        for i, s in enumerate(steps):
            if s == "mixout":
                P.dma("sp", ws[i][1], wd[i][1].rearrange("(k p) n -> p k n", p=128)[:, :, tsl])
            elif s == "ple":
                pTr = wd[i][3].rearrange("(k p) n -> p k n", p=128)
                for k in range(2):
                    P.dma("pool", ws[i][3][:, k, :], pTr[:, k, tsl])
        if steps[0] != "mixout":
            for m in range(KT):
                P.act(C.xb[:, m, :], C.x32[:, m, :], AF.Copy)
        for i, s in enumerate(steps):
            g = lng[:, i * KT:(i + 1) * KT]
            b = lnb[:, i * KT:(i + 1) * KT]
            if s == "ffn":
                emit_ffn(P, C, ws[i][0], ws[i][1], g, b)
            elif s == "mixout":
                emit_mixout(P, C, ws[i][1], ws[i][0], g, b)
            elif s == "ple":
                emit_ple(P, C, ws[i][3], ws[i][0], ws[i][2], ws[i][1], g, b)
        P.dma("sp", oTr[:, :, tsl], C.x32)


def build_dense(steps, ntok, nc=None):
    if nc is None:
        nc = bass.Bass("TRN2", target_bir_lowering=False)
    P = Prog(nc)
    xT = P.dram("xT", [D, ntok], F32, "ExternalInput")
    outT = P.dram("outT", [D, ntok], F32, "ExternalOutput")
    nln = len(steps)
    lng_d = P.dram("lng", [128, nln * KT], F32, "ExternalInput")
    lnb_d = P.dram("lnb", [128, nln * KT], F32, "ExternalInput")
    wd = []
    for i, s in enumerate(steps):
        if s == "ffn":
            wd.append((P.dram(f"w1_{i}", [D, 2 * DFF], F32, "ExternalInput"),
                       P.dram(f"w2_{i}", [DFF, D], F32, "ExternalInput")))
        elif s == "mixout":
            wd.append((P.dram(f"wo_{i}", [D, D], F32, "ExternalInput"),
                       P.dram(f"mixT_{i}", [D, ntok], BF16, "ExternalInput")))
        elif s == "ple":
            wd.append((P.dram(f"wg_{i}", [D, D], F32, "ExternalInput"),
                       P.dram(f"wp_{i}", [256, D], F32, "ExternalInput"),
                       P.dram(f"bg_{i}", [128, KT], F32, "ExternalInput"),
                       P.dram(f"pT_{i}", [256, ntok], F32, "ExternalInput")))
    emit_dense(P, steps, ntok, xT, outT, lng_d, lnb_d, wd)
    st = P.finalize()
    return nc, st

import os
POOL = os.environ.get("MIX_POOL", "pool")
LVL = int(os.environ.get("MIX_LVL", "9"))
ACT_PSUM_R = os.environ.get("MIX_ACT_PSUM_R", "1") == "1"
NOCARRY = os.environ.get("MIX_NOCARRY", "0") == "1"

D = 1024
KT = 8
TN = 512
NW = 1796
C_SQ, C_SK, C_SV = 0, 128, 256
C_GQ, C_GK, C_GV, C_GZ = 384, 640, 896, 1152
C_AB = 1408
C_SCB, C_SCC, C_SCH = 1412, 1540, 1668
NORM_EPS = 1e-6


def mix_consts():
    p = np.arange(128)[:, None]
    f = np.arange(128)[None, :]
    c = {}
    c["ident"] = (p == f).astype(np.float32)
    c["triU"] = (p <= f).astype(np.float32)
    c["triNeg"] = -(p >= f).astype(np.float32)
    c["SL"] = (p > f).astype(np.float32)
    c["SU"] = (p < f).astype(np.float32)
    c["UI"] = (p <= f).astype(np.float32)
    m = np.zeros((128, 4, 512), np.float32)
    tq = np.arange(512)[None, :]
    for i in range(4):
        m[:, i, :] = ((128 * i + p) < tq).astype(np.float32)
    c["mask"] = m.reshape(128, 2048)
    order = ["ident", "triU", "triNeg", "SL", "SU", "UI", "mask"]
    arr = np.concatenate([c[k] for k in order], axis=1)
    offs = {}
    o = 0
    for k in order:
        offs[k] = (o, o + c[k].shape[1])
        o += c[k].shape[1]
    return arr, offs


class Ring:
    def __init__(self, items):
        self.items = items
        self.i = 0

    def __call__(self):
        x = self.items[self.i % len(self.items)]
        self.i += 1
        return x


def emit_mix(P, S, hT, wm_d, cst_d, gcw_d, scw_d, hp_d, nw_d, o_sb, o_gdnT, o_sc,
             do_attn=True, do_gdn=True, do_sc=True):
    NT = S // TN
    NKB = S // 128
    carr, coff = mix_consts()

    cst = P.sbuf("cst_s", list(carr.shape), F32)
    P.dma("sp", cst, cst_d)

    def cs(k):
        a, b = coff[k]
        return cst[:, a:b]
    identF, triU, SL, SU, UI = cs("ident"), cs("triU"), cs("SL"), cs("SU"), cs("UI")
    maskF = cs("mask")
    identB = P.sbuf("identB", [128, 128], BF16)
    triNegB = P.sbuf("triNegB", [128, 128], BF16)
    P.copy("dve", identB, identF)
    P.copy("dve", triNegB, cs("triNeg"))
    ones1 = P.sbuf("ones1", [128, 128], F32)
    P.memset("dve", ones1, 1.0)
    onesRowB = P.sbuf("onesRowB", [1, 128], BF16)
    P.memset("dve", onesRowB, 1.0)
    onec = P.sbuf("onec", [128, 1], F32)
    P.memset("dve", onec, 1.0)
    epsc = P.sbuf("epsc", [128, 1], F32)
    P.memset("dve", epsc, NORM_EPS)
    gcw = P.sbuf("gcw_s", [128, 24], F32)
    scw = P.sbuf("scw_s", [128, 3], F32)
    hp = P.sbuf("hp_s", [128, 4], F32)
    nwb = P.sbuf("nw_s", [128, 128], F32)
    P.dma("sp", gcw, gcw_d)
    P.dma("sp", scw, scw_d)
    P.dma("sp", hp, hp_d)
    P.dma("sp", nwb, nw_d)
    nA = P.sbuf("nA", [128, 2], F32)
    P.act(nA, hp[:, 0:2], AF.Exp)
    P.ts("dve", nA, nA, -1.0, None, ALU.mult)
    dtb = hp[:, 2:4]
    wm = P.sbuf("wm_s", [128, KT, NW], BF16)
    for k in range(KT):
        P.dma("pool", wm[:, k, :], wm_d[k * 128:(k + 1) * 128, :])

    hb = P.sbuf("hb", [128, KT, TN], BF16)
    qT = P.sbuf("qT", [128, S], BF16)
    kT = P.sbuf("kT", [128, S], BF16)
    vA = P.sbuf("vA", [128, NKB, 128], BF16)
    pf = P.psum("pf", [128, 6, 512], F32)
    pbf = P.psum("pbf", [128, 2, 1024], BF16)
    bank = Ring([pf[:, i, :] for i in range(0, 4)])
    bank_o = Ring([pf[:, 4, :], pf[:, 5, :]])
    quart = lambda: bank()[:, 0:128]
    tbank = Ring([pbf[:, i, 0:128] for i in range(2)])
    tbs = [pbf[:, i, 0:128] for i in range(2)]
    hqs = [Ring([pf[:, 2 * h, 0:128], pf[:, 2 * h + 1, 0:128]]) for h in range(2)]
    sq = Ring([pf[:, 4, :], pf[:, 5, :]])
    eb = Ring([P.sbuf(f"e{i}", [128, TN], F32) for i in range(8)])
    Lb = Ring([P.sbuf(f"L{i}", [128, TN], BF16) for i in range(6)])
    eRb = Ring([P.sbuf(f"eR{i}", [128, TN], F32) for i in range(2)])
    attb = Ring([P.sbuf(f"att{i}", [128, TN], BF16) for i in range(7)])
    lsumb = [Ring([P.sbuf(f"ls{h}_{i}", [128, TN], BF16) for i in range(4)]) for h in range(2)]
    negOnesB = P.sbuf("negOnesB", [128, 128], BF16)
    P.memset("dve", negOnesB, -1.0)
    osb = [P.sbuf(f"osb{i}", [64, TN], BF16) for i in range(2)]
    raw = [P.sbuf(f"raw{f}", [128, 3 + TN], F32) for f in range(6)]
    ycv = P.sbuf("ycv", [128, TN], F32)
    ysl = ycv
    sqb = P.sbuf("sqb", [128, TN], F32)
    rnb = sqb
    gqT = [P.sbuf(f"gqT{h}", [128, TN], BF16) for h in range(2)]
    gkT = [P.sbuf(f"gkT{h}", [128, TN], BF16) for h in range(2)]
    gvT = [P.sbuf(f"gvT{h}", [128, TN], BF16) for h in range(2)]
    szbs = [P.sbuf(f"szb{i}", [128, 256], F32) for i in range(2)]
    sc5s = [{n: P.sbuf(f"sc{i}_{n}", [128, 2], F32) for n in
             ("beta", "nbeta", "g", "gc", "gtot", "egl", "eg", "bg", "kd", "tmp")} for i in range(2)]
    ab_sbs = [P.sbuf(f"ab_sb{i}", [128, 4], F32) for i in range(2)]
    ogs = [P.sbuf(f"og{i}", [128, 256], BF16) for i in range(2)]
    ogT = P.sbuf("ogT", [128, 2, TN], BF16)

    def t128(name, dt=F32):
        return P.sbuf(name, [128, 128], dt)
    hbuf = []
    for h in range(2):
        B = {}
        for n in ("gU", "E1", "Dall", "DL", "DUb", "DUI", "dB", "egRow", "D1s", "BRs", "KKs", "kbg", "vb", "u_sb", "junk", "onb", "o_s"):
            B[n] = t128(f"h{h}_{n}")
        for n in ("kdec", "wTs", "qkTs", "qdT", "vnew"):
            B[n] = t128(f"h{h}_{n}", BF16)
        B["Nb"] = Ring([t128(f"h{h}_Nb{i}") for i in range(3)])
        B["NTb"] = Ring([t128(f"h{h}_NTb{i}") for i in range(3)])
        B["XTb"] = Ring([t128(f"h{h}_XTb{i}") for i in range(3)])
        B["ss1"] = P.sbuf(f"h{h}_ss1", [128, 1], F32)
        B["rs1"] = P.sbuf(f"h{h}_rs1", [128, 1], F32)
        hbuf.append(B)
    Sst = [t128(f"S{h}") for h in range(2)]
    Sbf = [t128(f"Sb{h}", BF16) for h in range(2)]
    for h in range(2):
        P.memset("dve", Sst[h], 0.0)
        P.memset("dve", Sbf[h], 0.0)
    for f in range(6):
        P.memset("dve", raw[f][:, 0:3], 0.0)
    rawc = P.sbuf("rawc", [128, 2 + TN], F32)
    P.memset("dve", rawc[:, 0:2], 0.0)
    scB = P.sbuf("scB", [128, TN], F32)
    scC = P.sbuf("scC", [128, TN], F32)
    scy = scC
    sco = P.sbuf("sco", [128, TN], BF16)

    hTr = hT.rearrange("(k p) n -> p k n", p=128)

    def proj_fm(col0, ncol=128):
        pb = bank()
        for k in range(KT):
            P.mm(pb[0:ncol, :], wm[:, k, col0:col0 + ncol], hb[:, k, :], start=(k == 0), stop=(k == KT - 1))
        return pb

    for t in range(NT):
        tsl = slice(t * TN, (t + 1) * TN)
        for k in range(KT):
            P.dma("pool", hb[:, k, :], hTr[:, k, tsl])
        if do_attn:
            pq = proj_fm(C_SQ)
            P.ts("dve", qT[:, tsl], pq, 0.125, None, ALU.mult)
            if not os.environ.get("MIX_NOK"):
                pk = proj_fm(C_SK)
                P.act(kT[:, tsl], pk, AF.Copy)
            for s in range(0 if os.environ.get("MIX_NOV") else 4):
                pv = quart()
                for k in range(KT):
                    P.mm(pv, hb[:, k, s * 128:(s + 1) * 128], wm[:, k, C_SV:C_SV + 128], start=(k == 0), stop=(k == KT - 1))
                P.copy("dve", vA[:, 4 * t + s, :], pv)
            nkb = 4 * t + 4
            items = [(hd, kb) for kb in range(nkb - 1, -1, -1) for hd in range(2)]
            po_h = [bank_o(), bank_o()]
            st1 = {}
            st2 = {}
            lsum_cur = [None, None]

            def att_s1(hd, kb):
                ps = slice(64 * hd, 64 * hd + 64)
                pz = bank()
                P.mm(pz, kT[ps, kb * 128:(kb + 1) * 128], qT[ps, tsl])
                e = eb()
                P.act(e, pz, AF.Exp)
                if kb >= 4 * t:
                    i = kb - 4 * t
                    P.tt(POOL, e, e, maskF[:, i * 512:(i + 1) * 512], ALU.mult)
                L = Lb()
                P.act(L, e, AF.Ln, bias=onec[:, 0:1])
                carry = lsum_cur[hd]
                if kb > 0:
                    ns = lsumb[hd]()
                    if carry is None:
                        P.copy("dve", ns, L)
                    else:
                        P.tt("dve", ns, carry, L, ALU.add)
                    lsum_cur[hd] = ns
                st1[(hd, kb)] = (e, L, carry)

            def att_s2(hd, kb):
                e, L, carry = st1.pop((hd, kb))
                pr = bank()
                P.mm(pr, triNegB, L, start=True, stop=(carry is None))
                if carry is not None:
                    P.mm(pr, negOnesB, carry, start=False, stop=True)
                eR = eRb()
                P.act(eR, pr, AF.Exp)
                att = attb()
                P.tt(POOL, att, e, eR, ALU.mult)
                st2[(hd, kb)] = att

            def att_s3(hd, kb):
                att = st2.pop((hd, kb))
                P.mm(po_h[hd][0:64, :], vA[:, kb, 64 * hd:64 * hd + 64], att, start=(kb == nkb - 1), stop=(kb == 0))

            LOOK = int(os.environ.get("MIX_LOOK", "4"))
            for idx in range(len(items) + 2 * LOOK):
                if idx < len(items):
                    att_s1(*items[idx])
                if LOOK <= idx < len(items) + LOOK:
                    att_s2(*items[idx - LOOK])
                if idx >= 2 * LOOK:
                    att_s3(*items[idx - 2 * LOOK])
            for hd in range(2):
                P.act(osb[hd], po_h[hd][0:64, :], AF.Copy)
                P.dma("sp", o_sb[64 * hd:64 * hd + 64, tsl], osb[hd])
        if do_sc:
            pB = proj_fm(C_SCB)
            P.act(scB, pB, AF.Copy)
            pC = proj_fm(C_SCC)
            P.act(scC, pC, AF.Copy)
            pH = proj_fm(C_SCH)
            if t > 0:
                P.copy("dve", rawc[:, 0:2], rawc[:, TN:TN + 2])
            P.tt("dve", rawc[:, 2:2 + TN], scC, pH, ALU.mult)
            P.ts("dve", scy, rawc[:, 0:TN], scw[:, 0:1], None, ALU.mult)
            for i in (1, 2):
                P.stt("dve", scy, rawc[:, i:i + TN], scw[:, i:i + 1], scy, ALU.mult, ALU.add)
            P.tt("dve", sco, scB, scy, ALU.mult)
            P.dma("sp", o_sc[:, tsl], sco)
        if do_gdn:
            for f in range(6):
                col0 = C_GQ + f * 128
                pg = proj_fm(col0)
                if t > 0:
                    P.copy("dve", raw[f][:, 0:3], raw[f][:, TN:TN + 3])
                P.act(raw[f][:, 3:3 + TN], pg, AF.Copy)
                P.ts("dve", ycv, raw[f][:, 0:TN], gcw[:, f * 4:f * 4 + 1], None, ALU.mult)
                for i in (1, 2, 3):
                    P.stt("dve", ycv, raw[f][:, i:i + TN], gcw[:, f * 4 + i:f * 4 + i + 1], ycv, ALU.mult, ALU.add)
                h_ = f % 2
                if f >= 4:
                    P.act(gvT[h_], ycv, AF.Silu)
                    continue
                P.act(ysl, ycv, AF.Silu)
                P.act(sqb, ysl, AF.Square)
                pss = bank()
                P.mm(pss, ones1, sqb)
                P.act(rnb, pss, AF.Sqrt, bias=epsc[:, 0:1])
                P.recip(rnb, rnb)
                if f < 2:
                    P.stt("dve", gqT[h_], ysl, 128.0 ** -0.5, rnb, ALU.mult, ALU.mult)
                else:
                    P.tt("dve", gkT[h_], ysl, rnb, ALU.mult)
            def scal_task(c):
                csl = slice(c * 128, (c + 1) * 128)
                S5 = sc5s[c % 2]
                szb = szbs[c % 2]
                ab_sb = ab_sbs[c % 2]
                pzz = sq()
                for k in range(KT):
                    P.mm(pzz[:, 0:256], hb[:, k, csl], wm[:, k, C_GZ:C_GZ + 256], start=(k == 0), stop=(k == KT - 1))
                P.act(szb, pzz[:, 0:256], AF.Silu)
                yield
                pab = sq()
                for k in range(KT):
                    P.mm(pab[:, 0:4], hb[:, k, csl], wm[:, k, C_AB:C_AB + 4], start=(k == 0), stop=(k == KT - 1))
                P.copy("dve", ab_sb, pab[:, 0:4])
                yield
                P.act(S5["beta"], ab_sb[:, 2:4], AF.Sigmoid)
                P.tt("dve", S5["tmp"], ab_sb[:, 0:2], dtb, ALU.add)
                yield
                P.ts("dve", S5["nbeta"], S5["beta"], -1.0, None, ALU.mult)
                P.act(S5["tmp"], S5["tmp"], AF.Exp)
                yield
                P.act(S5["tmp"], S5["tmp"], AF.Ln, bias=onec[:, 0:1])
                yield
                P.tt("dve", S5["g"], S5["tmp"], nA, ALU.mult)
                yield
                pgc = sq()
                P.mm(pgc[:, 0:2], triU, S5["g"])
                pgt = sq()
                P.mm(pgt[:, 0:2], ones1, S5["g"])
                yield
                P.copy("dve", S5["gc"], pgc[:, 0:2])
                P.copy("dve", S5["gtot"], pgt[:, 0:2])
                yield
                P.act(S5["egl"], S5["gtot"], AF.Exp)
                P.act(S5["eg"], S5["gc"], AF.Exp)
                P.tt("dve", S5["kd"], S5["gtot"], S5["gc"], ALU.subtract)
                yield
                P.tt("dve", S5["bg"], S5["beta"], S5["eg"], ALU.mult)
                P.act(S5["kd"], S5["kd"], AF.Exp)
                yield

            def head_task(c, h_):
                csl = slice(c * 128, (c + 1) * 128)
                S5 = sc5s[c % 2]
                szb = szbs[c % 2]
                og = ogs[c % 2]
                B = hbuf[h_]
                hs = slice(h_, h_ + 1)
                kTc = gkT[h_][:, csl]
                qTc = gqT[h_][:, csl]
                P.ts("dve", B["gU"], triU, S5["g"][:, hs], None, ALU.mult)
                P.ts("dve", B["dB"], identF, S5["beta"][:, hs], None, ALU.mult)
                yield
                hq = hqs[h_]
                pD1 = hq()
                P.mm(pD1, ones1, B["gU"])
                pBR = hq()
                P.mm(pBR, ones1, B["dB"])
                yield
                P.copy("dve", B["D1s"], pD1)
                P.copy("dve", B["BRs"], pBR)
                yield
                pKK = hq()
                P.mm(pKK, kTc, kTc)
                yield
                P.copy("dve", B["KKs"], pKK)
                pKK = B["KKs"]
                pBR = B["BRs"]
                yield
                P.act(B["egRow"], B["D1s"], AF.Exp)
                P.ts("dve", B["E1"], B["D1s"], S5["gc"][:, hs], None, ALU.subtract)
                yield
                P.act(B["E1"], B["E1"], AF.Abs)
                yield
                P.act(B["Dall"], B["E1"], AF.Exp, scale=-1.0)
                yield
                P.tt("pool", B["DL"], B["Dall"], SL, ALU.mult)
                P.tt("pool", B["DUb"], B["Dall"], SU, ALU.mult)
                P.tt("pool", B["DUI"], B["Dall"], UI, ALU.mult)
                yield
                P.tt("dve", B["DUb"], B["DUb"], pBR, ALU.mult)
                N = B["Nb"]()
                NT_ = B["NTb"]()
                XT = B["XTb"]()
                P.stt("dve", N, pKK, S5["nbeta"][:, hs], B["DL"], ALU.mult, ALU.mult)
                yield
                P.stt("dve", NT_, pKK, -1.0, B["DUb"], ALU.mult, ALU.mult)
                yield
                P.tt("pool", XT, identF, NT_, ALU.add)
                ptk = tbs[h_]
                P.transpose(ptk, kTc, identB)
                yield
                P.ts("dve", B["kbg"], ptk, S5["bg"][:, hs], None, ALU.mult)
                P.ts("dve", B["kdec"], ptk, S5["kd"][:, hs], None, ALU.mult)
                yield
                ptv = tbs[h_]
                P.transpose(ptv, gvT[h_][:, csl], identB)
                yield
                P.ts("dve", B["vb"], ptv, S5["beta"][:, hs], None, ALU.mult)
                pqk = hq()
                P.mm(pqk, kTc, qTc)
                yield
                P.tt("dve", B["qkTs"], pqk, B["DUI"], ALU.mult)
                P.tt("pool", B["qdT"], qTc, B["egRow"], ALU.mult)
                yield
                for kk in range(1, 7):
                    pN = hq()
                    P.mm(pN, NT_, N)
                    if kk < 6:
                        pNT = hq()
                        P.mm(pNT, N, NT_)
                    yield
                    N2 = B["Nb"]()
                    P.copy("act", N2, pN)
                    if kk < 6:
                        NT2 = B["NTb"]()
                        P.copy("dve", NT2, pNT)
                    yield
                    pX = hq()
                    P.mm(pX, N2, XT)
                    yield
                    XT2 = B["XTb"]()
                    P.tt("dve", XT2, XT, pX, ALU.add)
                    N, XT = N2, XT2
                    if kk < 6:
                        NT_ = NT2
                    yield
                pu = hq()
                P.mm(pu, XT, B["vb"])
                pw = hq()
                P.mm(pw, B["kbg"], XT)
                yield
                P.copy("act", B["u_sb"], pu)
                P.copy("act", B["wTs"], pw)
                yield
                p1 = hq()
                P.mm(p1, B["wTs"], Sbf[h_])
                yield
                P.tt("dve", B["vnew"], B["u_sb"], p1, ALU.subtract)
                yield
                p2 = hq()
                P.mm(p2, B["qdT"], Sbf[h_], start=True, stop=False)
                P.mm(p2, B["qkTs"], B["vnew"], start=False, stop=True)
                p3 = hq()
                P.mm(p3, B["kdec"], B["vnew"])
                yield
                P.stt("dve", Sst[h_], Sst[h_], S5["egl"][:, hs], p3, ALU.mult, ALU.add)
                P.copy("dve", B["o_s"], p2)
                yield
                P.copy("act", Sbf[h_], Sst[h_])
                P.act(B["junk"], B["o_s"], AF.Square, accum_out=B["ss1"])
                yield
                P.act(B["rs1"], B["ss1"], AF.Sqrt, scale=1.0 / 128.0, bias=epsc[:, 0:1])
                yield
                P.recip(B["rs1"], B["rs1"])
                yield
                P.stt("dve", B["onb"], B["o_s"], B["rs1"][:, 0:1], nwb, ALU.mult, ALU.mult)
                yield
                P.tt("dve", og[:, h_ * 128:(h_ + 1) * 128], B["onb"], szb[:, h_ * 128:(h_ + 1) * 128], ALU.mult)
                yield

            def run_tasks(tasks):
                tasks = list(tasks)
                while tasks:
                    for g in list(tasks):
                        try:
                            next(g)
                        except StopIteration:
                            tasks.remove(g)

            run_tasks([scal_task(0)])
            for c in range(4):
                csl = slice(c * 128, (c + 1) * 128)
                tl = [head_task(c, 0), head_task(c, 1)]
                if c < 3:
                    tl.append(scal_task(c + 1))
                run_tasks(tl)
                og = ogs[c % 2]
                for h_ in range(2):
                    ptg = tbs[h_]
                    P.transpose(ptg, og[:, h_ * 128:(h_ + 1) * 128], identB)
                    P.copy("dve", ogT[:, h_, csl], ptg)
            for h_ in range(2):
                P.dma("sp", o_gdnT[h_ * 128:(h_ + 1) * 128, tsl], ogT[:, h_, :])


def build_mix(S, nc=None, do_attn=True, do_gdn=True, do_sc=True):
    if nc is None:
        nc = bass.Bass("TRN2", target_bir_lowering=False)
    P = Prog(nc)
    carr, coff = mix_consts()
    hT = P.dram("hT", [D, S], F32, "ExternalInput")
    wm_d = P.dram("wm", [D, NW], F32, "ExternalInput")
    cst_d = P.dram("cst", list(carr.shape), F32, "ExternalInput")
    gcw_d = P.dram("gcw", [128, 24], F32, "ExternalInput")
    scw_d = P.dram("scw", [128, 3], F32, "ExternalInput")
    hp_d = P.dram("hp", [128, 4], F32, "ExternalInput")
    nw_d = P.dram("nw", [128, 128], F32, "ExternalInput")
    o_sb = P.dram("o_sb", [128, S], BF16, "ExternalOutput")
    o_gdnT = P.dram("o_gdnT", [256, S], BF16, "ExternalOutput")
    o_sc = P.dram("o_sc", [128, S], BF16, "ExternalOutput")
    emit_mix(P, S, hT, wm_d, cst_d, gcw_d, scw_d, hp_d, nw_d, o_sb, o_gdnT, o_sc, do_attn, do_gdn, do_sc)
    st = P.finalize()
    return nc, st, carr


NSTEP = 4


def build_fused(S, depth=2, nc=None):
    if nc is None:
        nc = bass.Bass("TRN2", target_bir_lowering=False)
    P = Prog(nc)
    carr, _ = mix_consts()
    xT = P.dram("xT", [D, S], F32, "ExternalInput")
    outT = P.dram("outT", [D, S], F32, "ExternalOutput")
    lng = P.dram("lng", [128, depth * NSTEP * KT], F32, "ExternalInput")
    lnb = P.dram("lnb", [128, depth * NSTEP * KT], F32, "ExternalInput")
    cst = P.dram("cst", list(carr.shape), F32, "ExternalInput")
    W = []
    for i in range(depth):
        w = {}
        for f in range(2):
            w[f"w1{f}"] = P.dram(f"w1_{i}_{f}", [D, 2 * DFF], F32, "ExternalInput")
            w[f"w2{f}"] = P.dram(f"w2_{i}_{f}", [DFF, D], F32, "ExternalInput")
        for j in range(2):
            w[f"wm{j}"] = P.dram(f"wm_{i}_{j}", [D, NW], F32, "ExternalInput")
            w[f"gcw{j}"] = P.dram(f"gcw_{i}_{j}", [128, 24], F32, "ExternalInput")
            w[f"scw{j}"] = P.dram(f"scw_{i}_{j}", [128, 3], F32, "ExternalInput")
            w[f"hp{j}"] = P.dram(f"hp_{i}_{j}", [128, 4], F32, "ExternalInput")
        w["nw"] = P.dram(f"nw_{i}", [128, 128], F32, "ExternalInput")
        w["wo"] = P.dram(f"wo_{i}", [D, D], F32, "ExternalInput")
        w["wg"] = P.dram(f"wg_{i}", [D, D], F32, "ExternalInput")
        w["wp"] = P.dram(f"wp_{i}", [256, D], F32, "ExternalInput")
        w["bg"] = P.dram(f"bg_{i}", [128, KT], F32, "ExternalInput")
        w["pT"] = P.dram(f"pT_{i}", [256, S], F32, "ExternalInput")
        W.append(w)
    H = P.dram("H_scr", [D, S], F32, "Internal")
    X1 = P.dram("X1_scr", [D, S], F32, "Internal")
    X2 = P.dram("X2_scr", [D, S], F32, "Internal")
    MIX = P.dram("MIX_scr", [D, S], BF16, "Internal")

    def ln(i, s):
        o = (i * NSTEP + s) * KT
        return lng[:, o:o + KT], lnb[:, o:o + KT]

    cur = xT
    for i in range(depth):
        w = W[i]
        with P.stage(f"A{i}"):
            g, b = ln(i, 0)
            emit_dense(P, ["ffn"], S, cur, H, g, b, [(w["w10"], w["w20"])])
        for j in range(2):
            with P.stage(f"B{i}{j}"):
                emit_mix(P, S, H, w[f"wm{j}"], cst, w[f"gcw{j}"], w[f"scw{j}"], w[f"hp{j}"], w["nw"],
                         MIX[j * 128:(j + 1) * 128, :], MIX[256 + j * 256:256 + (j + 1) * 256, :],
                         MIX[768 + j * 128:768 + (j + 1) * 128, :])
        with P.stage(f"M{i}"):
            g, b = ln(i, 1)
            emit_dense(P, ["mixout"], S, H, X1, g, b, [(w["wo"], MIX)])
        with P.stage(f"F{i}"):
            g, b = ln(i, 2)
            emit_dense(P, ["ffn"], S, X1, X2, g, b, [(w["w11"], w["w21"])])
        with P.stage(f"P{i}"):
            g, b = ln(i, 3)
            dst = outT if i == depth - 1 else X1
            emit_dense(P, ["ple"], S, X2, dst, g, b, [(w["wg"], w["wp"], w["bg"], w["pT"])])
        cur = X1
    return nc, P.tot, carr


def _relay(v):
    return np.ascontiguousarray(np.asarray(v, np.float32).reshape(8, 128).T)


def fused_inputs(b, S, depth, carr, x, p, ln_g, ln_b, ffn_w_in, ffn_w_out, mix_w_in, gdn_conv_w, gdn_a_log,
                 gdn_dt_bias, gdn_norm_w, sc_conv_w, mix_w_out, ple_w_proj, ple_w_gate, ple_b_gate, shared=None):
    m = {} if shared is None else dict(shared)
    m["xT"] = np.ascontiguousarray(x[b].T)
    for i in range(depth):
        m[f"pT_{i}"] = np.ascontiguousarray(p[i, b].T)
    if shared is not None:
        return m
    m["cst"] = carr
    m["lng"] = np.ascontiguousarray(np.concatenate([_relay(ln_g[i, s]) for i in range(depth) for s in range(4)], 1))
    m["lnb"] = np.ascontiguousarray(np.concatenate([_relay(ln_b[i, s]) for i in range(depth) for s in range(4)], 1))
    OFF_SB = 768
    OFF_QKV = OFF_SB + 1536
    OFF_A = OFF_QKV + 512
    OFF_Bt = OFF_A + 4
    OFF_SC = OFF_Bt + 4
    for i in range(depth):
        for f in range(2):
            m[f"w1_{i}_{f}"] = np.ascontiguousarray(ffn_w_in[i, f])
            m[f"w2_{i}_{f}"] = np.ascontiguousarray(ffn_w_out[i, f])
        for j in range(2):
            cols = []
            for base in (0, 256, 512):
                cols.append(np.arange(base + j * 128, base + (j + 1) * 128))
            for base in (OFF_SB, OFF_SB + 512, OFF_SB + 1024, OFF_QKV):
                cols.append(np.arange(base + j * 256, base + (j + 1) * 256))
            cols.append(np.arange(OFF_A + j * 2, OFF_A + j * 2 + 2))
            cols.append(np.arange(OFF_Bt + j * 2, OFF_Bt + j * 2 + 2))
            for base in (OFF_SC, OFF_SC + 256, OFF_SC + 512):
                cols.append(np.arange(base + j * 128, base + (j + 1) * 128))
            cols = np.concatenate(cols)
            m[f"wm_{i}_{j}"] = np.ascontiguousarray(mix_w_in[i][:, cols])
            gidx = np.concatenate([np.arange(base + j * 256, base + (j + 1) * 256) for base in (0, 512, 1024)])
            m[f"gcw_{i}_{j}"] = np.ascontiguousarray(gdn_conv_w[i][:, gidx].reshape(4, 6, 128).transpose(2, 1, 0).reshape(128, 24))
            m[f"scw_{i}_{j}"] = np.ascontiguousarray(sc_conv_w[i][:, j * 128:(j + 1) * 128].T)
            m[f"hp_{i}_{j}"] = np.ascontiguousarray(np.tile(np.concatenate(
                [gdn_a_log[i][2 * j:2 * j + 2], gdn_dt_bias[i][2 * j:2 * j + 2]])[None, :], (128, 1)).astype(np.float32))
        m[f"nw_{i}"] = np.ascontiguousarray(np.tile(gdn_norm_w[i][None, :], (128, 1)).astype(np.float32))
        m[f"wo_{i}"] = np.ascontiguousarray(mix_w_out[i])
        m[f"wg_{i}"] = np.ascontiguousarray(ple_w_gate[i])
        m[f"wp_{i}"] = np.ascontiguousarray(ple_w_proj[i])
        m[f"bg_{i}"] = _relay(ple_b_gate[i])
    return m


from concourse.bass_utils import run_bass_kernel_spmd

BATCH, SEQ, DEPTH = 4, 8192, 2


def kernel(x, p, ln_g, ln_b, ffn_w_in, ffn_w_out, mix_w_in, gdn_conv_w, gdn_a_log,
           gdn_dt_bias, gdn_norm_w, sc_conv_w, mix_w_out, ple_w_proj, ple_w_gate, ple_b_gate):
    f = lambda a: np.asarray(a, np.float32)
    args = dict(x=f(x), p=f(p), ln_g=f(ln_g), ln_b=f(ln_b), ffn_w_in=f(ffn_w_in), ffn_w_out=f(ffn_w_out),
                mix_w_in=f(mix_w_in), gdn_conv_w=f(gdn_conv_w), gdn_a_log=f(gdn_a_log), gdn_dt_bias=f(gdn_dt_bias),
                gdn_norm_w=f(gdn_norm_w), sc_conv_w=f(sc_conv_w), mix_w_out=f(mix_w_out), ple_w_proj=f(ple_w_proj),
                ple_w_gate=f(ple_w_gate), ple_b_gate=f(ple_b_gate))
    nc, _, carr = build_fused(SEQ, DEPTH)
    m0 = fused_inputs(0, SEQ, DEPTH, carr, **args)
    shared = {k: v for k, v in m0.items() if k != "xT" and not k.startswith("pT_")}
    maps = [m0] + [fused_inputs(b, SEQ, DEPTH, carr, shared=shared, **args) for b in range(1, BATCH)]
    res = run_bass_kernel_spmd(nc, maps, core_ids=list(range(BATCH)))
    out = np.empty((BATCH, SEQ, D), np.float32)
    for b in range(BATCH):
        out[b] = res.results[b]["outT"].T
    return out
```

```python
import numpy as np
from contextlib import ExitStack, contextmanager
import concourse.bass as bass
import concourse.mybir as mybir

F32 = mybir.dt.float32
BF16 = mybir.dt.bfloat16
AF = mybir.ActivationFunctionType
ALU = mybir.AluOpType
AX = mybir.AxisListType


def _prod(xs):
    r = 1
    for x in xs:
        r *= int(x)
    return r


class Op:
    __slots__ = ("eng", "fn", "reads", "writes", "dma", "deps", "sig", "dsem", "dval", "dprev", "pe_mm")

    def __init__(self, eng, fn, reads, writes, dma, pe_mm=False):
        self.eng = eng
        self.fn = fn
        self.reads = reads
        self.writes = writes
        self.dma = dma
        self.deps = ()
        self.sig = None
        self.dsem = None
        self.dval = None
        self.dprev = None
        self.pe_mm = pe_mm


class Prog:
    NDSEM = 24

    def __init__(self, nc, same_engine_sync=None):
        self.nc = nc
        self.ops = []
        self.tinfo = {}
        self.hist = {}
        import os as _os
        if same_engine_sync is None:
            same_engine_sync = _os.environ.get("FW_SES", "1") == "1"
        self.same_engine_sync = same_engine_sync
        self.engs = {"pe": nc.tensor, "act": nc.scalar, "dve": nc.vector, "pool": nc.gpsimd, "sp": nc.sync}
        self._n = 0
        self.stk = None
        self.sname = ""
        self.esem = None
        self.tot = dict(n_ops=0, n_waits=0, n_dma=0)

    @contextmanager
    def stage(self, name):
        self.stk = ExitStack()
        self.sname = name + "_"
        self.ops = []
        try:
            yield self
            self.finalize(barrier=True)
        finally:
            self.stk.close()
            self.stk = None
            self.sname = ""
            self.ops = []

    def sbuf(self, name, shape, dt):
        name = self.sname + name
        if self.stk is not None:
            t = self.stk.enter_context(self.nc.sbuf_tensor(name, [int(s) for s in shape], dt))
        else:
            t = self.nc.alloc_sbuf_tensor(name, [int(s) for s in shape], dt)
        self.tinfo[name] = ("sb", _prod(shape[1:]))
        return t.ap()

    def psum(self, name, shape, dt=F32):
        name = self.sname + name
        if self.stk is not None:
            t = self.stk.enter_context(self.nc.psum_tensor(name, [int(s) for s in shape], dt))
        else:
            t = self.nc.alloc_psum_tensor(name, [int(s) for s in shape], dt)
        self.tinfo[name] = ("ps", _prod(shape[1:]))
        return t.ap()

    def dram(self, name, shape, dt, kind):
        t = self.nc.dram_tensor(name, [int(s) for s in shape], dt, kind=kind)
        self.tinfo[name] = ("const" if kind == "ExternalInput" else "dram", None)
        return t.ap()

    def rect(self, ap):
        name = ap.tensor.name
        kind, ps = self.tinfo[name]
        off = int(ap.offset)
        dims = ap.ap
        if kind in ("dram", "const"):
            hi = off + sum((c - 1) * abs(s) for s, c in dims) + 1
            return (name, 0, 1, off, hi)
        p0 = off // ps
        f0 = off % ps
        pc = dims[0][1]
        hi = f0 + sum((c - 1) * abs(s) for s, c in dims[1:]) + 1
        return (name, p0, p0 + pc, f0, hi)

    def add(self, eng, fn, reads=(), writes=(), dma=False, pe_mm=False):
        rr = []
        for a in reads:
            if a is None or isinstance(a, (int, float)):
                continue
            r = self.rect(a)
            if self.tinfo[r[0]][0] == "const":
                continue
            rr.append(r)
        ww = [self.rect(a) for a in writes]
        op = Op(eng, fn, rr, ww, dma, pe_mm)
        self.ops.append(op)
        return op

    def mm(self, out, lhsT, rhs, start=True, stop=True):
        self.add("pe", lambda e: e.matmul(out, lhsT, rhs, start=start, stop=stop),
                 reads=[lhsT, rhs], writes=[out], pe_mm=True)

    def transpose(self, out, in_, ident):
        self.add("pe", lambda e: e.transpose(out, in_, ident), reads=[in_, ident], writes=[out], pe_mm=True)

    def act(self, out, in_, func, bias=None, scale=None, accum_out=None):
        kw = {}
        if bias is not None:
            kw["bias"] = bias
        if scale is not None:
            kw["scale"] = scale
        if accum_out is not None:
            kw["accum_out"] = accum_out
        rd = [in_]
        if bias is not None and not isinstance(bias, (int, float)):
            rd.append(bias)
        if scale is not None and not isinstance(scale, (int, float)):
            rd.append(scale)
        wr = [out] + ([accum_out] if accum_out is not None else [])
        self.add("act", lambda e: e.activation(out, in_, func, **kw), reads=rd, writes=wr)

    def tt(self, eng, out, in0, in1, op):
        self.add(eng, lambda e: e.tensor_tensor(out, in0, in1, op), reads=[in0, in1], writes=[out])

    def ts(self, eng, out, in0, s1, s2, op0, op1=None):
        rd = [in0] + [s for s in (s1, s2) if s is not None and not isinstance(s, (int, float))]
        if op1 is None:
            self.add(eng, lambda e: e.tensor_scalar(out, in0, s1, None, op0), reads=rd, writes=[out])
        else:
            self.add(eng, lambda e: e.tensor_scalar(out, in0, s1, s2, op0, op1), reads=rd, writes=[out])

    def stt(self, eng, out, in0, scalar, in1, op0, op1):
        rd = [in0, in1] + ([scalar] if not isinstance(scalar, (int, float)) else [])
        self.add(eng, lambda e: e.scalar_tensor_tensor(out, in0, scalar, in1, op0, op1), reads=rd, writes=[out])

    def copy(self, eng, out, in_):
        if eng == "act":
            self.add(eng, lambda e: e.copy(out, in_), reads=[in_], writes=[out])
        else:
            self.add(eng, lambda e: e.tensor_copy(out, in_), reads=[in_], writes=[out])

    def recip(self, out, in_):
        self.add("dve", lambda e: e.reciprocal(out, in_), reads=[in_], writes=[out])

    def memset(self, eng, out, val):
        self.add(eng, lambda e: e.memset(out, val), reads=[], writes=[out])

    def dma(self, q, out, in_):
        self.add(q, lambda e: e.dma_start(out=out, in_=in_), reads=[in_], writes=[out], dma=True)

    @staticmethod
    def _ov(a, b):
        return a[1] < b[2] and b[1] < a[2] and a[3] < b[4] and b[3] < a[4]

    @staticmethod
    def _contains(a, b):
        return a[1] <= b[1] and b[2] <= a[2] and a[3] <= b[3] and b[4] <= a[4]

    def finalize(self, barrier=False):
        ops = self.ops
        hist = {}
        for i, op in enumerate(ops):
            deps = set()
            for r in op.reads:
                for seg in hist.get(r[0], ()):
                    if seg[1] is not None and self._ov(seg[0], r):
                        deps.add(seg[1])
            for w in op.writes:
                for seg in hist.get(w[0], ()):
                    if self._ov(seg[0], w):
                        if seg[1] is not None:
                            deps.add(seg[1])
                        deps.update(seg[2].values())
                        deps.update(seg[3])
            for r in op.reads:
                lst = hist.setdefault(r[0], [])
                found = None
                for seg in lst:
                    if seg[0] == r:
                        found = seg
                        break
                if found is None:
                    found = [r, None, {}, []]
                    lst.append(found)
                if op.dma:
                    found[3].append(i)
                else:
                    found[2][op.eng] = i
            for w in op.writes:
                lst = hist.setdefault(w[0], [])
                lst[:] = [seg for seg in lst if not self._contains(w, seg[0])]
                lst.append([w, i, {}, []])
            deps.discard(i)
            op.deps = sorted(deps)
        need = [False] * len(ops)
        for i, op in enumerate(ops):
            for j in op.deps:
                pj = ops[j]
                if pj.dma:
                    continue
                if pj.eng == op.eng and not op.dma:
                    if pj.eng == "pe" or not self.same_engine_sync:
                        continue
                need[j] = True
        if barrier:
            last = {}
            for i, op in enumerate(ops):
                if not op.dma:
                    last[op.eng] = i
            for i in last.values():
                need[i] = True
        nc = self.nc
        if self.esem is None:
            self.esem = {k: nc.alloc_semaphore(name=f"e_{k}") for k in self.engs}
            self.dsems = [nc.alloc_semaphore(name=f"d_{i}") for i in range(self.NDSEM)]
            self.ecount = {k: 0 for k in self.engs}
            self.dcount = [0] * self.NDSEM
            self.nd = 0
            self.known = {k: {} for k in self.engs}
        esem, dsems, ecount, dcount, known = self.esem, self.dsems, self.ecount, self.dcount, self.known
        for i, op in enumerate(ops):
            if op.dma:
                sidx = self.nd % self.NDSEM
                self.nd += 1
                op.dprev = dcount[sidx]
                dcount[sidx] += 16
                op.dsem = sidx
                op.dval = dcount[sidx]
            elif need[i]:
                ecount[op.eng] += 1
                op.sig = ecount[op.eng]
        nwaits = 0
        for i, op in enumerate(ops):
            e = self.engs[op.eng]
            kn = known[op.eng]
            waits = {}
            for j in op.deps:
                pj = ops[j]
                if pj.dma:
                    key = ("d", pj.dsem)
                    val = pj.dval
                else:
                    if pj.eng == op.eng and not op.dma:
                        if pj.eng == "pe" or not self.same_engine_sync:
                            continue
                    key = ("e", pj.eng)
                    val = pj.sig
                if kn.get(key, 0) >= val:
                    continue
                if waits.get(key, 0) < val:
                    waits[key] = val
            if op.dma and op.dprev > 0:
                key = ("d", op.dsem)
                if kn.get(key, 0) < op.dprev and waits.get(key, 0) < op.dprev:
                    waits[key] = op.dprev
            for key, val in waits.items():
                sem = dsems[key[1]] if key[0] == "d" else esem[key[1]]
                e.wait_ge(sem, val)
                kn[key] = val
                nwaits += 1
            ins = op.fn(e)
            if op.dma:
                ins.then_inc(dsems[op.dsem], 16)
            elif op.sig is not None:
                ins.then_inc(esem[op.eng], 1)
        targets = list(self.engs) if barrier else ["sp"]
        for k in targets:
            e = self.engs[k]
            kn = known[k]
            for sidx in range(self.NDSEM):
                if dcount[sidx] > kn.get(("d", sidx), 0):
                    e.wait_ge(dsems[sidx], dcount[sidx])
                    kn[("d", sidx)] = dcount[sidx]
                    nwaits += 1
            for k2 in ("pe", "act", "dve", "pool"):
                if ecount[k2] > kn.get(("e", k2), 0):
                    e.wait_ge(esem[k2], ecount[k2])
                    kn[("e", k2)] = ecount[k2]
                    nwaits += 1
        self.stats = dict(n_ops=len(ops), n_waits=nwaits, sigs=dict(ecount), n_dma=self.nd)
        self.tot["n_ops"] += len(ops)
        self.tot["n_waits"] += nwaits
        return self.stats


D = 1024
DFF = 2816
KT = 8
TN = 512
ALPHA = 4.0 ** 0.25
LN_EPS = 1e-5


class DenseCtx:
    pass


def load_w(P, name, w_dram, rows, cols, nsplit=1, q="pool"):
    kt = rows // 128
    w = P.sbuf(name, [128, kt, cols], BF16)
    step = cols // nsplit
    for s in range(nsplit):
        for k in range(kt):
            P.dma(q, w[:, k, s * step:(s + 1) * step], w_dram[k * 128:(k + 1) * 128, s * step:(s + 1) * step])
    return w


def emit_ln(P, C, g, b):
    x32, xb = C.x32, C.xb
    pm = C.pstat[0]
    for m in range(KT):
        P.mm(pm, C.onesF, x32[:, m, :], start=(m == 0), stop=(m == KT - 1))
    for m in range(KT):
        P.tt("dve", x32[:, m, :], x32[:, m, :], pm, ALU.subtract)
    pv = C.pstat[1]
    for m in range(KT):
        sq = C.sq[m % 2]
        P.act(sq, x32[:, m, :], AF.Square)
        P.mm(pv, C.onesF, sq, start=(m == 0), stop=(m == KT - 1))
    P.act(C.rstd, pv, AF.Sqrt, bias=C.epsc[:, 0:1])
    P.recip(C.rstd, C.rstd)
    for m in range(KT):
        P.tt("dve", x32[:, m, :], x32[:, m, :], C.rstd, ALU.mult)
        P.act(x32[:, m, :], x32[:, m, :], AF.Identity, bias=b[:, m:m + 1], scale=g[:, m:m + 1])
        P.act(xb[:, m, :], x32[:, m, :], AF.Copy)


def emit_ffn(P, C, w1, w2, g, b):
    x32, xb, hT = C.x32, C.xb, C.hT
    NC = DFF // 128
    for c in range(NC):
        pg = C.pg[c % 2]
        pu = C.pu[c % 2]
        for k in range(KT):
            P.mm(pg, w1[:, k, c * 128:(c + 1) * 128], xb[:, k, :], start=(k == 0), stop=(k == KT - 1))
        for k in range(KT):
            P.mm(pu, w1[:, k, DFF + c * 128:DFF + (c + 1) * 128], xb[:, k, :], start=(k == 0), stop=(k == KT - 1))
        sg = C.sg[c % 2]
        P.act(sg, pg, AF.Silu)
        P.stt("dve", hT[:, c, :], sg, 0.5, pu, ALU.mult, ALU.mult)
    for m in range(KT):
        py = C.py[m % 2]
        for c in range(NC):
            P.mm(py, w2[:, c, m * 128:(m + 1) * 128], hT[:, c, :], start=(c == 0), stop=(c == NC - 1))
        P.stt("dve", x32[:, m, :], x32[:, m, :], ALPHA, py, ALU.mult, ALU.add)
    emit_ln(P, C, g, b)


def emit_mixout(P, C, mixb, wo, g, b):
    x32 = C.x32
    for m in range(KT):
        py = C.py[m % 2]
        for k in range(KT):
            P.mm(py, wo[:, k, m * 128:(m + 1) * 128], mixb[:, k, :], start=(k == 0), stop=(k == KT - 1))
        P.stt("dve", x32[:, m, :], x32[:, m, :], ALPHA, py, ALU.mult, ALU.add)
    emit_ln(P, C, g, b)


def emit_ple(P, C, pb, wg, bg, wp, g, b):
    x32, xb = C.x32, C.xb
    for m in range(KT):
        pgt = C.pg[m % 2]
        ppj = C.pu[m % 2]
        for k in range(KT):
            P.mm(pgt, wg[:, k, m * 128:(m + 1) * 128], xb[:, k, :], start=(k == 0), stop=(k == KT - 1))
        for k in range(2):
            P.mm(ppj, wp[:, k, m * 128:(m + 1) * 128], pb[:, k, :], start=(k == 0), stop=(k == 1))
        sg = C.sg[m % 2]
        P.act(sg, pgt, AF.Sigmoid, bias=bg[:, m:m + 1])
        P.tt("dve", sg, sg, ppj, ALU.mult)
        P.stt("dve", x32[:, m, :], x32[:, m, :], ALPHA, sg, ALU.mult, ALU.add)
    emit_ln(P, C, g, b)


def emit_dense(P, steps, ntok, xT, outT, lng_d, lnb_d, wd):
    nt = ntok // TN
    nln = len(steps)
    C = DenseCtx()
    C.x32 = P.sbuf("x32", [128, KT, TN], F32)
    C.xb = P.sbuf("xb", [128, KT, TN], BF16)
    C.onesF = P.sbuf("onesF", [128, 128], F32)
    C.sq = [P.sbuf(f"sq{i}", [128, TN], F32) for i in range(2)]
    C.sg = [P.sbuf(f"sg{i}", [128, TN], F32) for i in range(2)]
    C.rstd = P.sbuf("rstd", [128, TN], F32)
    C.pg = [P.psum(f"pg{i}", [128, TN]) for i in range(2)]
    C.pu = [P.psum(f"pu{i}", [128, TN]) for i in range(2)]
    C.py = [P.psum(f"py{i}", [128, TN]) for i in range(2)]
    C.pstat = [P.psum(f"pst{i}", [128, TN]) for i in range(2)]
    lng = P.sbuf("lng_s", [128, nln * KT], F32)
    lnb = P.sbuf("lnb_s", [128, nln * KT], F32)
    P.dma("sp", lng, lng_d)
    P.dma("sp", lnb, lnb_d)
    P.memset("dve", C.onesF, 1.0 / D)
    C.epsc = P.sbuf("epsc", [128, 1], F32)
    P.memset("dve", C.epsc, LN_EPS)
    ws = []
    need_hT = False
    for i, s in enumerate(steps):
        if s == "ffn":
            need_hT = True
            w1 = load_w(P, f"w1s_{i}", wd[i][0], D, 2 * DFF, nsplit=4)
            w2 = load_w(P, f"w2s_{i}", wd[i][1], DFF, D)
            ws.append((w1, w2))
        elif s == "mixout":
            wo = load_w(P, f"wos_{i}", wd[i][0], D, D)
            mixb = P.sbuf(f"mixb_{i}", [128, KT, TN], BF16)
            ws.append((wo, mixb))
        elif s == "ple":
            wg = load_w(P, f"wgs_{i}", wd[i][0], D, D)
            wp = load_w(P, f"wps_{i}", wd[i][1], 256, D)
            bg = P.sbuf(f"bgs_{i}", [128, KT], F32)
            P.dma("sp", bg, wd[i][2])
            pb = P.sbuf(f"pb_{i}", [128, 2, TN], BF16)
            ws.append((wg, wp, bg, pb))
    if need_hT:
        C.hT = P.sbuf("hT", [128, DFF // 128, TN], BF16)
    xTr = xT.rearrange("(k p) n -> p k n", p=128)
    oTr = outT.rearrange("(k p) n -> p k n", p=128)
    x32s = [C.x32, P.sbuf("x32b", [128, KT, TN], F32)]
    mixbs = {}
    pbs = {}
    for i, s in enumerate(steps):
        if s == "mixout":
            mixbs[i] = [ws[i][1], P.sbuf(f"mixb2_{i}", [128, KT, TN], BF16)]
        elif s == "ple":
            pbs[i] = [ws[i][3], P.sbuf(f"pb2_{i}", [128, 2, TN], BF16)]

    def load_tile(t):
        tsl = slice(t * TN, (t + 1) * TN)
        P.dma("sp", x32s[t % 2], xTr[:, :, tsl])
        for i, s in enumerate(steps):
            if s == "mixout":
                P.dma("sp", mixbs[i][t % 2], wd[i][1].rearrange("(k p) n -> p k n", p=128)[:, :, tsl])
            elif s == "ple":
                pTr = wd[i][3].rearrange("(k p) n -> p k n", p=128)
                for k in range(2):
                    P.dma("pool", pbs[i][t % 2][:, k, :], pTr[:, k, tsl])

    load_tile(0)
    for t in range(nt):
        tsl = slice(t * TN, (t + 1) * TN)
        C.x32 = x32s[t % 2]
        if steps[0] != "mixout":
            for m in range(KT):
                P.act(C.xb[:, m, :], C.x32[:, m, :], AF.Copy)
        if t + 1 < nt:
            load_tile(t + 1)
        for i, s in enumerate(steps):
            g = lng[:, i * KT:(i + 1) * KT]
            b = lnb[:, i * KT:(i + 1) * KT]
            if s == "ffn":
                emit_ffn(P, C, ws[i][0], ws[i][1], g, b)
            elif s == "mixout":
                emit_mixout(P, C, mixbs[i][t % 2], ws[i][0], g, b)
            elif s == "ple":
                emit_ple(P, C, pbs[i][t % 2], ws[i][0], ws[i][2], ws[i][1], g, b)
        P.dma("pool", oTr[:, :, tsl], C.x32)


def build_dense(steps, ntok, nc=None):
    if nc is None:
        nc = bass.Bass("TRN2", target_bir_lowering=False)
    P = Prog(nc)
    xT = P.dram("xT", [D, ntok], F32, "ExternalInput")
    outT = P.dram("outT", [D, ntok], F32, "ExternalOutput")
    nln = len(steps)
    lng_d = P.dram("lng", [128, nln * KT], F32, "ExternalInput")
    lnb_d = P.dram("lnb", [128, nln * KT], F32, "ExternalInput")
    wd = []
    for i, s in enumerate(steps):
        if s == "ffn":
            wd.append((P.dram(f"w1_{i}", [D, 2 * DFF], F32, "ExternalInput"),
                       P.dram(f"w2_{i}", [DFF, D], F32, "ExternalInput")))
        elif s == "mixout":
            wd.append((P.dram(f"wo_{i}", [D, D], F32, "ExternalInput"),
                       P.dram(f"mixT_{i}", [D, ntok], BF16, "ExternalInput")))
        elif s == "ple":
            wd.append((P.dram(f"wg_{i}", [D, D], F32, "ExternalInput"),
                       P.dram(f"wp_{i}", [256, D], F32, "ExternalInput"),
                       P.dram(f"bg_{i}", [128, KT], F32, "ExternalInput"),
                       P.dram(f"pT_{i}", [256, ntok], F32, "ExternalInput")))
    emit_dense(P, steps, ntok, xT, outT, lng_d, lnb_d, wd)
    st = P.finalize()
    return nc, st

import os
POOL = os.environ.get("MIX_POOL", "pool")
LVL = int(os.environ.get("MIX_LVL", "9"))
ACT_PSUM_R = os.environ.get("MIX_ACT_PSUM_R", "1") == "1"
NOCARRY = os.environ.get("MIX_NOCARRY", "0") == "1"

D = 1024
KT = 8
TN = 512
NW = 1796
C_SQ, C_SK, C_SV = 0, 128, 256
C_GQ, C_GK, C_GV, C_GZ = 384, 640, 896, 1152
C_AB = 1408
C_SCB, C_SCC, C_SCH = 1412, 1540, 1668
NORM_EPS = 1e-6


def mix_consts():
    p = np.arange(128)[:, None]
    f = np.arange(128)[None, :]
    c = {}
    c["ident"] = (p == f).astype(np.float32)
    c["triU"] = (p <= f).astype(np.float32)
    c["triNeg"] = -(p >= f).astype(np.float32)
    c["SL"] = (p > f).astype(np.float32)
    c["SU"] = (p < f).astype(np.float32)
    c["UI"] = (p <= f).astype(np.float32)
    m = np.zeros((128, 4, 512), np.float32)
    tq = np.arange(512)[None, :]
    for i in range(4):
        m[:, i, :] = ((128 * i + p) < tq).astype(np.float32)
    c["mask"] = m.reshape(128, 2048)
    order = ["ident", "triU", "triNeg", "SL", "SU", "UI", "mask"]
    arr = np.concatenate([c[k] for k in order], axis=1)
    offs = {}
    o = 0
    for k in order:
        offs[k] = (o, o + c[k].shape[1])
        o += c[k].shape[1]
    return arr, offs


class Ring:
    def __init__(self, items):
        self.items = items
        self.i = 0

    def __call__(self):
        x = self.items[self.i % len(self.items)]
        self.i += 1
        return x


def emit_mix(P, S, hT, wm_d, cst_d, gcw_d, scw_d, hp_d, nw_d, o_sb, o_gdnT, o_sc,
             do_attn=True, do_gdn=True, do_sc=True):
    NT = S // TN
    NKB = S // 128
    carr, coff = mix_consts()

    cst = P.sbuf("cst_s", list(carr.shape), F32)
    P.dma("sp", cst, cst_d)

    def cs(k):
        a, b = coff[k]
        return cst[:, a:b]
    identF, triU, SL, SU, UI = cs("ident"), cs("triU"), cs("SL"), cs("SU"), cs("UI")
    maskF = cs("mask")
    identB = P.sbuf("identB", [128, 128], BF16)
    triNegB = P.sbuf("triNegB", [128, 128], BF16)
    P.copy("dve", identB, identF)
    P.copy("dve", triNegB, cs("triNeg"))
    ones1 = P.sbuf("ones1", [128, 128], F32)
    P.memset("dve", ones1, 1.0)
    onesRowB = P.sbuf("onesRowB", [1, 128], BF16)
    P.memset("dve", onesRowB, 1.0)
    onec = P.sbuf("onec", [128, 1], F32)
    P.memset("dve", onec, 1.0)
    epsc = P.sbuf("epsc", [128, 1], F32)
    P.memset("dve", epsc, NORM_EPS)
    gcw = P.sbuf("gcw_s", [128, 24], F32)
    scw = P.sbuf("scw_s", [128, 3], F32)
    hp = P.sbuf("hp_s", [128, 4], F32)
    nwb = P.sbuf("nw_s", [128, 128], F32)
    P.dma("sp", gcw, gcw_d)
    P.dma("sp", scw, scw_d)
    P.dma("sp", hp, hp_d)
    P.dma("sp", nwb, nw_d)
    nA = P.sbuf("nA", [128, 2], F32)
    P.act(nA, hp[:, 0:2], AF.Exp)
    P.ts("dve", nA, nA, -1.0, None, ALU.mult)
    dtb = hp[:, 2:4]
    wm = P.sbuf("wm_s", [128, KT, NW], BF16)
    for k in range(KT):
        P.dma("pool", wm[:, k, :], wm_d[k * 128:(k + 1) * 128, :])

    hb = P.sbuf("hb", [128, KT, TN], BF16)
    qT = P.sbuf("qT", [128, S], BF16)
    kT = P.sbuf("kT", [128, S], BF16)
    vA = P.sbuf("vA", [128, NKB, 128], BF16)
    pf = P.psum("pf", [128, 6, 512], F32)
    pbf = P.psum("pbf", [128, 2, 1024], BF16)
    bank = Ring([pf[:, i, :] for i in range(0, 4)])
    bank_o = Ring([pf[:, 4, :], pf[:, 5, :]])
    quart = lambda: bank()[:, 0:128]
    tbank = Ring([pbf[:, i, 0:128] for i in range(2)])
    tbs = [pbf[:, i, 0:128] for i in range(2)]
    hqs = [Ring([pf[:, 2 * h, 0:128], pf[:, 2 * h + 1, 0:128]]) for h in range(2)]
    sq = Ring([pf[:, 4, :], pf[:, 5, :]])
    eb = Ring([P.sbuf(f"e{i}", [128, TN], F32) for i in range(8)])
    Lb = Ring([P.sbuf(f"L{i}", [128, TN], BF16) for i in range(6)])
    eRb = Ring([P.sbuf(f"eR{i}", [128, TN], F32) for i in range(2)])
    attb = Ring([P.sbuf(f"att{i}", [128, TN], BF16) for i in range(7)])
    lsumb = [Ring([P.sbuf(f"ls{h}_{i}", [128, TN], BF16) for i in range(4)]) for h in range(2)]
    negOnesB = P.sbuf("negOnesB", [128, 128], BF16)
    P.memset("dve", negOnesB, -1.0)
    osb = [P.sbuf(f"osb{i}", [64, TN], BF16) for i in range(2)]
    raw = [P.sbuf(f"raw{f}", [128, 3 + TN], F32) for f in range(6)]
    ycv = P.sbuf("ycv", [128, TN], F32)
    ysl = ycv
    sqb = P.sbuf("sqb", [128, TN], F32)
    rnb = sqb
    gqT = [P.sbuf(f"gqT{h}", [128, TN], BF16) for h in range(2)]
    gkT = [P.sbuf(f"gkT{h}", [128, TN], BF16) for h in range(2)]
    gvT = [P.sbuf(f"gvT{h}", [128, TN], BF16) for h in range(2)]
    szbs = [P.sbuf(f"szb{i}", [128, 256], F32) for i in range(2)]
    sc5s = [{n: P.sbuf(f"sc{i}_{n}", [128, 2], F32) for n in
             ("beta", "nbeta", "g", "gc", "gtot", "egl", "eg", "bg", "kd", "tmp")} for i in range(2)]
    ab_sbs = [P.sbuf(f"ab_sb{i}", [128, 4], F32) for i in range(2)]
    ogs = [P.sbuf(f"og{i}", [128, 256], BF16) for i in range(2)]
    ogT = P.sbuf("ogT", [128, 2, TN], BF16)

    def t128(name, dt=F32):
        return P.sbuf(name, [128, 128], dt)
    hbuf = []
    for h in range(2):
        B = {}
        for n in ("gU", "E1", "Dall", "DL", "DUb", "DUI", "dB", "egRow", "D1s", "BRs", "KKs", "kbg", "vb", "u_sb", "junk", "onb", "o_s"):
            B[n] = t128(f"h{h}_{n}")
        for n in ("kdec", "wTs", "qkTs", "qdT", "vnew"):
            B[n] = t128(f"h{h}_{n}", BF16)
        B["Nb"] = Ring([t128(f"h{h}_Nb{i}") for i in range(3)])
        B["NTb"] = Ring([t128(f"h{h}_NTb{i}") for i in range(3)])
        B["XTb"] = Ring([t128(f"h{h}_XTb{i}") for i in range(3)])
        B["ss1"] = P.sbuf(f"h{h}_ss1", [128, 1], F32)
        B["rs1"] = P.sbuf(f"h{h}_rs1", [128, 1], F32)
        hbuf.append(B)
    Sst = [t128(f"S{h}") for h in range(2)]
    Sbf = [t128(f"Sb{h}", BF16) for h in range(2)]
    for h in range(2):
        P.memset("dve", Sst[h], 0.0)
        P.memset("dve", Sbf[h], 0.0)
    for f in range(6):
        P.memset("dve", raw[f][:, 0:3], 0.0)
    rawc = P.sbuf("rawc", [128, 2 + TN], F32)
    P.memset("dve", rawc[:, 0:2], 0.0)
    scB = P.sbuf("scB", [128, TN], F32)
    scC = P.sbuf("scC", [128, TN], F32)
    scy = scC
    sco = P.sbuf("sco", [128, TN], BF16)

    hTr = hT.rearrange("(k p) n -> p k n", p=128)

    def proj_fm(col0, ncol=128):
        pb = bank()
        for k in range(KT):
            P.mm(pb[0:ncol, :], wm[:, k, col0:col0 + ncol], hb[:, k, :], start=(k == 0), stop=(k == KT - 1))
        return pb

    for t in range(NT):
        tsl = slice(t * TN, (t + 1) * TN)
        for k in range(KT):
            P.dma("pool", hb[:, k, :], hTr[:, k, tsl])
        if do_attn:
            pq = proj_fm(C_SQ)
            P.ts("dve", qT[:, tsl], pq, 0.125, None, ALU.mult)
            if not os.environ.get("MIX_NOK"):
                pk = proj_fm(C_SK)
                P.act(kT[:, tsl], pk, AF.Copy)
            for s in range(0 if os.environ.get("MIX_NOV") else 4):
                pv = quart()
                for k in range(KT):
                    P.mm(pv, hb[:, k, s * 128:(s + 1) * 128], wm[:, k, C_SV:C_SV + 128], start=(k == 0), stop=(k == KT - 1))
                P.copy("dve", vA[:, 4 * t + s, :], pv)
            nkb = 4 * t + 4
            items = [(hd, kb) for kb in range(nkb - 1, -1, -1) for hd in range(2)]
            po_h = [bank_o(), bank_o()]
            st1 = {}
            st2 = {}
            lsum_cur = [None, None]

            def att_s1(hd, kb):
                ps = slice(64 * hd, 64 * hd + 64)
                pz = bank()
                P.mm(pz, kT[ps, kb * 128:(kb + 1) * 128], qT[ps, tsl])
                e = eb()
                P.act(e, pz, AF.Exp)
                if kb >= 4 * t:
                    i = kb - 4 * t
                    P.tt(POOL, e, e, maskF[:, i * 512:(i + 1) * 512], ALU.mult)
                L = Lb()
                P.act(L, e, AF.Ln, bias=onec[:, 0:1])
                carry = lsum_cur[hd]
                if kb > 0:
                    ns = lsumb[hd]()
                    if carry is None:
                        P.copy("dve", ns, L)
                    else:
                        P.tt("dve", ns, carry, L, ALU.add)
                    lsum_cur[hd] = ns
                st1[(hd, kb)] = (e, L, carry)

            def att_s2(hd, kb):
                e, L, carry = st1.pop((hd, kb))
                pr = bank()
                P.mm(pr, triNegB, L, start=True, stop=(carry is None))
                if carry is not None:
                    P.mm(pr, negOnesB, carry, start=False, stop=True)
                eR = eRb()
                P.act(eR, pr, AF.Exp)
                att = attb()
                P.tt(POOL, att, e, eR, ALU.mult)
                st2[(hd, kb)] = att

            def att_s3(hd, kb):
                att = st2.pop((hd, kb))
                P.mm(po_h[hd][0:64, :], vA[:, kb, 64 * hd:64 * hd + 64], att, start=(kb == nkb - 1), stop=(kb == 0))

            LOOK = int(os.environ.get("MIX_LOOK", "4"))
            for idx in range(len(items) + 2 * LOOK):
                if idx < len(items):
                    att_s1(*items[idx])
                if LOOK <= idx < len(items) + LOOK:
                    att_s2(*items[idx - LOOK])
                if idx >= 2 * LOOK:
                    att_s3(*items[idx - 2 * LOOK])
            for hd in range(2):
                P.act(osb[hd], po_h[hd][0:64, :], AF.Copy)
                P.dma("sp", o_sb[64 * hd:64 * hd + 64, tsl], osb[hd])
        if do_sc:
            pB = proj_fm(C_SCB)
            P.act(scB, pB, AF.Copy)
            pC = proj_fm(C_SCC)
            P.act(scC, pC, AF.Copy)
            pH = proj_fm(C_SCH)
            if t > 0:
                P.copy("dve", rawc[:, 0:2], rawc[:, TN:TN + 2])
            P.tt("dve", rawc[:, 2:2 + TN], scC, pH, ALU.mult)
            P.ts("dve", scy, rawc[:, 0:TN], scw[:, 0:1], None, ALU.mult)
            for i in (1, 2):
                P.stt("dve", scy, rawc[:, i:i + TN], scw[:, i:i + 1], scy, ALU.mult, ALU.add)
            P.tt("dve", sco, scB, scy, ALU.mult)
            P.dma("sp", o_sc[:, tsl], sco)
        if do_gdn:
            for f in range(6):
                col0 = C_GQ + f * 128
                pg = proj_fm(col0)
                if t > 0:
                    P.copy("dve", raw[f][:, 0:3], raw[f][:, TN:TN + 3])
                P.act(raw[f][:, 3:3 + TN], pg, AF.Copy)
                P.ts("dve", ycv, raw[f][:, 0:TN], gcw[:, f * 4:f * 4 + 1], None, ALU.mult)
                for i in (1, 2, 3):
                    P.stt("dve", ycv, raw[f][:, i:i + TN], gcw[:, f * 4 + i:f * 4 + i + 1], ycv, ALU.mult, ALU.add)
                h_ = f % 2
                if f >= 4:
                    P.act(gvT[h_], ycv, AF.Silu)
                    continue
                P.act(ysl, ycv, AF.Silu)
                P.act(sqb, ysl, AF.Square)
                pss = bank()
                P.mm(pss, ones1, sqb)
                P.act(rnb, pss, AF.Sqrt, bias=epsc[:, 0:1])
                P.recip(rnb, rnb)
                if f < 2:
                    P.stt("dve", gqT[h_], ysl, 128.0 ** -0.5, rnb, ALU.mult, ALU.mult)
                else:
                    P.tt("dve", gkT[h_], ysl, rnb, ALU.mult)
            def scal_task(c):
                csl = slice(c * 128, (c + 1) * 128)
                S5 = sc5s[c % 2]
                szb = szbs[c % 2]
                ab_sb = ab_sbs[c % 2]
                pzz = sq()
                for k in range(KT):
                    P.mm(pzz[:, 0:256], hb[:, k, csl], wm[:, k, C_GZ:C_GZ + 256], start=(k == 0), stop=(k == KT - 1))
                P.act(szb, pzz[:, 0:256], AF.Silu)
                yield
                pab = sq()
                for k in range(KT):
                    P.mm(pab[:, 0:4], hb[:, k, csl], wm[:, k, C_AB:C_AB + 4], start=(k == 0), stop=(k == KT - 1))
                P.copy("dve", ab_sb, pab[:, 0:4])
                yield
                P.act(S5["beta"], ab_sb[:, 2:4], AF.Sigmoid)
                P.tt("dve", S5["tmp"], ab_sb[:, 0:2], dtb, ALU.add)
                yield
                P.ts("dve", S5["nbeta"], S5["beta"], -1.0, None, ALU.mult)
                P.act(S5["tmp"], S5["tmp"], AF.Exp)
                yield
                P.act(S5["tmp"], S5["tmp"], AF.Ln, bias=onec[:, 0:1])
                yield
                P.tt("dve", S5["g"], S5["tmp"], nA, ALU.mult)
                yield
                pgc = sq()
                P.mm(pgc[:, 0:2], triU, S5["g"])
                pgt = sq()
                P.mm(pgt[:, 0:2], ones1, S5["g"])
                yield
                P.copy("dve", S5["gc"], pgc[:, 0:2])
                P.copy("dve", S5["gtot"], pgt[:, 0:2])
                yield
                P.act(S5["egl"], S5["gtot"], AF.Exp)
                P.act(S5["eg"], S5["gc"], AF.Exp)
                P.tt("dve", S5["kd"], S5["gtot"], S5["gc"], ALU.subtract)
                yield
                P.tt("dve", S5["bg"], S5["beta"], S5["eg"], ALU.mult)
                P.act(S5["kd"], S5["kd"], AF.Exp)
                yield

            def head_task(c, h_):
                csl = slice(c * 128, (c + 1) * 128)
                S5 = sc5s[c % 2]
                szb = szbs[c % 2]
                og = ogs[c % 2]
                B = hbuf[h_]
                hs = slice(h_, h_ + 1)
                kTc = gkT[h_][:, csl]
                qTc = gqT[h_][:, csl]
                P.ts("dve", B["gU"], triU, S5["g"][:, hs], None, ALU.mult)
                P.ts("dve", B["dB"], identF, S5["beta"][:, hs], None, ALU.mult)
                yield
                hq = hqs[h_]
                pD1 = hq()
                P.mm(pD1, ones1, B["gU"])
                pBR = hq()
                P.mm(pBR, ones1, B["dB"])
                yield
                P.copy("dve", B["D1s"], pD1)
                P.copy("dve", B["BRs"], pBR)
                yield
                pKK = hq()
                P.mm(pKK, kTc, kTc)
                yield
                P.copy("dve", B["KKs"], pKK)
                pKK = B["KKs"]
                pBR = B["BRs"]
                yield
                P.act(B["egRow"], B["D1s"], AF.Exp)
                P.ts("dve", B["E1"], B["D1s"], S5["gc"][:, hs], None, ALU.subtract)
                yield
                P.act(B["E1"], B["E1"], AF.Abs)
                yield
                P.act(B["Dall"], B["E1"], AF.Exp, scale=-1.0)
                yield
                P.tt("pool", B["DL"], B["Dall"], SL, ALU.mult)
                P.tt("pool", B["DUb"], B["Dall"], SU, ALU.mult)
                P.tt("pool", B["DUI"], B["Dall"], UI, ALU.mult)
                yield
                P.tt("dve", B["DUb"], B["DUb"], pBR, ALU.mult)
                N = B["Nb"]()
                NT_ = B["NTb"]()
                XT = B["XTb"]()
                P.stt("dve", N, pKK, S5["nbeta"][:, hs], B["DL"], ALU.mult, ALU.mult)
                yield
                P.stt("dve", NT_, pKK, -1.0, B["DUb"], ALU.mult, ALU.mult)
                yield
                P.tt("pool", XT, identF, NT_, ALU.add)
                ptk = tbs[h_]
                P.transpose(ptk, kTc, identB)
                yield
                P.ts("dve", B["kbg"], ptk, S5["bg"][:, hs], None, ALU.mult)
                P.ts("dve", B["kdec"], ptk, S5["kd"][:, hs], None, ALU.mult)
                yield
                ptv = tbs[h_]
                P.transpose(ptv, gvT[h_][:, csl], identB)
                yield
                P.ts("dve", B["vb"], ptv, S5["beta"][:, hs], None, ALU.mult)
                pqk = hq()
                P.mm(pqk, kTc, qTc)
                yield
                P.tt("dve", B["qkTs"], pqk, B["DUI"], ALU.mult)
                P.tt("pool", B["qdT"], qTc, B["egRow"], ALU.mult)
                yield
                for kk in range(1, 7):
                    pN = hq()
                    P.mm(pN, NT_, N)
                    if kk < 6:
                        pNT = hq()
                        P.mm(pNT, N, NT_)
                    yield
                    N2 = B["Nb"]()
                    P.copy("act", N2, pN)
                    if kk < 6:
                        NT2 = B["NTb"]()
                        P.copy("dve", NT2, pNT)
                    yield
                    pX = hq()
                    P.mm(pX, N2, XT)
                    yield
                    XT2 = B["XTb"]()
                    P.tt("dve", XT2, XT, pX, ALU.add)
                    N, XT = N2, XT2
                    if kk < 6:
                        NT_ = NT2
                    yield
                pu = hq()
                P.mm(pu, XT, B["vb"])
                pw = hq()
                P.mm(pw, B["kbg"], XT)
                yield
                P.copy("act", B["u_sb"], pu)
                P.copy("act", B["wTs"], pw)
                yield
                p1 = hq()
                P.mm(p1, B["wTs"], Sbf[h_])
                yield
                P.tt("dve", B["vnew"], B["u_sb"], p1, ALU.subtract)
                yield
                p2 = hq()
                P.mm(p2, B["qdT"], Sbf[h_], start=True, stop=False)
                P.mm(p2, B["qkTs"], B["vnew"], start=False, stop=True)
                p3 = hq()
                P.mm(p3, B["kdec"], B["vnew"])
                yield
                P.stt("dve", Sst[h_], Sst[h_], S5["egl"][:, hs], p3, ALU.mult, ALU.add)
                P.copy("dve", B["o_s"], p2)
                yield
                P.copy("act", Sbf[h_], Sst[h_])
                P.act(B["junk"], B["o_s"], AF.Square, accum_out=B["ss1"])
                yield
                P.act(B["rs1"], B["ss1"], AF.Sqrt, scale=1.0 / 128.0, bias=epsc[:, 0:1])
                yield
                P.recip(B["rs1"], B["rs1"])
                yield
                P.stt("dve", B["onb"], B["o_s"], B["rs1"][:, 0:1], nwb, ALU.mult, ALU.mult)
                yield
                P.tt("dve", og[:, h_ * 128:(h_ + 1) * 128], B["onb"], szb[:, h_ * 128:(h_ + 1) * 128], ALU.mult)
                yield

            def run_tasks(tasks):
                tasks = list(tasks)
                while tasks:
                    for g in list(tasks):
                        try:
                            next(g)
                        except StopIteration:
                            tasks.remove(g)

            run_tasks([scal_task(0)])
            for c in range(4):
                csl = slice(c * 128, (c + 1) * 128)
                tl = [head_task(c, 0), head_task(c, 1)]
                if c < 3:
                    tl.append(scal_task(c + 1))
                run_tasks(tl)
                og = ogs[c % 2]
                for h_ in range(2):
                    ptg = tbs[h_]
                    P.transpose(ptg, og[:, h_ * 128:(h_ + 1) * 128], identB)
                    P.copy("dve", ogT[:, h_, csl], ptg)
            for h_ in range(2):
                P.dma("sp", o_gdnT[h_ * 128:(h_ + 1) * 128, tsl], ogT[:, h_, :])


def build_mix(S, nc=None, do_attn=True, do_gdn=True, do_sc=True):
    if nc is None:
        nc = bass.Bass("TRN2", target_bir_lowering=False)
    P = Prog(nc)
    carr, coff = mix_consts()
    hT = P.dram("hT", [D, S], F32, "ExternalInput")
    wm_d = P.dram("wm", [D, NW], F32, "ExternalInput")
    cst_d = P.dram("cst", list(carr.shape), F32, "ExternalInput")
    gcw_d = P.dram("gcw", [128, 24], F32, "ExternalInput")
    scw_d = P.dram("scw", [128, 3], F32, "ExternalInput")
    hp_d = P.dram("hp", [128, 4], F32, "ExternalInput")
    nw_d = P.dram("nw", [128, 128], F32, "ExternalInput")
    o_sb = P.dram("o_sb", [128, S], BF16, "ExternalOutput")
    o_gdnT = P.dram("o_gdnT", [256, S], BF16, "ExternalOutput")
    o_sc = P.dram("o_sc", [128, S], BF16, "ExternalOutput")
    emit_mix(P, S, hT, wm_d, cst_d, gcw_d, scw_d, hp_d, nw_d, o_sb, o_gdnT, o_sc, do_attn, do_gdn, do_sc)
    st = P.finalize()
    return nc, st, carr


NSTEP = 4


def build_fused(S, depth=2, nc=None):
    if nc is None:
        nc = bass.Bass("TRN2", target_bir_lowering=False)
    P = Prog(nc)
    carr, _ = mix_consts()
    xT = P.dram("xT", [D, S], F32, "ExternalInput")
    outT = P.dram("outT", [D, S], F32, "ExternalOutput")
    lng = P.dram("lng", [128, depth * NSTEP * KT], F32, "ExternalInput")
    lnb = P.dram("lnb", [128, depth * NSTEP * KT], F32, "ExternalInput")
    cst = P.dram("cst", list(carr.shape), F32, "ExternalInput")
    W = []
    for i in range(depth):
        w = {}
        for f in range(2):
            w[f"w1{f}"] = P.dram(f"w1_{i}_{f}", [D, 2 * DFF], F32, "ExternalInput")
            w[f"w2{f}"] = P.dram(f"w2_{i}_{f}", [DFF, D], F32, "ExternalInput")
        for j in range(2):
            w[f"wm{j}"] = P.dram(f"wm_{i}_{j}", [D, NW], F32, "ExternalInput")
            w[f"gcw{j}"] = P.dram(f"gcw_{i}_{j}", [128, 24], F32, "ExternalInput")
            w[f"scw{j}"] = P.dram(f"scw_{i}_{j}", [128, 3], F32, "ExternalInput")
            w[f"hp{j}"] = P.dram(f"hp_{i}_{j}", [128, 4], F32, "ExternalInput")
        w["nw"] = P.dram(f"nw_{i}", [128, 128], F32, "ExternalInput")
        w["wo"] = P.dram(f"wo_{i}", [D, D], F32, "ExternalInput")
        w["wg"] = P.dram(f"wg_{i}", [D, D], F32, "ExternalInput")
        w["wp"] = P.dram(f"wp_{i}", [256, D], F32, "ExternalInput")
        w["bg"] = P.dram(f"bg_{i}", [128, KT], F32, "ExternalInput")
        w["pT"] = P.dram(f"pT_{i}", [256, S], F32, "ExternalInput")
        W.append(w)
    H = P.dram("H_scr", [D, S], F32, "Internal")
    X1 = P.dram("X1_scr", [D, S], F32, "Internal")
    X2 = P.dram("X2_scr", [D, S], F32, "Internal")
    MIX = P.dram("MIX_scr", [D, S], BF16, "Internal")

    def ln(i, s):
        o = (i * NSTEP + s) * KT
        return lng[:, o:o + KT], lnb[:, o:o + KT]

    cur = xT
    for i in range(depth):
        w = W[i]
        with P.stage(f"A{i}"):
            g, b = ln(i, 0)
            emit_dense(P, ["ffn"], S, cur, H, g, b, [(w["w10"], w["w20"])])
        for j in range(2):
            with P.stage(f"B{i}{j}"):
                emit_mix(P, S, H, w[f"wm{j}"], cst, w[f"gcw{j}"], w[f"scw{j}"], w[f"hp{j}"], w["nw"],
                         MIX[j * 128:(j + 1) * 128, :], MIX[256 + j * 256:256 + (j + 1) * 256, :],
                         MIX[768 + j * 128:768 + (j + 1) * 128, :])
        with P.stage(f"M{i}"):
            g, b = ln(i, 1)
            emit_dense(P, ["mixout"], S, H, X1, g, b, [(w["wo"], MIX)])
        with P.stage(f"F{i}"):
            g, b = ln(i, 2)
            emit_dense(P, ["ffn"], S, X1, X2, g, b, [(w["w11"], w["w21"])])
        with P.stage(f"P{i}"):
            g, b = ln(i, 3)
            dst = outT if i == depth - 1 else X1
            emit_dense(P, ["ple"], S, X2, dst, g, b, [(w["wg"], w["wp"], w["bg"], w["pT"])])
        cur = X1
    return nc, P.tot, carr


def _relay(v):
    return np.ascontiguousarray(np.asarray(v, np.float32).reshape(8, 128).T)


def fused_inputs(b, S, depth, carr, x, p, ln_g, ln_b, ffn_w_in, ffn_w_out, mix_w_in, gdn_conv_w, gdn_a_log,
                 gdn_dt_bias, gdn_norm_w, sc_conv_w, mix_w_out, ple_w_proj, ple_w_gate, ple_b_gate, shared=None):
    m = {} if shared is None else dict(shared)
    m["xT"] = np.ascontiguousarray(x[b].T)
    for i in range(depth):
        m[f"pT_{i}"] = np.ascontiguousarray(p[i, b].T)
    if shared is not None:
        return m
    m["cst"] = carr
    m["lng"] = np.ascontiguousarray(np.concatenate([_relay(ln_g[i, s]) for i in range(depth) for s in range(4)], 1))
    m["lnb"] = np.ascontiguousarray(np.concatenate([_relay(ln_b[i, s]) for i in range(depth) for s in range(4)], 1))
    OFF_SB = 768
    OFF_QKV = OFF_SB + 1536
    OFF_A = OFF_QKV + 512
    OFF_Bt = OFF_A + 4
    OFF_SC = OFF_Bt + 4
    for i in range(depth):
        for f in range(2):
            m[f"w1_{i}_{f}"] = np.ascontiguousarray(ffn_w_in[i, f])
            m[f"w2_{i}_{f}"] = np.ascontiguousarray(ffn_w_out[i, f])
        for j in range(2):
            cols = []
            for base in (0, 256, 512):
                cols.append(np.arange(base + j * 128, base + (j + 1) * 128))
            for base in (OFF_SB, OFF_SB + 512, OFF_SB + 1024, OFF_QKV):
                cols.append(np.arange(base + j * 256, base + (j + 1) * 256))
            cols.append(np.arange(OFF_A + j * 2, OFF_A + j * 2 + 2))
            cols.append(np.arange(OFF_Bt + j * 2, OFF_Bt + j * 2 + 2))
            for base in (OFF_SC, OFF_SC + 256, OFF_SC + 512):
                cols.append(np.arange(base + j * 128, base + (j + 1) * 128))
            cols = np.concatenate(cols)
            m[f"wm_{i}_{j}"] = np.ascontiguousarray(mix_w_in[i][:, cols])
            gidx = np.concatenate([np.arange(base + j * 256, base + (j + 1) * 256) for base in (0, 512, 1024)])
            m[f"gcw_{i}_{j}"] = np.ascontiguousarray(gdn_conv_w[i][:, gidx].reshape(4, 6, 128).transpose(2, 1, 0).reshape(128, 24))
            m[f"scw_{i}_{j}"] = np.ascontiguousarray(sc_conv_w[i][:, j * 128:(j + 1) * 128].T)
            m[f"hp_{i}_{j}"] = np.ascontiguousarray(np.tile(np.concatenate(
                [gdn_a_log[i][2 * j:2 * j + 2], gdn_dt_bias[i][2 * j:2 * j + 2]])[None, :], (128, 1)).astype(np.float32))
        m[f"nw_{i}"] = np.ascontiguousarray(np.tile(gdn_norm_w[i][None, :], (128, 1)).astype(np.float32))
        m[f"wo_{i}"] = np.ascontiguousarray(mix_w_out[i])
        m[f"wg_{i}"] = np.ascontiguousarray(ple_w_gate[i])
        m[f"wp_{i}"] = np.ascontiguousarray(ple_w_proj[i])
        m[f"bg_{i}"] = _relay(ple_b_gate[i])
    return m


from concourse.bass_utils import run_bass_kernel_spmd

BATCH, SEQ, DEPTH = 4, 8192, 2


def kernel(x, p, ln_g, ln_b, ffn_w_in, ffn_w_out, mix_w_in, gdn_conv_w, gdn_a_log,
           gdn_dt_bias, gdn_norm_w, sc_conv_w, mix_w_out, ple_w_proj, ple_w_gate, ple_b_gate):
    f = lambda a: np.asarray(a, np.float32)
    args = dict(x=f(x), p=f(p), ln_g=f(ln_g), ln_b=f(ln_b), ffn_w_in=f(ffn_w_in), ffn_w_out=f(ffn_w_out),
                mix_w_in=f(mix_w_in), gdn_conv_w=f(gdn_conv_w), gdn_a_log=f(gdn_a_log), gdn_dt_bias=f(gdn_dt_bias),
                gdn_norm_w=f(gdn_norm_w), sc_conv_w=f(sc_conv_w), mix_w_out=f(mix_w_out), ple_w_proj=f(ple_w_proj),
                ple_w_gate=f(ple_w_gate), ple_b_gate=f(ple_b_gate))
    nc, _, carr = build_fused(SEQ, DEPTH)
    m0 = fused_inputs(0, SEQ, DEPTH, carr, **args)
    shared = {k: v for k, v in m0.items() if k != "xT" and not k.startswith("pT_")}
    maps = [m0] + [fused_inputs(b, SEQ, DEPTH, carr, shared=shared, **args) for b in range(1, BATCH)]
    res = run_bass_kernel_spmd(nc, maps, core_ids=list(range(BATCH)))
    out = np.empty((BATCH, SEQ, D), np.float32)
    for b in range(BATCH):
        out[b] = res.results[b]["outT"].T
    return out
```

```python
import numpy as np
from contextlib import ExitStack, contextmanager
import concourse.bass as bass
import concourse.mybir as mybir

F32 = mybir.dt.float32
BF16 = mybir.dt.bfloat16
AF = mybir.ActivationFunctionType
ALU = mybir.AluOpType
AX = mybir.AxisListType


def _prod(xs):
    r = 1
    for x in xs:
        r *= int(x)
    return r


class Op:
    __slots__ = ("eng", "fn", "reads", "writes", "dma", "deps", "sig", "dsem", "dval", "dprev", "pe_mm")

    def __init__(self, eng, fn, reads, writes, dma, pe_mm=False):
        self.eng = eng
        self.fn = fn
        self.reads = reads
        self.writes = writes
        self.dma = dma
        self.deps = ()
        self.sig = None
        self.dsem = None
        self.dval = None
        self.dprev = None
        self.pe_mm = pe_mm


class Prog:
    NDSEM = 24

    def __init__(self, nc, same_engine_sync=None):
        self.nc = nc
        self.ops = []
        self.tinfo = {}
        self.hist = {}
        import os as _os
        if same_engine_sync is None:
            same_engine_sync = _os.environ.get("FW_SES", "1") == "1"
        self.same_engine_sync = same_engine_sync
        self.engs = {"pe": nc.tensor, "act": nc.scalar, "dve": nc.vector, "pool": nc.gpsimd, "sp": nc.sync}
        self._n = 0
        self.stk = None
        self.sname = ""
        self.esem = None
        self.tot = dict(n_ops=0, n_waits=0, n_dma=0)

    @contextmanager
    def stage(self, name):
        self.stk = ExitStack()
        self.sname = name + "_"
        self.ops = []
        try:
            yield self
            self.finalize(barrier=True)
        finally:
            self.stk.close()
            self.stk = None
            self.sname = ""
            self.ops = []

    def sbuf(self, name, shape, dt):
        name = self.sname + name
        if self.stk is not None:
            t = self.stk.enter_context(self.nc.sbuf_tensor(name, [int(s) for s in shape], dt))
        else:
            t = self.nc.alloc_sbuf_tensor(name, [int(s) for s in shape], dt)
        self.tinfo[name] = ("sb", _prod(shape[1:]))
        return t.ap()

    def psum(self, name, shape, dt=F32):
        name = self.sname + name
        if self.stk is not None:
            t = self.stk.enter_context(self.nc.psum_tensor(name, [int(s) for s in shape], dt))
        else:
            t = self.nc.alloc_psum_tensor(name, [int(s) for s in shape], dt)
        self.tinfo[name] = ("ps", _prod(shape[1:]))
        return t.ap()

    def dram(self, name, shape, dt, kind):
        t = self.nc.dram_tensor(name, [int(s) for s in shape], dt, kind=kind)
        self.tinfo[name] = ("const" if kind == "ExternalInput" else "dram", None)
        return t.ap()

    def rect(self, ap):
        name = ap.tensor.name
        kind, ps = self.tinfo[name]
        off = int(ap.offset)
        dims = ap.ap
        if kind in ("dram", "const"):
            hi = off + sum((c - 1) * abs(s) for s, c in dims) + 1
            return (name, 0, 1, off, hi)
        p0 = off // ps
        f0 = off % ps
        pc = dims[0][1]
        hi = f0 + sum((c - 1) * abs(s) for s, c in dims[1:]) + 1
        return (name, p0, p0 + pc, f0, hi)

    def add(self, eng, fn, reads=(), writes=(), dma=False, pe_mm=False):
        rr = []
        for a in reads:
            if a is None or isinstance(a, (int, float)):
                continue
            r = self.rect(a)
            if self.tinfo[r[0]][0] == "const":
                continue
            rr.append(r)
        ww = [self.rect(a) for a in writes]
        op = Op(eng, fn, rr, ww, dma, pe_mm)
        self.ops.append(op)
        return op

    def mm(self, out, lhsT, rhs, start=True, stop=True):
        self.add("pe", lambda e: e.matmul(out, lhsT, rhs, start=start, stop=stop),
                 reads=[lhsT, rhs], writes=[out], pe_mm=True)

    def transpose(self, out, in_, ident):
        self.add("pe", lambda e: e.transpose(out, in_, ident), reads=[in_, ident], writes=[out], pe_mm=True)

    def act(self, out, in_, func, bias=None, scale=None, accum_out=None):
        kw = {}
        if bias is not None:
            kw["bias"] = bias
        if scale is not None:
            kw["scale"] = scale
        if accum_out is not None:
            kw["accum_out"] = accum_out
        rd = [in_]
        if bias is not None and not isinstance(bias, (int, float)):
            rd.append(bias)
        if scale is not None and not isinstance(scale, (int, float)):
            rd.append(scale)
        wr = [out] + ([accum_out] if accum_out is not None else [])
        self.add("act", lambda e: e.activation(out, in_, func, **kw), reads=rd, writes=wr)

    def tt(self, eng, out, in0, in1, op):
        self.add(eng, lambda e: e.tensor_tensor(out, in0, in1, op), reads=[in0, in1], writes=[out])

    def ts(self, eng, out, in0, s1, s2, op0, op1=None):
        rd = [in0] + [s for s in (s1, s2) if s is not None and not isinstance(s, (int, float))]
        if op1 is None:
            self.add(eng, lambda e: e.tensor_scalar(out, in0, s1, None, op0), reads=rd, writes=[out])
        else:
            self.add(eng, lambda e: e.tensor_scalar(out, in0, s1, s2, op0, op1), reads=rd, writes=[out])

    def stt(self, eng, out, in0, scalar, in1, op0, op1):
        rd = [in0, in1] + ([scalar] if not isinstance(scalar, (int, float)) else [])
        self.add(eng, lambda e: e.scalar_tensor_tensor(out, in0, scalar, in1, op0, op1), reads=rd, writes=[out])

    def copy(self, eng, out, in_):
        if eng == "act":
            self.add(eng, lambda e: e.copy(out, in_), reads=[in_], writes=[out])
        else:
            self.add(eng, lambda e: e.tensor_copy(out, in_), reads=[in_], writes=[out])

    def recip(self, out, in_):
        self.add("dve", lambda e: e.reciprocal(out, in_), reads=[in_], writes=[out])

    def memset(self, eng, out, val):
        self.add(eng, lambda e: e.memset(out, val), reads=[], writes=[out])

    def dma(self, q, out, in_):
        self.add(q, lambda e: e.dma_start(out=out, in_=in_), reads=[in_], writes=[out], dma=True)

    @staticmethod
    def _ov(a, b):
        return a[1] < b[2] and b[1] < a[2] and a[3] < b[4] and b[3] < a[4]

    @staticmethod
    def _contains(a, b):
        return a[1] <= b[1] and b[2] <= a[2] and a[3] <= b[3] and b[4] <= a[4]

    def finalize(self, barrier=False):
        ops = self.ops
        hist = {}
        for i, op in enumerate(ops):
            deps = set()
            for r in op.reads:
                for seg in hist.get(r[0], ()):
                    if seg[1] is not None and self._ov(seg[0], r):
                        deps.add(seg[1])
            for w in op.writes:
                for seg in hist.get(w[0], ()):
                    if self._ov(seg[0], w):
                        if seg[1] is not None:
                            deps.add(seg[1])
                        deps.update(seg[2].values())
                        deps.update(seg[3])
            for r in op.reads:
                lst = hist.setdefault(r[0], [])
                found = None
                for seg in lst:
                    if seg[0] == r:
                        found = seg
                        break
                if found is None:
                    found = [r, None, {}, []]
                    lst.append(found)
                if op.dma:
                    found[3].append(i)
                else:
                    found[2][op.eng] = i
            for w in op.writes:
                lst = hist.setdefault(w[0], [])
                lst[:] = [seg for seg in lst if not self._contains(w, seg[0])]
                lst.append([w, i, {}, []])
            deps.discard(i)
            op.deps = sorted(deps)
        need = [False] * len(ops)
        for i, op in enumerate(ops):
            for j in op.deps:
                pj = ops[j]
                if pj.dma:
                    continue
                if pj.eng == op.eng and not op.dma:
                    if pj.eng == "pe" or not self.same_engine_sync:
                        continue
                need[j] = True
        if barrier:
            last = {}
            for i, op in enumerate(ops):
                if not op.dma:
                    last[op.eng] = i
            for i in last.values():
                need[i] = True
        nc = self.nc
        if self.esem is None:
            self.esem = {k: nc.alloc_semaphore(name=f"e_{k}") for k in self.engs}
            self.dsems = [nc.alloc_semaphore(name=f"d_{i}") for i in range(self.NDSEM)]
            self.ecount = {k: 0 for k in self.engs}
            self.dcount = [0] * self.NDSEM
            self.nd = 0
            self.known = {k: {} for k in self.engs}
        esem, dsems, ecount, dcount, known = self.esem, self.dsems, self.ecount, self.dcount, self.known
        for i, op in enumerate(ops):
            if op.dma:
                sidx = self.nd % self.NDSEM
                self.nd += 1
                op.dprev = dcount[sidx]
                dcount[sidx] += 16
                op.dsem = sidx
                op.dval = dcount[sidx]
            elif need[i]:
                ecount[op.eng] += 1
                op.sig = ecount[op.eng]
        nwaits = 0
        for i, op in enumerate(ops):
            e = self.engs[op.eng]
            kn = known[op.eng]
            waits = {}
            for j in op.deps:
                pj = ops[j]
                if pj.dma:
                    key = ("d", pj.dsem)
                    val = pj.dval
                else:
                    if pj.eng == op.eng and not op.dma:
                        if pj.eng == "pe" or not self.same_engine_sync:
                            continue
                    key = ("e", pj.eng)
                    val = pj.sig
                if kn.get(key, 0) >= val:
                    continue
                if waits.get(key, 0) < val:
                    waits[key] = val
            if op.dma and op.dprev > 0:
                key = ("d", op.dsem)
                if kn.get(key, 0) < op.dprev and waits.get(key, 0) < op.dprev:
                    waits[key] = op.dprev
            for key, val in waits.items():
                sem = dsems[key[1]] if key[0] == "d" else esem[key[1]]
                e.wait_ge(sem, val)
                kn[key] = val
                nwaits += 1
            ins = op.fn(e)
            if op.dma:
                ins.then_inc(dsems[op.dsem], 16)
            elif op.sig is not None:
                ins.then_inc(esem[op.eng], 1)
        targets = list(self.engs) if barrier else ["sp"]
        for k in targets:
            e = self.engs[k]
            kn = known[k]
            for sidx in range(self.NDSEM):
                if dcount[sidx] > kn.get(("d", sidx), 0):
                    e.wait_ge(dsems[sidx], dcount[sidx])
                    kn[("d", sidx)] = dcount[sidx]
                    nwaits += 1
            for k2 in ("pe", "act", "dve", "pool"):
                if ecount[k2] > kn.get(("e", k2), 0):
                    e.wait_ge(esem[k2], ecount[k2])
                    kn[("e", k2)] = ecount[k2]
                    nwaits += 1
        self.stats = dict(n_ops=len(ops), n_waits=nwaits, sigs=dict(ecount), n_dma=self.nd)
        self.tot["n_ops"] += len(ops)
        self.tot["n_waits"] += nwaits
        return self.stats


D = 1024
DFF = 2816
KT = 8
TN = 512
ALPHA = 4.0 ** 0.25
LN_EPS = 1e-5


import os as _os
PIPE_FFN = _os.environ.get("PIPE_FFN", "1") == "1"


class DenseCtx:
    pass


def load_w(P, name, w_dram, rows, cols, nsplit=1, q="pool"):
    kt = rows // 128
    w = P.sbuf(name, [128, kt, cols], BF16)
    step = cols // nsplit
    for s in range(nsplit):
        for k in range(kt):
            P.dma(q, w[:, k, s * step:(s + 1) * step], w_dram[k * 128:(k + 1) * 128, s * step:(s + 1) * step])
    return w


def emit_ln(P, C, g, b):
    x32, xb = C.x32, C.xb
    pm = C.pstat[0]
    for m in range(KT):
        P.mm(pm, C.onesF, x32[:, m, :], start=(m == 0), stop=(m == KT - 1))
    for m in range(KT):
        P.tt("dve", x32[:, m, :], x32[:, m, :], pm, ALU.subtract)
    pv = C.pstat[1]
    for m in range(KT):
        sq = C.sq[m % 2]
        P.act(sq, x32[:, m, :], AF.Square)
        P.mm(pv, C.onesF, sq, start=(m == 0), stop=(m == KT - 1))
    P.act(C.rstd, pv, AF.Sqrt, bias=C.epsc[:, 0:1])
    P.recip(C.rstd, C.rstd)
    for m in range(KT):
        P.tt("dve", x32[:, m, :], x32[:, m, :], C.rstd, ALU.mult)
        P.act(x32[:, m, :], x32[:, m, :], AF.Identity, bias=b[:, m:m + 1], scale=g[:, m:m + 1])
        P.act(xb[:, m, :], x32[:, m, :], AF.Copy)


def gen_ln(P, C, x32, xb, g, b, write_xb=True):
    pm = C.pstat[0]
    for m in range(KT):
        P.mm(pm, C.onesF, x32[:, m, :], start=(m == 0), stop=(m == KT - 1))
    yield
    for m in range(KT):
        P.tt("dve", x32[:, m, :], x32[:, m, :], pm, ALU.subtract)
        if m % 2 == 1:
            yield
    pv = C.pstat[1]
    for m in range(KT):
        sq = C.sq[m % 2]
        P.act(sq, x32[:, m, :], AF.Square)
        yield
        P.mm(pv, C.onesF, sq, start=(m == 0), stop=(m == KT - 1))
    yield
    P.act(C.rstd, pv, AF.Sqrt, bias=C.epsc[:, 0:1])
    yield
    P.recip(C.rstd, C.rstd)
    yield
    for m in range(KT):
        P.tt("dve", x32[:, m, :], x32[:, m, :], C.rstd, ALU.mult)
        yield
        P.act(x32[:, m, :], x32[:, m, :], AF.Identity, bias=b[:, m:m + 1], scale=g[:, m:m + 1])
        if write_xb:
            P.act(xb[:, m, :], x32[:, m, :], AF.Copy)
    yield


def gen_ffn_up(P, C, w1, xb):
    hT = C.hT
    NC = DFF // 128
    for c in range(NC):
        pg = C.pg[c % 2]
        pu = C.pu[c % 2]
        for k in range(KT):
            P.mm(pg, w1[:, k, c * 128:(c + 1) * 128], xb[:, k, :], start=(k == 0), stop=(k == KT - 1))
        for k in range(KT):
            P.mm(pu, w1[:, k, DFF + c * 128:DFF + (c + 1) * 128], xb[:, k, :], start=(k == 0), stop=(k == KT - 1))
        sg = C.sg[c % 2]
        P.act(sg, pg, AF.Silu)
        P.stt("dve", hT[:, c, :], sg, 0.5, pu, ALU.mult, ALU.mult)
        yield


def emit_ffn_down(P, C, w2, x32):
    hT = C.hT
    NC = DFF // 128
    for m in range(KT):
        py = C.py[m % 2]
        for c in range(NC):
            P.mm(py, w2[:, c, m * 128:(m + 1) * 128], hT[:, c, :], start=(c == 0), stop=(c == NC - 1))
        P.stt("dve", x32[:, m, :], x32[:, m, :], ALPHA, py, ALU.mult, ALU.add)


def run_rr(tasks):
    tasks = list(tasks)
    while tasks:
        for gt in list(tasks):
            try:
                next(gt)
            except StopIteration:
                tasks.remove(gt)


def emit_ffn_stage_pipelined(P, C, nt, x32s, xbs, w1, w2, g, b, load_tile, store_tile):
    load_tile(0)
    for m in range(KT):
        P.act(xbs[0][:, m, :], x32s[0][:, m, :], AF.Copy)
    if nt > 1:
        load_tile(1)
    run_rr([gen_ffn_up(P, C, w1, xbs[0])])
    for t in range(nt):
        emit_ffn_down(P, C, w2, x32s[t % 2])
        tasks = [gen_ln(P, C, x32s[t % 2], None, g, b, write_xb=False)]
        if t + 1 < nt:
            nb = (t + 1) % 2
            for m in range(KT):
                P.act(xbs[nb][:, m, :], x32s[nb][:, m, :], AF.Copy)
            tasks.append(gen_ffn_up(P, C, w1, xbs[nb]))
        run_rr(tasks)
        store_tile(t)
        if t + 2 < nt:
            load_tile(t + 2)


def emit_ffn(P, C, w1, w2, g, b):
    x32, xb, hT = C.x32, C.xb, C.hT
    NC = DFF // 128
    for c in range(NC):
        pg = C.pg[c % 2]
        pu = C.pu[c % 2]
        for k in range(KT):
            P.mm(pg, w1[:, k, c * 128:(c + 1) * 128], xb[:, k, :], start=(k == 0), stop=(k == KT - 1))
        for k in range(KT):
            P.mm(pu, w1[:, k, DFF + c * 128:DFF + (c + 1) * 128], xb[:, k, :], start=(k == 0), stop=(k == KT - 1))
        sg = C.sg[c % 2]
        P.act(sg, pg, AF.Silu)
        P.stt("dve", hT[:, c, :], sg, 0.5, pu, ALU.mult, ALU.mult)
    for m in range(KT):
        py = C.py[m % 2]
        for c in range(NC):
            P.mm(py, w2[:, c, m * 128:(m + 1) * 128], hT[:, c, :], start=(c == 0), stop=(c == NC - 1))
        P.stt("dve", x32[:, m, :], x32[:, m, :], ALPHA, py, ALU.mult, ALU.add)
    emit_ln(P, C, g, b)


def emit_mixout(P, C, mixb, wo, g, b):
    x32 = C.x32
    for m in range(KT):
        py = C.py[m % 2]
        for k in range(KT):
            P.mm(py, wo[:, k, m * 128:(m + 1) * 128], mixb[:, k, :], start=(k == 0), stop=(k == KT - 1))
        P.stt("dve", x32[:, m, :], x32[:, m, :], ALPHA, py, ALU.mult, ALU.add)
    emit_ln(P, C, g, b)


def emit_ple(P, C, pb, wg, bg, wp, g, b):
    x32, xb = C.x32, C.xb
    for m in range(KT):
        pgt = C.pg[m % 2]
        ppj = C.pu[m % 2]
        for k in range(KT):
            P.mm(pgt, wg[:, k, m * 128:(m + 1) * 128], xb[:, k, :], start=(k == 0), stop=(k == KT - 1))
        for k in range(2):
            P.mm(ppj, wp[:, k, m * 128:(m + 1) * 128], pb[:, k, :], start=(k == 0), stop=(k == 1))
        sg = C.sg[m % 2]
        P.act(sg, pgt, AF.Sigmoid, bias=bg[:, m:m + 1])
        P.tt("dve", sg, sg, ppj, ALU.mult)
        P.stt("dve", x32[:, m, :], x32[:, m, :], ALPHA, sg, ALU.mult, ALU.add)
    emit_ln(P, C, g, b)


def emit_dense(P, steps, ntok, xT, outT, lng_d, lnb_d, wd):
    nt = ntok // TN
    nln = len(steps)
    C = DenseCtx()
    C.x32 = P.sbuf("x32", [128, KT, TN], F32)
    C.xb = P.sbuf("xb", [128, KT, TN], BF16)
    C.onesF = P.sbuf("onesF", [128, 128], F32)
    C.sq = [P.sbuf(f"sq{i}", [128, TN], F32) for i in range(2)]
    C.sg = [P.sbuf(f"sg{i}", [128, TN], F32) for i in range(2)]
    C.rstd = P.sbuf("rstd", [128, TN], F32)
    C.pg = [P.psum(f"pg{i}", [128, TN]) for i in range(2)]
    C.pu = [P.psum(f"pu{i}", [128, TN]) for i in range(2)]
    C.py = [P.psum(f"py{i}", [128, TN]) for i in range(2)]
    C.pstat = [P.psum(f"pst{i}", [128, TN]) for i in range(2)]
    lng = P.sbuf("lng_s", [128, nln * KT], F32)
    lnb = P.sbuf("lnb_s", [128, nln * KT], F32)
    P.dma("sp", lng, lng_d)
    P.dma("sp", lnb, lnb_d)
    P.memset("dve", C.onesF, 1.0 / D)
    C.epsc = P.sbuf("epsc", [128, 1], F32)
    P.memset("dve", C.epsc, LN_EPS)
    ws = []
    need_hT = False
    for i, s in enumerate(steps):
        if s == "ffn":
            need_hT = True
            w1 = load_w(P, f"w1s_{i}", wd[i][0], D, 2 * DFF, nsplit=4)
            w2 = load_w(P, f"w2s_{i}", wd[i][1], DFF, D)
            ws.append((w1, w2))
        elif s == "mixout":
            wo = load_w(P, f"wos_{i}", wd[i][0], D, D)
            mixb = P.sbuf(f"mixb_{i}", [128, KT, TN], BF16)
            ws.append((wo, mixb))
        elif s == "ple":
            wg = load_w(P, f"wgs_{i}", wd[i][0], D, D)
            wp = load_w(P, f"wps_{i}", wd[i][1], 256, D)
            bg = P.sbuf(f"bgs_{i}", [128, KT], F32)
            P.dma("sp", bg, wd[i][2])
            pb = P.sbuf(f"pb_{i}", [128, 2, TN], BF16)
            ws.append((wg, wp, bg, pb))
    if need_hT:
        C.hT = P.sbuf("hT", [128, DFF // 128, TN], BF16)
    xTr = xT.rearrange("(k p) n -> p k n", p=128)
    oTr = outT.rearrange("(k p) n -> p k n", p=128)
    x32s = [C.x32, P.sbuf("x32b", [128, KT, TN], F32)]
    mixbs = {}
    pbs = {}
    for i, s in enumerate(steps):
        if s == "mixout":
            mixbs[i] = [ws[i][1], P.sbuf(f"mixb2_{i}", [128, KT, TN], BF16)]
        elif s == "ple":
            pbs[i] = [ws[i][3], P.sbuf(f"pb2_{i}", [128, 2, TN], BF16)]

    def load_tile(t):
        tsl = slice(t * TN, (t + 1) * TN)
        P.dma("sp", x32s[t % 2], xTr[:, :, tsl])
        for i, s in enumerate(steps):
            if s == "mixout":
                P.dma("sp", mixbs[i][t % 2], wd[i][1].rearrange("(k p) n -> p k n", p=128)[:, :, tsl])
            elif s == "ple":
                pTr = wd[i][3].rearrange("(k p) n -> p k n", p=128)
                for k in range(2):
                    P.dma("pool", pbs[i][t % 2][:, k, :], pTr[:, k, tsl])

    if steps == ["ffn"] and PIPE_FFN:
        xbs = [C.xb, C.xb]

        def store_tile(t):
            P.dma("pool", oTr[:, :, t * TN:(t + 1) * TN], x32s[t % 2])
        emit_ffn_stage_pipelined(P, C, nt, x32s, xbs, ws[0][0], ws[0][1], lng[:, 0:KT], lnb[:, 0:KT], load_tile, store_tile)
        return
    load_tile(0)
    for t in range(nt):
        tsl = slice(t * TN, (t + 1) * TN)
        C.x32 = x32s[t % 2]
        if steps[0] != "mixout":
            for m in range(KT):
                P.act(C.xb[:, m, :], C.x32[:, m, :], AF.Copy)
        if t + 1 < nt:
            load_tile(t + 1)
        for i, s in enumerate(steps):
            g = lng[:, i * KT:(i + 1) * KT]
            b = lnb[:, i * KT:(i + 1) * KT]
            if s == "ffn":
                emit_ffn(P, C, ws[i][0], ws[i][1], g, b)
            elif s == "mixout":
                emit_mixout(P, C, mixbs[i][t % 2], ws[i][0], g, b)
            elif s == "ple":
                emit_ple(P, C, pbs[i][t % 2], ws[i][0], ws[i][2], ws[i][1], g, b)
        P.dma("pool", oTr[:, :, tsl], C.x32)


def build_dense(steps, ntok, nc=None):
    if nc is None:
        nc = bass.Bass("TRN2", target_bir_lowering=False)
    P = Prog(nc)
    xT = P.dram("xT", [D, ntok], F32, "ExternalInput")
    outT = P.dram("outT", [D, ntok], F32, "ExternalOutput")
    nln = len(steps)
    lng_d = P.dram("lng", [128, nln * KT], F32, "ExternalInput")
    lnb_d = P.dram("lnb", [128, nln * KT], F32, "ExternalInput")
    wd = []
    for i, s in enumerate(steps):
        if s == "ffn":
            wd.append((P.dram(f"w1_{i}", [D, 2 * DFF], F32, "ExternalInput"),
                       P.dram(f"w2_{i}", [DFF, D], F32, "ExternalInput")))
        elif s == "mixout":
            wd.append((P.dram(f"wo_{i}", [D, D], F32, "ExternalInput"),
                       P.dram(f"mixT_{i}", [D, ntok], BF16, "ExternalInput")))
        elif s == "ple":
            wd.append((P.dram(f"wg_{i}", [D, D], F32, "ExternalInput"),
                       P.dram(f"wp_{i}", [256, D], F32, "ExternalInput"),
                       P.dram(f"bg_{i}", [128, KT], F32, "ExternalInput"),
                       P.dram(f"pT_{i}", [256, ntok], F32, "ExternalInput")))
    emit_dense(P, steps, ntok, xT, outT, lng_d, lnb_d, wd)
    st = P.finalize()
    return nc, st

import os
POOL = os.environ.get("MIX_POOL", "pool")
LVL = int(os.environ.get("MIX_LVL", "9"))
ACT_PSUM_R = os.environ.get("MIX_ACT_PSUM_R", "1") == "1"
NOCARRY = os.environ.get("MIX_NOCARRY", "0") == "1"

D = 1024
KT = 8
TN = 512
NW = 1796
C_SQ, C_SK, C_SV = 0, 128, 256
C_GQ, C_GK, C_GV, C_GZ = 384, 640, 896, 1152
C_AB = 1408
C_SCB, C_SCC, C_SCH = 1412, 1540, 1668
NORM_EPS = 1e-6


def mix_consts():
    p = np.arange(128)[:, None]
    f = np.arange(128)[None, :]
    c = {}
    c["ident"] = (p == f).astype(np.float32)
    c["triU"] = (p <= f).astype(np.float32)
    c["triNeg"] = -(p >= f).astype(np.float32)
    c["SL"] = (p > f).astype(np.float32)
    c["SU"] = (p < f).astype(np.float32)
    c["UI"] = (p <= f).astype(np.float32)
    m = np.zeros((128, 4, 512), np.float32)
    tq = np.arange(512)[None, :]
    for i in range(4):
        m[:, i, :] = ((128 * i + p) < tq).astype(np.float32)
    c["mask"] = m.reshape(128, 2048)
    order = ["ident", "triU", "triNeg", "SL", "SU", "UI", "mask"]
    arr = np.concatenate([c[k] for k in order], axis=1)
    offs = {}
    o = 0
    for k in order:
        offs[k] = (o, o + c[k].shape[1])
        o += c[k].shape[1]
    return arr, offs


class Ring:
    def __init__(self, items):
        self.items = items
        self.i = 0

    def __call__(self):
        x = self.items[self.i % len(self.items)]
        self.i += 1
        return x


def emit_mix(P, S, hT, wm_d, cst_d, gcw_d, scw_d, hp_d, nw_d, o_sb, o_gdnT, o_sc,
             do_attn=True, do_gdn=True, do_sc=True):
    NT = S // TN
    NKB = S // 128
    carr, coff = mix_consts()

    cst = P.sbuf("cst_s", list(carr.shape), F32)
    P.dma("sp", cst, cst_d)

    def cs(k):
        a, b = coff[k]
        return cst[:, a:b]
    identF, triU, SL, SU, UI = cs("ident"), cs("triU"), cs("SL"), cs("SU"), cs("UI")
    maskF = cs("mask")
    identB = P.sbuf("identB", [128, 128], BF16)
    triNegB = P.sbuf("triNegB", [128, 128], BF16)
    P.copy("dve", identB, identF)
    P.copy("dve", triNegB, cs("triNeg"))
    ones1 = P.sbuf("ones1", [128, 128], F32)
    P.memset("dve", ones1, 1.0)
    onesRowB = P.sbuf("onesRowB", [1, 128], BF16)
    P.memset("dve", onesRowB, 1.0)
    onec = P.sbuf("onec", [128, 1], F32)
    P.memset("dve", onec, 1.0)
    epsc = P.sbuf("epsc", [128, 1], F32)
    P.memset("dve", epsc, NORM_EPS)
    gcw = P.sbuf("gcw_s", [128, 24], F32)
    scw = P.sbuf("scw_s", [128, 3], F32)
    hp = P.sbuf("hp_s", [128, 4], F32)
    nwb = P.sbuf("nw_s", [128, 128], F32)
    P.dma("sp", gcw, gcw_d)
    P.dma("sp", scw, scw_d)
    P.dma("sp", hp, hp_d)
    P.dma("sp", nwb, nw_d)
    nA = P.sbuf("nA", [128, 2], F32)
    P.act(nA, hp[:, 0:2], AF.Exp)
    P.ts("dve", nA, nA, -1.0, None, ALU.mult)
    dtb = hp[:, 2:4]
    wm = P.sbuf("wm_s", [128, KT, NW], BF16)
    for k in range(KT):
        P.dma("pool", wm[:, k, :], wm_d[k * 128:(k + 1) * 128, :])

    hb = P.sbuf("hb", [128, KT, TN], BF16)
    qT = P.sbuf("qT", [128, S], BF16)
    kT = P.sbuf("kT", [128, S], BF16)
    vA = P.sbuf("vA", [128, NKB, 128], BF16)
    pf = P.psum("pf", [128, 6, 512], F32)
    pbf = P.psum("pbf", [128, 2, 1024], BF16)
    bank = Ring([pf[:, i, :] for i in range(0, 4)])
    bank_o = Ring([pf[:, 4, :], pf[:, 5, :]])
    gbank = Ring([pf[:, i, :] for i in range(6)])
    tb2 = Ring([pbf[:, i, 0:128] for i in range(2)])
    quart = lambda: bank()[:, 0:128]
    tbank = Ring([pbf[:, i, 0:128] for i in range(2)])
    tbs = [pbf[:, i, 0:128] for i in range(2)]
    hqs = [Ring([pf[:, 2 * h, 0:128], pf[:, 2 * h + 1, 0:128]]) for h in range(2)]
    sq = Ring([pf[:, 4, :], pf[:, 5, :]])
    eb = Ring([P.sbuf(f"e{i}", [128, TN], F32) for i in range(8)])
    Lb = Ring([P.sbuf(f"L{i}", [128, TN], BF16) for i in range(6)])
    eRb = Ring([P.sbuf(f"eR{i}", [128, TN], F32) for i in range(2)])
    attb = Ring([P.sbuf(f"att{i}", [128, TN], BF16) for i in range(7)])
    lsumb = [Ring([P.sbuf(f"ls{h}_{i}", [128, TN], BF16) for i in range(4)]) for h in range(2)]
    negOnesB = P.sbuf("negOnesB", [128, 128], BF16)
    P.memset("dve", negOnesB, -1.0)
    osb = [P.sbuf(f"osb{i}", [64, TN], BF16) for i in range(2)]
    raw = [P.sbuf(f"raw{f}", [128, 3 + TN], F32) for f in range(6)]
    ycv = P.sbuf("ycv", [128, TN], F32)
    ysl = ycv
    sqb = P.sbuf("sqb", [128, TN], F32)
    rnb = sqb
    gqT = [P.sbuf(f"gqT{h}", [128, TN], BF16) for h in range(2)]
    gkT = [P.sbuf(f"gkT{h}", [128, TN], BF16) for h in range(2)]
    gvT = [P.sbuf(f"gvT{h}", [128, TN], BF16) for h in range(2)]
    szbs = [P.sbuf(f"szb{i}", [128, 256], F32) for i in range(3)]
    sc5s = [{n: P.sbuf(f"sc{i}_{n}", [128, 2], F32) for n in
             ("beta", "nbeta", "g", "gc", "gtot", "egl", "eg", "bg", "kd", "tmp")} for i in range(3)]
    ab_sbs = [P.sbuf(f"ab_sb{i}", [128, 4], F32) for i in range(3)]
    ogs = [P.sbuf(f"og{i}", [128, 256], BF16) for i in range(2)]
    ogT = P.sbuf("ogT", [128, 2, TN], BF16)

    def t128(name, dt=F32):
        return P.sbuf(name, [128, 128], dt)
    hbuf = []
    for h in range(2):
        B = {}
        for n in ("gU", "E1", "Dall", "DL", "DUb", "DUI", "dB", "egRow", "D1s", "BRs", "KKs", "kbg", "vb", "u_sb", "junk", "onb", "o_s"):
            B[n] = t128(f"h{h}_{n}")
        for n in ("kdec", "wTs", "qkTs", "qdT", "vnew"):
            B[n] = t128(f"h{h}_{n}", BF16)
        B["Nb"] = Ring([t128(f"h{h}_Nb{i}") for i in range(3)])
        B["NTb"] = Ring([t128(f"h{h}_NTb{i}") for i in range(3)])
        B["XTb"] = Ring([t128(f"h{h}_XTb{i}") for i in range(3)])
        B["ss1"] = P.sbuf(f"h{h}_ss1", [128, 1], F32)
        B["rs1"] = P.sbuf(f"h{h}_rs1", [128, 1], F32)
        hbuf.append(B)
    Sst = [t128(f"S{h}") for h in range(2)]
    Sbf = [t128(f"Sb{h}", BF16) for h in range(2)]
    hand = [[], []]
    for h in range(2):
        hand[0].append({n: hbuf[h][n] for n in ("u_sb", "kdec", "wTs", "qkTs", "qdT")})
        eR_t = eRb.items[0]
        L_t = Lb.items[h]
        hand[1].append({"u_sb": eR_t[:, h * 128:(h + 1) * 128],
                        "kdec": L_t[:, 0:128], "wTs": L_t[:, 128:256], "qkTs": L_t[:, 256:384], "qdT": L_t[:, 384:512]})
    for h in range(2):
        P.memset("dve", Sst[h], 0.0)
        P.memset("dve", Sbf[h], 0.0)
    for f in range(6):
        P.memset("dve", raw[f][:, 0:3], 0.0)
    rawc = P.sbuf("rawc", [128, 2 + TN], F32)
    P.memset("dve", rawc[:, 0:2], 0.0)
    scB = P.sbuf("scB", [128, TN], F32)
    scC = P.sbuf("scC", [128, TN], F32)
    scy = scC
    sco = P.sbuf("sco", [128, TN], BF16)

    hTr = hT.rearrange("(k p) n -> p k n", p=128)

    def proj_fm(col0, ncol=128):
        pb = bank()
        for k in range(KT):
            P.mm(pb[0:ncol, :], wm[:, k, col0:col0 + ncol], hb[:, k, :], start=(k == 0), stop=(k == KT - 1))
        return pb

    for t in range(NT):
        tsl = slice(t * TN, (t + 1) * TN)
        for k in range(KT):
            P.dma("pool", hb[:, k, :], hTr[:, k, tsl])
        if do_attn:
            pq = proj_fm(C_SQ)
            P.ts("dve", qT[:, tsl], pq, 0.125, None, ALU.mult)
            if not os.environ.get("MIX_NOK"):
                pk = proj_fm(C_SK)
                P.act(kT[:, tsl], pk, AF.Copy)
            for s in range(0 if os.environ.get("MIX_NOV") else 4):
                pv = quart()
                for k in range(KT):
                    P.mm(pv, hb[:, k, s * 128:(s + 1) * 128], wm[:, k, C_SV:C_SV + 128], start=(k == 0), stop=(k == KT - 1))
                P.copy("dve", vA[:, 4 * t + s, :], pv)
            nkb = 4 * t + 4
            items = [(hd, kb) for kb in range(nkb - 1, -1, -1) for hd in range(2)]
            po_h = [bank_o(), bank_o()]
            st1 = {}
            st2 = {}
            lsum_cur = [None, None]

            def att_s1(hd, kb):
                ps = slice(64 * hd, 64 * hd + 64)
                pz = bank()
                P.mm(pz, kT[ps, kb * 128:(kb + 1) * 128], qT[ps, tsl])
                e = eb()
                P.act(e, pz, AF.Exp)
                if kb >= 4 * t:
                    i = kb - 4 * t
                    P.tt(POOL, e, e, maskF[:, i * 512:(i + 1) * 512], ALU.mult)
                L = Lb()
                P.act(L, e, AF.Ln, bias=onec[:, 0:1])
                carry = lsum_cur[hd]
                if kb > 0:
                    ns = lsumb[hd]()
                    if carry is None:
                        P.copy("dve", ns, L)
                    else:
                        P.tt("dve", ns, carry, L, ALU.add)
                    lsum_cur[hd] = ns
                st1[(hd, kb)] = (e, L, carry)

            def att_s2(hd, kb):
                e, L, carry = st1.pop((hd, kb))
                pr = bank()
                P.mm(pr, triNegB, L, start=True, stop=(carry is None))
                if carry is not None:
                    P.mm(pr, negOnesB, carry, start=False, stop=True)
                eR = eRb()
                P.act(eR, pr, AF.Exp)
                att = attb()
                P.tt(POOL, att, e, eR, ALU.mult)
                st2[(hd, kb)] = att

            def att_s3(hd, kb):
                att = st2.pop((hd, kb))
                P.mm(po_h[hd][0:64, :], vA[:, kb, 64 * hd:64 * hd + 64], att, start=(kb == nkb - 1), stop=(kb == 0))

            LOOK = int(os.environ.get("MIX_LOOK", "4"))
            for idx in range(len(items) + 2 * LOOK):
                if idx < len(items):
                    att_s1(*items[idx])
                if LOOK <= idx < len(items) + LOOK:
                    att_s2(*items[idx - LOOK])
                if idx >= 2 * LOOK:
                    att_s3(*items[idx - 2 * LOOK])
            for hd in range(2):
                P.act(osb[hd], po_h[hd][0:64, :], AF.Copy)
                P.dma("sp", o_sb[64 * hd:64 * hd + 64, tsl], osb[hd])
        if do_sc:
            pB = proj_fm(C_SCB)
            P.act(scB, pB, AF.Copy)
            pC = proj_fm(C_SCC)
            P.act(scC, pC, AF.Copy)
            pH = proj_fm(C_SCH)
            if t > 0:
                P.copy("dve", rawc[:, 0:2], rawc[:, TN:TN + 2])
            P.tt("dve", rawc[:, 2:2 + TN], scC, pH, ALU.mult)
            P.ts("dve", scy, rawc[:, 0:TN], scw[:, 0:1], None, ALU.mult)
            for i in (1, 2):
                P.stt("dve", scy, rawc[:, i:i + TN], scw[:, i:i + 1], scy, ALU.mult, ALU.add)
            P.tt("dve", sco, scB, scy, ALU.mult)
            P.dma("sp", o_sc[:, tsl], sco)
        if do_gdn:
            for f in range(6):
                col0 = C_GQ + f * 128
                pg = proj_fm(col0)
                if t > 0:
                    P.copy("dve", raw[f][:, 0:3], raw[f][:, TN:TN + 3])
                P.act(raw[f][:, 3:3 + TN], pg, AF.Copy)
                P.ts("dve", ycv, raw[f][:, 0:TN], gcw[:, f * 4:f * 4 + 1], None, ALU.mult)
                for i in (1, 2, 3):
                    P.stt("dve", ycv, raw[f][:, i:i + TN], gcw[:, f * 4 + i:f * 4 + i + 1], ycv, ALU.mult, ALU.add)
                h_ = f % 2
                if f >= 4:
                    P.act(gvT[h_], ycv, AF.Silu)
                    continue
                P.act(ysl, ycv, AF.Silu)
                P.act(sqb, ysl, AF.Square)
                pss = bank()
                P.mm(pss, ones1, sqb)
                P.act(rnb, pss, AF.Sqrt, bias=epsc[:, 0:1])
                P.recip(rnb, rnb)
                if f < 2:
                    P.stt("dve", gqT[h_], ysl, 128.0 ** -0.5, rnb, ALU.mult, ALU.mult)
                else:
                    P.tt("dve", gkT[h_], ysl, rnb, ALU.mult)
            gb = gbank
            def scal_task(c):
                csl = slice(c * 128, (c + 1) * 128)
                S5 = sc5s[c % 3]
                szb = szbs[c % 3]
                ab_sb = ab_sbs[c % 3]
                pzz = gb()
                for k in range(KT):
                    P.mm(pzz[:, 0:256], hb[:, k, csl], wm[:, k, C_GZ:C_GZ + 256], start=(k == 0), stop=(k == KT - 1))
                P.act(szb, pzz[:, 0:256], AF.Silu)
                yield
                pab = gb()
                for k in range(KT):
                    P.mm(pab[:, 0:4], hb[:, k, csl], wm[:, k, C_AB:C_AB + 4], start=(k == 0), stop=(k == KT - 1))
                P.copy("dve", ab_sb, pab[:, 0:4])
                yield
                P.act(S5["beta"], ab_sb[:, 2:4], AF.Sigmoid)
                P.tt("dve", S5["tmp"], ab_sb[:, 0:2], dtb, ALU.add)
                yield
                P.ts("dve", S5["nbeta"], S5["beta"], -1.0, None, ALU.mult)
                P.act(S5["tmp"], S5["tmp"], AF.Exp)
                yield
                P.act(S5["tmp"], S5["tmp"], AF.Ln, bias=onec[:, 0:1])
                yield
                P.tt("dve", S5["g"], S5["tmp"], nA, ALU.mult)
                yield
                pgc = gb()
                P.mm(pgc[:, 0:2], triU, S5["g"])
                P.copy("dve", S5["gc"], pgc[:, 0:2])
                pgt = gb()
                P.mm(pgt[:, 0:2], ones1, S5["g"])
                P.copy("dve", S5["gtot"], pgt[:, 0:2])
                yield
                P.act(S5["egl"], S5["gtot"], AF.Exp)
                P.act(S5["eg"], S5["gc"], AF.Exp)
                P.tt("dve", S5["kd"], S5["gtot"], S5["gc"], ALU.subtract)
                yield
                P.tt("dve", S5["bg"], S5["beta"], S5["eg"], ALU.mult)
                P.act(S5["kd"], S5["kd"], AF.Exp)
                yield

            def prep_task(c, h_):
                csl = slice(c * 128, (c + 1) * 128)
                S5 = sc5s[c % 3]
                B = hbuf[h_]
                HO = hand[c % 2][h_]
                hs = slice(h_, h_ + 1)
                kTc = gkT[h_][:, csl]
                qTc = gqT[h_][:, csl]
                P.ts("dve", B["gU"], triU, S5["g"][:, hs], None, ALU.mult)
                P.ts("dve", B["dB"], identF, S5["beta"][:, hs], None, ALU.mult)
                yield
                pD1 = gb()[:, 0:128]
                P.mm(pD1, ones1, B["gU"])
                P.copy("dve", B["D1s"], pD1)
                yield
                pBR = gb()[:, 0:128]
                P.mm(pBR, ones1, B["dB"])
                P.copy("dve", B["BRs"], pBR)
                yield
                pKK = gb()[:, 0:128]
                P.mm(pKK, kTc, kTc)
                P.copy("dve", B["KKs"], pKK)
                yield
                P.act(B["egRow"], B["D1s"], AF.Exp)
                P.ts("dve", B["E1"], B["D1s"], S5["gc"][:, hs], None, ALU.subtract)
                yield
                P.act(B["E1"], B["E1"], AF.Abs)
                yield
                P.act(B["Dall"], B["E1"], AF.Exp, scale=-1.0)
                yield
                P.tt("pool", B["DL"], B["Dall"], SL, ALU.mult)
                P.tt("pool", B["DUb"], B["Dall"], SU, ALU.mult)
                P.tt("pool", B["DUI"], B["Dall"], UI, ALU.mult)
                yield
                P.tt("dve", B["DUb"], B["DUb"], B["BRs"], ALU.mult)
                N = B["Nb"]()
                NT_ = B["NTb"]()
                XT = B["XTb"]()
                P.stt("dve", N, B["KKs"], S5["nbeta"][:, hs], B["DL"], ALU.mult, ALU.mult)
                yield
                P.stt("dve", NT_, B["KKs"], -1.0, B["DUb"], ALU.mult, ALU.mult)
                yield
                P.tt("pool", XT, identF, NT_, ALU.add)
                ptk = tb2()
                P.transpose(ptk, kTc, identB)
                P.ts("dve", B["kbg"], ptk, S5["bg"][:, hs], None, ALU.mult)
                P.ts("dve", HO["kdec"], ptk, S5["kd"][:, hs], None, ALU.mult)
                yield
                ptv = tb2()
                P.transpose(ptv, gvT[h_][:, csl], identB)
                P.ts("dve", B["vb"], ptv, S5["beta"][:, hs], None, ALU.mult)
                yield
                pqk = gb()[:, 0:128]
                P.mm(pqk, kTc, qTc)
                P.tt("dve", HO["qkTs"], pqk, B["DUI"], ALU.mult)
                P.tt("pool", HO["qdT"], qTc, B["egRow"], ALU.mult)
                yield
                for kk in range(1, 7):
                    pN = gb()[:, 0:128]
                    P.mm(pN, NT_, N)
                    N2 = B["Nb"]()
                    P.copy("act", N2, pN)
                    yield
                    if kk < 6:
                        pNT = gb()[:, 0:128]
                        P.mm(pNT, N, NT_)
                        NT2 = B["NTb"]()
                        P.copy("dve", NT2, pNT)
                        yield
                    pX = gb()[:, 0:128]
                    P.mm(pX, N2, XT)
                    XT2 = B["XTb"]()
                    P.tt("dve", XT2, XT, pX, ALU.add)
                    N, XT = N2, XT2
                    if kk < 6:
                        NT_ = NT2
                    yield
                pu = gb()[:, 0:128]
                P.mm(pu, XT, B["vb"])
                P.copy("act", HO["u_sb"], pu)
                yield
                pw = gb()[:, 0:128]
                P.mm(pw, B["kbg"], XT)
                P.copy("act", HO["wTs"], pw)
                yield

            def scan_task(c, h_):
                S5 = sc5s[c % 3]
                szb = szbs[c % 3]
                og = ogs[c % 2]
                B = hbuf[h_]
                HO = hand[c % 2][h_]
                hs = slice(h_, h_ + 1)
                p1 = gb()[:, 0:128]
                P.mm(p1, HO["wTs"], Sbf[h_])
                P.tt("dve", B["vnew"], HO["u_sb"], p1, ALU.subtract)
                yield
                p2 = gb()[:, 0:128]
                P.mm(p2, HO["qdT"], Sbf[h_], start=True, stop=False)
                P.mm(p2, HO["qkTs"], B["vnew"], start=False, stop=True)
                P.copy("dve", B["o_s"], p2)
                p3 = gb()[:, 0:128]
                P.mm(p3, HO["kdec"], B["vnew"])
                P.stt("dve", Sst[h_], Sst[h_], S5["egl"][:, hs], p3, ALU.mult, ALU.add)
                yield
                P.copy("act", Sbf[h_], Sst[h_])
                P.act(B["junk"], B["o_s"], AF.Square, accum_out=B["ss1"])
                yield
                P.act(B["rs1"], B["ss1"], AF.Sqrt, scale=1.0 / 128.0, bias=epsc[:, 0:1])
                yield
                P.recip(B["rs1"], B["rs1"])
                yield
                P.stt("dve", B["onb"], B["o_s"], B["rs1"][:, 0:1], nwb, ALU.mult, ALU.mult)
                yield
                P.tt("dve", og[:, h_ * 128:(h_ + 1) * 128], B["onb"], szb[:, h_ * 128:(h_ + 1) * 128], ALU.mult)
                yield

            def run_tasks(tasks):
                tasks = list(tasks)
                while tasks:
                    for g in list(tasks):
                        try:
                            next(g)
                        except StopIteration:
                            tasks.remove(g)

            run_tasks([scal_task(0)])
            run_tasks([prep_task(0, 0), prep_task(0, 1), scal_task(1)])
            for c in range(4):
                csl = slice(c * 128, (c + 1) * 128)
                tl = [scan_task(c, 0), scan_task(c, 1)]
                if c + 1 < 4:
                    tl += [prep_task(c + 1, 0), prep_task(c + 1, 1)]
                if c + 2 < 4:
                    tl.append(scal_task(c + 2))
                run_tasks(tl)
                og = ogs[c % 2]
                for h_ in range(2):
                    ptg = tb2()
                    P.transpose(ptg, og[:, h_ * 128:(h_ + 1) * 128], identB)
                    P.copy("dve", ogT[:, h_, csl], ptg)
            for h_ in range(2):
                P.dma("sp", o_gdnT[h_ * 128:(h_ + 1) * 128, tsl], ogT[:, h_, :])


def build_mix(S, nc=None, do_attn=True, do_gdn=True, do_sc=True):
    if nc is None:
        nc = bass.Bass("TRN2", target_bir_lowering=False)
    P = Prog(nc)
    carr, coff = mix_consts()
    hT = P.dram("hT", [D, S], F32, "ExternalInput")
    wm_d = P.dram("wm", [D, NW], F32, "ExternalInput")
    cst_d = P.dram("cst", list(carr.shape), F32, "ExternalInput")
    gcw_d = P.dram("gcw", [128, 24], F32, "ExternalInput")
    scw_d = P.dram("scw", [128, 3], F32, "ExternalInput")
    hp_d = P.dram("hp", [128, 4], F32, "ExternalInput")
    nw_d = P.dram("nw", [128, 128], F32, "ExternalInput")
    o_sb = P.dram("o_sb", [128, S], BF16, "ExternalOutput")
    o_gdnT = P.dram("o_gdnT", [256, S], BF16, "ExternalOutput")
    o_sc = P.dram("o_sc", [128, S], BF16, "ExternalOutput")
    emit_mix(P, S, hT, wm_d, cst_d, gcw_d, scw_d, hp_d, nw_d, o_sb, o_gdnT, o_sc, do_attn, do_gdn, do_sc)
    st = P.finalize()
    return nc, st, carr


NSTEP = 4


def build_fused(S, depth=2, nc=None):
    if nc is None:
        nc = bass.Bass("TRN2", target_bir_lowering=False)
    P = Prog(nc)
    carr, _ = mix_consts()
    xT = P.dram("xT", [D, S], F32, "ExternalInput")
    outT = P.dram("outT", [D, S], F32, "ExternalOutput")
    lng = P.dram("lng", [128, depth * NSTEP * KT], F32, "ExternalInput")
    lnb = P.dram("lnb", [128, depth * NSTEP * KT], F32, "ExternalInput")
    cst = P.dram("cst", list(carr.shape), F32, "ExternalInput")
    W = []
    for i in range(depth):
        w = {}
        for f in range(2):
            w[f"w1{f}"] = P.dram(f"w1_{i}_{f}", [D, 2 * DFF], F32, "ExternalInput")
            w[f"w2{f}"] = P.dram(f"w2_{i}_{f}", [DFF, D], F32, "ExternalInput")
        for j in range(2):
            w[f"wm{j}"] = P.dram(f"wm_{i}_{j}", [D, NW], F32, "ExternalInput")
            w[f"gcw{j}"] = P.dram(f"gcw_{i}_{j}", [128, 24], F32, "ExternalInput")
            w[f"scw{j}"] = P.dram(f"scw_{i}_{j}", [128, 3], F32, "ExternalInput")
            w[f"hp{j}"] = P.dram(f"hp_{i}_{j}", [128, 4], F32, "ExternalInput")
        w["nw"] = P.dram(f"nw_{i}", [128, 128], F32, "ExternalInput")
        w["wo"] = P.dram(f"wo_{i}", [D, D], F32, "ExternalInput")
        w["wg"] = P.dram(f"wg_{i}", [D, D], F32, "ExternalInput")
        w["wp"] = P.dram(f"wp_{i}", [256, D], F32, "ExternalInput")
        w["bg"] = P.dram(f"bg_{i}", [128, KT], F32, "ExternalInput")
        w["pT"] = P.dram(f"pT_{i}", [256, S], F32, "ExternalInput")
        W.append(w)
    H = P.dram("H_scr", [D, S], F32, "Internal")
    X1 = P.dram("X1_scr", [D, S], F32, "Internal")
    X2 = P.dram("X2_scr", [D, S], F32, "Internal")
    MIX = P.dram("MIX_scr", [D, S], BF16, "Internal")

    def ln(i, s):
        o = (i * NSTEP + s) * KT
        return lng[:, o:o + KT], lnb[:, o:o + KT]

    cur = xT
    for i in range(depth):
        w = W[i]
        with P.stage(f"A{i}"):
            g, b = ln(i, 0)
            emit_dense(P, ["ffn"], S, cur, H, g, b, [(w["w10"], w["w20"])])
        for j in range(2):
            with P.stage(f"B{i}{j}"):
                emit_mix(P, S, H, w[f"wm{j}"], cst, w[f"gcw{j}"], w[f"scw{j}"], w[f"hp{j}"], w["nw"],
                         MIX[j * 128:(j + 1) * 128, :], MIX[256 + j * 256:256 + (j + 1) * 256, :],
                         MIX[768 + j * 128:768 + (j + 1) * 128, :])
        with P.stage(f"M{i}"):
            g, b = ln(i, 1)
            emit_dense(P, ["mixout"], S, H, X1, g, b, [(w["wo"], MIX)])
        with P.stage(f"F{i}"):
            g, b = ln(i, 2)
            emit_dense(P, ["ffn"], S, X1, X2, g, b, [(w["w11"], w["w21"])])
        with P.stage(f"P{i}"):
            g, b = ln(i, 3)
            dst = outT if i == depth - 1 else X1
            emit_dense(P, ["ple"], S, X2, dst, g, b, [(w["wg"], w["wp"], w["bg"], w["pT"])])
        cur = X1
    return nc, P.tot, carr


def _relay(v):
    return np.ascontiguousarray(np.asarray(v, np.float32).reshape(8, 128).T)


def fused_inputs(b, S, depth, carr, x, p, ln_g, ln_b, ffn_w_in, ffn_w_out, mix_w_in, gdn_conv_w, gdn_a_log,
                 gdn_dt_bias, gdn_norm_w, sc_conv_w, mix_w_out, ple_w_proj, ple_w_gate, ple_b_gate, shared=None):
    m = {} if shared is None else dict(shared)
    m["xT"] = np.ascontiguousarray(x[b].T)
    for i in range(depth):
        m[f"pT_{i}"] = np.ascontiguousarray(p[i, b].T)
    if shared is not None:
        return m
    m["cst"] = carr
    m["lng"] = np.ascontiguousarray(np.concatenate([_relay(ln_g[i, s]) for i in range(depth) for s in range(4)], 1))
    m["lnb"] = np.ascontiguousarray(np.concatenate([_relay(ln_b[i, s]) for i in range(depth) for s in range(4)], 1))
    OFF_SB = 768
    OFF_QKV = OFF_SB + 1536
    OFF_A = OFF_QKV + 512
    OFF_Bt = OFF_A + 4
    OFF_SC = OFF_Bt + 4
    for i in range(depth):
        for f in range(2):
            m[f"w1_{i}_{f}"] = np.ascontiguousarray(ffn_w_in[i, f])
            m[f"w2_{i}_{f}"] = np.ascontiguousarray(ffn_w_out[i, f])
        for j in range(2):
            cols = []
            for base in (0, 256, 512):
                cols.append(np.arange(base + j * 128, base + (j + 1) * 128))
            for base in (OFF_SB, OFF_SB + 512, OFF_SB + 1024, OFF_QKV):
                cols.append(np.arange(base + j * 256, base + (j + 1) * 256))
            cols.append(np.arange(OFF_A + j * 2, OFF_A + j * 2 + 2))
            cols.append(np.arange(OFF_Bt + j * 2, OFF_Bt + j * 2 + 2))
            for base in (OFF_SC, OFF_SC + 256, OFF_SC + 512):
                cols.append(np.arange(base + j * 128, base + (j + 1) * 128))
            cols = np.concatenate(cols)
            m[f"wm_{i}_{j}"] = np.ascontiguousarray(mix_w_in[i][:, cols])
            gidx = np.concatenate([np.arange(base + j * 256, base + (j + 1) * 256) for base in (0, 512, 1024)])
            m[f"gcw_{i}_{j}"] = np.ascontiguousarray(gdn_conv_w[i][:, gidx].reshape(4, 6, 128).transpose(2, 1, 0).reshape(128, 24))
            m[f"scw_{i}_{j}"] = np.ascontiguousarray(sc_conv_w[i][:, j * 128:(j + 1) * 128].T)
            m[f"hp_{i}_{j}"] = np.ascontiguousarray(np.tile(np.concatenate(
                [gdn_a_log[i][2 * j:2 * j + 2], gdn_dt_bias[i][2 * j:2 * j + 2]])[None, :], (128, 1)).astype(np.float32))
        m[f"nw_{i}"] = np.ascontiguousarray(np.tile(gdn_norm_w[i][None, :], (128, 1)).astype(np.float32))
        m[f"wo_{i}"] = np.ascontiguousarray(mix_w_out[i])
        m[f"wg_{i}"] = np.ascontiguousarray(ple_w_gate[i])
        m[f"wp_{i}"] = np.ascontiguousarray(ple_w_proj[i])
        m[f"bg_{i}"] = _relay(ple_b_gate[i])
    return m


from concourse.bass_utils import run_bass_kernel_spmd

BATCH, SEQ, DEPTH = 4, 8192, 2


def kernel(x, p, ln_g, ln_b, ffn_w_in, ffn_w_out, mix_w_in, gdn_conv_w, gdn_a_log,
           gdn_dt_bias, gdn_norm_w, sc_conv_w, mix_w_out, ple_w_proj, ple_w_gate, ple_b_gate):
    f = lambda a: np.asarray(a, np.float32)
    args = dict(x=f(x), p=f(p), ln_g=f(ln_g), ln_b=f(ln_b), ffn_w_in=f(ffn_w_in), ffn_w_out=f(ffn_w_out),
                mix_w_in=f(mix_w_in), gdn_conv_w=f(gdn_conv_w), gdn_a_log=f(gdn_a_log), gdn_dt_bias=f(gdn_dt_bias),
                gdn_norm_w=f(gdn_norm_w), sc_conv_w=f(sc_conv_w), mix_w_out=f(mix_w_out), ple_w_proj=f(ple_w_proj),
                ple_w_gate=f(ple_w_gate), ple_b_gate=f(ple_b_gate))
    nc, _, carr = build_fused(SEQ, DEPTH)
    m0 = fused_inputs(0, SEQ, DEPTH, carr, **args)
    shared = {k: v for k, v in m0.items() if k != "xT" and not k.startswith("pT_")}
    maps = [m0] + [fused_inputs(b, SEQ, DEPTH, carr, shared=shared, **args) for b in range(1, BATCH)]
    res = run_bass_kernel_spmd(nc, maps, core_ids=list(range(BATCH)))
    out = np.empty((BATCH, SEQ, D), np.float32)
    for b in range(BATCH):
        out[b] = res.results[b]["outT"].T
    return out
```

```python
import numpy as np
from contextlib import ExitStack, contextmanager
import concourse.bass as bass
import concourse.mybir as mybir

F32 = mybir.dt.float32
BF16 = mybir.dt.bfloat16
AF = mybir.ActivationFunctionType
ALU = mybir.AluOpType
AX = mybir.AxisListType


def _prod(xs):
    r = 1
    for x in xs:
        r *= int(x)
    return r


class Op:
    __slots__ = ("eng", "fn", "reads", "writes", "dma", "deps", "sig", "dsem", "dval", "dprev", "pe_mm")

    def __init__(self, eng, fn, reads, writes, dma, pe_mm=False):
        self.eng = eng
        self.fn = fn
        self.reads = reads
        self.writes = writes
        self.dma = dma
        self.deps = ()
        self.sig = None
        self.dsem = None
        self.dval = None
        self.dprev = None
        self.pe_mm = pe_mm


class Prog:
    NDSEM = 24

    def __init__(self, nc, same_engine_sync=None):
        self.nc = nc
        self.ops = []
        self.tinfo = {}
        self.hist = {}
        import os as _os
        if same_engine_sync is None:
            same_engine_sync = _os.environ.get("FW_SES", "1") == "1"
        self.same_engine_sync = same_engine_sync
        self.engs = {"pe": nc.tensor, "act": nc.scalar, "dve": nc.vector, "pool": nc.gpsimd, "sp": nc.sync}
        self._n = 0
        self.stk = None
        self.sname = ""
        self.esem = None
        self.tot = dict(n_ops=0, n_waits=0, n_dma=0)

    @contextmanager
    def stage(self, name):
        self.stk = ExitStack()
        self.sname = name + "_"
        self.ops = []
        try:
            yield self
            self.finalize(barrier=True)
        finally:
            self.stk.close()
            self.stk = None
            self.sname = ""
            self.ops = []

    def sbuf(self, name, shape, dt):
        name = self.sname + name
        if self.stk is not None:
            t = self.stk.enter_context(self.nc.sbuf_tensor(name, [int(s) for s in shape], dt))
        else:
            t = self.nc.alloc_sbuf_tensor(name, [int(s) for s in shape], dt)
        self.tinfo[name] = ("sb", _prod(shape[1:]))
        return t.ap()

    def psum(self, name, shape, dt=F32):
        name = self.sname + name
        if self.stk is not None:
            t = self.stk.enter_context(self.nc.psum_tensor(name, [int(s) for s in shape], dt))
        else:
            t = self.nc.alloc_psum_tensor(name, [int(s) for s in shape], dt)
        self.tinfo[name] = ("ps", _prod(shape[1:]))
        return t.ap()

    def dram(self, name, shape, dt, kind):
        t = self.nc.dram_tensor(name, [int(s) for s in shape], dt, kind=kind)
        self.tinfo[name] = ("const" if kind == "ExternalInput" else "dram", None)
        return t.ap()

    def rect(self, ap):
        name = ap.tensor.name
        kind, ps = self.tinfo[name]
        off = int(ap.offset)
        dims = ap.ap
        if kind in ("dram", "const"):
            hi = off + sum((c - 1) * abs(s) for s, c in dims) + 1
            return (name, 0, 1, off, hi)
        p0 = off // ps
        f0 = off % ps
        pc = dims[0][1]
        hi = f0 + sum((c - 1) * abs(s) for s, c in dims[1:]) + 1
        return (name, p0, p0 + pc, f0, hi)

    def add(self, eng, fn, reads=(), writes=(), dma=False, pe_mm=False):
        rr = []
        for a in reads:
            if a is None or isinstance(a, (int, float)):
                continue
            r = self.rect(a)
            if self.tinfo[r[0]][0] == "const":
                continue
            rr.append(r)
        ww = [self.rect(a) for a in writes]
        op = Op(eng, fn, rr, ww, dma, pe_mm)
        self.ops.append(op)
        return op

    def mm(self, out, lhsT, rhs, start=True, stop=True):
        self.add("pe", lambda e: e.matmul(out, lhsT, rhs, start=start, stop=stop),
                 reads=[lhsT, rhs], writes=[out], pe_mm=True)

    def transpose(self, out, in_, ident):
        self.add("pe", lambda e: e.transpose(out, in_, ident), reads=[in_, ident], writes=[out], pe_mm=True)

    def act(self, out, in_, func, bias=None, scale=None, accum_out=None):
        kw = {}
        if bias is not None:
            kw["bias"] = bias
        if scale is not None:
            kw["scale"] = scale
        if accum_out is not None:
            kw["accum_out"] = accum_out
        rd = [in_]
        if bias is not None and not isinstance(bias, (int, float)):
            rd.append(bias)
        if scale is not None and not isinstance(scale, (int, float)):
            rd.append(scale)
        wr = [out] + ([accum_out] if accum_out is not None else [])
        self.add("act", lambda e: e.activation(out, in_, func, **kw), reads=rd, writes=wr)

    def tt(self, eng, out, in0, in1, op):
        self.add(eng, lambda e: e.tensor_tensor(out, in0, in1, op), reads=[in0, in1], writes=[out])

    def ts(self, eng, out, in0, s1, s2, op0, op1=None):
        rd = [in0] + [s for s in (s1, s2) if s is not None and not isinstance(s, (int, float))]
        if op1 is None:
            self.add(eng, lambda e: e.tensor_scalar(out, in0, s1, None, op0), reads=rd, writes=[out])
        else:
            self.add(eng, lambda e: e.tensor_scalar(out, in0, s1, s2, op0, op1), reads=rd, writes=[out])

    def stt(self, eng, out, in0, scalar, in1, op0, op1):
        rd = [in0, in1] + ([scalar] if not isinstance(scalar, (int, float)) else [])
        self.add(eng, lambda e: e.scalar_tensor_tensor(out, in0, scalar, in1, op0, op1), reads=rd, writes=[out])

    def copy(self, eng, out, in_):
        if eng == "act":
            self.add(eng, lambda e: e.copy(out, in_), reads=[in_], writes=[out])
        else:
            self.add(eng, lambda e: e.tensor_copy(out, in_), reads=[in_], writes=[out])

    def recip(self, out, in_):
        self.add("dve", lambda e: e.reciprocal(out, in_), reads=[in_], writes=[out])

    def memset(self, eng, out, val):
        self.add(eng, lambda e: e.memset(out, val), reads=[], writes=[out])

    def dma(self, q, out, in_):
        self.add(q, lambda e: e.dma_start(out=out, in_=in_), reads=[in_], writes=[out], dma=True)

    @staticmethod
    def _ov(a, b):
        return a[1] < b[2] and b[1] < a[2] and a[3] < b[4] and b[3] < a[4]

    @staticmethod
    def _contains(a, b):
        return a[1] <= b[1] and b[2] <= a[2] and a[3] <= b[3] and b[4] <= a[4]

    def finalize(self, barrier=False):
        ops = self.ops
        hist = {}
        for i, op in enumerate(ops):
            deps = set()
            for r in op.reads:
                for seg in hist.get(r[0], ()):
                    if seg[1] is not None and self._ov(seg[0], r):
                        deps.add(seg[1])
            for w in op.writes:
                for seg in hist.get(w[0], ()):
                    if self._ov(seg[0], w):
                        if seg[1] is not None:
                            deps.add(seg[1])
                        deps.update(seg[2].values())
                        deps.update(seg[3])
            for r in op.reads:
                lst = hist.setdefault(r[0], [])
                found = None
                for seg in lst:
                    if seg[0] == r:
                        found = seg
                        break
                if found is None:
                    found = [r, None, {}, []]
                    lst.append(found)
                if op.dma:
                    found[3].append(i)
                else:
                    found[2][op.eng] = i
            for w in op.writes:
                lst = hist.setdefault(w[0], [])
                lst[:] = [seg for seg in lst if not self._contains(w, seg[0])]
                lst.append([w, i, {}, []])
            deps.discard(i)
            op.deps = sorted(deps)
        need = [False] * len(ops)
        for i, op in enumerate(ops):
            for j in op.deps:
                pj = ops[j]
                if pj.dma:
                    continue
                if pj.eng == op.eng and not op.dma:
                    if pj.eng == "pe" or not self.same_engine_sync:
                        continue
                need[j] = True
        if barrier:
            last = {}
            for i, op in enumerate(ops):
                if not op.dma:
                    last[op.eng] = i
            for i in last.values():
                need[i] = True
        nc = self.nc
        if self.esem is None:
            self.esem = {k: nc.alloc_semaphore(name=f"e_{k}") for k in self.engs}
            self.dsems = [nc.alloc_semaphore(name=f"d_{i}") for i in range(self.NDSEM)]
            self.ecount = {k: 0 for k in self.engs}
            self.dcount = [0] * self.NDSEM
            self.nd = 0
            self.known = {k: {} for k in self.engs}
        esem, dsems, ecount, dcount, known = self.esem, self.dsems, self.ecount, self.dcount, self.known
        for i, op in enumerate(ops):
            if op.dma:
                sidx = self.nd % self.NDSEM
                self.nd += 1
                op.dprev = dcount[sidx]
                dcount[sidx] += 16
                op.dsem = sidx
                op.dval = dcount[sidx]
            elif need[i]:
                ecount[op.eng] += 1
                op.sig = ecount[op.eng]
        nwaits = 0
        for i, op in enumerate(ops):
            e = self.engs[op.eng]
            kn = known[op.eng]
            waits = {}
            for j in op.deps:
                pj = ops[j]
                if pj.dma:
                    key = ("d", pj.dsem)
                    val = pj.dval
                else:
                    if pj.eng == op.eng and not op.dma:
                        if pj.eng == "pe" or not self.same_engine_sync:
                            continue
                    key = ("e", pj.eng)
                    val = pj.sig
                if kn.get(key, 0) >= val:
                    continue
                if waits.get(key, 0) < val:
                    waits[key] = val
            if op.dma and op.dprev > 0:
                key = ("d", op.dsem)
                if kn.get(key, 0) < op.dprev and waits.get(key, 0) < op.dprev:
                    waits[key] = op.dprev
            for key, val in waits.items():
                sem = dsems[key[1]] if key[0] == "d" else esem[key[1]]
                e.wait_ge(sem, val)
                kn[key] = val
                nwaits += 1
            ins = op.fn(e)
            if op.dma:
                ins.then_inc(dsems[op.dsem], 16)
            elif op.sig is not None:
                ins.then_inc(esem[op.eng], 1)
        targets = list(self.engs) if barrier else ["sp"]
        for k in targets:
            e = self.engs[k]
            kn = known[k]
            for sidx in range(self.NDSEM):
                if dcount[sidx] > kn.get(("d", sidx), 0):
                    e.wait_ge(dsems[sidx], dcount[sidx])
                    kn[("d", sidx)] = dcount[sidx]
                    nwaits += 1
            for k2 in ("pe", "act", "dve", "pool"):
                if ecount[k2] > kn.get(("e", k2), 0):
                    e.wait_ge(esem[k2], ecount[k2])
                    kn[("e", k2)] = ecount[k2]
                    nwaits += 1
        self.stats = dict(n_ops=len(ops), n_waits=nwaits, sigs=dict(ecount), n_dma=self.nd)
        self.tot["n_ops"] += len(ops)
        self.tot["n_waits"] += nwaits
        return self.stats


D = 1024
DFF = 2816
KT = 8
TN = 512
ALPHA = 4.0 ** 0.25
LN_EPS = 1e-5


import os as _os
PIPE_FFN = _os.environ.get("PIPE_FFN", "1") == "1"


class DenseCtx:
    pass


def load_w(P, name, w_dram, rows, cols, nsplit=1, q="pool"):
    kt = rows // 128
    w = P.sbuf(name, [128, kt, cols], BF16)
    step = cols // nsplit
    for s in range(nsplit):
        for k in range(kt):
            P.dma(q, w[:, k, s * step:(s + 1) * step], w_dram[k * 128:(k + 1) * 128, s * step:(s + 1) * step])
    return w


def emit_ln(P, C, g, b):
    x32, xb = C.x32, C.xb
    pm = C.pstat[0]
    for m in range(KT):
        P.mm(pm, C.onesF, x32[:, m, :], start=(m == 0), stop=(m == KT - 1))
    for m in range(KT):
        P.tt("dve", x32[:, m, :], x32[:, m, :], pm, ALU.subtract)
    pv = C.pstat[1]
    for m in range(KT):
        sq = C.sq[m % 2]
        P.act(sq, x32[:, m, :], AF.Square)
        P.mm(pv, C.onesF, sq, start=(m == 0), stop=(m == KT - 1))
    P.act(C.rstd, pv, AF.Sqrt, bias=C.epsc[:, 0:1])
    P.recip(C.rstd, C.rstd)
    for m in range(KT):
        P.tt("dve", x32[:, m, :], x32[:, m, :], C.rstd, ALU.mult)
        P.act(x32[:, m, :], x32[:, m, :], AF.Identity, bias=b[:, m:m + 1], scale=g[:, m:m + 1])
        P.act(xb[:, m, :], x32[:, m, :], AF.Copy)


def gen_ln(P, C, x32, xb, g, b, write_xb=True):
    pm = C.pstat[0]
    for m in range(KT):
        P.mm(pm, C.onesF, x32[:, m, :], start=(m == 0), stop=(m == KT - 1))
    yield
    for m in range(KT):
        P.tt("dve", x32[:, m, :], x32[:, m, :], pm, ALU.subtract)
        if m % 2 == 1:
            yield
    pv = C.pstat[1]
    for m in range(KT):
        sq = C.sq[m % 2]
        P.act(sq, x32[:, m, :], AF.Square)
        yield
        P.mm(pv, C.onesF, sq, start=(m == 0), stop=(m == KT - 1))
    yield
    P.act(C.rstd, pv, AF.Sqrt, bias=C.epsc[:, 0:1])
    yield
    P.recip(C.rstd, C.rstd)
    yield
    for m in range(KT):
        P.tt("dve", x32[:, m, :], x32[:, m, :], C.rstd, ALU.mult)
        yield
        P.act(x32[:, m, :], x32[:, m, :], AF.Identity, bias=b[:, m:m + 1], scale=g[:, m:m + 1])
        if write_xb:
            P.act(xb[:, m, :], x32[:, m, :], AF.Copy)
    yield


def gen_ffn_up(P, C, w1, xb):
    hT = C.hT
    NC = DFF // 128
    for c in range(NC):
        pg = C.pg[c % 2]
        pu = C.pu[c % 2]
        for k in range(KT):
            P.mm(pg, w1[:, k, c * 128:(c + 1) * 128], xb[:, k, :], start=(k == 0), stop=(k == KT - 1))
        for k in range(KT):
            P.mm(pu, w1[:, k, DFF + c * 128:DFF + (c + 1) * 128], xb[:, k, :], start=(k == 0), stop=(k == KT - 1))
        sg = C.sg[c % 2]
        P.act(sg, pg, AF.Silu)
        P.stt("dve", hT[:, c, :], sg, 0.5, pu, ALU.mult, ALU.mult)
        yield


def emit_ffn_down(P, C, w2, x32):
    hT = C.hT
    NC = DFF // 128
    for m in range(KT):
        py = C.py[m % 2]
        for c in range(NC):
            P.mm(py, w2[:, c, m * 128:(m + 1) * 128], hT[:, c, :], start=(c == 0), stop=(c == NC - 1))
        P.stt("dve", x32[:, m, :], x32[:, m, :], ALPHA, py, ALU.mult, ALU.add)


def run_rr(tasks):
    tasks = list(tasks)
    while tasks:
        for gt in list(tasks):
            try:
                next(gt)
            except StopIteration:
                tasks.remove(gt)


def emit_ffn_stage_pipelined(P, C, nt, x32s, xbs, w1, w2, g, b, load_tile, store_tile):
    load_tile(0)
    for m in range(KT):
        P.act(xbs[0][:, m, :], x32s[0][:, m, :], AF.Copy)
    if nt > 1:
        load_tile(1)
    run_rr([gen_ffn_up(P, C, w1, xbs[0])])
    for t in range(nt):
        emit_ffn_down(P, C, w2, x32s[t % 2])
        tasks = [gen_ln(P, C, x32s[t % 2], None, g, b, write_xb=False)]
        if t + 1 < nt:
            nb = (t + 1) % 2
            for m in range(KT):
                P.act(xbs[nb][:, m, :], x32s[nb][:, m, :], AF.Copy)
            tasks.append(gen_ffn_up(P, C, w1, xbs[nb]))
        run_rr(tasks)
        store_tile(t)
        if t + 2 < nt:
            load_tile(t + 2)


def emit_ffn(P, C, w1, w2, g, b):
    x32, xb, hT = C.x32, C.xb, C.hT
    NC = DFF // 128
    for c in range(NC):
        pg = C.pg[c % 2]
        pu = C.pu[c % 2]
        for k in range(KT):
            P.mm(pg, w1[:, k, c * 128:(c + 1) * 128], xb[:, k, :], start=(k == 0), stop=(k == KT - 1))
        for k in range(KT):
            P.mm(pu, w1[:, k, DFF + c * 128:DFF + (c + 1) * 128], xb[:, k, :], start=(k == 0), stop=(k == KT - 1))
        sg = C.sg[c % 2]
        P.act(sg, pg, AF.Silu)
        P.stt("dve", hT[:, c, :], sg, 0.5, pu, ALU.mult, ALU.mult)
    for m in range(KT):
        py = C.py[m % 2]
        for c in range(NC):
            P.mm(py, w2[:, c, m * 128:(m + 1) * 128], hT[:, c, :], start=(c == 0), stop=(c == NC - 1))
        P.stt("dve", x32[:, m, :], x32[:, m, :], ALPHA, py, ALU.mult, ALU.add)
    emit_ln(P, C, g, b)


def emit_mixout(P, C, mixb, wo, g, b):
    x32 = C.x32
    for m in range(KT):
        py = C.py[m % 2]
        for k in range(KT):
            P.mm(py, wo[:, k, m * 128:(m + 1) * 128], mixb[:, k, :], start=(k == 0), stop=(k == KT - 1))
        P.stt("dve", x32[:, m, :], x32[:, m, :], ALPHA, py, ALU.mult, ALU.add)
    emit_ln(P, C, g, b)


def emit_ple(P, C, pb, wg, bg, wp, g, b):
    x32, xb = C.x32, C.xb
    for m in range(KT):
        pgt = C.pg[m % 2]
        ppj = C.pu[m % 2]
        for k in range(KT):
            P.mm(pgt, wg[:, k, m * 128:(m + 1) * 128], xb[:, k, :], start=(k == 0), stop=(k == KT - 1))
        for k in range(2):
            P.mm(ppj, wp[:, k, m * 128:(m + 1) * 128], pb[:, k, :], start=(k == 0), stop=(k == 1))
        sg = C.sg[m % 2]
        P.act(sg, pgt, AF.Sigmoid, bias=bg[:, m:m + 1])
        P.tt("dve", sg, sg, ppj, ALU.mult)
        P.stt("dve", x32[:, m, :], x32[:, m, :], ALPHA, sg, ALU.mult, ALU.add)
    emit_ln(P, C, g, b)


def emit_dense(P, steps, ntok, xT, outT, lng_d, lnb_d, wd):
    nt = ntok // TN
    nln = len(steps)
    C = DenseCtx()
    C.x32 = P.sbuf("x32", [128, KT, TN], F32)
    C.xb = P.sbuf("xb", [128, KT, TN], BF16)
    C.onesF = P.sbuf("onesF", [128, 128], F32)
    C.sq = [P.sbuf(f"sq{i}", [128, TN], F32) for i in range(2)]
    C.sg = [P.sbuf(f"sg{i}", [128, TN], F32) for i in range(2)]
    C.rstd = P.sbuf("rstd", [128, TN], F32)
    C.pg = [P.psum(f"pg{i}", [128, TN]) for i in range(2)]
    C.pu = [P.psum(f"pu{i}", [128, TN]) for i in range(2)]
    C.py = [P.psum(f"py{i}", [128, TN]) for i in range(2)]
    C.pstat = [P.psum(f"pst{i}", [128, TN]) for i in range(2)]
    lng = P.sbuf("lng_s", [128, nln * KT], F32)
    lnb = P.sbuf("lnb_s", [128, nln * KT], F32)
    P.dma("sp", lng, lng_d)
    P.dma("sp", lnb, lnb_d)
    P.memset("dve", C.onesF, 1.0 / D)
    C.epsc = P.sbuf("epsc", [128, 1], F32)
    P.memset("dve", C.epsc, LN_EPS)
    ws = []
    need_hT = False
    for i, s in enumerate(steps):
        if s == "ffn":
            need_hT = True
            w1 = load_w(P, f"w1s_{i}", wd[i][0], D, 2 * DFF, nsplit=4)
            w2 = load_w(P, f"w2s_{i}", wd[i][1], DFF, D)
            ws.append((w1, w2))
        elif s == "mixout":
            wo = load_w(P, f"wos_{i}", wd[i][0], D, D)
            mixb = P.sbuf(f"mixb_{i}", [128, KT, TN], BF16)
            ws.append((wo, mixb))
        elif s == "ple":
            wg = load_w(P, f"wgs_{i}", wd[i][0], D, D)
            wp = load_w(P, f"wps_{i}", wd[i][1], 256, D)
            bg = P.sbuf(f"bgs_{i}", [128, KT], F32)
            P.dma("sp", bg, wd[i][2])
            pb = P.sbuf(f"pb_{i}", [128, 2, TN], BF16)
            ws.append((wg, wp, bg, pb))
    if need_hT:
        C.hT = P.sbuf("hT", [128, DFF // 128, TN], BF16)
    xTr = xT.rearrange("(k p) n -> p k n", p=128)
    oTr = outT.rearrange("(k p) n -> p k n", p=128)
    x32s = [C.x32, P.sbuf("x32b", [128, KT, TN], F32)]
    mixbs = {}
    pbs = {}
    for i, s in enumerate(steps):
        if s == "mixout":
            mixbs[i] = [ws[i][1], P.sbuf(f"mixb2_{i}", [128, KT, TN], BF16)]
        elif s == "ple":
            pbs[i] = [ws[i][3], P.sbuf(f"pb2_{i}", [128, 2, TN], BF16)]

    def load_tile(t):
        tsl = slice(t * TN, (t + 1) * TN)
        P.dma("sp", x32s[t % 2], xTr[:, :, tsl])
        for i, s in enumerate(steps):
            if s == "mixout":
                P.dma("sp", mixbs[i][t % 2], wd[i][1].rearrange("(k p) n -> p k n", p=128)[:, :, tsl])
            elif s == "ple":
                pTr = wd[i][3].rearrange("(k p) n -> p k n", p=128)
                for k in range(2):
                    P.dma("pool", pbs[i][t % 2][:, k, :], pTr[:, k, tsl])

    if steps == ["ffn"] and PIPE_FFN:
        xbs = [C.xb, C.xb]

        def store_tile(t):
            P.dma("pool", oTr[:, :, t * TN:(t + 1) * TN], x32s[t % 2])
        emit_ffn_stage_pipelined(P, C, nt, x32s, xbs, ws[0][0], ws[0][1], lng[:, 0:KT], lnb[:, 0:KT], load_tile, store_tile)
        return
    load_tile(0)
    for t in range(nt):
        tsl = slice(t * TN, (t + 1) * TN)
        C.x32 = x32s[t % 2]
        if steps[0] != "mixout":
            for m in range(KT):
                P.act(C.xb[:, m, :], C.x32[:, m, :], AF.Copy)
        if t + 1 < nt:
            load_tile(t + 1)
        for i, s in enumerate(steps):
            g = lng[:, i * KT:(i + 1) * KT]
            b = lnb[:, i * KT:(i + 1) * KT]
            if s == "ffn":
                emit_ffn(P, C, ws[i][0], ws[i][1], g, b)
            elif s == "mixout":
                emit_mixout(P, C, mixbs[i][t % 2], ws[i][0], g, b)
            elif s == "ple":
                emit_ple(P, C, pbs[i][t % 2], ws[i][0], ws[i][2], ws[i][1], g, b)
        P.dma("pool", oTr[:, :, tsl], C.x32)


def build_dense(steps, ntok, nc=None):
    if nc is None:
        nc = bass.Bass("TRN2", target_bir_lowering=False)
    P = Prog(nc)
    xT = P.dram("xT", [D, ntok], F32, "ExternalInput")
    outT = P.dram("outT", [D, ntok], F32, "ExternalOutput")
    nln = len(steps)
    lng_d = P.dram("lng", [128, nln * KT], F32, "ExternalInput")
    lnb_d = P.dram("lnb", [128, nln * KT], F32, "ExternalInput")
    wd = []
    for i, s in enumerate(steps):
        if s == "ffn":
            wd.append((P.dram(f"w1_{i}", [D, 2 * DFF], F32, "ExternalInput"),
                       P.dram(f"w2_{i}", [DFF, D], F32, "ExternalInput")))
        elif s == "mixout":
            wd.append((P.dram(f"wo_{i}", [D, D], F32, "ExternalInput"),
                       P.dram(f"mixT_{i}", [D, ntok], BF16, "ExternalInput")))
        elif s == "ple":
            wd.append((P.dram(f"wg_{i}", [D, D], F32, "ExternalInput"),
                       P.dram(f"wp_{i}", [256, D], F32, "ExternalInput"),
                       P.dram(f"bg_{i}", [128, KT], F32, "ExternalInput"),
                       P.dram(f"pT_{i}", [256, ntok], F32, "ExternalInput")))
    emit_dense(P, steps, ntok, xT, outT, lng_d, lnb_d, wd)
    st = P.finalize()
    return nc, st

import os
POOL = os.environ.get("MIX_POOL", "pool")
LVL = int(os.environ.get("MIX_LVL", "9"))
ACT_PSUM_R = os.environ.get("MIX_ACT_PSUM_R", "1") == "1"
NOCARRY = os.environ.get("MIX_NOCARRY", "0") == "1"

D = 1024
KT = 8
TN = 512
NW = 1796
C_SQ, C_SK, C_SV = 0, 128, 256
C_GQ, C_GK, C_GV, C_GZ = 384, 640, 896, 1152
C_AB = 1408
C_SCB, C_SCC, C_SCH = 1412, 1540, 1668
NORM_EPS = 1e-6


def mix_consts():
    p = np.arange(128)[:, None]
    f = np.arange(128)[None, :]
    c = {}
    c["ident"] = (p == f).astype(np.float32)
    c["triU"] = (p <= f).astype(np.float32)
    c["triNeg"] = -(p >= f).astype(np.float32)
    c["SL"] = (p > f).astype(np.float32)
    c["SU"] = (p < f).astype(np.float32)
    c["UI"] = (p <= f).astype(np.float32)
    m = np.zeros((128, 4, 512), np.float32)
    tq = np.arange(512)[None, :]
    for i in range(4):
        m[:, i, :] = ((128 * i + p) < tq).astype(np.float32)
    c["mask"] = m.reshape(128, 2048)
    order = ["ident", "triU", "triNeg", "SL", "SU", "UI", "mask"]
    arr = np.concatenate([c[k] for k in order], axis=1)
    offs = {}
    o = 0
    for k in order:
        offs[k] = (o, o + c[k].shape[1])
        o += c[k].shape[1]
    return arr, offs


class Ring:
    def __init__(self, items):
        self.items = items
        self.i = 0

    def __call__(self):
        x = self.items[self.i % len(self.items)]
        self.i += 1
        return x


def emit_mix(P, S, hT, wm_d, cst_d, gcw_d, scw_d, hp_d, nw_d, o_sb, o_gdnT, o_sc,
             do_attn=True, do_gdn=True, do_sc=True):
    NT = S // TN
    NKB = S // 128
    carr, coff = mix_consts()

    cst = P.sbuf("cst_s", list(carr.shape), F32)
    P.dma("sp", cst, cst_d)

    def cs(k):
        a, b = coff[k]
        return cst[:, a:b]
    identF, triU, SL, SU, UI = cs("ident"), cs("triU"), cs("SL"), cs("SU"), cs("UI")
    maskF = cs("mask")
    identB = P.sbuf("identB", [128, 128], BF16)
    triNegB = P.sbuf("triNegB", [128, 128], BF16)
    P.copy("dve", identB, identF)
    P.copy("dve", triNegB, cs("triNeg"))
    ones1 = P.sbuf("ones1", [128, 128], F32)
    P.memset("dve", ones1, 1.0)
    onesRowB = P.sbuf("onesRowB", [1, 128], BF16)
    P.memset("dve", onesRowB, 1.0)
    onec = P.sbuf("onec", [128, 1], F32)
    P.memset("dve", onec, 1.0)
    epsc = P.sbuf("epsc", [128, 1], F32)
    P.memset("dve", epsc, NORM_EPS)
    gcw = P.sbuf("gcw_s", [128, 24], F32)
    scw = P.sbuf("scw_s", [128, 3], F32)
    hp = P.sbuf("hp_s", [128, 4], F32)
    nwb = P.sbuf("nw_s", [128, 128], F32)
    P.dma("sp", gcw, gcw_d)
    P.dma("sp", scw, scw_d)
    P.dma("sp", hp, hp_d)
    P.dma("sp", nwb, nw_d)
    nA = P.sbuf("nA", [128, 2], F32)
    P.act(nA, hp[:, 0:2], AF.Exp)
    P.ts("dve", nA, nA, -1.0, None, ALU.mult)
    dtb = hp[:, 2:4]
    wm = P.sbuf("wm_s", [128, KT, NW], BF16)
    for k in range(KT):
        P.dma("pool", wm[:, k, :], wm_d[k * 128:(k + 1) * 128, :])

    hb = P.sbuf("hb", [128, KT, TN], BF16)
    qT = P.sbuf("qT", [128, S], BF16)
    kT = P.sbuf("kT", [128, S], BF16)
    vA = P.sbuf("vA", [128, NKB, 128], BF16)
    pf = P.psum("pf", [128, 6, 512], F32)
    pbf = P.psum("pbf", [128, 2, 1024], BF16)
    bank = Ring([pf[:, i, :] for i in range(0, 4)])
    bank_o = Ring([pf[:, 4, :], pf[:, 5, :]])
    gbank = Ring([pf[:, i, :] for i in range(6)])
    tb2 = Ring([pbf[:, i, 0:128] for i in range(2)])
    quart = lambda: bank()[:, 0:128]
    tbank = Ring([pbf[:, i, 0:128] for i in range(2)])
    tbs = [pbf[:, i, 0:128] for i in range(2)]
    hqs = [Ring([pf[:, 2 * h, 0:128], pf[:, 2 * h + 1, 0:128]]) for h in range(2)]
    sq = Ring([pf[:, 4, :], pf[:, 5, :]])
    eb = Ring([P.sbuf(f"e{i}", [128, TN], F32) for i in range(8)])
    Lb = Ring([P.sbuf(f"L{i}", [128, TN], BF16) for i in range(6)])
    eRb = Ring([P.sbuf(f"eR{i}", [128, TN], F32) for i in range(2)])
    attb = Ring([P.sbuf(f"att{i}", [128, TN], BF16) for i in range(7)])
    lsumb = [Ring([P.sbuf(f"ls{h}_{i}", [128, TN], BF16) for i in range(4)]) for h in range(2)]
    negOnesB = P.sbuf("negOnesB", [128, 128], BF16)
    P.memset("dve", negOnesB, -1.0)
    osb = [P.sbuf(f"osb{i}", [64, TN], BF16) for i in range(2)]
    raw = [P.sbuf(f"raw{f}", [128, 3 + TN], F32) for f in range(6)]
    ycv = P.sbuf("ycv", [128, TN], F32)
    ysl = ycv
    sqb = P.sbuf("sqb", [128, TN], F32)
    rnb = sqb
    gqT = [P.sbuf(f"gqT{h}", [128, TN], BF16) for h in range(2)]
    gkT = [P.sbuf(f"gkT{h}", [128, TN], BF16) for h in range(2)]
    gvT = [P.sbuf(f"gvT{h}", [128, TN], BF16) for h in range(2)]
    szbs = [P.sbuf(f"szb{i}", [128, 256], F32) for i in range(3)]
    sc5s = [{n: P.sbuf(f"sc{i}_{n}", [128, 2], F32) for n in
             ("beta", "nbeta", "g", "gc", "gtot", "egl", "eg", "bg", "kd", "tmp")} for i in range(3)]
    ab_sbs = [P.sbuf(f"ab_sb{i}", [128, 4], F32) for i in range(3)]
    ogs = [P.sbuf(f"og{i}", [128, 256], BF16) for i in range(2)]
    ogT = P.sbuf("ogT", [128, 2, TN], BF16)

    def t128(name, dt=F32):
        return P.sbuf(name, [128, 128], dt)
    hbuf = []
    for h in range(2):
        B = {}
        for n in ("gU", "E1", "Dall", "DL", "DUb", "DUI", "dB", "egRow", "D1s", "BRs", "KKs", "kbg", "vb", "u_sb", "junk", "onb", "o_s"):
            B[n] = t128(f"h{h}_{n}")
        for n in ("kdec", "wTs", "qkTs", "qdT", "vnew"):
            B[n] = t128(f"h{h}_{n}", BF16)
        B["Nb"] = Ring([t128(f"h{h}_Nb{i}") for i in range(3)])
        B["NTb"] = Ring([t128(f"h{h}_NTb{i}") for i in range(3)])
        B["XTb"] = Ring([t128(f"h{h}_XTb{i}") for i in range(3)])
        B["ss1"] = P.sbuf(f"h{h}_ss1", [128, 1], F32)
        B["rs1"] = P.sbuf(f"h{h}_rs1", [128, 1], F32)
        hbuf.append(B)
    Sst = [t128(f"S{h}") for h in range(2)]
    Sbf = [t128(f"Sb{h}", BF16) for h in range(2)]
    hand = [[], []]
    for h in range(2):
        hand[0].append({n: hbuf[h][n] for n in ("u_sb", "kdec", "wTs", "qkTs", "qdT")})
        eR_t = eRb.items[0]
        L_t = Lb.items[h]
        hand[1].append({"u_sb": eR_t[:, h * 128:(h + 1) * 128],
                        "kdec": L_t[:, 0:128], "wTs": L_t[:, 128:256], "qkTs": L_t[:, 256:384], "qdT": L_t[:, 384:512]})
    for h in range(2):
        P.memset("dve", Sst[h], 0.0)
        P.memset("dve", Sbf[h], 0.0)
    for f in range(6):
        P.memset("dve", raw[f][:, 0:3], 0.0)
    rawc = P.sbuf("rawc", [128, 2 + TN], F32)
    P.memset("dve", rawc[:, 0:2], 0.0)
    scB = P.sbuf("scB", [128, TN], F32)
    scC = P.sbuf("scC", [128, TN], F32)
    scy = scC
    sco = P.sbuf("sco", [128, TN], BF16)

    hTr = hT.rearrange("(k p) n -> p k n", p=128)

    def proj_fm(col0, ncol=128):
        pb = bank()
        for k in range(KT):
            P.mm(pb[0:ncol, :], wm[:, k, col0:col0 + ncol], hb[:, k, :], start=(k == 0), stop=(k == KT - 1))
        return pb

    for t in range(NT):
        tsl = slice(t * TN, (t + 1) * TN)
        for k in range(KT):
            P.dma("pool", hb[:, k, :], hTr[:, k, tsl])
        if do_attn:
            pq = proj_fm(C_SQ)
            P.ts("dve", qT[:, tsl], pq, 0.125, None, ALU.mult)
            if not os.environ.get("MIX_NOK"):
                pk = proj_fm(C_SK)
                P.act(kT[:, tsl], pk, AF.Copy)
            for s in range(0 if os.environ.get("MIX_NOV") else 4):
                pv = quart()
                for k in range(KT):
                    P.mm(pv, hb[:, k, s * 128:(s + 1) * 128], wm[:, k, C_SV:C_SV + 128], start=(k == 0), stop=(k == KT - 1))
                P.copy("dve", vA[:, 4 * t + s, :], pv)
            nkb = 4 * t + 4
            items = [(hd, kb) for kb in range(nkb - 1, -1, -1) for hd in range(2)]
            po_h = [bank_o(), bank_o()]
            st1 = {}
            st2 = {}
            lsum_cur = [None, None]

            st0 = {}

            def att_s0(hd, kb):
                ps = slice(64 * hd, 64 * hd + 64)
                pz = bank()
                P.mm(pz, kT[ps, kb * 128:(kb + 1) * 128], qT[ps, tsl])
                e = eb()
                P.act(e, pz, AF.Exp)
                st0[(hd, kb)] = e

            def att_s1(hd, kb):
                e = st0.pop((hd, kb))
                if kb >= 4 * t:
                    i = kb - 4 * t
                    P.tt(POOL, e, e, maskF[:, i * 512:(i + 1) * 512], ALU.mult)
                L = Lb()
                P.act(L, e, AF.Ln, bias=onec[:, 0:1])
                carry = lsum_cur[hd]
                if kb > 0:
                    ns = lsumb[hd]()
                    if carry is None:
                        P.copy("dve", ns, L)
                    else:
                        P.tt("dve", ns, carry, L, ALU.add)
                    lsum_cur[hd] = ns
                st1[(hd, kb)] = (e, L, carry)

            def att_s2(hd, kb):
                e, L, carry = st1.pop((hd, kb))
                pr = bank()
                P.mm(pr, triNegB, L, start=True, stop=(carry is None))
                if carry is not None:
                    P.mm(pr, negOnesB, carry, start=False, stop=True)
                eR = eRb()
                P.act(eR, pr, AF.Exp)
                att = attb()
                P.tt(POOL, att, e, eR, ALU.mult)
                st2[(hd, kb)] = att

            def att_s3(hd, kb):
                att = st2.pop((hd, kb))
                P.mm(po_h[hd][0:64, :], vA[:, kb, 64 * hd:64 * hd + 64], att, start=(kb == nkb - 1), stop=(kb == 0))

            LOOK = int(os.environ.get("MIX_LOOK", "4"))
            SK = int(os.environ.get("MIX_SK", "2"))
            n_it = len(items)
            for idx in range(n_it + SK + 2 * LOOK):
                if idx < n_it:
                    att_s0(*items[idx])
                if SK <= idx < n_it + SK:
                    att_s1(*items[idx - SK])
                if SK + LOOK <= idx < n_it + SK + LOOK:
                    att_s2(*items[idx - SK - LOOK])
                if idx >= SK + 2 * LOOK:
                    att_s3(*items[idx - SK - 2 * LOOK])
            for hd in range(2):
                P.act(osb[hd], po_h[hd][0:64, :], AF.Copy)
                P.dma("sp", o_sb[64 * hd:64 * hd + 64, tsl], osb[hd])
        if do_sc:
            pB = proj_fm(C_SCB)
            P.act(scB, pB, AF.Copy)
            pC = proj_fm(C_SCC)
            P.act(scC, pC, AF.Copy)
            pH = proj_fm(C_SCH)
            if t > 0:
                P.copy("dve", rawc[:, 0:2], rawc[:, TN:TN + 2])
            P.tt("dve", rawc[:, 2:2 + TN], scC, pH, ALU.mult)
            P.ts("dve", scy, rawc[:, 0:TN], scw[:, 0:1], None, ALU.mult)
            for i in (1, 2):
                P.stt("dve", scy, rawc[:, i:i + TN], scw[:, i:i + 1], scy, ALU.mult, ALU.add)
            P.tt("dve", sco, scB, scy, ALU.mult)
            P.dma("sp", o_sc[:, tsl], sco)
        if do_gdn:
            for f in range(6):
                col0 = C_GQ + f * 128
                pg = proj_fm(col0)
                if t > 0:
                    P.copy("dve", raw[f][:, 0:3], raw[f][:, TN:TN + 3])
                P.act(raw[f][:, 3:3 + TN], pg, AF.Copy)
                P.ts("dve", ycv, raw[f][:, 0:TN], gcw[:, f * 4:f * 4 + 1], None, ALU.mult)
                for i in (1, 2, 3):
                    P.stt("dve", ycv, raw[f][:, i:i + TN], gcw[:, f * 4 + i:f * 4 + i + 1], ycv, ALU.mult, ALU.add)
                h_ = f % 2
                if f >= 4:
                    P.act(gvT[h_], ycv, AF.Silu)
                    continue
                P.act(ysl, ycv, AF.Silu)
                P.act(sqb, ysl, AF.Square)
                pss = bank()
                P.mm(pss, ones1, sqb)
                P.act(rnb, pss, AF.Sqrt, bias=epsc[:, 0:1])
                P.recip(rnb, rnb)
                if f < 2:
                    P.stt("dve", gqT[h_], ysl, 128.0 ** -0.5, rnb, ALU.mult, ALU.mult)
                else:
                    P.tt("dve", gkT[h_], ysl, rnb, ALU.mult)
            gb = gbank
            def scal_task(c):
                csl = slice(c * 128, (c + 1) * 128)
                S5 = sc5s[c % 3]
                szb = szbs[c % 3]
                ab_sb = ab_sbs[c % 3]
                pzz = gb()
                for k in range(KT):
                    P.mm(pzz[:, 0:256], hb[:, k, csl], wm[:, k, C_GZ:C_GZ + 256], start=(k == 0), stop=(k == KT - 1))
                P.act(szb, pzz[:, 0:256], AF.Silu)
                yield
                pab = gb()
                for k in range(KT):
                    P.mm(pab[:, 0:4], hb[:, k, csl], wm[:, k, C_AB:C_AB + 4], start=(k == 0), stop=(k == KT - 1))
                P.copy("dve", ab_sb, pab[:, 0:4])
                yield
                P.act(S5["beta"], ab_sb[:, 2:4], AF.Sigmoid)
                P.tt("dve", S5["tmp"], ab_sb[:, 0:2], dtb, ALU.add)
                yield
                P.ts("dve", S5["nbeta"], S5["beta"], -1.0, None, ALU.mult)
                P.act(S5["tmp"], S5["tmp"], AF.Exp)
                yield
                P.act(S5["tmp"], S5["tmp"], AF.Ln, bias=onec[:, 0:1])
                yield
                P.tt("dve", S5["g"], S5["tmp"], nA, ALU.mult)
                yield
                pgc = gb()
                P.mm(pgc[:, 0:2], triU, S5["g"])
                P.copy("dve", S5["gc"], pgc[:, 0:2])
                pgt = gb()
                P.mm(pgt[:, 0:2], ones1, S5["g"])
                P.copy("dve", S5["gtot"], pgt[:, 0:2])
                yield
                P.act(S5["egl"], S5["gtot"], AF.Exp)
                P.act(S5["eg"], S5["gc"], AF.Exp)
                P.tt("dve", S5["kd"], S5["gtot"], S5["gc"], ALU.subtract)
                yield
                P.tt("dve", S5["bg"], S5["beta"], S5["eg"], ALU.mult)
                P.act(S5["kd"], S5["kd"], AF.Exp)
                yield

            def prep_task(c, h_):
                csl = slice(c * 128, (c + 1) * 128)
                S5 = sc5s[c % 3]
                B = hbuf[h_]
                HO = hand[c % 2][h_]
                hs = slice(h_, h_ + 1)
                kTc = gkT[h_][:, csl]
                qTc = gqT[h_][:, csl]
                P.ts("dve", B["gU"], triU, S5["g"][:, hs], None, ALU.mult)
                P.ts("dve", B["dB"], identF, S5["beta"][:, hs], None, ALU.mult)
                yield
                pD1 = gb()[:, 0:128]
                P.mm(pD1, ones1, B["gU"])
                P.copy("dve", B["D1s"], pD1)
                yield
                pBR = gb()[:, 0:128]
                P.mm(pBR, ones1, B["dB"])
                P.copy("dve", B["BRs"], pBR)
                yield
                pKK = gb()[:, 0:128]
                P.mm(pKK, kTc, kTc)
                P.copy("dve", B["KKs"], pKK)
                yield
                P.act(B["egRow"], B["D1s"], AF.Exp)
                P.ts("dve", B["E1"], B["D1s"], S5["gc"][:, hs], None, ALU.subtract)
                yield
                P.act(B["E1"], B["E1"], AF.Abs)
                yield
                P.act(B["Dall"], B["E1"], AF.Exp, scale=-1.0)
                yield
                P.tt("pool", B["DL"], B["Dall"], SL, ALU.mult)
                P.tt("pool", B["DUb"], B["Dall"], SU, ALU.mult)
                P.tt("pool", B["DUI"], B["Dall"], UI, ALU.mult)
                yield
                P.tt("dve", B["DUb"], B["DUb"], B["BRs"], ALU.mult)
                N = B["Nb"]()
                NT_ = B["NTb"]()
                XT = B["XTb"]()
                P.stt("dve", N, B["KKs"], S5["nbeta"][:, hs], B["DL"], ALU.mult, ALU.mult)
                yield
                P.stt("dve", NT_, B["KKs"], -1.0, B["DUb"], ALU.mult, ALU.mult)
                yield
                P.tt("pool", XT, identF, NT_, ALU.add)
                ptk = tb2()
                P.transpose(ptk, kTc, identB)
                P.ts("dve", B["kbg"], ptk, S5["bg"][:, hs], None, ALU.mult)
                P.ts("dve", HO["kdec"], ptk, S5["kd"][:, hs], None, ALU.mult)
                yield
                ptv = tb2()
                P.transpose(ptv, gvT[h_][:, csl], identB)
                P.ts("dve", B["vb"], ptv, S5["beta"][:, hs], None, ALU.mult)
                yield
                pqk = gb()[:, 0:128]
                P.mm(pqk, kTc, qTc)
                P.tt("dve", HO["qkTs"], pqk, B["DUI"], ALU.mult)
                P.tt("pool", HO["qdT"], qTc, B["egRow"], ALU.mult)
                yield
                for kk in range(1, 7):
                    pN = gb()[:, 0:128]
                    P.mm(pN, NT_, N)
                    N2 = B["Nb"]()
                    P.copy("act", N2, pN)
                    yield
                    if kk < 6:
                        pNT = gb()[:, 0:128]
                        P.mm(pNT, N, NT_)
                        NT2 = B["NTb"]()
                        P.copy("dve", NT2, pNT)
                        yield
                    pX = gb()[:, 0:128]
                    P.mm(pX, N2, XT)
                    XT2 = B["XTb"]()
                    P.tt("dve", XT2, XT, pX, ALU.add)
                    N, XT = N2, XT2
                    if kk < 6:
                        NT_ = NT2
                    yield
                pu = gb()[:, 0:128]
                P.mm(pu, XT, B["vb"])
                P.copy("act", HO["u_sb"], pu)
                yield
                pw = gb()[:, 0:128]
                P.mm(pw, B["kbg"], XT)
                P.copy("act", HO["wTs"], pw)
                yield

            def scan_task(c, h_):
                S5 = sc5s[c % 3]
                szb = szbs[c % 3]
                og = ogs[c % 2]
                B = hbuf[h_]
                HO = hand[c % 2][h_]
                hs = slice(h_, h_ + 1)
                p1 = gb()[:, 0:128]
                P.mm(p1, HO["wTs"], Sbf[h_])
                P.tt("dve", B["vnew"], HO["u_sb"], p1, ALU.subtract)
                yield
                p2 = gb()[:, 0:128]
                P.mm(p2, HO["qdT"], Sbf[h_], start=True, stop=False)
                P.mm(p2, HO["qkTs"], B["vnew"], start=False, stop=True)
                P.copy("dve", B["o_s"], p2)
                p3 = gb()[:, 0:128]
                P.mm(p3, HO["kdec"], B["vnew"])
                P.stt("dve", Sst[h_], Sst[h_], S5["egl"][:, hs], p3, ALU.mult, ALU.add)
                yield
                P.copy("act", Sbf[h_], Sst[h_])
                P.act(B["junk"], B["o_s"], AF.Square, accum_out=B["ss1"])
                yield
                P.act(B["rs1"], B["ss1"], AF.Sqrt, scale=1.0 / 128.0, bias=epsc[:, 0:1])
                yield
                P.recip(B["rs1"], B["rs1"])
                yield
                P.stt("dve", B["onb"], B["o_s"], B["rs1"][:, 0:1], nwb, ALU.mult, ALU.mult)
                yield
                P.tt("dve", og[:, h_ * 128:(h_ + 1) * 128], B["onb"], szb[:, h_ * 128:(h_ + 1) * 128], ALU.mult)
                yield

            def run_tasks(tasks):
                tasks = list(tasks)
                while tasks:
                    for g in list(tasks):
                        try:
                            next(g)
                        except StopIteration:
                            tasks.remove(g)

            run_tasks([scal_task(0)])
            run_tasks([prep_task(0, 0), prep_task(0, 1), scal_task(1)])
            for c in range(4):
                csl = slice(c * 128, (c + 1) * 128)
                tl = [scan_task(c, 0), scan_task(c, 1)]
                if c + 1 < 4:
                    tl += [prep_task(c + 1, 0), prep_task(c + 1, 1)]
                if c + 2 < 4:
                    tl.append(scal_task(c + 2))
                run_tasks(tl)
                og = ogs[c % 2]
                for h_ in range(2):
                    ptg = tb2()
                    P.transpose(ptg, og[:, h_ * 128:(h_ + 1) * 128], identB)
                    P.copy("dve", ogT[:, h_, csl], ptg)
            for h_ in range(2):
                P.dma("sp", o_gdnT[h_ * 128:(h_ + 1) * 128, tsl], ogT[:, h_, :])


def build_mix(S, nc=None, do_attn=True, do_gdn=True, do_sc=True):
    if nc is None:
        nc = bass.Bass("TRN2", target_bir_lowering=False)
    P = Prog(nc)
    carr, coff = mix_consts()
    hT = P.dram("hT", [D, S], F32, "ExternalInput")
    wm_d = P.dram("wm", [D, NW], F32, "ExternalInput")
    cst_d = P.dram("cst", list(carr.shape), F32, "ExternalInput")
    gcw_d = P.dram("gcw", [128, 24], F32, "ExternalInput")
    scw_d = P.dram("scw", [128, 3], F32, "ExternalInput")
    hp_d = P.dram("hp", [128, 4], F32, "ExternalInput")
    nw_d = P.dram("nw", [128, 128], F32, "ExternalInput")
    o_sb = P.dram("o_sb", [128, S], BF16, "ExternalOutput")
    o_gdnT = P.dram("o_gdnT", [256, S], BF16, "ExternalOutput")
    o_sc = P.dram("o_sc", [128, S], BF16, "ExternalOutput")
    emit_mix(P, S, hT, wm_d, cst_d, gcw_d, scw_d, hp_d, nw_d, o_sb, o_gdnT, o_sc, do_attn, do_gdn, do_sc)
    st = P.finalize()
    return nc, st, carr


NSTEP = 4


def build_fused(S, depth=2, nc=None):
    if nc is None:
        nc = bass.Bass("TRN2", target_bir_lowering=False)
    P = Prog(nc)
    carr, _ = mix_consts()
    xT = P.dram("xT", [D, S], F32, "ExternalInput")
    outT = P.dram("outT", [D, S], F32, "ExternalOutput")
    lng = P.dram("lng", [128, depth * NSTEP * KT], F32, "ExternalInput")
    lnb = P.dram("lnb", [128, depth * NSTEP * KT], F32, "ExternalInput")
    cst = P.dram("cst", list(carr.shape), F32, "ExternalInput")
    W = []
    for i in range(depth):
        w = {}
        for f in range(2):
            w[f"w1{f}"] = P.dram(f"w1_{i}_{f}", [D, 2 * DFF], F32, "ExternalInput")
            w[f"w2{f}"] = P.dram(f"w2_{i}_{f}", [DFF, D], F32, "ExternalInput")
        for j in range(2):
            w[f"wm{j}"] = P.dram(f"wm_{i}_{j}", [D, NW], F32, "ExternalInput")
            w[f"gcw{j}"] = P.dram(f"gcw_{i}_{j}", [128, 24], F32, "ExternalInput")
            w[f"scw{j}"] = P.dram(f"scw_{i}_{j}", [128, 3], F32, "ExternalInput")
            w[f"hp{j}"] = P.dram(f"hp_{i}_{j}", [128, 4], F32, "ExternalInput")
        w["nw"] = P.dram(f"nw_{i}", [128, 128], F32, "ExternalInput")
        w["wo"] = P.dram(f"wo_{i}", [D, D], F32, "ExternalInput")
        w["wg"] = P.dram(f"wg_{i}", [D, D], F32, "ExternalInput")
        w["wp"] = P.dram(f"wp_{i}", [256, D], F32, "ExternalInput")
        w["bg"] = P.dram(f"bg_{i}", [128, KT], F32, "ExternalInput")
        w["pT"] = P.dram(f"pT_{i}", [256, S], F32, "ExternalInput")
        W.append(w)
    H = P.dram("H_scr", [D, S], F32, "Internal")
    X1 = P.dram("X1_scr", [D, S], F32, "Internal")
    X2 = P.dram("X2_scr", [D, S], F32, "Internal")
    MIX = P.dram("MIX_scr", [D, S], BF16, "Internal")

    def ln(i, s):
        o = (i * NSTEP + s) * KT
        return lng[:, o:o + KT], lnb[:, o:o + KT]

    cur = xT
    for i in range(depth):
        w = W[i]
        with P.stage(f"A{i}"):
            g, b = ln(i, 0)
            emit_dense(P, ["ffn"], S, cur, H, g, b, [(w["w10"], w["w20"])])
        for j in range(2):
            with P.stage(f"B{i}{j}"):
                emit_mix(P, S, H, w[f"wm{j}"], cst, w[f"gcw{j}"], w[f"scw{j}"], w[f"hp{j}"], w["nw"],
                         MIX[j * 128:(j + 1) * 128, :], MIX[256 + j * 256:256 + (j + 1) * 256, :],
                         MIX[768 + j * 128:768 + (j + 1) * 128, :])
        with P.stage(f"M{i}"):
            g, b = ln(i, 1)
            emit_dense(P, ["mixout"], S, H, X1, g, b, [(w["wo"], MIX)])
        with P.stage(f"F{i}"):
            g, b = ln(i, 2)
            emit_dense(P, ["ffn"], S, X1, X2, g, b, [(w["w11"], w["w21"])])
        with P.stage(f"P{i}"):
            g, b = ln(i, 3)
            dst = outT if i == depth - 1 else X1
            emit_dense(P, ["ple"], S, X2, dst, g, b, [(w["wg"], w["wp"], w["bg"], w["pT"])])
        cur = X1
    return nc, P.tot, carr


def _relay(v):
    return np.ascontiguousarray(np.asarray(v, np.float32).reshape(8, 128).T)


def fused_inputs(b, S, depth, carr, x, p, ln_g, ln_b, ffn_w_in, ffn_w_out, mix_w_in, gdn_conv_w, gdn_a_log,
                 gdn_dt_bias, gdn_norm_w, sc_conv_w, mix_w_out, ple_w_proj, ple_w_gate, ple_b_gate, shared=None):
    m = {} if shared is None else dict(shared)
    m["xT"] = np.ascontiguousarray(x[b].T)
    for i in range(depth):
        m[f"pT_{i}"] = np.ascontiguousarray(p[i, b].T)
    if shared is not None:
        return m
    m["cst"] = carr
    m["lng"] = np.ascontiguousarray(np.concatenate([_relay(ln_g[i, s]) for i in range(depth) for s in range(4)], 1))
    m["lnb"] = np.ascontiguousarray(np.concatenate([_relay(ln_b[i, s]) for i in range(depth) for s in range(4)], 1))
    OFF_SB = 768
    OFF_QKV = OFF_SB + 1536
    OFF_A = OFF_QKV + 512
    OFF_Bt = OFF_A + 4
    OFF_SC = OFF_Bt + 4
    for i in range(depth):
        for f in range(2):
            m[f"w1_{i}_{f}"] = np.ascontiguousarray(ffn_w_in[i, f])
            m[f"w2_{i}_{f}"] = np.ascontiguousarray(ffn_w_out[i, f])
        for j in range(2):
            cols = []
            for base in (0, 256, 512):
                cols.append(np.arange(base + j * 128, base + (j + 1) * 128))
            for base in (OFF_SB, OFF_SB + 512, OFF_SB + 1024, OFF_QKV):
                cols.append(np.arange(base + j * 256, base + (j + 1) * 256))
            cols.append(np.arange(OFF_A + j * 2, OFF_A + j * 2 + 2))
            cols.append(np.arange(OFF_Bt + j * 2, OFF_Bt + j * 2 + 2))
            for base in (OFF_SC, OFF_SC + 256, OFF_SC + 512):
                cols.append(np.arange(base + j * 128, base + (j + 1) * 128))
            cols = np.concatenate(cols)
            m[f"wm_{i}_{j}"] = np.ascontiguousarray(mix_w_in[i][:, cols])
            gidx = np.concatenate([np.arange(base + j * 256, base + (j + 1) * 256) for base in (0, 512, 1024)])
            m[f"gcw_{i}_{j}"] = np.ascontiguousarray(gdn_conv_w[i][:, gidx].reshape(4, 6, 128).transpose(2, 1, 0).reshape(128, 24))
            m[f"scw_{i}_{j}"] = np.ascontiguousarray(sc_conv_w[i][:, j * 128:(j + 1) * 128].T)
            m[f"hp_{i}_{j}"] = np.ascontiguousarray(np.tile(np.concatenate(
                [gdn_a_log[i][2 * j:2 * j + 2], gdn_dt_bias[i][2 * j:2 * j + 2]])[None, :], (128, 1)).astype(np.float32))
        m[f"nw_{i}"] = np.ascontiguousarray(np.tile(gdn_norm_w[i][None, :], (128, 1)).astype(np.float32))
        m[f"wo_{i}"] = np.ascontiguousarray(mix_w_out[i])
        m[f"wg_{i}"] = np.ascontiguousarray(ple_w_gate[i])
        m[f"wp_{i}"] = np.ascontiguousarray(ple_w_proj[i])
        m[f"bg_{i}"] = _relay(ple_b_gate[i])
    return m


from concourse.bass_utils import run_bass_kernel_spmd

BATCH, SEQ, DEPTH = 4, 8192, 2


def kernel(x, p, ln_g, ln_b, ffn_w_in, ffn_w_out, mix_w_in, gdn_conv_w, gdn_a_log,
           gdn_dt_bias, gdn_norm_w, sc_conv_w, mix_w_out, ple_w_proj, ple_w_gate, ple_b_gate):
    f = lambda a: np.asarray(a, np.float32)
    args = dict(x=f(x), p=f(p), ln_g=f(ln_g), ln_b=f(ln_b), ffn_w_in=f(ffn_w_in), ffn_w_out=f(ffn_w_out),
                mix_w_in=f(mix_w_in), gdn_conv_w=f(gdn_conv_w), gdn_a_log=f(gdn_a_log), gdn_dt_bias=f(gdn_dt_bias),
                gdn_norm_w=f(gdn_norm_w), sc_conv_w=f(sc_conv_w), mix_w_out=f(mix_w_out), ple_w_proj=f(ple_w_proj),
                ple_w_gate=f(ple_w_gate), ple_b_gate=f(ple_b_gate))
    nc, _, carr = build_fused(SEQ, DEPTH)
    m0 = fused_inputs(0, SEQ, DEPTH, carr, **args)
    shared = {k: v for k, v in m0.items() if k != "xT" and not k.startswith("pT_")}
    maps = [m0] + [fused_inputs(b, SEQ, DEPTH, carr, shared=shared, **args) for b in range(1, BATCH)]
    res = run_bass_kernel_spmd(nc, maps, core_ids=list(range(BATCH)))
    out = np.empty((BATCH, SEQ, D), np.float32)
    for b in range(BATCH):
        out[b] = res.results[b]["outT"].T
    return out
```

```python
import numpy as np
from contextlib import ExitStack, contextmanager
import concourse.bass as bass
import concourse.mybir as mybir

F32 = mybir.dt.float32
BF16 = mybir.dt.bfloat16
AF = mybir.ActivationFunctionType
ALU = mybir.AluOpType
AX = mybir.AxisListType


def _prod(xs):
    r = 1
    for x in xs:
        r *= int(x)
    return r


class Op:
    __slots__ = ("eng", "fn", "reads", "writes", "dma", "deps", "sig", "dsem", "dval", "dprev", "pe_mm")

    def __init__(self, eng, fn, reads, writes, dma, pe_mm=False):
        self.eng = eng
        self.fn = fn
        self.reads = reads
        self.writes = writes
        self.dma = dma
        self.deps = ()
        self.sig = None
        self.dsem = None
        self.dval = None
        self.dprev = None
        self.pe_mm = pe_mm


class Prog:
    NDSEM = 24

    def __init__(self, nc, same_engine_sync=None):
        self.nc = nc
        self.ops = []
        self.tinfo = {}
        self.hist = {}
        import os as _os
        if same_engine_sync is None:
            same_engine_sync = _os.environ.get("FW_SES", "1") == "1"
        self.same_engine_sync = same_engine_sync
        self.engs = {"pe": nc.tensor, "act": nc.scalar, "dve": nc.vector, "pool": nc.gpsimd, "sp": nc.sync}
        self._n = 0
        self.stk = None
        self.sname = ""
        self.esem = None
        self.tot = dict(n_ops=0, n_waits=0, n_dma=0)

    @contextmanager
    def stage(self, name):
        self.stk = ExitStack()
        self.sname = name + "_"
        self.ops = []
        try:
            yield self
            self.finalize(barrier=True)
        finally:
            self.stk.close()
            self.stk = None
            self.sname = ""
            self.ops = []

    def sbuf(self, name, shape, dt):
        name = self.sname + name
        if self.stk is not None:
            t = self.stk.enter_context(self.nc.sbuf_tensor(name, [int(s) for s in shape], dt))
        else:
            t = self.nc.alloc_sbuf_tensor(name, [int(s) for s in shape], dt)
        self.tinfo[name] = ("sb", _prod(shape[1:]))
        return t.ap()

    def psum(self, name, shape, dt=F32):
        name = self.sname + name
        if self.stk is not None:
            t = self.stk.enter_context(self.nc.psum_tensor(name, [int(s) for s in shape], dt))
        else:
            t = self.nc.alloc_psum_tensor(name, [int(s) for s in shape], dt)
        self.tinfo[name] = ("ps", _prod(shape[1:]))
        return t.ap()

    def dram(self, name, shape, dt, kind):
        t = self.nc.dram_tensor(name, [int(s) for s in shape], dt, kind=kind)
        self.tinfo[name] = ("const" if kind == "ExternalInput" else "dram", None)
        return t.ap()

    def rect(self, ap):
        name = ap.tensor.name
        kind, ps = self.tinfo[name]
        off = int(ap.offset)
        dims = ap.ap
        if kind in ("dram", "const"):
            hi = off + sum((c - 1) * abs(s) for s, c in dims) + 1
            return (name, 0, 1, off, hi)
        p0 = off // ps
        f0 = off % ps
        pc = dims[0][1]
        hi = f0 + sum((c - 1) * abs(s) for s, c in dims[1:]) + 1
        return (name, p0, p0 + pc, f0, hi)

    def add(self, eng, fn, reads=(), writes=(), dma=False, pe_mm=False):
        rr = []
        for a in reads:
            if a is None or isinstance(a, (int, float)):
                continue
            r = self.rect(a)
            if self.tinfo[r[0]][0] == "const":
                continue
            rr.append(r)
        ww = [self.rect(a) for a in writes]
        op = Op(eng, fn, rr, ww, dma, pe_mm)
        self.ops.append(op)
        return op

    def mm(self, out, lhsT, rhs, start=True, stop=True):
        self.add("pe", lambda e: e.matmul(out, lhsT, rhs, start=start, stop=stop),
                 reads=[lhsT, rhs], writes=[out], pe_mm=True)

    def transpose(self, out, in_, ident):
        self.add("pe", lambda e: e.transpose(out, in_, ident), reads=[in_, ident], writes=[out], pe_mm=True)

    def act(self, out, in_, func, bias=None, scale=None, accum_out=None):
        kw = {}
        if bias is not None:
            kw["bias"] = bias
        if scale is not None:
            kw["scale"] = scale
        if accum_out is not None:
            kw["accum_out"] = accum_out
        rd = [in_]
        if bias is not None and not isinstance(bias, (int, float)):
            rd.append(bias)
        if scale is not None and not isinstance(scale, (int, float)):
            rd.append(scale)
        wr = [out] + ([accum_out] if accum_out is not None else [])
        self.add("act", lambda e: e.activation(out, in_, func, **kw), reads=rd, writes=wr)

    def tt(self, eng, out, in0, in1, op):
        self.add(eng, lambda e: e.tensor_tensor(out, in0, in1, op), reads=[in0, in1], writes=[out])

    def ts(self, eng, out, in0, s1, s2, op0, op1=None):
        rd = [in0] + [s for s in (s1, s2) if s is not None and not isinstance(s, (int, float))]
        if op1 is None:
            self.add(eng, lambda e: e.tensor_scalar(out, in0, s1, None, op0), reads=rd, writes=[out])
        else:
            self.add(eng, lambda e: e.tensor_scalar(out, in0, s1, s2, op0, op1), reads=rd, writes=[out])

    def stt(self, eng, out, in0, scalar, in1, op0, op1):
        rd = [in0, in1] + ([scalar] if not isinstance(scalar, (int, float)) else [])
        self.add(eng, lambda e: e.scalar_tensor_tensor(out, in0, scalar, in1, op0, op1), reads=rd, writes=[out])

    def copy(self, eng, out, in_):
        if eng == "act":
            self.add(eng, lambda e: e.copy(out, in_), reads=[in_], writes=[out])
        else:
            self.add(eng, lambda e: e.tensor_copy(out, in_), reads=[in_], writes=[out])

    def recip(self, out, in_):
        self.add("dve", lambda e: e.reciprocal(out, in_), reads=[in_], writes=[out])

    def memset(self, eng, out, val):
        self.add(eng, lambda e: e.memset(out, val), reads=[], writes=[out])

    def dma(self, q, out, in_):
        self.add(q, lambda e: e.dma_start(out=out, in_=in_), reads=[in_], writes=[out], dma=True)

    @staticmethod
    def _ov(a, b):
        return a[1] < b[2] and b[1] < a[2] and a[3] < b[4] and b[3] < a[4]

    @staticmethod
    def _contains(a, b):
        return a[1] <= b[1] and b[2] <= a[2] and a[3] <= b[3] and b[4] <= a[4]

    def finalize(self, barrier=False):
        ops = self.ops
        hist = {}
        for i, op in enumerate(ops):
            deps = set()
            for r in op.reads:
                for seg in hist.get(r[0], ()):
                    if seg[1] is not None and self._ov(seg[0], r):
                        deps.add(seg[1])
            for w in op.writes:
                for seg in hist.get(w[0], ()):
                    if self._ov(seg[0], w):
                        if seg[1] is not None:
                            deps.add(seg[1])
                        deps.update(seg[2].values())
                        deps.update(seg[3])
            for r in op.reads:
                lst = hist.setdefault(r[0], [])
                found = None
                for seg in lst:
                    if seg[0] == r:
                        found = seg
                        break
                if found is None:
                    found = [r, None, {}, []]
                    lst.append(found)
                if op.dma:
                    found[3].append(i)
                else:
                    found[2][op.eng] = i
            for w in op.writes:
                lst = hist.setdefault(w[0], [])
                lst[:] = [seg for seg in lst if not self._contains(w, seg[0])]
                lst.append([w, i, {}, []])
            deps.discard(i)
            op.deps = sorted(deps)
        need = [False] * len(ops)
        for i, op in enumerate(ops):
            for j in op.deps:
                pj = ops[j]
                if pj.dma:
                    continue
                if pj.eng == op.eng and not op.dma:
                    if pj.eng == "pe" or not self.same_engine_sync:
                        continue
                need[j] = True
        if barrier:
            last = {}
            for i, op in enumerate(ops):
                if not op.dma:
                    last[op.eng] = i
            for i in last.values():
                need[i] = True
        nc = self.nc
        if self.esem is None:
            self.esem = {k: nc.alloc_semaphore(name=f"e_{k}") for k in self.engs}
            self.dsems = [nc.alloc_semaphore(name=f"d_{i}") for i in range(self.NDSEM)]
            self.ecount = {k: 0 for k in self.engs}
            self.dcount = [0] * self.NDSEM
            self.nd = 0
            self.known = {k: {} for k in self.engs}
        esem, dsems, ecount, dcount, known = self.esem, self.dsems, self.ecount, self.dcount, self.known
        for i, op in enumerate(ops):
            if op.dma:
                sidx = self.nd % self.NDSEM
                self.nd += 1
                op.dprev = dcount[sidx]
                dcount[sidx] += 16
                op.dsem = sidx
                op.dval = dcount[sidx]
            elif need[i]:
                ecount[op.eng] += 1
                op.sig = ecount[op.eng]
        nwaits = 0
        for i, op in enumerate(ops):
            e = self.engs[op.eng]
            kn = known[op.eng]
            waits = {}
            for j in op.deps:
                pj = ops[j]
                if pj.dma:
                    key = ("d", pj.dsem)
                    val = pj.dval
                else:
                    if pj.eng == op.eng and not op.dma:
                        if pj.eng == "pe" or not self.same_engine_sync:
                            continue
                    key = ("e", pj.eng)
                    val = pj.sig
                if kn.get(key, 0) >= val:
                    continue
                if waits.get(key, 0) < val:
                    waits[key] = val
            if op.dma and op.dprev > 0:
                key = ("d", op.dsem)
                if kn.get(key, 0) < op.dprev and waits.get(key, 0) < op.dprev:
                    waits[key] = op.dprev
            for key, val in waits.items():
                sem = dsems[key[1]] if key[0] == "d" else esem[key[1]]
                e.wait_ge(sem, val)
                kn[key] = val
                nwaits += 1
            ins = op.fn(e)
            if op.dma:
                ins.then_inc(dsems[op.dsem], 16)
            elif op.sig is not None:
                ins.then_inc(esem[op.eng], 1)
        targets = list(self.engs) if barrier else ["sp"]
        for k in targets:
            e = self.engs[k]
            kn = known[k]
            for sidx in range(self.NDSEM):
                if dcount[sidx] > kn.get(("d", sidx), 0):
                    e.wait_ge(dsems[sidx], dcount[sidx])
                    kn[("d", sidx)] = dcount[sidx]
                    nwaits += 1
            for k2 in ("pe", "act", "dve", "pool"):
                if ecount[k2] > kn.get(("e", k2), 0):
                    e.wait_ge(esem[k2], ecount[k2])
                    kn[("e", k2)] = ecount[k2]
                    nwaits += 1
        self.stats = dict(n_ops=len(ops), n_waits=nwaits, sigs=dict(ecount), n_dma=self.nd)
        self.tot["n_ops"] += len(ops)
        self.tot["n_waits"] += nwaits
        return self.stats


D = 1024
DFF = 2816
KT = 8
TN = 512
ALPHA = 4.0 ** 0.25
LN_EPS = 1e-5


import os as _os
PIPE_FFN = _os.environ.get("PIPE_FFN", "1") == "1"


class DenseCtx:
    pass


def load_w(P, name, w_dram, rows, cols, nsplit=1, q="pool"):
    kt = rows // 128
    w = P.sbuf(name, [128, kt, cols], BF16)
    step = cols // nsplit
    for s in range(nsplit):
        for k in range(kt):
            P.dma(q, w[:, k, s * step:(s + 1) * step], w_dram[k * 128:(k + 1) * 128, s * step:(s + 1) * step])
    return w


def emit_ln(P, C, g, b):
    x32, xb = C.x32, C.xb
    pm = C.pstat[0]
    for m in range(KT):
        P.mm(pm, C.onesF, x32[:, m, :], start=(m == 0), stop=(m == KT - 1))
    for m in range(KT):
        P.tt("dve", x32[:, m, :], x32[:, m, :], pm, ALU.subtract)
    pv = C.pstat[1]
    for m in range(KT):
        sq = C.sq[m % 2]
        P.act(sq, x32[:, m, :], AF.Square)
        P.mm(pv, C.onesF, sq, start=(m == 0), stop=(m == KT - 1))
    P.act(C.rstd, pv, AF.Sqrt, bias=C.epsc[:, 0:1])
    P.recip(C.rstd, C.rstd)
    for m in range(KT):
        P.tt("dve", x32[:, m, :], x32[:, m, :], C.rstd, ALU.mult)
        P.act(x32[:, m, :], x32[:, m, :], AF.Identity, bias=b[:, m:m + 1], scale=g[:, m:m + 1])
        P.act(xb[:, m, :], x32[:, m, :], AF.Copy)


def gen_ln(P, C, x32, xb, g, b, write_xb=True):
    pm = C.pstat[0]
    for m in range(KT):
        P.mm(pm, C.onesF, x32[:, m, :], start=(m == 0), stop=(m == KT - 1))
    yield
    for m in range(KT):
        P.tt("dve", x32[:, m, :], x32[:, m, :], pm, ALU.subtract)
        if m % 2 == 1:
            yield
    pv = C.pstat[1]
    for m in range(KT):
        sq = C.sq[m % 2]
        P.act(sq, x32[:, m, :], AF.Square)
        yield
        P.mm(pv, C.onesF, sq, start=(m == 0), stop=(m == KT - 1))
    yield
    P.act(C.rstd, pv, AF.Sqrt, bias=C.epsc[:, 0:1])
    yield
    P.recip(C.rstd, C.rstd)
    yield
    for m in range(KT):
        P.tt("dve", x32[:, m, :], x32[:, m, :], C.rstd, ALU.mult)
        yield
        P.act(x32[:, m, :], x32[:, m, :], AF.Identity, bias=b[:, m:m + 1], scale=g[:, m:m + 1])
        if write_xb:
            P.act(xb[:, m, :], x32[:, m, :], AF.Copy)
    yield


def gen_ffn_up(P, C, w1, xb):
    hT = C.hT
    NC = DFF // 128
    for c in range(NC):
        pg = C.pg[c % 2]
        pu = C.pu[c % 2]
        for k in range(KT):
            P.mm(pg, w1[:, k, c * 128:(c + 1) * 128], xb[:, k, :], start=(k == 0), stop=(k == KT - 1))
        for k in range(KT):
            P.mm(pu, w1[:, k, DFF + c * 128:DFF + (c + 1) * 128], xb[:, k, :], start=(k == 0), stop=(k == KT - 1))
        sg = C.sg[c % 2]
        P.act(sg, pg, AF.Silu)
        P.stt("dve", hT[:, c, :], sg, 0.5, pu, ALU.mult, ALU.mult)
        yield


def emit_ffn_down(P, C, w2, x32):
    hT = C.hT
    NC = DFF // 128
    for m in range(KT):
        py = C.py[m % 2]
        for c in range(NC):
            P.mm(py, w2[:, c, m * 128:(m + 1) * 128], hT[:, c, :], start=(c == 0), stop=(c == NC - 1))
        P.stt("dve", x32[:, m, :], x32[:, m, :], ALPHA, py, ALU.mult, ALU.add)


def run_rr(tasks):
    tasks = list(tasks)
    while tasks:
        for gt in list(tasks):
            try:
                next(gt)
            except StopIteration:
                tasks.remove(gt)


def emit_ffn_stage_pipelined(P, C, nt, x32s, xbs, w1, w2, g, b, load_tile, store_tile):
    load_tile(0)
    for m in range(KT):
        P.act(xbs[0][:, m, :], x32s[0][:, m, :], AF.Copy)
    if nt > 1:
        load_tile(1)
    run_rr([gen_ffn_up(P, C, w1, xbs[0])])
    for t in range(nt):
        emit_ffn_down(P, C, w2, x32s[t % 2])
        tasks = [gen_ln(P, C, x32s[t % 2], None, g, b, write_xb=False)]
        if t + 1 < nt:
            nb = (t + 1) % 2
            for m in range(KT):
                P.act(xbs[nb][:, m, :], x32s[nb][:, m, :], AF.Copy)
            tasks.append(gen_ffn_up(P, C, w1, xbs[nb]))
        run_rr(tasks)
        store_tile(t)
        if t + 2 < nt:
            load_tile(t + 2)


def emit_ffn(P, C, w1, w2, g, b):
    x32, xb, hT = C.x32, C.xb, C.hT
    NC = DFF // 128
    for c in range(NC):
        pg = C.pg[c % 2]
        pu = C.pu[c % 2]
        for k in range(KT):
            P.mm(pg, w1[:, k, c * 128:(c + 1) * 128], xb[:, k, :], start=(k == 0), stop=(k == KT - 1))
        for k in range(KT):
            P.mm(pu, w1[:, k, DFF + c * 128:DFF + (c + 1) * 128], xb[:, k, :], start=(k == 0), stop=(k == KT - 1))
        sg = C.sg[c % 2]
        P.act(sg, pg, AF.Silu)
        P.stt("dve", hT[:, c, :], sg, 0.5, pu, ALU.mult, ALU.mult)
    for m in range(KT):
        py = C.py[m % 2]
        for c in range(NC):
            P.mm(py, w2[:, c, m * 128:(m + 1) * 128], hT[:, c, :], start=(c == 0), stop=(c == NC - 1))
        P.stt("dve", x32[:, m, :], x32[:, m, :], ALPHA, py, ALU.mult, ALU.add)
    emit_ln(P, C, g, b)


def emit_mixout(P, C, mixb, wo, g, b):
    x32 = C.x32
    for m in range(KT):
        py = C.py[m % 2]
        for k in range(KT):
            P.mm(py, wo[:, k, m * 128:(m + 1) * 128], mixb[:, k, :], start=(k == 0), stop=(k == KT - 1))
        P.stt("dve", x32[:, m, :], x32[:, m, :], ALPHA, py, ALU.mult, ALU.add)
    emit_ln(P, C, g, b)


def emit_ple(P, C, pb, wg, bg, wp, g, b):
    x32, xb = C.x32, C.xb
    for m in range(KT):
        pgt = C.pg[m % 2]
        ppj = C.pu[m % 2]
        for k in range(KT):
            P.mm(pgt, wg[:, k, m * 128:(m + 1) * 128], xb[:, k, :], start=(k == 0), stop=(k == KT - 1))
        for k in range(2):
            P.mm(ppj, wp[:, k, m * 128:(m + 1) * 128], pb[:, k, :], start=(k == 0), stop=(k == 1))
        sg = C.sg[m % 2]
        P.act(sg, pgt, AF.Sigmoid, bias=bg[:, m:m + 1])
        P.tt("dve", sg, sg, ppj, ALU.mult)
        P.stt("dve", x32[:, m, :], x32[:, m, :], ALPHA, sg, ALU.mult, ALU.add)
    emit_ln(P, C, g, b)


def emit_dense(P, steps, ntok, xT, outT, lng_d, lnb_d, wd):
    nt = ntok // TN
    nln = len(steps)
    C = DenseCtx()
    C.x32 = P.sbuf("x32", [128, KT, TN], F32)
    C.xb = P.sbuf("xb", [128, KT, TN], BF16)
    C.onesF = P.sbuf("onesF", [128, 128], F32)
    C.sq = [P.sbuf(f"sq{i}", [128, TN], F32) for i in range(2)]
    C.sg = [P.sbuf(f"sg{i}", [128, TN], F32) for i in range(2)]
    C.rstd = P.sbuf("rstd", [128, TN], F32)
    C.pg = [P.psum(f"pg{i}", [128, TN]) for i in range(2)]
    C.pu = [P.psum(f"pu{i}", [128, TN]) for i in range(2)]
    C.py = [P.psum(f"py{i}", [128, TN]) for i in range(2)]
    C.pstat = [P.psum(f"pst{i}", [128, TN]) for i in range(2)]
    lng = P.sbuf("lng_s", [128, nln * KT], F32)
    lnb = P.sbuf("lnb_s", [128, nln * KT], F32)
    P.dma("sp", lng, lng_d)
    P.dma("sp", lnb, lnb_d)
    P.memset("dve", C.onesF, 1.0 / D)
    C.epsc = P.sbuf("epsc", [128, 1], F32)
    P.memset("dve", C.epsc, LN_EPS)
    ws = []
    need_hT = False
    for i, s in enumerate(steps):
        if s == "ffn":
            need_hT = True
            w1 = load_w(P, f"w1s_{i}", wd[i][0], D, 2 * DFF, nsplit=4)
            w2 = load_w(P, f"w2s_{i}", wd[i][1], DFF, D)
            ws.append((w1, w2))
        elif s == "mixout":
            wo = load_w(P, f"wos_{i}", wd[i][0], D, D)
            mixb = P.sbuf(f"mixb_{i}", [128, KT, TN], BF16)
            ws.append((wo, mixb))
        elif s == "ple":
            wg = load_w(P, f"wgs_{i}", wd[i][0], D, D)
            wp = load_w(P, f"wps_{i}", wd[i][1], 256, D)
            bg = P.sbuf(f"bgs_{i}", [128, KT], F32)
            P.dma("sp", bg, wd[i][2])
            pb = P.sbuf(f"pb_{i}", [128, 2, TN], BF16)
            ws.append((wg, wp, bg, pb))
    if need_hT:
        C.hT = P.sbuf("hT", [128, DFF // 128, TN], BF16)
    xTr = xT.rearrange("(k p) n -> p k n", p=128)
    oTr = outT.rearrange("(k p) n -> p k n", p=128)
    x32s = [C.x32, P.sbuf("x32b", [128, KT, TN], F32)]
    mixbs = {}
    pbs = {}
    for i, s in enumerate(steps):
        if s == "mixout":
            mixbs[i] = [ws[i][1], P.sbuf(f"mixb2_{i}", [128, KT, TN], BF16)]
        elif s == "ple":
            pbs[i] = [ws[i][3], P.sbuf(f"pb2_{i}", [128, 2, TN], BF16)]

    def load_tile(t):
        tsl = slice(t * TN, (t + 1) * TN)
        P.dma("sp", x32s[t % 2], xTr[:, :, tsl])
        for i, s in enumerate(steps):
            if s == "mixout":
                P.dma("sp", mixbs[i][t % 2], wd[i][1].rearrange("(k p) n -> p k n", p=128)[:, :, tsl])
            elif s == "ple":
                pTr = wd[i][3].rearrange("(k p) n -> p k n", p=128)
                for k in range(2):
                    P.dma("pool", pbs[i][t % 2][:, k, :], pTr[:, k, tsl])

    if steps == ["ffn"] and PIPE_FFN:
        xbs = [C.xb, C.xb]

        def store_tile(t):
            P.dma("pool", oTr[:, :, t * TN:(t + 1) * TN], x32s[t % 2])
        emit_ffn_stage_pipelined(P, C, nt, x32s, xbs, ws[0][0], ws[0][1], lng[:, 0:KT], lnb[:, 0:KT], load_tile, store_tile)
        return
    load_tile(0)
    for t in range(nt):
        tsl = slice(t * TN, (t + 1) * TN)
        C.x32 = x32s[t % 2]
        if steps[0] != "mixout":
            for m in range(KT):
                P.act(C.xb[:, m, :], C.x32[:, m, :], AF.Copy)
        if t + 1 < nt:
            load_tile(t + 1)
        for i, s in enumerate(steps):
            g = lng[:, i * KT:(i + 1) * KT]
            b = lnb[:, i * KT:(i + 1) * KT]
            if s == "ffn":
                emit_ffn(P, C, ws[i][0], ws[i][1], g, b)
            elif s == "mixout":
                emit_mixout(P, C, mixbs[i][t % 2], ws[i][0], g, b)
            elif s == "ple":
                emit_ple(P, C, pbs[i][t % 2], ws[i][0], ws[i][2], ws[i][1], g, b)
        P.dma("pool", oTr[:, :, tsl], C.x32)


def build_dense(steps, ntok, nc=None):
    if nc is None:
        nc = bass.Bass("TRN2", target_bir_lowering=False)
    P = Prog(nc)
    xT = P.dram("xT", [D, ntok], F32, "ExternalInput")
    outT = P.dram("outT", [D, ntok], F32, "ExternalOutput")
    nln = len(steps)
    lng_d = P.dram("lng", [128, nln * KT], F32, "ExternalInput")
    lnb_d = P.dram("lnb", [128, nln * KT], F32, "ExternalInput")
    wd = []
    for i, s in enumerate(steps):
        if s == "ffn":
            wd.append((P.dram(f"w1_{i}", [D, 2 * DFF], F32, "ExternalInput"),
                       P.dram(f"w2_{i}", [DFF, D], F32, "ExternalInput")))
        elif s == "mixout":
            wd.append((P.dram(f"wo_{i}", [D, D], F32, "ExternalInput"),
                       P.dram(f"mixT_{i}", [D, ntok], BF16, "ExternalInput")))
        elif s == "ple":
            wd.append((P.dram(f"wg_{i}", [D, D], F32, "ExternalInput"),
                       P.dram(f"wp_{i}", [256, D], F32, "ExternalInput"),
                       P.dram(f"bg_{i}", [128, KT], F32, "ExternalInput"),
                       P.dram(f"pT_{i}", [256, ntok], F32, "ExternalInput")))
    emit_dense(P, steps, ntok, xT, outT, lng_d, lnb_d, wd)
    st = P.finalize()
    return nc, st

import os
POOL = os.environ.get("MIX_POOL", "pool")
LVL = int(os.environ.get("MIX_LVL", "9"))
ACT_PSUM_R = os.environ.get("MIX_ACT_PSUM_R", "1") == "1"
NOCARRY = os.environ.get("MIX_NOCARRY", "0") == "1"

D = 1024
KT = 8
TN = 512
NW = 1796
C_SQ, C_SK, C_SV = 0, 128, 256
C_GQ, C_GK, C_GV, C_GZ = 384, 640, 896, 1152
C_AB = 1408
C_SCB, C_SCC, C_SCH = 1412, 1540, 1668
NORM_EPS = 1e-6


def mix_consts():
    p = np.arange(128)[:, None]
    f = np.arange(128)[None, :]
    c = {}
    c["ident"] = (p == f).astype(np.float32)
    c["triU"] = (p <= f).astype(np.float32)
    c["triNeg"] = -(p >= f).astype(np.float32)
    c["SL"] = (p > f).astype(np.float32)
    c["SU"] = (p < f).astype(np.float32)
    c["UI"] = (p <= f).astype(np.float32)
    m = np.zeros((128, 4, 512), np.float32)
    tq = np.arange(512)[None, :]
    for i in range(4):
        m[:, i, :] = ((128 * i + p) < tq).astype(np.float32)
    c["mask"] = m.reshape(128, 2048)
    order = ["ident", "triU", "triNeg", "SL", "SU", "UI", "mask"]
    arr = np.concatenate([c[k] for k in order], axis=1)
    offs = {}
    o = 0
    for k in order:
        offs[k] = (o, o + c[k].shape[1])
        o += c[k].shape[1]
    return arr, offs


class Ring:
    def __init__(self, items):
        self.items = items
        self.i = 0

    def __call__(self):
        x = self.items[self.i % len(self.items)]
        self.i += 1
        return x


def emit_mix(P, S, hT, wm_d, cst_d, gcw_d, scw_d, hp_d, nw_d, o_sb, o_gdnT, o_sc,
             do_attn=True, do_gdn=True, do_sc=True):
    NT = S // TN
    NKB = S // 128
    carr, coff = mix_consts()

    cst = P.sbuf("cst_s", list(carr.shape), F32)
    P.dma("sp", cst, cst_d)

    def cs(k):
        a, b = coff[k]
        return cst[:, a:b]
    identF, triU, SL, SU, UI = cs("ident"), cs("triU"), cs("SL"), cs("SU"), cs("UI")
    maskF = cs("mask")
    identB = P.sbuf("identB", [128, 128], BF16)
    triNegB = P.sbuf("triNegB", [128, 128], BF16)
    P.copy("dve", identB, identF)
    P.copy("dve", triNegB, cs("triNeg"))
    ones1 = P.sbuf("ones1", [128, 128], F32)
    P.memset("dve", ones1, 1.0)
    onesRowB = P.sbuf("onesRowB", [1, 128], BF16)
    P.memset("dve", onesRowB, 1.0)
    onec = P.sbuf("onec", [128, 1], F32)
    P.memset("dve", onec, 1.0)
    epsc = P.sbuf("epsc", [128, 1], F32)
    P.memset("dve", epsc, NORM_EPS)
    gcw = P.sbuf("gcw_s", [128, 24], F32)
    scw = P.sbuf("scw_s", [128, 3], F32)
    hp = P.sbuf("hp_s", [128, 4], F32)
    nwb = P.sbuf("nw_s", [128, 128], F32)
    P.dma("sp", gcw, gcw_d)
    P.dma("sp", scw, scw_d)
    P.dma("sp", hp, hp_d)
    P.dma("sp", nwb, nw_d)
    nA = P.sbuf("nA", [128, 2], F32)
    P.act(nA, hp[:, 0:2], AF.Exp)
    P.ts("dve", nA, nA, -1.0, None, ALU.mult)
    dtb = hp[:, 2:4]
    wm = P.sbuf("wm_s", [128, KT, NW], BF16)
    for k in range(KT):
        P.dma("pool", wm[:, k, :], wm_d[k * 128:(k + 1) * 128, :])

    hb = P.sbuf("hb", [128, KT, TN], BF16)
    qT = P.sbuf("qT", [128, S], BF16)
    kT = P.sbuf("kT", [128, S], BF16)
    vA = P.sbuf("vA", [128, NKB, 128], BF16)
    pf = P.psum("pf", [128, 6, 512], F32)
    pbf = P.psum("pbf", [128, 2, 1024], BF16)
    bank = Ring([pf[:, i, :] for i in range(0, 4)])
    bank_o = Ring([pf[:, 4, :], pf[:, 5, :]])
    gbank = Ring([pf[:, i, :] for i in range(6)])
    tb2 = Ring([pbf[:, i, 0:128] for i in range(2)])
    quart = lambda: bank()[:, 0:128]
    tbank = Ring([pbf[:, i, 0:128] for i in range(2)])
    tbs = [pbf[:, i, 0:128] for i in range(2)]
    hqs = [Ring([pf[:, 2 * h, 0:128], pf[:, 2 * h + 1, 0:128]]) for h in range(2)]
    sq = Ring([pf[:, 4, :], pf[:, 5, :]])
    eb = Ring([P.sbuf(f"e{i}", [128, TN], F32) for i in range(8)])
    Lb = Ring([P.sbuf(f"L{i}", [128, TN], BF16) for i in range(6)])
    eRb = Ring([P.sbuf(f"eR{i}", [128, TN], F32) for i in range(2)])
    attb = Ring([P.sbuf(f"att{i}", [128, TN], BF16) for i in range(7)])
    lsumb = [Ring([P.sbuf(f"ls{h}_{i}", [128, TN], BF16) for i in range(4)]) for h in range(2)]
    negOnesB = P.sbuf("negOnesB", [128, 128], BF16)
    P.memset("dve", negOnesB, -1.0)
    osb = [P.sbuf(f"osb{i}", [64, TN], BF16) for i in range(2)]
    raw = [P.sbuf(f"raw{f}", [128, 3 + TN], F32) for f in range(6)]
    ycv = P.sbuf("ycv", [128, TN], F32)
    ysl = ycv
    sqb = P.sbuf("sqb", [128, TN], F32)
    rnb = sqb
    gqT = [P.sbuf(f"gqT{h}", [128, TN], BF16) for h in range(2)]
    gkT = [P.sbuf(f"gkT{h}", [128, TN], BF16) for h in range(2)]
    gvT = [P.sbuf(f"gvT{h}", [128, TN], BF16) for h in range(2)]
    szbs = [P.sbuf(f"szb{i}", [128, 256], F32) for i in range(3)]
    sc5s = [{n: P.sbuf(f"sc{i}_{n}", [128, 2], F32) for n in
             ("beta", "nbeta", "g", "gc", "gtot", "egl", "eg", "bg", "kd", "tmp")} for i in range(3)]
    ab_sbs = [P.sbuf(f"ab_sb{i}", [128, 4], F32) for i in range(3)]
    ogs = [P.sbuf(f"og{i}", [128, 256], BF16) for i in range(2)]
    ogT = P.sbuf("ogT", [128, 2, TN], BF16)

    def t128(name, dt=F32):
        return P.sbuf(name, [128, 128], dt)
    hbuf = []
    for h in range(2):
        B = {}
        for n in ("gU", "E1", "Dall", "DL", "DUb", "DUI", "dB", "egRow", "D1s", "BRs", "KKs", "kbg", "vb", "u_sb", "junk", "onb", "o_s"):
            B[n] = t128(f"h{h}_{n}")
        for n in ("kdec", "wTs", "qkTs", "qdT", "vnew"):
            B[n] = t128(f"h{h}_{n}", BF16)
        B["Nb"] = Ring([t128(f"h{h}_Nb{i}") for i in range(3)])
        B["NTb"] = Ring([t128(f"h{h}_NTb{i}") for i in range(3)])
        B["XTb"] = Ring([t128(f"h{h}_XTb{i}") for i in range(3)])
        B["ss1"] = P.sbuf(f"h{h}_ss1", [128, 1], F32)
        B["rs1"] = P.sbuf(f"h{h}_rs1", [128, 1], F32)
        hbuf.append(B)
    Sst = [t128(f"S{h}") for h in range(2)]
    Sbf = [t128(f"Sb{h}", BF16) for h in range(2)]
    hand = [[], []]
    for h in range(2):
        hand[0].append({n: hbuf[h][n] for n in ("u_sb", "kdec", "wTs", "qkTs", "qdT")})
        eR_t = eRb.items[0]
        L_t = Lb.items[h]
        hand[1].append({"u_sb": eR_t[:, h * 128:(h + 1) * 128],
                        "kdec": L_t[:, 0:128], "wTs": L_t[:, 128:256], "qkTs": L_t[:, 256:384], "qdT": L_t[:, 384:512]})
    for h in range(2):
        P.memset("dve", Sst[h], 0.0)
        P.memset("dve", Sbf[h], 0.0)
    for f in range(6):
        P.memset("dve", raw[f][:, 0:3], 0.0)
    rawc = P.sbuf("rawc", [128, 2 + TN], F32)
    P.memset("dve", rawc[:, 0:2], 0.0)
    scB = P.sbuf("scB", [128, TN], F32)
    scC = P.sbuf("scC", [128, TN], F32)
    scy = scC
    sco = P.sbuf("sco", [128, TN], BF16)

    hTr = hT.rearrange("(k p) n -> p k n", p=128)

    def proj_fm(col0, ncol=128):
        pb = bank()
        for k in range(KT):
            P.mm(pb[0:ncol, :], wm[:, k, col0:col0 + ncol], hb[:, k, :], start=(k == 0), stop=(k == KT - 1))
        return pb

    for t in range(NT):
        tsl = slice(t * TN, (t + 1) * TN)
        for k in range(KT):
            P.dma("pool", hb[:, k, :], hTr[:, k, tsl])
        if do_attn:
            pq = proj_fm(C_SQ)
            P.ts("dve", qT[:, tsl], pq, 0.125, None, ALU.mult)
            if not os.environ.get("MIX_NOK"):
                pk = proj_fm(C_SK)
                P.act(kT[:, tsl], pk, AF.Copy)
            for s in range(0 if os.environ.get("MIX_NOV") else 4):
                pv = quart()
                for k in range(KT):
                    P.mm(pv, hb[:, k, s * 128:(s + 1) * 128], wm[:, k, C_SV:C_SV + 128], start=(k == 0), stop=(k == KT - 1))
                P.copy("dve", vA[:, 4 * t + s, :], pv)
            nkb = 4 * t + 4
            items = [(hd, kb) for kb in range(nkb - 1, -1, -1) for hd in range(2)]
            po_h = [bank_o(), bank_o()]
            st1 = {}
            st2 = {}
            lsum_cur = [None, None]

            st0 = {}

            def att_s0(hd, kb):
                ps = slice(64 * hd, 64 * hd + 64)
                pz = bank()
                P.mm(pz, kT[ps, kb * 128:(kb + 1) * 128], qT[ps, tsl])
                e = eb()
                P.act(e, pz, AF.Exp)
                st0[(hd, kb)] = e

            def att_s1(hd, kb):
                e = st0.pop((hd, kb))
                if kb >= 4 * t:
                    i = kb - 4 * t
                    P.tt(POOL, e, e, maskF[:, i * 512:(i + 1) * 512], ALU.mult)
                L = Lb()
                P.act(L, e, AF.Ln, bias=onec[:, 0:1])
                carry = lsum_cur[hd]
                if kb > 0:
                    ns = lsumb[hd]()
                    if carry is None:
                        P.copy("dve", ns, L)
                    else:
                        P.tt("dve", ns, carry, L, ALU.add)
                    lsum_cur[hd] = ns
                st1[(hd, kb)] = (e, L, carry)

            def att_s2(hd, kb):
                e, L, carry = st1.pop((hd, kb))
                pr = bank()
                P.mm(pr, triNegB, L, start=True, stop=(carry is None))
                if carry is not None:
                    P.mm(pr, negOnesB, carry, start=False, stop=True)
                eR = eRb()
                P.act(eR, pr, AF.Exp)
                att = attb()
                P.tt(os.environ.get("MIX_ATTENG", "dve"), att, e, eR, ALU.mult)
                st2[(hd, kb)] = att

            def att_s3(hd, kb):
                att = st2.pop((hd, kb))
                P.mm(po_h[hd][0:64, :], vA[:, kb, 64 * hd:64 * hd + 64], att, start=(kb == nkb - 1), stop=(kb == 0))

            LOOK = int(os.environ.get("MIX_LOOK", "3"))
            SK = int(os.environ.get("MIX_SK", "2"))
            n_it = len(items)
            for idx in range(n_it + SK + 2 * LOOK):
                if idx < n_it:
                    att_s0(*items[idx])
                if SK <= idx < n_it + SK:
                    att_s1(*items[idx - SK])
                if SK + LOOK <= idx < n_it + SK + LOOK:
                    att_s2(*items[idx - SK - LOOK])
                if idx >= SK + 2 * LOOK:
                    att_s3(*items[idx - SK - 2 * LOOK])
            for hd in range(2):
                P.act(osb[hd], po_h[hd][0:64, :], AF.Copy)
                P.dma("sp", o_sb[64 * hd:64 * hd + 64, tsl], osb[hd])
        if do_sc:
            pB = proj_fm(C_SCB)
            P.act(scB, pB, AF.Copy)
            pC = proj_fm(C_SCC)
            P.act(scC, pC, AF.Copy)
            pH = proj_fm(C_SCH)
            if t > 0:
                P.copy("dve", rawc[:, 0:2], rawc[:, TN:TN + 2])
            P.tt("dve", rawc[:, 2:2 + TN], scC, pH, ALU.mult)
            P.ts("dve", scy, rawc[:, 0:TN], scw[:, 0:1], None, ALU.mult)
            for i in (1, 2):
                P.stt("dve", scy, rawc[:, i:i + TN], scw[:, i:i + 1], scy, ALU.mult, ALU.add)
            P.tt("dve", sco, scB, scy, ALU.mult)
            P.dma("sp", o_sc[:, tsl], sco)
        if do_gdn:
            for f in range(6):
                col0 = C_GQ + f * 128
                pg = proj_fm(col0)
                if t > 0:
                    P.copy("dve", raw[f][:, 0:3], raw[f][:, TN:TN + 3])
                P.act(raw[f][:, 3:3 + TN], pg, AF.Copy)
                P.ts("dve", ycv, raw[f][:, 0:TN], gcw[:, f * 4:f * 4 + 1], None, ALU.mult)
                for i in (1, 2, 3):
                    P.stt("dve", ycv, raw[f][:, i:i + TN], gcw[:, f * 4 + i:f * 4 + i + 1], ycv, ALU.mult, ALU.add)
                h_ = f % 2
                if f >= 4:
                    P.act(gvT[h_], ycv, AF.Silu)
                    continue
                P.act(ysl, ycv, AF.Silu)
                P.act(sqb, ysl, AF.Square)
                pss = bank()
                P.mm(pss, ones1, sqb)
                P.act(rnb, pss, AF.Sqrt, bias=epsc[:, 0:1])
                P.recip(rnb, rnb)
                if f < 2:
                    P.stt("dve", gqT[h_], ysl, 128.0 ** -0.5, rnb, ALU.mult, ALU.mult)
                else:
                    P.tt("dve", gkT[h_], ysl, rnb, ALU.mult)
            gb = gbank
            def scal_task(c):
                csl = slice(c * 128, (c + 1) * 128)
                S5 = sc5s[c % 3]
                szb = szbs[c % 3]
                ab_sb = ab_sbs[c % 3]
                pzz = gb()
                for k in range(KT):
                    P.mm(pzz[:, 0:256], hb[:, k, csl], wm[:, k, C_GZ:C_GZ + 256], start=(k == 0), stop=(k == KT - 1))
                P.act(szb, pzz[:, 0:256], AF.Silu)
                yield
                pab = gb()
                for k in range(KT):
                    P.mm(pab[:, 0:4], hb[:, k, csl], wm[:, k, C_AB:C_AB + 4], start=(k == 0), stop=(k == KT - 1))
                P.copy("dve", ab_sb, pab[:, 0:4])
                yield
                P.act(S5["beta"], ab_sb[:, 2:4], AF.Sigmoid)
                P.tt("dve", S5["tmp"], ab_sb[:, 0:2], dtb, ALU.add)
                yield
                P.ts("dve", S5["nbeta"], S5["beta"], -1.0, None, ALU.mult)
                P.act(S5["tmp"], S5["tmp"], AF.Exp)
                yield
                P.act(S5["tmp"], S5["tmp"], AF.Ln, bias=onec[:, 0:1])
                yield
                P.tt("dve", S5["g"], S5["tmp"], nA, ALU.mult)
                yield
                pgc = gb()
                P.mm(pgc[:, 0:2], triU, S5["g"])
                P.copy("dve", S5["gc"], pgc[:, 0:2])
                pgt = gb()
                P.mm(pgt[:, 0:2], ones1, S5["g"])
                P.copy("dve", S5["gtot"], pgt[:, 0:2])
                yield
                P.act(S5["egl"], S5["gtot"], AF.Exp)
                P.act(S5["eg"], S5["gc"], AF.Exp)
                P.tt("dve", S5["kd"], S5["gtot"], S5["gc"], ALU.subtract)
                yield
                P.tt("dve", S5["bg"], S5["beta"], S5["eg"], ALU.mult)
                P.act(S5["kd"], S5["kd"], AF.Exp)
                yield

            def prep_task(c, h_):
                csl = slice(c * 128, (c + 1) * 128)
                S5 = sc5s[c % 3]
                B = hbuf[h_]
                HO = hand[c % 2][h_]
                hs = slice(h_, h_ + 1)
                kTc = gkT[h_][:, csl]
                qTc = gqT[h_][:, csl]
                P.ts("dve", B["gU"], triU, S5["g"][:, hs], None, ALU.mult)
                P.ts("dve", B["dB"], identF, S5["beta"][:, hs], None, ALU.mult)
                yield
                pD1 = gb()[:, 0:128]
                P.mm(pD1, ones1, B["gU"])
                P.copy("dve", B["D1s"], pD1)
                yield
                pBR = gb()[:, 0:128]
                P.mm(pBR, ones1, B["dB"])
                P.copy("dve", B["BRs"], pBR)
                yield
                pKK = gb()[:, 0:128]
                P.mm(pKK, kTc, kTc)
                P.copy("dve", B["KKs"], pKK)
                yield
                P.act(B["egRow"], B["D1s"], AF.Exp)
                P.ts("dve", B["E1"], B["D1s"], S5["gc"][:, hs], None, ALU.subtract)
                yield
                P.act(B["E1"], B["E1"], AF.Abs)
                yield
                P.act(B["Dall"], B["E1"], AF.Exp, scale=-1.0)
                yield
                P.tt("dve", B["DL"], B["Dall"], SL, ALU.mult)
                P.tt("pool", B["DUb"], B["Dall"], SU, ALU.mult)
                P.tt("pool", B["DUI"], B["Dall"], UI, ALU.mult)
                yield
                P.tt("dve", B["DUb"], B["DUb"], B["BRs"], ALU.mult)
                N = B["Nb"]()
                NT_ = B["NTb"]()
                XT = B["XTb"]()
                P.stt("dve", N, B["KKs"], S5["nbeta"][:, hs], B["DL"], ALU.mult, ALU.mult)
                yield
                P.stt("dve", NT_, B["KKs"], -1.0, B["DUb"], ALU.mult, ALU.mult)
                yield
                P.tt("pool", XT, identF, NT_, ALU.add)
                ptk = tb2()
                P.transpose(ptk, kTc, identB)
                P.ts("dve", B["kbg"], ptk, S5["bg"][:, hs], None, ALU.mult)
                P.ts("dve", HO["kdec"], ptk, S5["kd"][:, hs], None, ALU.mult)
                yield
                ptv = tb2()
                P.transpose(ptv, gvT[h_][:, csl], identB)
                P.ts("dve", B["vb"], ptv, S5["beta"][:, hs], None, ALU.mult)
                yield
                pqk = gb()[:, 0:128]
                P.mm(pqk, kTc, qTc)
                P.tt("dve", HO["qkTs"], pqk, B["DUI"], ALU.mult)
                P.tt("pool", HO["qdT"], qTc, B["egRow"], ALU.mult)
                yield
                for kk in range(1, 7):
                    pN = gb()[:, 0:128]
                    P.mm(pN, NT_, N)
                    N2 = B["Nb"]()
                    P.copy("act", N2, pN)
                    yield
                    if kk < 6:
                        pNT = gb()[:, 0:128]
                        P.mm(pNT, N, NT_)
                        NT2 = B["NTb"]()
                        P.copy("dve", NT2, pNT)
                        yield
                    pX = gb()[:, 0:128]
                    P.mm(pX, N2, XT)
                    XT2 = B["XTb"]()
                    P.tt("dve", XT2, XT, pX, ALU.add)
                    N, XT = N2, XT2
                    if kk < 6:
                        NT_ = NT2
                    yield
                pu = gb()[:, 0:128]
                P.mm(pu, XT, B["vb"])
                P.copy("act", HO["u_sb"], pu)
                yield
                pw = gb()[:, 0:128]
                P.mm(pw, B["kbg"], XT)
                P.copy("act", HO["wTs"], pw)
                yield

            def scan_task(c, h_):
                S5 = sc5s[c % 3]
                szb = szbs[c % 3]
                og = ogs[c % 2]
                B = hbuf[h_]
                HO = hand[c % 2][h_]
                hs = slice(h_, h_ + 1)
                p1 = gb()[:, 0:128]
                P.mm(p1, HO["wTs"], Sbf[h_])
                P.tt("dve", B["vnew"], HO["u_sb"], p1, ALU.subtract)
                yield
                p2 = gb()[:, 0:128]
                P.mm(p2, HO["qdT"], Sbf[h_], start=True, stop=False)
                P.mm(p2, HO["qkTs"], B["vnew"], start=False, stop=True)
                P.copy("dve", B["o_s"], p2)
                p3 = gb()[:, 0:128]
                P.mm(p3, HO["kdec"], B["vnew"])
                P.stt("dve", Sst[h_], Sst[h_], S5["egl"][:, hs], p3, ALU.mult, ALU.add)
                yield
                P.copy("act", Sbf[h_], Sst[h_])
                P.act(B["junk"], B["o_s"], AF.Square, accum_out=B["ss1"])
                yield
                P.act(B["rs1"], B["ss1"], AF.Sqrt, scale=1.0 / 128.0, bias=epsc[:, 0:1])
                yield
                P.recip(B["rs1"], B["rs1"])
                yield
                P.stt("dve", B["onb"], B["o_s"], B["rs1"][:, 0:1], nwb, ALU.mult, ALU.mult)
                yield
                P.tt("dve", og[:, h_ * 128:(h_ + 1) * 128], B["onb"], szb[:, h_ * 128:(h_ + 1) * 128], ALU.mult)
                yield

            def run_tasks(tasks):
                tasks = list(tasks)
                while tasks:
                    for g in list(tasks):
                        try:
                            next(g)
                        except StopIteration:
                            tasks.remove(g)

            run_tasks([scal_task(0)])
            run_tasks([prep_task(0, 0), prep_task(0, 1), scal_task(1)])
            for c in range(4):
                csl = slice(c * 128, (c + 1) * 128)
                tl = [scan_task(c, 0), scan_task(c, 1)]
                if c + 1 < 4:
                    tl += [prep_task(c + 1, 0), prep_task(c + 1, 1)]
                if c + 2 < 4:
                    tl.append(scal_task(c + 2))
                run_tasks(tl)
                og = ogs[c % 2]
                for h_ in range(2):
                    ptg = tb2()
                    P.transpose(ptg, og[:, h_ * 128:(h_ + 1) * 128], identB)
                    P.copy("dve", ogT[:, h_, csl], ptg)
            for h_ in range(2):
                P.dma("sp", o_gdnT[h_ * 128:(h_ + 1) * 128, tsl], ogT[:, h_, :])


def build_mix(S, nc=None, do_attn=True, do_gdn=True, do_sc=True):
    if nc is None:
        nc = bass.Bass("TRN2", target_bir_lowering=False)
    P = Prog(nc)
    carr, coff = mix_consts()
    hT = P.dram("hT", [D, S], F32, "ExternalInput")
    wm_d = P.dram("wm", [D, NW], F32, "ExternalInput")
    cst_d = P.dram("cst", list(carr.shape), F32, "ExternalInput")
    gcw_d = P.dram("gcw", [128, 24], F32, "ExternalInput")
    scw_d = P.dram("scw", [128, 3], F32, "ExternalInput")
    hp_d = P.dram("hp", [128, 4], F32, "ExternalInput")
    nw_d = P.dram("nw", [128, 128], F32, "ExternalInput")
    o_sb = P.dram("o_sb", [128, S], BF16, "ExternalOutput")
    o_gdnT = P.dram("o_gdnT", [256, S], BF16, "ExternalOutput")
    o_sc = P.dram("o_sc", [128, S], BF16, "ExternalOutput")
    emit_mix(P, S, hT, wm_d, cst_d, gcw_d, scw_d, hp_d, nw_d, o_sb, o_gdnT, o_sc, do_attn, do_gdn, do_sc)
    st = P.finalize()
    return nc, st, carr


NSTEP = 4


def build_fused(S, depth=2, nc=None):
    if nc is None:
        nc = bass.Bass("TRN2", target_bir_lowering=False)
    P = Prog(nc)
    carr, _ = mix_consts()
    xT = P.dram("xT", [D, S], F32, "ExternalInput")
    outT = P.dram("outT", [D, S], F32, "ExternalOutput")
    lng = P.dram("lng", [128, depth * NSTEP * KT], F32, "ExternalInput")
    lnb = P.dram("lnb", [128, depth * NSTEP * KT], F32, "ExternalInput")
    cst = P.dram("cst", list(carr.shape), F32, "ExternalInput")
    W = []
    for i in range(depth):
        w = {}
        for f in range(2):
            w[f"w1{f}"] = P.dram(f"w1_{i}_{f}", [D, 2 * DFF], F32, "ExternalInput")
            w[f"w2{f}"] = P.dram(f"w2_{i}_{f}", [DFF, D], F32, "ExternalInput")
        for j in range(2):
            w[f"wm{j}"] = P.dram(f"wm_{i}_{j}", [D, NW], F32, "ExternalInput")
            w[f"gcw{j}"] = P.dram(f"gcw_{i}_{j}", [128, 24], F32, "ExternalInput")
            w[f"scw{j}"] = P.dram(f"scw_{i}_{j}", [128, 3], F32, "ExternalInput")
            w[f"hp{j}"] = P.dram(f"hp_{i}_{j}", [128, 4], F32, "ExternalInput")
        w["nw"] = P.dram(f"nw_{i}", [128, 128], F32, "ExternalInput")
        w["wo"] = P.dram(f"wo_{i}", [D, D], F32, "ExternalInput")
        w["wg"] = P.dram(f"wg_{i}", [D, D], F32, "ExternalInput")
        w["wp"] = P.dram(f"wp_{i}", [256, D], F32, "ExternalInput")
        w["bg"] = P.dram(f"bg_{i}", [128, KT], F32, "ExternalInput")
        w["pT"] = P.dram(f"pT_{i}", [256, S], F32, "ExternalInput")
        W.append(w)
    H = P.dram("H_scr", [D, S], F32, "Internal")
    X1 = P.dram("X1_scr", [D, S], F32, "Internal")
    X2 = P.dram("X2_scr", [D, S], F32, "Internal")
    MIX = P.dram("MIX_scr", [D, S], BF16, "Internal")

    def ln(i, s):
        o = (i * NSTEP + s) * KT
        return lng[:, o:o + KT], lnb[:, o:o + KT]

    cur = xT
    for i in range(depth):
        w = W[i]
        with P.stage(f"A{i}"):
            g, b = ln(i, 0)
            emit_dense(P, ["ffn"], S, cur, H, g, b, [(w["w10"], w["w20"])])
        for j in range(2):
            with P.stage(f"B{i}{j}"):
                emit_mix(P, S, H, w[f"wm{j}"], cst, w[f"gcw{j}"], w[f"scw{j}"], w[f"hp{j}"], w["nw"],
                         MIX[j * 128:(j + 1) * 128, :], MIX[256 + j * 256:256 + (j + 1) * 256, :],
                         MIX[768 + j * 128:768 + (j + 1) * 128, :])
        with P.stage(f"M{i}"):
            g, b = ln(i, 1)
            emit_dense(P, ["mixout"], S, H, X1, g, b, [(w["wo"], MIX)])
        with P.stage(f"F{i}"):
            g, b = ln(i, 2)
            emit_dense(P, ["ffn"], S, X1, X2, g, b, [(w["w11"], w["w21"])])
        with P.stage(f"P{i}"):
            g, b = ln(i, 3)
            dst = outT if i == depth - 1 else X1
            emit_dense(P, ["ple"], S, X2, dst, g, b, [(w["wg"], w["wp"], w["bg"], w["pT"])])
        cur = X1
    return nc, P.tot, carr


def _relay(v):
    return np.ascontiguousarray(np.asarray(v, np.float32).reshape(8, 128).T)


def fused_inputs(b, S, depth, carr, x, p, ln_g, ln_b, ffn_w_in, ffn_w_out, mix_w_in, gdn_conv_w, gdn_a_log,
                 gdn_dt_bias, gdn_norm_w, sc_conv_w, mix_w_out, ple_w_proj, ple_w_gate, ple_b_gate, shared=None):
    m = {} if shared is None else dict(shared)
    m["xT"] = np.ascontiguousarray(x[b].T)
    for i in range(depth):
        m[f"pT_{i}"] = np.ascontiguousarray(p[i, b].T)
    if shared is not None:
        return m
    m["cst"] = carr
    m["lng"] = np.ascontiguousarray(np.concatenate([_relay(ln_g[i, s]) for i in range(depth) for s in range(4)], 1))
    m["lnb"] = np.ascontiguousarray(np.concatenate([_relay(ln_b[i, s]) for i in range(depth) for s in range(4)], 1))
    OFF_SB = 768
    OFF_QKV = OFF_SB + 1536
    OFF_A = OFF_QKV + 512
    OFF_Bt = OFF_A + 4
    OFF_SC = OFF_Bt + 4
    for i in range(depth):
        for f in range(2):
            m[f"w1_{i}_{f}"] = np.ascontiguousarray(ffn_w_in[i, f])
            m[f"w2_{i}_{f}"] = np.ascontiguousarray(ffn_w_out[i, f])
        for j in range(2):
            cols = []
            for base in (0, 256, 512):
                cols.append(np.arange(base + j * 128, base + (j + 1) * 128))
            for base in (OFF_SB, OFF_SB + 512, OFF_SB + 1024, OFF_QKV):
                cols.append(np.arange(base + j * 256, base + (j + 1) * 256))
            cols.append(np.arange(OFF_A + j * 2, OFF_A + j * 2 + 2))
            cols.append(np.arange(OFF_Bt + j * 2, OFF_Bt + j * 2 + 2))
            for base in (OFF_SC, OFF_SC + 256, OFF_SC + 512):
                cols.append(np.arange(base + j * 128, base + (j + 1) * 128))
            cols = np.concatenate(cols)
            m[f"wm_{i}_{j}"] = np.ascontiguousarray(mix_w_in[i][:, cols])
            gidx = np.concatenate([np.arange(base + j * 256, base + (j + 1) * 256) for base in (0, 512, 1024)])
            m[f"gcw_{i}_{j}"] = np.ascontiguousarray(gdn_conv_w[i][:, gidx].reshape(4, 6, 128).transpose(2, 1, 0).reshape(128, 24))
            m[f"scw_{i}_{j}"] = np.ascontiguousarray(sc_conv_w[i][:, j * 128:(j + 1) * 128].T)
            m[f"hp_{i}_{j}"] = np.ascontiguousarray(np.tile(np.concatenate(
                [gdn_a_log[i][2 * j:2 * j + 2], gdn_dt_bias[i][2 * j:2 * j + 2]])[None, :], (128, 1)).astype(np.float32))
        m[f"nw_{i}"] = np.ascontiguousarray(np.tile(gdn_norm_w[i][None, :], (128, 1)).astype(np.float32))
        m[f"wo_{i}"] = np.ascontiguousarray(mix_w_out[i])
        m[f"wg_{i}"] = np.ascontiguousarray(ple_w_gate[i])
        m[f"wp_{i}"] = np.ascontiguousarray(ple_w_proj[i])
        m[f"bg_{i}"] = _relay(ple_b_gate[i])
    return m


from concourse.bass_utils import run_bass_kernel_spmd

BATCH, SEQ, DEPTH = 4, 8192, 2


def kernel(x, p, ln_g, ln_b, ffn_w_in, ffn_w_out, mix_w_in, gdn_conv_w, gdn_a_log,
           gdn_dt_bias, gdn_norm_w, sc_conv_w, mix_w_out, ple_w_proj, ple_w_gate, ple_b_gate):
    f = lambda a: np.asarray(a, np.float32)
    args = dict(x=f(x), p=f(p), ln_g=f(ln_g), ln_b=f(ln_b), ffn_w_in=f(ffn_w_in), ffn_w_out=f(ffn_w_out),
                mix_w_in=f(mix_w_in), gdn_conv_w=f(gdn_conv_w), gdn_a_log=f(gdn_a_log), gdn_dt_bias=f(gdn_dt_bias),
                gdn_norm_w=f(gdn_norm_w), sc_conv_w=f(sc_conv_w), mix_w_out=f(mix_w_out), ple_w_proj=f(ple_w_proj),
                ple_w_gate=f(ple_w_gate), ple_b_gate=f(ple_b_gate))
    nc, _, carr = build_fused(SEQ, DEPTH)
    m0 = fused_inputs(0, SEQ, DEPTH, carr, **args)
    shared = {k: v for k, v in m0.items() if k != "xT" and not k.startswith("pT_")}
    maps = [m0] + [fused_inputs(b, SEQ, DEPTH, carr, shared=shared, **args) for b in range(1, BATCH)]
    res = run_bass_kernel_spmd(nc, maps, core_ids=list(range(BATCH)))
    out = np.empty((BATCH, SEQ, D), np.float32)
    for b in range(BATCH):
        out[b] = res.results[b]["outT"].T
    return out
```

```python
import numpy as np
from contextlib import ExitStack, contextmanager
import concourse.bass as bass
import concourse.mybir as mybir

F32 = mybir.dt.float32
BF16 = mybir.dt.bfloat16
AF = mybir.ActivationFunctionType
ALU = mybir.AluOpType
AX = mybir.AxisListType


def _prod(xs):
    r = 1
    for x in xs:
        r *= int(x)
    return r


class Op:
    __slots__ = ("eng", "fn", "reads", "writes", "dma", "deps", "sig", "dsem", "dval", "dprev", "pe_mm")

    def __init__(self, eng, fn, reads, writes, dma, pe_mm=False):
        self.eng = eng
        self.fn = fn
        self.reads = reads
        self.writes = writes
        self.dma = dma
        self.deps = ()
        self.sig = None
        self.dsem = None
        self.dval = None
        self.dprev = None
        self.pe_mm = pe_mm


class Prog:
    NDSEM = 24

    def __init__(self, nc, same_engine_sync=None):
        self.nc = nc
        self.ops = []
        self.tinfo = {}
        self.hist = {}
        import os as _os
        if same_engine_sync is None:
            same_engine_sync = _os.environ.get("FW_SES", "1") == "1"
        self.same_engine_sync = same_engine_sync
        self.engs = {"pe": nc.tensor, "act": nc.scalar, "dve": nc.vector, "pool": nc.gpsimd, "sp": nc.sync}
        self._n = 0
        self.stk = None
        self.sname = ""
        self.esem = None
        self.tot = dict(n_ops=0, n_waits=0, n_dma=0)

    @contextmanager
    def stage(self, name):
        self.stk = ExitStack()
        self.sname = name + "_"
        self.ops = []
        try:
            yield self
            self.finalize(barrier=True)
        finally:
            self.stk.close()
            self.stk = None
            self.sname = ""
            self.ops = []

    def sbuf(self, name, shape, dt):
        name = self.sname + name
        if self.stk is not None:
            t = self.stk.enter_context(self.nc.sbuf_tensor(name, [int(s) for s in shape], dt))
        else:
            t = self.nc.alloc_sbuf_tensor(name, [int(s) for s in shape], dt)
        self.tinfo[name] = ("sb", _prod(shape[1:]))
        return t.ap()

    def psum(self, name, shape, dt=F32):
        name = self.sname + name
        if self.stk is not None:
            t = self.stk.enter_context(self.nc.psum_tensor(name, [int(s) for s in shape], dt))
        else:
            t = self.nc.alloc_psum_tensor(name, [int(s) for s in shape], dt)
        self.tinfo[name] = ("ps", _prod(shape[1:]))
        return t.ap()

    def dram(self, name, shape, dt, kind):
        t = self.nc.dram_tensor(name, [int(s) for s in shape], dt, kind=kind)
        self.tinfo[name] = ("const" if kind == "ExternalInput" else "dram", None)
        return t.ap()

    def rect(self, ap):
        name = ap.tensor.name
        kind, ps = self.tinfo[name]
        off = int(ap.offset)
        dims = ap.ap
        if kind in ("dram", "const"):
            hi = off + sum((c - 1) * abs(s) for s, c in dims) + 1
            return (name, 0, 1, off, hi)
        p0 = off // ps
        f0 = off % ps
        pc = dims[0][1]
        hi = f0 + sum((c - 1) * abs(s) for s, c in dims[1:]) + 1
        return (name, p0, p0 + pc, f0, hi)

    def add(self, eng, fn, reads=(), writes=(), dma=False, pe_mm=False):
        rr = []
        for a in reads:
            if a is None or isinstance(a, (int, float)):
                continue
            r = self.rect(a)
            if self.tinfo[r[0]][0] == "const":
                continue
            rr.append(r)
        ww = [self.rect(a) for a in writes]
        op = Op(eng, fn, rr, ww, dma, pe_mm)
        self.ops.append(op)
        return op

    def mm(self, out, lhsT, rhs, start=True, stop=True):
        self.add("pe", lambda e: e.matmul(out, lhsT, rhs, start=start, stop=stop),
                 reads=[lhsT, rhs], writes=[out], pe_mm=True)

    def transpose(self, out, in_, ident):
        self.add("pe", lambda e: e.transpose(out, in_, ident), reads=[in_, ident], writes=[out], pe_mm=True)

    def act(self, out, in_, func, bias=None, scale=None, accum_out=None):
        kw = {}
        if bias is not None:
            kw["bias"] = bias
        if scale is not None:
            kw["scale"] = scale
        if accum_out is not None:
            kw["accum_out"] = accum_out
        rd = [in_]
        if bias is not None and not isinstance(bias, (int, float)):
            rd.append(bias)
        if scale is not None and not isinstance(scale, (int, float)):
            rd.append(scale)
        wr = [out] + ([accum_out] if accum_out is not None else [])
        self.add("act", lambda e: e.activation(out, in_, func, **kw), reads=rd, writes=wr)

    def tt(self, eng, out, in0, in1, op):
        self.add(eng, lambda e: e.tensor_tensor(out, in0, in1, op), reads=[in0, in1], writes=[out])

    def ts(self, eng, out, in0, s1, s2, op0, op1=None):
        rd = [in0] + [s for s in (s1, s2) if s is not None and not isinstance(s, (int, float))]
        if op1 is None:
            self.add(eng, lambda e: e.tensor_scalar(out, in0, s1, None, op0), reads=rd, writes=[out])
        else:
            self.add(eng, lambda e: e.tensor_scalar(out, in0, s1, s2, op0, op1), reads=rd, writes=[out])

    def stt(self, eng, out, in0, scalar, in1, op0, op1):
        rd = [in0, in1] + ([scalar] if not isinstance(scalar, (int, float)) else [])
        self.add(eng, lambda e: e.scalar_tensor_tensor(out, in0, scalar, in1, op0, op1), reads=rd, writes=[out])

    def copy(self, eng, out, in_):
        if eng == "act":
            self.add(eng, lambda e: e.copy(out, in_), reads=[in_], writes=[out])
        else:
            self.add(eng, lambda e: e.tensor_copy(out, in_), reads=[in_], writes=[out])

    def recip(self, out, in_):
        self.add("dve", lambda e: e.reciprocal(out, in_), reads=[in_], writes=[out])

    def memset(self, eng, out, val):
        self.add(eng, lambda e: e.memset(out, val), reads=[], writes=[out])

    def dma(self, q, out, in_):
        self.add(q, lambda e: e.dma_start(out=out, in_=in_), reads=[in_], writes=[out], dma=True)

    @staticmethod
    def _ov(a, b):
        return a[1] < b[2] and b[1] < a[2] and a[3] < b[4] and b[3] < a[4]

    @staticmethod
    def _contains(a, b):
        return a[1] <= b[1] and b[2] <= a[2] and a[3] <= b[3] and b[4] <= a[4]

    def finalize(self, barrier=False):
        ops = self.ops
        hist = {}
        for i, op in enumerate(ops):
            deps = set()
            for r in op.reads:
                for seg in hist.get(r[0], ()):
                    if seg[1] is not None and self._ov(seg[0], r):
                        deps.add(seg[1])
            for w in op.writes:
                for seg in hist.get(w[0], ()):
                    if self._ov(seg[0], w):
                        if seg[1] is not None:
                            deps.add(seg[1])
                        deps.update(seg[2].values())
                        deps.update(seg[3])
            for r in op.reads:
                lst = hist.setdefault(r[0], [])
                found = None
                for seg in lst:
                    if seg[0] == r:
                        found = seg
                        break
                if found is None:
                    found = [r, None, {}, []]
                    lst.append(found)
                if op.dma:
                    found[3].append(i)
                else:
                    found[2][op.eng] = i
            for w in op.writes:
                lst = hist.setdefault(w[0], [])
                lst[:] = [seg for seg in lst if not self._contains(w, seg[0])]
                lst.append([w, i, {}, []])
            deps.discard(i)
            op.deps = sorted(deps)
        need = [False] * len(ops)
        for i, op in enumerate(ops):
            for j in op.deps:
                pj = ops[j]
                if pj.dma:
                    continue
                if pj.eng == op.eng and not op.dma:
                    if pj.eng == "pe" or not self.same_engine_sync:
                        continue
                need[j] = True
        if barrier:
            last = {}
            for i, op in enumerate(ops):
                if not op.dma:
                    last[op.eng] = i
            for i in last.values():
                need[i] = True
        nc = self.nc
        if self.esem is None:
            self.esem = {k: nc.alloc_semaphore(name=f"e_{k}") for k in self.engs}
            self.dsems = [nc.alloc_semaphore(name=f"d_{i}") for i in range(self.NDSEM)]
            self.ecount = {k: 0 for k in self.engs}
            self.dcount = [0] * self.NDSEM
            self.nd = 0
            self.known = {k: {} for k in self.engs}
        esem, dsems, ecount, dcount, known = self.esem, self.dsems, self.ecount, self.dcount, self.known
        for i, op in enumerate(ops):
            if op.dma:
                sidx = self.nd % self.NDSEM
                self.nd += 1
                op.dprev = dcount[sidx]
                dcount[sidx] += 16
                op.dsem = sidx
                op.dval = dcount[sidx]
            elif need[i]:
                ecount[op.eng] += 1
                op.sig = ecount[op.eng]
        nwaits = 0
        for i, op in enumerate(ops):
            e = self.engs[op.eng]
            kn = known[op.eng]
            waits = {}
            for j in op.deps:
                pj = ops[j]
                if pj.dma:
                    key = ("d", pj.dsem)
                    val = pj.dval
                else:
                    if pj.eng == op.eng and not op.dma:
                        if pj.eng == "pe" or not self.same_engine_sync:
                            continue
                    key = ("e", pj.eng)
                    val = pj.sig
                if kn.get(key, 0) >= val:
                    continue
                if waits.get(key, 0) < val:
                    waits[key] = val
            if op.dma and op.dprev > 0:
                key = ("d", op.dsem)
                if kn.get(key, 0) < op.dprev and waits.get(key, 0) < op.dprev:
                    waits[key] = op.dprev
            for key, val in waits.items():
                sem = dsems[key[1]] if key[0] == "d" else esem[key[1]]
                e.wait_ge(sem, val)
                kn[key] = val
                nwaits += 1
            ins = op.fn(e)
            if op.dma:
                ins.then_inc(dsems[op.dsem], 16)
            elif op.sig is not None:
                ins.then_inc(esem[op.eng], 1)
        targets = list(self.engs) if barrier else ["sp"]
        for k in targets:
            e = self.engs[k]
            kn = known[k]
            for sidx in range(self.NDSEM):
                if dcount[sidx] > kn.get(("d", sidx), 0):
                    e.wait_ge(dsems[sidx], dcount[sidx])
                    kn[("d", sidx)] = dcount[sidx]
                    nwaits += 1
            for k2 in ("pe", "act", "dve", "pool"):
                if ecount[k2] > kn.get(("e", k2), 0):
                    e.wait_ge(esem[k2], ecount[k2])
                    kn[("e", k2)] = ecount[k2]
                    nwaits += 1
        self.stats = dict(n_ops=len(ops), n_waits=nwaits, sigs=dict(ecount), n_dma=self.nd)
        self.tot["n_ops"] += len(ops)
        self.tot["n_waits"] += nwaits
        return self.stats


D = 1024
DFF = 2816
KT = 8
TN = 512
ALPHA = 4.0 ** 0.25
LN_EPS = 1e-5


import os as _os
PIPE_FFN = _os.environ.get("PIPE_FFN", "1") == "1"


class DenseCtx:
    pass


def load_w(P, name, w_dram, rows, cols, nsplit=1, q="pool"):
    kt = rows // 128
    w = P.sbuf(name, [128, kt, cols], BF16)
    step = cols // nsplit
    for s in range(nsplit):
        for k in range(kt):
            P.dma(q, w[:, k, s * step:(s + 1) * step], w_dram[k * 128:(k + 1) * 128, s * step:(s + 1) * step])
    return w


def emit_ln(P, C, g, b, write_xb=True):
    x32, xb = C.x32, C.xb
    pm = C.pstat[0]
    for m in range(KT):
        P.mm(pm, C.onesF, x32[:, m, :], start=(m == 0), stop=(m == KT - 1))
    for m in range(KT):
        P.tt("dve", x32[:, m, :], x32[:, m, :], pm, ALU.subtract)
    pv = C.pstat[1]
    for m in range(KT):
        sq = C.sq[m % 2]
        P.act(sq, x32[:, m, :], AF.Square)
        P.mm(pv, C.onesF, sq, start=(m == 0), stop=(m == KT - 1))
    P.act(C.rstd, pv, AF.Sqrt, bias=C.epsc[:, 0:1])
    P.recip(C.rstd, C.rstd)
    for m in range(KT):
        P.tt("dve", x32[:, m, :], x32[:, m, :], C.rstd, ALU.mult)
        P.act(x32[:, m, :], x32[:, m, :], AF.Identity, bias=b[:, m:m + 1], scale=g[:, m:m + 1])
        if write_xb:
            P.act(xb[:, m, :], x32[:, m, :], AF.Copy)


def gen_ln(P, C, x32, xb, g, b, write_xb=True):
    pm = C.pstat[0]
    for m in range(KT):
        P.mm(pm, C.onesF, x32[:, m, :], start=(m == 0), stop=(m == KT - 1))
    yield
    for m in range(KT):
        P.tt("dve", x32[:, m, :], x32[:, m, :], pm, ALU.subtract)
        if m % 2 == 1:
            yield
    pv = C.pstat[1]
    for m in range(KT):
        sq = C.sq[m % 2]
        P.act(sq, x32[:, m, :], AF.Square)
        yield
        P.mm(pv, C.onesF, sq, start=(m == 0), stop=(m == KT - 1))
    yield
    P.act(C.rstd, pv, AF.Sqrt, bias=C.epsc[:, 0:1])
    yield
    P.recip(C.rstd, C.rstd)
    yield
    for m in range(KT):
        P.tt("dve", x32[:, m, :], x32[:, m, :], C.rstd, ALU.mult)
        yield
        P.act(x32[:, m, :], x32[:, m, :], AF.Identity, bias=b[:, m:m + 1], scale=g[:, m:m + 1])
        if write_xb:
            P.act(xb[:, m, :], x32[:, m, :], AF.Copy)
    yield


def gen_ffn_up(P, C, w1, xb):
    hT = C.hT
    NC = DFF // 128
    for c in range(NC):
        pg = C.pg[c % 2]
        pu = C.pu[c % 2]
        for k in range(KT):
            P.mm(pg, w1[:, k, c * 128:(c + 1) * 128], xb[:, k, :], start=(k == 0), stop=(k == KT - 1))
        for k in range(KT):
            P.mm(pu, w1[:, k, DFF + c * 128:DFF + (c + 1) * 128], xb[:, k, :], start=(k == 0), stop=(k == KT - 1))
        sg = C.sg[c % 2]
        P.act(sg, pg, AF.Silu)
        P.stt("dve", hT[:, c, :], sg, 0.5, pu, ALU.mult, ALU.mult)
        yield


def emit_ffn_down(P, C, w2, x32):
    hT = C.hT
    NC = DFF // 128
    for m in range(KT):
        py = C.py[m % 2]
        for c in range(NC):
            P.mm(py, w2[:, c, m * 128:(m + 1) * 128], hT[:, c, :], start=(c == 0), stop=(c == NC - 1))
        P.stt("dve", x32[:, m, :], x32[:, m, :], ALPHA, py, ALU.mult, ALU.add)


def run_rr(tasks):
    tasks = list(tasks)
    while tasks:
        for gt in list(tasks):
            try:
                next(gt)
            except StopIteration:
                tasks.remove(gt)


def emit_ffn_stage_pipelined(P, C, nt, x32s, xbs, w1, w2, g, b, load_tile, store_tile):
    load_tile(0)
    for m in range(KT):
        P.act(xbs[0][:, m, :], x32s[0][:, m, :], AF.Copy)
    if nt > 1:
        load_tile(1)
    run_rr([gen_ffn_up(P, C, w1, xbs[0])])
    for t in range(nt):
        emit_ffn_down(P, C, w2, x32s[t % 2])
        tasks = [gen_ln(P, C, x32s[t % 2], None, g, b, write_xb=False)]
        if t + 1 < nt:
            nb = (t + 1) % 2
            for m in range(KT):
                P.act(xbs[nb][:, m, :], x32s[nb][:, m, :], AF.Copy)
            tasks.append(gen_ffn_up(P, C, w1, xbs[nb]))
        run_rr(tasks)
        store_tile(t)
        if t + 2 < nt:
            load_tile(t + 2)


def emit_ffn(P, C, w1, w2, g, b):
    x32, xb, hT = C.x32, C.xb, C.hT
    NC = DFF // 128
    for c in range(NC):
        pg = C.pg[c % 2]
        pu = C.pu[c % 2]
        for k in range(KT):
            P.mm(pg, w1[:, k, c * 128:(c + 1) * 128], xb[:, k, :], start=(k == 0), stop=(k == KT - 1))
        for k in range(KT):
            P.mm(pu, w1[:, k, DFF + c * 128:DFF + (c + 1) * 128], xb[:, k, :], start=(k == 0), stop=(k == KT - 1))
        sg = C.sg[c % 2]
        P.act(sg, pg, AF.Silu)
        P.stt("dve", hT[:, c, :], sg, 0.5, pu, ALU.mult, ALU.mult)
    for m in range(KT):
        py = C.py[m % 2]
        for c in range(NC):
            P.mm(py, w2[:, c, m * 128:(m + 1) * 128], hT[:, c, :], start=(c == 0), stop=(c == NC - 1))
        P.stt("dve", x32[:, m, :], x32[:, m, :], ALPHA, py, ALU.mult, ALU.add)
    emit_ln(P, C, g, b)


def emit_mixout(P, C, mixb, wo, g, b, write_xb=True):
    x32 = C.x32
    for m in range(KT):
        py = C.py[m % 2]
        for k in range(KT):
            P.mm(py, wo[:, k, m * 128:(m + 1) * 128], mixb[:, k, :], start=(k == 0), stop=(k == KT - 1))
        P.stt("dve", x32[:, m, :], x32[:, m, :], ALPHA, py, ALU.mult, ALU.add)
    emit_ln(P, C, g, b, write_xb)


def emit_ple(P, C, pb, wg, bg, wp, g, b, write_xb=True):
    x32, xb = C.x32, C.xb
    for m in range(KT):
        pgt = C.pg[m % 2]
        ppj = C.pu[m % 2]
        for k in range(KT):
            P.mm(pgt, wg[:, k, m * 128:(m + 1) * 128], xb[:, k, :], start=(k == 0), stop=(k == KT - 1))
        for k in range(2):
            P.mm(ppj, wp[:, k, m * 128:(m + 1) * 128], pb[:, k, :], start=(k == 0), stop=(k == 1))
        sg = C.sg[m % 2]
        P.act(sg, pgt, AF.Sigmoid, bias=bg[:, m:m + 1])
        P.tt("dve", sg, sg, ppj, ALU.mult)
        P.stt("dve", x32[:, m, :], x32[:, m, :], ALPHA, sg, ALU.mult, ALU.add)
    emit_ln(P, C, g, b, write_xb)


def emit_dense(P, steps, ntok, xT, outT, lng_d, lnb_d, wd):
    nt = ntok // TN
    nln = len(steps)
    C = DenseCtx()
    C.x32 = P.sbuf("x32", [128, KT, TN], F32)
    C.xb = P.sbuf("xb", [128, KT, TN], BF16)
    C.onesF = P.sbuf("onesF", [128, 128], F32)
    C.sq = [P.sbuf(f"sq{i}", [128, TN], F32) for i in range(2)]
    C.sg = [P.sbuf(f"sg{i}", [128, TN], F32) for i in range(2)]
    C.rstd = P.sbuf("rstd", [128, TN], F32)
    C.pg = [P.psum(f"pg{i}", [128, TN]) for i in range(2)]
    C.pu = [P.psum(f"pu{i}", [128, TN]) for i in range(2)]
    C.py = [P.psum(f"py{i}", [128, TN]) for i in range(2)]
    C.pstat = [P.psum(f"pst{i}", [128, TN]) for i in range(2)]
    lng = P.sbuf("lng_s", [128, nln * KT], F32)
    lnb = P.sbuf("lnb_s", [128, nln * KT], F32)
    P.dma("sp", lng, lng_d)
    P.dma("sp", lnb, lnb_d)
    P.memset("dve", C.onesF, 1.0 / D)
    C.epsc = P.sbuf("epsc", [128, 1], F32)
    P.memset("dve", C.epsc, LN_EPS)
    ws = []
    need_hT = False
    for i, s in enumerate(steps):
        if s == "ffn":
            need_hT = True
            w1 = load_w(P, f"w1s_{i}", wd[i][0], D, 2 * DFF, nsplit=4)
            w2 = load_w(P, f"w2s_{i}", wd[i][1], DFF, D)
            ws.append((w1, w2))
        elif s == "mixout":
            wo = load_w(P, f"wos_{i}", wd[i][0], D, D)
            mixb = P.sbuf(f"mixb_{i}", [128, KT, TN], BF16)
            ws.append((wo, mixb))
        elif s == "ple":
            wg = load_w(P, f"wgs_{i}", wd[i][0], D, D)
            wp = load_w(P, f"wps_{i}", wd[i][1], 256, D)
            bg = P.sbuf(f"bgs_{i}", [128, KT], F32)
            P.dma("sp", bg, wd[i][2])
            pb = P.sbuf(f"pb_{i}", [128, 2, TN], BF16)
            ws.append((wg, wp, bg, pb))
    if need_hT:
        C.hT = P.sbuf("hT", [128, DFF // 128, TN], BF16)
    xTr = xT.rearrange("(k p) n -> p k n", p=128)
    oTr = outT.rearrange("(k p) n -> p k n", p=128)
    x32s = [C.x32, P.sbuf("x32b", [128, KT, TN], F32)]
    mixbs = {}
    pbs = {}
    for i, s in enumerate(steps):
        if s == "mixout":
            mixbs[i] = [ws[i][1], P.sbuf(f"mixb2_{i}", [128, KT, TN], BF16)]
        elif s == "ple":
            pbs[i] = [ws[i][3], P.sbuf(f"pb2_{i}", [128, 2, TN], BF16)]

    def load_tile(t):
        tsl = slice(t * TN, (t + 1) * TN)
        P.dma("sp", x32s[t % 2], xTr[:, :, tsl])
        for i, s in enumerate(steps):
            if s == "mixout":
                P.dma("sp", mixbs[i][t % 2], wd[i][1].rearrange("(k p) n -> p k n", p=128)[:, :, tsl])
            elif s == "ple":
                pTr = wd[i][3].rearrange("(k p) n -> p k n", p=128)
                for k in range(2):
                    P.dma("pool", pbs[i][t % 2][:, k, :], pTr[:, k, tsl])

    if steps == ["ffn"] and PIPE_FFN:
        xbs = [C.xb, C.xb]

        def store_tile(t):
            P.dma("pool", oTr[:, :, t * TN:(t + 1) * TN], x32s[t % 2])
        emit_ffn_stage_pipelined(P, C, nt, x32s, xbs, ws[0][0], ws[0][1], lng[:, 0:KT], lnb[:, 0:KT], load_tile, store_tile)
        return
    load_tile(0)
    for t in range(nt):
        tsl = slice(t * TN, (t + 1) * TN)
        C.x32 = x32s[t % 2]
        if steps[0] != "mixout":
            for m in range(KT):
                P.act(C.xb[:, m, :], C.x32[:, m, :], AF.Copy)
        if t + 1 < nt:
            load_tile(t + 1)
        for i, s in enumerate(steps):
            g = lng[:, i * KT:(i + 1) * KT]
            b = lnb[:, i * KT:(i + 1) * KT]
            if s == "ffn":
                emit_ffn(P, C, ws[i][0], ws[i][1], g, b)
            elif s == "mixout":
                emit_mixout(P, C, mixbs[i][t % 2], ws[i][0], g, b, write_xb=(i < len(steps) - 1))
            elif s == "ple":
                emit_ple(P, C, pbs[i][t % 2], ws[i][0], ws[i][2], ws[i][1], g, b, write_xb=(i < len(steps) - 1))
        P.dma("pool", oTr[:, :, tsl], C.x32)


def build_dense(steps, ntok, nc=None):
    if nc is None:
        nc = bass.Bass("TRN2", target_bir_lowering=False)
    P = Prog(nc)
    xT = P.dram("xT", [D, ntok], F32, "ExternalInput")
    outT = P.dram("outT", [D, ntok], F32, "ExternalOutput")
    nln = len(steps)
    lng_d = P.dram("lng", [128, nln * KT], F32, "ExternalInput")
    lnb_d = P.dram("lnb", [128, nln * KT], F32, "ExternalInput")
    wd = []
    for i, s in enumerate(steps):
        if s == "ffn":
            wd.append((P.dram(f"w1_{i}", [D, 2 * DFF], F32, "ExternalInput"),
                       P.dram(f"w2_{i}", [DFF, D], F32, "ExternalInput")))
        elif s == "mixout":
            wd.append((P.dram(f"wo_{i}", [D, D], F32, "ExternalInput"),
                       P.dram(f"mixT_{i}", [D, ntok], BF16, "ExternalInput")))
        elif s == "ple":
            wd.append((P.dram(f"wg_{i}", [D, D], F32, "ExternalInput"),
                       P.dram(f"wp_{i}", [256, D], F32, "ExternalInput"),
                       P.dram(f"bg_{i}", [128, KT], F32, "ExternalInput"),
                       P.dram(f"pT_{i}", [256, ntok], F32, "ExternalInput")))
    emit_dense(P, steps, ntok, xT, outT, lng_d, lnb_d, wd)
    st = P.finalize()
    return nc, st

import os
POOL = os.environ.get("MIX_POOL", "pool")
LVL = int(os.environ.get("MIX_LVL", "9"))
ACT_PSUM_R = os.environ.get("MIX_ACT_PSUM_R", "1") == "1"
NOCARRY = os.environ.get("MIX_NOCARRY", "0") == "1"

D = 1024
KT = 8
TN = 512
NW = 1796
C_SQ, C_SK, C_SV = 0, 128, 256
C_GQ, C_GK, C_GV, C_GZ = 384, 640, 896, 1152
C_AB = 1408
C_SCB, C_SCC, C_SCH = 1412, 1540, 1668
NORM_EPS = 1e-6


def mix_consts():
    p = np.arange(128)[:, None]
    f = np.arange(128)[None, :]
    c = {}
    c["ident"] = (p == f).astype(np.float32)
    c["triU"] = (p <= f).astype(np.float32)
    c["triNeg"] = -(p >= f).astype(np.float32)
    c["SL"] = (p > f).astype(np.float32)
    c["SU"] = (p < f).astype(np.float32)
    c["UI"] = (p <= f).astype(np.float32)
    m = np.zeros((128, 4, 512), np.float32)
    tq = np.arange(512)[None, :]
    for i in range(4):
        m[:, i, :] = ((128 * i + p) < tq).astype(np.float32)
    c["mask"] = m.reshape(128, 2048)
    order = ["ident", "triU", "triNeg", "SL", "SU", "UI", "mask"]
    arr = np.concatenate([c[k] for k in order], axis=1)
    offs = {}
    o = 0
    for k in order:
        offs[k] = (o, o + c[k].shape[1])
        o += c[k].shape[1]
    return arr, offs


class Ring:
    def __init__(self, items):
        self.items = items
        self.i = 0

    def __call__(self):
        x = self.items[self.i % len(self.items)]
        self.i += 1
        return x


def emit_mix(P, S, hT, wm_d, cst_d, gcw_d, scw_d, hp_d, nw_d, o_sb, o_gdnT, o_sc,
             do_attn=True, do_gdn=True, do_sc=True):
    NT = S // TN
    NKB = S // 128
    carr, coff = mix_consts()

    cst = P.sbuf("cst_s", list(carr.shape), F32)
    P.dma("sp", cst, cst_d)

    def cs(k):
        a, b = coff[k]
        return cst[:, a:b]
    identF, triU, SL, SU, UI = cs("ident"), cs("triU"), cs("SL"), cs("SU"), cs("UI")
    maskF = cs("mask")
    identB = P.sbuf("identB", [128, 128], BF16)
    triNegB = P.sbuf("triNegB", [128, 128], BF16)
    P.copy("dve", identB, identF)
    P.copy("dve", triNegB, cs("triNeg"))
    ones1 = P.sbuf("ones1", [128, 128], F32)
    P.memset("dve", ones1, 1.0)
    onesRowB = P.sbuf("onesRowB", [1, 128], BF16)
    P.memset("dve", onesRowB, 1.0)
    onec = P.sbuf("onec", [128, 1], F32)
    P.memset("dve", onec, 1.0)
    epsc = P.sbuf("epsc", [128, 1], F32)
    P.memset("dve", epsc, NORM_EPS)
    gcw = P.sbuf("gcw_s", [128, 24], F32)
    scw = P.sbuf("scw_s", [128, 3], F32)
    hp = P.sbuf("hp_s", [128, 4], F32)
    nwb = P.sbuf("nw_s", [128, 128], F32)
    P.dma("sp", gcw, gcw_d)
    P.dma("sp", scw, scw_d)
    P.dma("sp", hp, hp_d)
    P.dma("sp", nwb, nw_d)
    nA = P.sbuf("nA", [128, 2], F32)
    P.act(nA, hp[:, 0:2], AF.Exp)
    P.ts("dve", nA, nA, -1.0, None, ALU.mult)
    dtb = hp[:, 2:4]
    wm = P.sbuf("wm_s", [128, KT, NW], BF16)
    for k in range(KT):
        P.dma("pool", wm[:, k, :], wm_d[k * 128:(k + 1) * 128, :])

    hb = P.sbuf("hb", [128, KT, TN], BF16)
    qT = P.sbuf("qT", [128, S], BF16)
    kT = P.sbuf("kT", [128, S], BF16)
    vA = P.sbuf("vA", [128, NKB, 128], BF16)
    pf = P.psum("pf", [128, 6, 512], F32)
    pbf = P.psum("pbf", [128, 2, 1024], BF16)
    bank = Ring([pf[:, i, :] for i in range(0, 4)])
    bank_o = Ring([pf[:, 4, :], pf[:, 5, :]])
    gbank = Ring([pf[:, i, :] for i in range(6)])
    tb2 = Ring([pbf[:, i, 0:128] for i in range(2)])
    quart = lambda: bank()[:, 0:128]
    tbank = Ring([pbf[:, i, 0:128] for i in range(2)])
    tbs = [pbf[:, i, 0:128] for i in range(2)]
    hqs = [Ring([pf[:, 2 * h, 0:128], pf[:, 2 * h + 1, 0:128]]) for h in range(2)]
    sq = Ring([pf[:, 4, :], pf[:, 5, :]])
    eb = Ring([P.sbuf(f"e{i}", [128, TN], F32) for i in range(8)])
    Lb = Ring([P.sbuf(f"L{i}", [128, TN], BF16) for i in range(6)])
    eRb = Ring([P.sbuf(f"eR{i}", [128, TN], F32) for i in range(2)])
    attb = Ring([P.sbuf(f"att{i}", [128, TN], BF16) for i in range(7)])
    lsumb = [Ring([P.sbuf(f"ls{h}_{i}", [128, TN], BF16) for i in range(4)]) for h in range(2)]
    negOnesB = P.sbuf("negOnesB", [128, 128], BF16)
    P.memset("dve", negOnesB, -1.0)
    osb = [P.sbuf(f"osb{i}", [64, TN], BF16) for i in range(2)]
    raw = [P.sbuf(f"raw{f}", [128, 3 + TN], F32) for f in range(6)]
    ycv = P.sbuf("ycv", [128, TN], F32)
    ysl = ycv
    sqb = P.sbuf("sqb", [128, TN], F32)
    rnb = sqb
    gqT = [P.sbuf(f"gqT{h}", [128, TN], BF16) for h in range(2)]
    gkT = [P.sbuf(f"gkT{h}", [128, TN], BF16) for h in range(2)]
    gvT = [P.sbuf(f"gvT{h}", [128, TN], BF16) for h in range(2)]
    szbs = [P.sbuf(f"szb{i}", [128, 256], F32) for i in range(3)]
    sc5s = [{n: P.sbuf(f"sc{i}_{n}", [128, 2], F32) for n in
             ("beta", "nbeta", "g", "gc", "gtot", "egl", "eg", "bg", "kd", "tmp")} for i in range(3)]
    ab_sbs = [P.sbuf(f"ab_sb{i}", [128, 4], F32) for i in range(3)]
    ogs = [P.sbuf(f"og{i}", [128, 256], BF16) for i in range(2)]
    ogT = P.sbuf("ogT", [128, 2, TN], BF16)

    def t128(name, dt=F32):
        return P.sbuf(name, [128, 128], dt)
    hbuf = []
    for h in range(2):
        B = {}
        for n in ("gU", "E1", "Dall", "DL", "DUb", "DUI", "dB", "egRow", "D1s", "BRs", "KKs", "kbg", "vb", "u_sb", "junk", "onb", "o_s"):
            B[n] = t128(f"h{h}_{n}")
        for n in ("kdec", "wTs", "qkTs", "qdT", "vnew"):
            B[n] = t128(f"h{h}_{n}", BF16)
        B["Nb"] = Ring([t128(f"h{h}_Nb{i}") for i in range(3)])
        B["NTb"] = Ring([t128(f"h{h}_NTb{i}") for i in range(3)])
        B["XTb"] = Ring([t128(f"h{h}_XTb{i}") for i in range(3)])
        B["ss1"] = P.sbuf(f"h{h}_ss1", [128, 1], F32)
        B["rs1"] = P.sbuf(f"h{h}_rs1", [128, 1], F32)
        hbuf.append(B)
    Sst = [t128(f"S{h}") for h in range(2)]
    Sbf = [t128(f"Sb{h}", BF16) for h in range(2)]
    hand = [[], []]
    for h in range(2):
        hand[0].append({n: hbuf[h][n] for n in ("u_sb", "kdec", "wTs", "qkTs", "qdT")})
        eR_t = eRb.items[0]
        L_t = Lb.items[h]
        hand[1].append({"u_sb": eR_t[:, h * 128:(h + 1) * 128],
                        "kdec": L_t[:, 0:128], "wTs": L_t[:, 128:256], "qkTs": L_t[:, 256:384], "qdT": L_t[:, 384:512]})
    for h in range(2):
        P.memset("dve", Sst[h], 0.0)
        P.memset("dve", Sbf[h], 0.0)
    for f in range(6):
        P.memset("dve", raw[f][:, 0:3], 0.0)
    rawc = P.sbuf("rawc", [128, 2 + TN], F32)
    P.memset("dve", rawc[:, 0:2], 0.0)
    scB = P.sbuf("scB", [128, TN], F32)
    scC = P.sbuf("scC", [128, TN], F32)
    scy = scC
    sco = P.sbuf("sco", [128, TN], BF16)

    hTr = hT.rearrange("(k p) n -> p k n", p=128)

    def proj_fm(col0, ncol=128):
        pb = bank()
        for k in range(KT):
            P.mm(pb[0:ncol, :], wm[:, k, col0:col0 + ncol], hb[:, k, :], start=(k == 0), stop=(k == KT - 1))
        return pb

    for t in range(NT):
        tsl = slice(t * TN, (t + 1) * TN)
        for k in range(KT):
            P.dma("pool", hb[:, k, :], hTr[:, k, tsl])
        if do_attn:
            pq = proj_fm(C_SQ)
            P.ts("dve", qT[:, tsl], pq, 0.125, None, ALU.mult)
            if not os.environ.get("MIX_NOK"):
                pk = proj_fm(C_SK)
                P.act(kT[:, tsl], pk, AF.Copy)
            for s in range(0 if os.environ.get("MIX_NOV") else 4):
                pv = quart()
                for k in range(KT):
                    P.mm(pv, hb[:, k, s * 128:(s + 1) * 128], wm[:, k, C_SV:C_SV + 128], start=(k == 0), stop=(k == KT - 1))
                P.copy("dve", vA[:, 4 * t + s, :], pv)
            nkb = 4 * t + 4
            items = [(hd, kb) for kb in range(nkb - 1, -1, -1) for hd in range(2)]
            po_h = [bank_o(), bank_o()]
            st1 = {}
            st2 = {}
            lsum_cur = [None, None]

            st0 = {}

            def att_s0(hd, kb):
                ps = slice(64 * hd, 64 * hd + 64)
                pz = bank()
                P.mm(pz, kT[ps, kb * 128:(kb + 1) * 128], qT[ps, tsl])
                e = eb()
                P.act(e, pz, AF.Exp)
                st0[(hd, kb)] = e

            def att_s1(hd, kb):
                e = st0.pop((hd, kb))
                if kb >= 4 * t:
                    i = kb - 4 * t
                    P.tt(POOL, e, e, maskF[:, i * 512:(i + 1) * 512], ALU.mult)
                L = Lb()
                P.act(L, e, AF.Ln, bias=onec[:, 0:1])
                carry = lsum_cur[hd]
                if kb > 0:
                    ns = lsumb[hd]()
                    if carry is None:
                        P.copy("dve", ns, L)
                    else:
                        P.tt("dve", ns, carry, L, ALU.add)
                    lsum_cur[hd] = ns
                st1[(hd, kb)] = (e, L, carry)

            def att_s2(hd, kb):
                e, L, carry = st1.pop((hd, kb))
                pr = bank()
                P.mm(pr, triNegB, L, start=True, stop=(carry is None))
                if carry is not None:
                    P.mm(pr, negOnesB, carry, start=False, stop=True)
                eR = eRb()
                P.act(eR, pr, AF.Exp)
                att = attb()
                P.tt(os.environ.get("MIX_ATTENG", "dve"), att, e, eR, ALU.mult)
                st2[(hd, kb)] = att

            def att_s3(hd, kb):
                att = st2.pop((hd, kb))
                P.mm(po_h[hd][0:64, :], vA[:, kb, 64 * hd:64 * hd + 64], att, start=(kb == nkb - 1), stop=(kb == 0))

            LOOK = int(os.environ.get("MIX_LOOK", "3"))
            SK = int(os.environ.get("MIX_SK", "2"))
            n_it = len(items)
            for idx in range(n_it + SK + 2 * LOOK):
                if idx < n_it:
                    att_s0(*items[idx])
                if SK <= idx < n_it + SK:
                    att_s1(*items[idx - SK])
                if SK + LOOK <= idx < n_it + SK + LOOK:
                    att_s2(*items[idx - SK - LOOK])
                if idx >= SK + 2 * LOOK:
                    att_s3(*items[idx - SK - 2 * LOOK])
            for hd in range(2):
                P.act(osb[hd], po_h[hd][0:64, :], AF.Copy)
                P.dma("sp", o_sb[64 * hd:64 * hd + 64, tsl], osb[hd])
        if do_sc:
            pB = proj_fm(C_SCB)
            P.act(scB, pB, AF.Copy)
            pC = proj_fm(C_SCC)
            P.act(scC, pC, AF.Copy)
            pH = proj_fm(C_SCH)
            if t > 0:
                P.copy("dve", rawc[:, 0:2], rawc[:, TN:TN + 2])
            P.tt("dve", rawc[:, 2:2 + TN], scC, pH, ALU.mult)
            P.ts("dve", scy, rawc[:, 0:TN], scw[:, 0:1], None, ALU.mult)
            for i in (1, 2):
                P.stt("dve", scy, rawc[:, i:i + TN], scw[:, i:i + 1], scy, ALU.mult, ALU.add)
            P.tt("dve", sco, scB, scy, ALU.mult)
            P.dma("sp", o_sc[:, tsl], sco)
        if do_gdn:
            for f in range(6):
                col0 = C_GQ + f * 128
                pg = proj_fm(col0)
                if t > 0:
                    P.copy("dve", raw[f][:, 0:3], raw[f][:, TN:TN + 3])
                P.act(raw[f][:, 3:3 + TN], pg, AF.Copy)
                P.ts("dve", ycv, raw[f][:, 0:TN], gcw[:, f * 4:f * 4 + 1], None, ALU.mult)
                for i in (1, 2, 3):
                    P.stt("dve", ycv, raw[f][:, i:i + TN], gcw[:, f * 4 + i:f * 4 + i + 1], ycv, ALU.mult, ALU.add)
                h_ = f % 2
                if f >= 4:
                    P.act(gvT[h_], ycv, AF.Silu)
                    continue
                P.act(ysl, ycv, AF.Silu)
                P.act(sqb, ysl, AF.Square)
                pss = bank()
                P.mm(pss, ones1, sqb)
                P.act(rnb, pss, AF.Sqrt, bias=epsc[:, 0:1])
                P.recip(rnb, rnb)
                if f < 2:
                    P.stt("dve", gqT[h_], ysl, 128.0 ** -0.5, rnb, ALU.mult, ALU.mult)
                else:
                    P.tt("dve", gkT[h_], ysl, rnb, ALU.mult)
            gb = gbank
            def scal_task(c):
                csl = slice(c * 128, (c + 1) * 128)
                S5 = sc5s[c % 3]
                szb = szbs[c % 3]
                ab_sb = ab_sbs[c % 3]
                pzz = gb()
                for k in range(KT):
                    P.mm(pzz[:, 0:256], hb[:, k, csl], wm[:, k, C_GZ:C_GZ + 256], start=(k == 0), stop=(k == KT - 1))
                P.act(szb, pzz[:, 0:256], AF.Silu)
                yield
                pab = gb()
                for k in range(KT):
                    P.mm(pab[:, 0:4], hb[:, k, csl], wm[:, k, C_AB:C_AB + 4], start=(k == 0), stop=(k == KT - 1))
                P.copy("dve", ab_sb, pab[:, 0:4])
                yield
                P.act(S5["beta"], ab_sb[:, 2:4], AF.Sigmoid)
                P.tt("dve", S5["tmp"], ab_sb[:, 0:2], dtb, ALU.add)
                yield
                P.ts("dve", S5["nbeta"], S5["beta"], -1.0, None, ALU.mult)
                P.act(S5["tmp"], S5["tmp"], AF.Exp)
                yield
                P.act(S5["tmp"], S5["tmp"], AF.Ln, bias=onec[:, 0:1])
                yield
                P.tt("dve", S5["g"], S5["tmp"], nA, ALU.mult)
                yield
                pgc = gb()
                P.mm(pgc[:, 0:2], triU, S5["g"])
                P.copy("dve", S5["gc"], pgc[:, 0:2])
                pgt = gb()
                P.mm(pgt[:, 0:2], ones1, S5["g"])
                P.copy("dve", S5["gtot"], pgt[:, 0:2])
                yield
                P.act(S5["egl"], S5["gtot"], AF.Exp)
                P.act(S5["eg"], S5["gc"], AF.Exp)
                P.tt("dve", S5["kd"], S5["gtot"], S5["gc"], ALU.subtract)
                yield
                P.tt("dve", S5["bg"], S5["beta"], S5["eg"], ALU.mult)
                P.act(S5["kd"], S5["kd"], AF.Exp)
                yield

            def prep_task(c, h_):
                csl = slice(c * 128, (c + 1) * 128)
                S5 = sc5s[c % 3]
                B = hbuf[h_]
                HO = hand[c % 2][h_]
                hs = slice(h_, h_ + 1)
                kTc = gkT[h_][:, csl]
                qTc = gqT[h_][:, csl]
                P.ts("dve", B["gU"], triU, S5["g"][:, hs], None, ALU.mult)
                P.ts("dve", B["dB"], identF, S5["beta"][:, hs], None, ALU.mult)
                yield
                pD1 = gb()[:, 0:128]
                P.mm(pD1, ones1, B["gU"])
                P.copy("dve", B["D1s"], pD1)
                yield
                pBR = gb()[:, 0:128]
                P.mm(pBR, ones1, B["dB"])
                P.copy("dve", B["BRs"], pBR)
                yield
                pKK = gb()[:, 0:128]
                P.mm(pKK, kTc, kTc)
                P.copy("dve", B["KKs"], pKK)
                yield
                P.act(B["egRow"], B["D1s"], AF.Exp)
                P.ts("dve", B["E1"], B["D1s"], S5["gc"][:, hs], None, ALU.subtract)
                yield
                P.act(B["E1"], B["E1"], AF.Abs)
                yield
                P.act(B["Dall"], B["E1"], AF.Exp, scale=-1.0)
                yield
                P.tt("dve", B["DL"], B["Dall"], SL, ALU.mult)
                P.tt("pool", B["DUb"], B["Dall"], SU, ALU.mult)
                P.tt("pool", B["DUI"], B["Dall"], UI, ALU.mult)
                yield
                P.tt("dve", B["DUb"], B["DUb"], B["BRs"], ALU.mult)
                N = B["Nb"]()
                NT_ = B["NTb"]()
                XT = B["XTb"]()
                P.stt("dve", N, B["KKs"], S5["nbeta"][:, hs], B["DL"], ALU.mult, ALU.mult)
                yield
                P.stt("dve", NT_, B["KKs"], -1.0, B["DUb"], ALU.mult, ALU.mult)
                yield
                P.tt("pool", XT, identF, NT_, ALU.add)
                ptk = tb2()
                P.transpose(ptk, kTc, identB)
                P.ts("dve", B["kbg"], ptk, S5["bg"][:, hs], None, ALU.mult)
                P.ts("dve", HO["kdec"], ptk, S5["kd"][:, hs], None, ALU.mult)
                yield
                ptv = tb2()
                P.transpose(ptv, gvT[h_][:, csl], identB)
                P.ts("dve", B["vb"], ptv, S5["beta"][:, hs], None, ALU.mult)
                yield
                pqk = gb()[:, 0:128]
                P.mm(pqk, kTc, qTc)
                P.tt("dve", HO["qkTs"], pqk, B["DUI"], ALU.mult)
                P.tt("pool", HO["qdT"], qTc, B["egRow"], ALU.mult)
                yield
                for kk in range(1, 7):
                    pN = gb()[:, 0:128]
                    P.mm(pN, NT_, N)
                    N2 = B["Nb"]()
                    P.copy("act", N2, pN)
                    yield
                    if kk < 6:
                        pNT = gb()[:, 0:128]
                        P.mm(pNT, N, NT_)
                        NT2 = B["NTb"]()
                        P.copy("dve", NT2, pNT)
                        yield
                    pX = gb()[:, 0:128]
                    P.mm(pX, N2, XT)
                    XT2 = B["XTb"]()
                    P.tt("dve", XT2, XT, pX, ALU.add)
                    N, XT = N2, XT2
                    if kk < 6:
                        NT_ = NT2
                    yield
                pu = gb()[:, 0:128]
                P.mm(pu, XT, B["vb"])
                P.copy("act", HO["u_sb"], pu)
                yield
                pw = gb()[:, 0:128]
                P.mm(pw, B["kbg"], XT)
                P.copy("act", HO["wTs"], pw)
                yield

            def scan_task(c, h_):
                S5 = sc5s[c % 3]
                szb = szbs[c % 3]
                og = ogs[c % 2]
                B = hbuf[h_]
                HO = hand[c % 2][h_]
                hs = slice(h_, h_ + 1)
                p1 = gb()[:, 0:128]
                P.mm(p1, HO["wTs"], Sbf[h_])
                P.tt("dve", B["vnew"], HO["u_sb"], p1, ALU.subtract)
                yield
                p2 = gb()[:, 0:128]
                P.mm(p2, HO["qdT"], Sbf[h_], start=True, stop=False)
                P.mm(p2, HO["qkTs"], B["vnew"], start=False, stop=True)
                P.copy("dve", B["o_s"], p2)
                p3 = gb()[:, 0:128]
                P.mm(p3, HO["kdec"], B["vnew"])
                P.stt("dve", Sst[h_], Sst[h_], S5["egl"][:, hs], p3, ALU.mult, ALU.add)
                yield
                P.copy("act", Sbf[h_], Sst[h_])
                P.act(B["junk"], B["o_s"], AF.Square, accum_out=B["ss1"])
                yield
                P.act(B["rs1"], B["ss1"], AF.Sqrt, scale=1.0 / 128.0, bias=epsc[:, 0:1])
                yield
                P.recip(B["rs1"], B["rs1"])
                yield
                P.stt("dve", B["onb"], B["o_s"], B["rs1"][:, 0:1], nwb, ALU.mult, ALU.mult)
                yield
                P.tt("dve", og[:, h_ * 128:(h_ + 1) * 128], B["onb"], szb[:, h_ * 128:(h_ + 1) * 128], ALU.mult)
                yield

            def run_tasks(tasks):
                tasks = list(tasks)
                while tasks:
                    for g in list(tasks):
                        try:
                            next(g)
                        except StopIteration:
                            tasks.remove(g)

            run_tasks([scal_task(0)])
            run_tasks([prep_task(0, 0), prep_task(0, 1), scal_task(1)])
            for c in range(4):
                csl = slice(c * 128, (c + 1) * 128)
                tl = [scan_task(c, 0), scan_task(c, 1)]
                if c + 1 < 4:
                    tl += [prep_task(c + 1, 0), prep_task(c + 1, 1)]
                if c + 2 < 4:
                    tl.append(scal_task(c + 2))
                run_tasks(tl)
                og = ogs[c % 2]
                for h_ in range(2):
                    ptg = tb2()
                    P.transpose(ptg, og[:, h_ * 128:(h_ + 1) * 128], identB)
                    P.copy("dve", ogT[:, h_, csl], ptg)
            for h_ in range(2):
                P.dma("sp", o_gdnT[h_ * 128:(h_ + 1) * 128, tsl], ogT[:, h_, :])


def build_mix(S, nc=None, do_attn=True, do_gdn=True, do_sc=True):
    if nc is None:
        nc = bass.Bass("TRN2", target_bir_lowering=False)
    P = Prog(nc)
    carr, coff = mix_consts()
    hT = P.dram("hT", [D, S], F32, "ExternalInput")
    wm_d = P.dram("wm", [D, NW], F32, "ExternalInput")
    cst_d = P.dram("cst", list(carr.shape), F32, "ExternalInput")
    gcw_d = P.dram("gcw", [128, 24], F32, "ExternalInput")
    scw_d = P.dram("scw", [128, 3], F32, "ExternalInput")
    hp_d = P.dram("hp", [128, 4], F32, "ExternalInput")
    nw_d = P.dram("nw", [128, 128], F32, "ExternalInput")
    o_sb = P.dram("o_sb", [128, S], BF16, "ExternalOutput")
    o_gdnT = P.dram("o_gdnT", [256, S], BF16, "ExternalOutput")
    o_sc = P.dram("o_sc", [128, S], BF16, "ExternalOutput")
    emit_mix(P, S, hT, wm_d, cst_d, gcw_d, scw_d, hp_d, nw_d, o_sb, o_gdnT, o_sc, do_attn, do_gdn, do_sc)
    st = P.finalize()
    return nc, st, carr


NSTEP = 4


def build_fused(S, depth=2, nc=None):
    if nc is None:
        nc = bass.Bass("TRN2", target_bir_lowering=False)
    P = Prog(nc)
    carr, _ = mix_consts()
    xT = P.dram("xT", [D, S], F32, "ExternalInput")
    outT = P.dram("outT", [D, S], F32, "ExternalOutput")
    lng = P.dram("lng", [128, depth * NSTEP * KT], F32, "ExternalInput")
    lnb = P.dram("lnb", [128, depth * NSTEP * KT], F32, "ExternalInput")
    cst = P.dram("cst", list(carr.shape), F32, "ExternalInput")
    W = []
    for i in range(depth):
        w = {}
        for f in range(2):
            w[f"w1{f}"] = P.dram(f"w1_{i}_{f}", [D, 2 * DFF], F32, "ExternalInput")
            w[f"w2{f}"] = P.dram(f"w2_{i}_{f}", [DFF, D], F32, "ExternalInput")
        for j in range(2):
            w[f"wm{j}"] = P.dram(f"wm_{i}_{j}", [D, NW], F32, "ExternalInput")
            w[f"gcw{j}"] = P.dram(f"gcw_{i}_{j}", [128, 24], F32, "ExternalInput")
            w[f"scw{j}"] = P.dram(f"scw_{i}_{j}", [128, 3], F32, "ExternalInput")
            w[f"hp{j}"] = P.dram(f"hp_{i}_{j}", [128, 4], F32, "ExternalInput")
        w["nw"] = P.dram(f"nw_{i}", [128, 128], F32, "ExternalInput")
        w["wo"] = P.dram(f"wo_{i}", [D, D], F32, "ExternalInput")
        w["wg"] = P.dram(f"wg_{i}", [D, D], F32, "ExternalInput")
        w["wp"] = P.dram(f"wp_{i}", [256, D], F32, "ExternalInput")
        w["bg"] = P.dram(f"bg_{i}", [128, KT], F32, "ExternalInput")
        w["pT"] = P.dram(f"pT_{i}", [256, S], F32, "ExternalInput")
        W.append(w)
    H = P.dram("H_scr", [D, S], F32, "Internal")
    X1 = P.dram("X1_scr", [D, S], F32, "Internal")
    X2 = P.dram("X2_scr", [D, S], F32, "Internal")
    MIX = P.dram("MIX_scr", [D, S], BF16, "Internal")

    def ln(i, s):
        o = (i * NSTEP + s) * KT
        return lng[:, o:o + KT], lnb[:, o:o + KT]

    cur = xT
    for i in range(depth):
        w = W[i]
        with P.stage(f"A{i}"):
            g, b = ln(i, 0)
            emit_dense(P, ["ffn"], S, cur, H, g, b, [(w["w10"], w["w20"])])
        for j in range(2):
            with P.stage(f"B{i}{j}"):
                emit_mix(P, S, H, w[f"wm{j}"], cst, w[f"gcw{j}"], w[f"scw{j}"], w[f"hp{j}"], w["nw"],
                         MIX[j * 128:(j + 1) * 128, :], MIX[256 + j * 256:256 + (j + 1) * 256, :],
                         MIX[768 + j * 128:768 + (j + 1) * 128, :])
        with P.stage(f"M{i}"):
            g, b = ln(i, 1)
            emit_dense(P, ["mixout"], S, H, X1, g, b, [(w["wo"], MIX)])
        with P.stage(f"F{i}"):
            g, b = ln(i, 2)
            emit_dense(P, ["ffn"], S, X1, X2, g, b, [(w["w11"], w["w21"])])
        with P.stage(f"P{i}"):
            g, b = ln(i, 3)
            dst = outT if i == depth - 1 else X1
            emit_dense(P, ["ple"], S, X2, dst, g, b, [(w["wg"], w["wp"], w["bg"], w["pT"])])
        cur = X1
    return nc, P.tot, carr


def _relay(v):
    return np.ascontiguousarray(np.asarray(v, np.float32).reshape(8, 128).T)


def fused_inputs(b, S, depth, carr, x, p, ln_g, ln_b, ffn_w_in, ffn_w_out, mix_w_in, gdn_conv_w, gdn_a_log,
                 gdn_dt_bias, gdn_norm_w, sc_conv_w, mix_w_out, ple_w_proj, ple_w_gate, ple_b_gate, shared=None):
    m = {} if shared is None else dict(shared)
    m["xT"] = np.ascontiguousarray(x[b].T)
    for i in range(depth):
        m[f"pT_{i}"] = np.ascontiguousarray(p[i, b].T)
    if shared is not None:
        return m
    m["cst"] = carr
    m["lng"] = np.ascontiguousarray(np.concatenate([_relay(ln_g[i, s]) for i in range(depth) for s in range(4)], 1))
    m["lnb"] = np.ascontiguousarray(np.concatenate([_relay(ln_b[i, s]) for i in range(depth) for s in range(4)], 1))
    OFF_SB = 768
    OFF_QKV = OFF_SB + 1536
    OFF_A = OFF_QKV + 512
    OFF_Bt = OFF_A + 4
    OFF_SC = OFF_Bt + 4
    for i in range(depth):
        for f in range(2):
            m[f"w1_{i}_{f}"] = np.ascontiguousarray(ffn_w_in[i, f])
            m[f"w2_{i}_{f}"] = np.ascontiguousarray(ffn_w_out[i, f])
        for j in range(2):
            cols = []
            for base in (0, 256, 512):
                cols.append(np.arange(base + j * 128, base + (j + 1) * 128))
            for base in (OFF_SB, OFF_SB + 512, OFF_SB + 1024, OFF_QKV):
                cols.append(np.arange(base + j * 256, base + (j + 1) * 256))
            cols.append(np.arange(OFF_A + j * 2, OFF_A + j * 2 + 2))
            cols.append(np.arange(OFF_Bt + j * 2, OFF_Bt + j * 2 + 2))
            for base in (OFF_SC, OFF_SC + 256, OFF_SC + 512):
                cols.append(np.arange(base + j * 128, base + (j + 1) * 128))
            cols = np.concatenate(cols)
            m[f"wm_{i}_{j}"] = np.ascontiguousarray(mix_w_in[i][:, cols])
            gidx = np.concatenate([np.arange(base + j * 256, base + (j + 1) * 256) for base in (0, 512, 1024)])
            m[f"gcw_{i}_{j}"] = np.ascontiguousarray(gdn_conv_w[i][:, gidx].reshape(4, 6, 128).transpose(2, 1, 0).reshape(128, 24))
            m[f"scw_{i}_{j}"] = np.ascontiguousarray(sc_conv_w[i][:, j * 128:(j + 1) * 128].T)
            m[f"hp_{i}_{j}"] = np.ascontiguousarray(np.tile(np.concatenate(
                [gdn_a_log[i][2 * j:2 * j + 2], gdn_dt_bias[i][2 * j:2 * j + 2]])[None, :], (128, 1)).astype(np.float32))
        m[f"nw_{i}"] = np.ascontiguousarray(np.tile(gdn_norm_w[i][None, :], (128, 1)).astype(np.float32))
        m[f"wo_{i}"] = np.ascontiguousarray(mix_w_out[i])
        m[f"wg_{i}"] = np.ascontiguousarray(ple_w_gate[i])
        m[f"wp_{i}"] = np.ascontiguousarray(ple_w_proj[i])
        m[f"bg_{i}"] = _relay(ple_b_gate[i])
    return m


from concourse.bass_utils import run_bass_kernel_spmd

BATCH, SEQ, DEPTH = 4, 8192, 2


def kernel(x, p, ln_g, ln_b, ffn_w_in, ffn_w_out, mix_w_in, gdn_conv_w, gdn_a_log,
           gdn_dt_bias, gdn_norm_w, sc_conv_w, mix_w_out, ple_w_proj, ple_w_gate, ple_b_gate):
    f = lambda a: np.asarray(a, np.float32)
    args = dict(x=f(x), p=f(p), ln_g=f(ln_g), ln_b=f(ln_b), ffn_w_in=f(ffn_w_in), ffn_w_out=f(ffn_w_out),
                mix_w_in=f(mix_w_in), gdn_conv_w=f(gdn_conv_w), gdn_a_log=f(gdn_a_log), gdn_dt_bias=f(gdn_dt_bias),
                gdn_norm_w=f(gdn_norm_w), sc_conv_w=f(sc_conv_w), mix_w_out=f(mix_w_out), ple_w_proj=f(ple_w_proj),
                ple_w_gate=f(ple_w_gate), ple_b_gate=f(ple_b_gate))
    nc, _, carr = build_fused(SEQ, DEPTH)
    m0 = fused_inputs(0, SEQ, DEPTH, carr, **args)
    shared = {k: v for k, v in m0.items() if k != "xT" and not k.startswith("pT_")}
    maps = [m0] + [fused_inputs(b, SEQ, DEPTH, carr, shared=shared, **args) for b in range(1, BATCH)]
    res = run_bass_kernel_spmd(nc, maps, core_ids=list(range(BATCH)))
    out = np.empty((BATCH, SEQ, D), np.float32)
    for b in range(BATCH):
        out[b] = res.results[b]["outT"].T
    return out
```
